# Optimizing a Trainium2 kernel written in Bass

```python
import jax
import jax.numpy as jnp
from jax import lax
import numpy as np

D_MODEL = 2048
BATCH = 4
SEQ = 4096
DEPTH = 4

GRID_W = 64
CTX_LEN = 256
N_EVEN = (DEPTH + 1) // 2
N_ODD = DEPTH // 2
EPS = 1e-6
NEG_INF = -1e30
S5_W = D_MODEL // 2
S5_CH = 16
S5_G = S5_W // S5_CH
S5_P = 64
M_W = D_MODEL // 2
M_HEADS = 4
M_DH = M_W // M_HEADS
M_CHUNK = 128
D_IN = S5_W + 4 * M_W + 4 * M_HEADS
D_MIX = S5_W + M_W
R_HEAD = 64
R_HEADS = D_MODEL // R_HEAD
R_DECAY_LORA = max(32, int(round(1.8 * D_MODEL ** 0.5 / 32)) * 32)
R_AAA_LORA = max(32, int(round(1.8 * D_MODEL ** 0.5 / 32)) * 32)
R_MV_LORA = max(32, int(round(1.3 * D_MODEL ** 0.5 / 32)) * 32)
R_GATE_LORA = max(32, int(round(0.6 * D_MODEL ** 0.8 / 32)) * 32)
R_LN_EPS = 64e-5
D_FF = ((8 * D_MODEL // 3 + 255) // 256) * 256

kernel_name = 'hybrid_s5_mlstm_rwkv7_prefix_dit'

F32 = jnp.float32


def _flip(t, rev, axis=1):
    return jnp.flip(t, axis=axis) if rev else t


def rmsnorm(x, w):
    xf = x.astype(F32)
    return xf * lax.rsqrt(jnp.mean(xf * xf, axis=-1, keepdims=True) + EPS) * w


def dwconv3(x, w, b):
    xp = jnp.pad(x, ((0, 0), (1, 1), (0, 0)))
    return xp[:, :-2] * w[0] + x * w[1] + xp[:, 2:] * w[2] + b


def qshift_grid(h):
    B, L, D = h.shape
    rows = L // GRID_W
    g = h.reshape(B, rows, GRID_W, D)
    q = D // 4
    left = jnp.pad(g[:, :, :-1, :q], ((0, 0), (0, 0), (1, 0), (0, 0)))
    right = jnp.pad(g[:, :, 1:, q:2 * q], ((0, 0), (0, 0), (0, 1), (0, 0)))
    up = jnp.pad(g[:, :-1, :, 2 * q:3 * q], ((0, 0), (1, 0), (0, 0), (0, 0)))
    down = jnp.pad(g[:, 1:, :, 3 * q:], ((0, 0), (0, 1), (0, 0), (0, 0)))
    return jnp.concatenate([left, right, up, down], axis=-1).reshape(B, L, D)


def shift_seq(h):
    half = h.shape[-1] // 2
    prev = jnp.pad(h[:, :-1, :half], ((0, 0), (1, 0), (0, 0)))
    nxt = jnp.pad(h[:, 1:, half:], ((0, 0), (0, 1), (0, 0)))
    return jnp.concatenate([prev, nxt], axis=-1)


def conv_ffn(h, w_up, conv_w, conv_b, w_down):
    u = dwconv3(h @ w_up, conv_w, conv_b)
    a, g = jnp.split(u, 2, axis=-1)
    return (a * jax.nn.silu(g)) @ w_down


def s5_discretize(lam_re, lam_im, log_step, b_re, b_im):
    step = jnp.exp(log_step.astype(F32))[:, None]
    lr, li = lam_re.astype(F32), lam_im.astype(F32)
    mag = jnp.exp(lr * step)
    ab_re, ab_im = mag * jnp.cos(li * step), mag * jnp.sin(li * step)
    den = lr * lr + li * li
    co_re = ((ab_re - 1.0) * lr + ab_im * li) / den
    co_im = (ab_im * lr - (ab_re - 1.0) * li) / den
    bb_re = co_re[..., None] * b_re - co_im[..., None] * b_im
    bb_im = co_re[..., None] * b_im + co_im[..., None] * b_re
    return ab_re, ab_im, bb_re, bb_im


def _complex_affine_combine(e1, e2):
    a1r, a1i, b1r, b1i = e1
    a2r, a2i, b2r, b2i = e2
    return (a1r * a2r - a1i * a2i, a1r * a2i + a1i * a2r,
            a2r * b1r - a2i * b1i + b2r, a2r * b1i + a2i * b1r + b2i)


def s5_scan(u, ab_re, ab_im, bb_re, bb_im, h0_re, h0_im):
    L = u.shape[1]
    bu_re = jnp.einsum('blgc,gpc->blgp', u, bb_re)
    bu_im = jnp.einsum('blgc,gpc->blgp', u, bb_im)
    bu_re = bu_re.at[:, 0].add(ab_re * h0_re - ab_im * h0_im)
    bu_im = bu_im.at[:, 0].add(ab_re * h0_im + ab_im * h0_re)
    a_re = jnp.broadcast_to(ab_re, (1, L) + ab_re.shape)
    a_im = jnp.broadcast_to(ab_im, (1, L) + ab_im.shape)
    _, _, h_re, h_im = lax.associative_scan(_complex_affine_combine, (a_re, a_im, bu_re, bu_im), axis=1)
    return h_re, h_im


def s5_readout(h_re, h_im, c_re, c_im):
    return jnp.einsum('gcp,blgp->blgc', c_re, h_re) - jnp.einsum('gcp,blgp->blgc', c_im, h_im)


def s5_output(y, u, p):
    B, L = y.shape[:2]
    y = jax.nn.gelu((y + p['d'].reshape(S5_G, S5_CH) * u).reshape(B, L, S5_W))
    return y * jax.nn.sigmoid(y @ p['w_glu'] + p['b_glu'])


def s5_mixer(u_c, u_l, p, need_ctx):
    def groups(u):
        B, L, _ = u.shape
        return u.astype(F32).reshape(B, L, S5_G, S5_CH)
    uc, ul = groups(u_c), groups(u_l)
    zero = jnp.zeros((ul.shape[0], S5_G, S5_P), F32)
    ys_c, ys_l = [], []
    for d in range(2):
        rev = d == 1
        ab_re, ab_im, bb_re, bb_im = s5_discretize(p['lam_re'][d], p['lam_im'][d], p['log_step'][d], p['b_re'][d], p['b_im'][d])
        hc_re, hc_im = s5_scan(_flip(uc, rev), ab_re, ab_im, bb_re, bb_im, zero, zero)
        hl_re, hl_im = s5_scan(_flip(ul, rev), ab_re, ab_im, bb_re, bb_im, hc_re[:, -1], hc_im[:, -1])
        ys_l.append(_flip(s5_readout(hl_re, hl_im, p['c_re'][d], p['c_im'][d]), rev))
        if need_ctx:
            ys_c.append(_flip(s5_readout(hc_re, hc_im, p['c_re'][d], p['c_im'][d]), rev))
    out_l = s5_output(ys_l[0] + ys_l[1], ul, p)
    out_c = s5_output(ys_c[0] + ys_c[1], uc, p) if need_ctx else None
    return out_c, out_l


def mlstm_chunkwise(q, k, v, i_pre, f_pre, state, emit):
    B, H, L, DH = q.shape
    nc = L // M_CHUNK

    def chunks(t):
        return jnp.moveaxis(t.reshape((B, H, nc, M_CHUNK) + t.shape[3:]), 2, 0)

    tri = jnp.tril(jnp.ones((M_CHUNK, M_CHUNK), dtype=bool))

    def step(carry, xs):
        C, n, m = carry
        qc, kc, vc, ic, lf = xs
        b = jnp.cumsum(lf, axis=-1)
        b_last = b[..., -1]
        lw_state = b_last[..., None] - b + ic
        m_new = jnp.maximum(b_last + m, lw_state.max(-1))
        e_state = jnp.exp(lw_state - m_new[..., None])
        carry_w = jnp.exp(b_last + m - m_new)
        C_new = carry_w[..., None, None] * C + jnp.einsum('bhj,bhjd,bhje->bhde', e_state, kc, vc)
        n_new = carry_w[..., None] * n + jnp.einsum('bhj,bhjd->bhd', e_state, kc)
        if not emit:
            return (C_new, n_new, m_new), None
        logw = jnp.where(tri, b[..., :, None] - b[..., None, :] + ic[..., None, :], NEG_INF)
        inter = b + m[..., None]
        m_row = jnp.maximum(logw.max(-1), inter)
        s = jnp.einsum('bhid,bhjd->bhij', qc, kc) * jnp.exp(logw - m_row[..., None])
        w_inter = jnp.exp(inter - m_row)
        num = s @ vc + w_inter[..., None] * jnp.einsum('bhid,bhde->bhie', qc, C)
        den = s.sum(-1) + w_inter * jnp.einsum('bhid,bhd->bhi', qc, n)
        h = num / jnp.maximum(jnp.abs(den), jnp.exp(-m_row))[..., None]
        return (C_new, n_new, m_new), h

    logf = jax.nn.log_sigmoid(f_pre)
    state, hs = lax.scan(step, state, (chunks(q), chunks(k), chunks(v), chunks(i_pre), chunks(logf)))
    h = jnp.moveaxis(hs, 0, 2).reshape(B, H, L, DH) if emit else None
    return h, state


def mlstm_mixer(qk_c, v_c, o_c, g_c, qk_l, v_l, o_l, g_l, p, need_ctx):
    def heads(t):
        B, L, _ = t.shape
        return t.reshape(B, L, M_HEADS, M_DH).transpose(0, 2, 1, 3).astype(F32)

    def prep(qk, v, g):
        B, L, _ = qk.shape
        qk = jax.nn.silu(dwconv3(qk, p['conv_w'], p['conv_b']))
        q = heads(qk[..., :M_W])
        k = heads(qk[..., M_W:]) * (M_DH ** -0.5)
        gates = jnp.transpose(g.astype(F32).reshape(B, L, 2, 2, M_HEADS), (2, 3, 0, 4, 1))
        return q, k, heads(v), gates

    def out(h, o):
        B, H, L, DH = h.shape
        h = h * lax.rsqrt(jnp.mean(h * h, axis=-1, keepdims=True) + EPS)
        h = h.transpose(0, 2, 1, 3).reshape(B, L, M_W) * p['norm']
        return h * jax.nn.sigmoid(o)

    qc, kc, vc, gc = prep(qk_c, v_c, g_c)
    ql, kl, vl, gl = prep(qk_l, v_l, g_l)
    B = ql.shape[0]
    st0 = (jnp.zeros((B, M_HEADS, M_DH, M_DH), F32), jnp.zeros((B, M_HEADS, M_DH), F32),
           jnp.full((B, M_HEADS), NEG_INF, F32))
    hs_c, hs_l = [], []
    for d in range(2):
        rev = d == 1
        f = lambda t: _flip(t, rev, axis=2)
        hc, st = mlstm_chunkwise(f(qc), f(kc), f(vc), f(gc[d, 0]), f(gc[d, 1]), st0, need_ctx)
        hl, _ = mlstm_chunkwise(f(ql), f(kl), f(vl), f(gl[d, 0]), f(gl[d, 1]), st, True)
        hs_l.append(f(hl))
        if need_ctx:
            hs_c.append(f(hc))
    out_l = out(hs_l[0] + hs_l[1], o_l)
    out_c = out(hs_c[0] + hs_c[1], o_c) if need_ctx else None
    return out_c, out_l


def even_mixer(h_ctx, h_lat, p, need_ctx):
    cuts = [S5_W, S5_W + 2 * M_W, S5_W + 3 * M_W, S5_W + 4 * M_W]
    u_c, qk_c, v_c, o_c, g_c = jnp.split(h_ctx @ p['w_in'] + p['b_in'], cuts, axis=-1)
    u_l, qk_l, v_l, o_l, g_l = jnp.split(h_lat @ p['w_in'] + p['b_in'], cuts, axis=-1)
    s5_c, s5_l = s5_mixer(u_c, u_l, p, need_ctx)
    ml_c, ml_l = mlstm_mixer(qk_c, v_c, o_c, g_c, qk_l, v_l, o_l, g_l, p, need_ctx)
    out_l = jnp.concatenate([s5_l, ml_l], axis=-1) @ p['w_out']
    out_c = jnp.concatenate([s5_c, ml_c], axis=-1) @ p['w_out'] if need_ctx else None
    return out_c, out_l


def rwkv7_project(h, xx, p, v_first):
    B, L, D = h.shape
    xr, xw, xk, xv, xa, xg = [h + xx * p['mu'][i] for i in range(6)]
    r = xr @ p['w_r']
    k = xk @ p['w_k']
    v = xv @ p['w_v']
    if v_first is not None:
        v = v + (v_first - v) * jax.nn.sigmoid(p['v0'] + (xv @ p['v1']) @ p['v2'])
    hd = lambda t: t.reshape(B, L, R_HEADS, R_HEAD).astype(F32)
    kk = hd(k * p['k_k'])
    kk = kk * lax.rsqrt(jnp.maximum(jnp.sum(kk * kk, axis=-1, keepdims=True), 1e-24))
    decays, ks, bs = [], [], []
    for d in range(2):
        wlog = -jax.nn.softplus(-(p['w0'][d] + jnp.tanh(xw @ p['w1'][d]) @ p['w2'][d])) - 0.5
        a = jax.nn.sigmoid(p['a0'][d] + (xa @ p['a1'][d]) @ p['a2'][d])
        decays.append(hd(jnp.exp(-jnp.exp(wlog))))
        ks.append(hd(k * (1.0 + (a - 1.0) * p['k_a'])))
        bs.append(kk * hd(a))
    return dict(r=hd(r), v=v, vh=hd(v), kk=kk, decay=decays, k=ks, b=bs, xg=xg)


def rwkv7_scan(r, w, k, v, a, b, S0, emit):
    def step(S, xs):
        rt, wt, kt, vt, at, bt = xs
        sa = jnp.einsum('bhvk,bhk->bhv', S, at)
        S = S * wt[:, :, None, :] + sa[..., None] * bt[:, :, None, :] + vt[..., None] * kt[:, :, None, :]
        return S, (jnp.einsum('bhvk,bhk->bhv', S, rt) if emit else None)
    xs = tuple(jnp.moveaxis(t, 1, 0) for t in (r, w, k, v, a, b))
    S, ys = lax.scan(step, S0, xs)
    return (jnp.moveaxis(ys, 0, 1) if emit else None), S


def rwkv7_output(y, q, p):
    B, L, H, N = y.shape
    mu = jnp.mean(y, axis=-1, keepdims=True)
    var = jnp.mean(jnp.square(y - mu), axis=-1, keepdims=True)
    y = ((y - mu) * lax.rsqrt(var + R_LN_EPS)).reshape(B, L, D_MODEL) * p['ln_w'] + p['ln_b']
    bonus = (jnp.sum(q['r'] * (q['k'][0] + q['k'][1]) * p['r_k'], axis=-1, keepdims=True) * q['vh']).reshape(B, L, D_MODEL)
    g = jax.nn.sigmoid(q['xg'] @ p['g1']) @ p['g2']
    return ((y + bonus) * g) @ p['w_o']


def rwkv7_mixer(h_c, h_l, p, vf_c, vf_l, need_ctx):
    pc = rwkv7_project(h_c, shift_seq(h_c) - h_c, p, vf_c)
    pl = rwkv7_project(h_l, qshift_grid(h_l) - h_l, p, vf_l)
    S0 = jnp.zeros((h_l.shape[0], R_HEADS, R_HEAD, R_HEAD), F32)
    ys_c, ys_l = [], []
    for d in range(2):
        rev = d == 1
        args_c = [_flip(t, rev) for t in (pc['r'], pc['decay'][d], pc['k'][d], pc['vh'], -pc['kk'], pc['b'][d])]
        yc, S = rwkv7_scan(*args_c, S0, need_ctx)
        args_l = [_flip(t, rev) for t in (pl['r'], pl['decay'][d], pl['k'][d], pl['vh'], -pl['kk'], pl['b'][d])]
        yl, _ = rwkv7_scan(*args_l, S, True)
        ys_l.append(_flip(yl, rev))
        if need_ctx:
            ys_c.append(_flip(yc, rev))
    out_l = rwkv7_output(ys_l[0] + ys_l[1], pl, p)
    out_c = rwkv7_output(ys_c[0] + ys_c[1], pc, p) if need_ctx else None
    return out_c, out_l, pc['v'], pl['v']


def setup_inputs(seed: int = 0) -> dict:
    key = jax.random.key(seed)
    ks = iter(jax.random.split(key, 64))

    def nrm(shape, scale):
        return jax.random.normal(next(ks), shape, F32) * scale

    def unif(shape, lo, hi):
        return jax.random.uniform(next(ks), shape, F32, lo, hi)

    D = D_MODEL
    g0 = S5_W + 4 * M_W
    fbias = jnp.linspace(3.0, 6.0, M_HEADS, dtype=F32)
    b_in = nrm((N_EVEN, D_IN), 0.02)
    b_in = b_in.at[:, g0 + M_HEADS:g0 + 2 * M_HEADS].add(fbias).at[:, g0 + 3 * M_HEADS:].add(fbias)
    n_idx = jnp.arange(S5_P, dtype=F32)
    decay_profile = -6.5 + 5.0 * (jnp.arange(D, dtype=F32) / (D - 1)) ** 0.9
    return {
        'x': nrm((BATCH, SEQ, D), 1.0),
        'c': nrm((BATCH, D), 1.0),
        'ctx': nrm((BATCH, CTX_LEN, D), 1.0),
        'c_ctx': nrm((D,), 1.0),
        'ada_w': nrm((DEPTH, D, 6 * D), 0.5 * D ** -0.5),
        'ada_b': nrm((DEPTH, 6 * D), 0.02),
        'norm_mix': 1.0 + nrm((DEPTH, D), 0.02),
        'norm_ffn': 1.0 + nrm((DEPTH, D), 0.02),
        'ffn_w_up': nrm((DEPTH, D, 2 * D_FF), D ** -0.5),
        'ffn_conv_w': nrm((DEPTH, 3, 2 * D_FF), 3 ** -0.5),
        'ffn_conv_b': nrm((DEPTH, 2 * D_FF), 0.02),
        'ffn_w_down': nrm((DEPTH, D_FF, D), D_FF ** -0.5),
        'norm_final': 1.0 + nrm((D,), 0.02),
        'ev_w_in': nrm((N_EVEN, D, D_IN), D ** -0.5),
        'ev_b_in': b_in,
        'ev_w_out': nrm((N_EVEN, D_MIX, D), D_MIX ** -0.5),
        's5_lam_re': -0.5 + nrm((N_EVEN, 2, S5_G, S5_P), 0.01),
        's5_lam_im': jnp.pi * n_idx + nrm((N_EVEN, 2, S5_G, S5_P), 0.01),
        's5_log_step': unif((N_EVEN, 2, S5_G), float(np.log(1e-3)), float(np.log(1e-1))),
        's5_b_re': nrm((N_EVEN, 2, S5_G, S5_P, S5_CH), (2 * S5_CH) ** -0.5),
        's5_b_im': nrm((N_EVEN, 2, S5_G, S5_P, S5_CH), (2 * S5_CH) ** -0.5),
        's5_c_re': nrm((N_EVEN, 2, S5_G, S5_CH, S5_P), S5_P ** -0.5),
        's5_c_im': nrm((N_EVEN, 2, S5_G, S5_CH, S5_P), S5_P ** -0.5),
        's5_d': nrm((N_EVEN, S5_W), 1.0),
        's5_w_glu': nrm((N_EVEN, S5_W, S5_W), S5_W ** -0.5),
        's5_b_glu': nrm((N_EVEN, S5_W), 0.02),
        'ml_conv_w': nrm((N_EVEN, 3, 2 * M_W), 3 ** -0.5),
        'ml_conv_b': nrm((N_EVEN, 2 * M_W), 0.02),
        'ml_norm': 1.0 + nrm((N_EVEN, M_W), 0.02),
        'rw_mu': unif((N_ODD, 6, D), 0.0, 1.0),
        'rw_w_r': nrm((N_ODD, D, D), D ** -0.5),
        'rw_w_k': nrm((N_ODD, D, D), D ** -0.5),
        'rw_w_v': nrm((N_ODD, D, D), D ** -0.5),
        'rw_w_o': nrm((N_ODD, D, D), D ** -0.5),
        'rw_w0': decay_profile + nrm((N_ODD, 2, D), 0.1),
        'rw_w1': nrm((N_ODD, 2, D, R_DECAY_LORA), D ** -0.5),
        'rw_w2': nrm((N_ODD, 2, R_DECAY_LORA, D), 0.1 * R_DECAY_LORA ** -0.5),
        'rw_a0': nrm((N_ODD, 2, D), 0.1),
        'rw_a1': nrm((N_ODD, 2, D, R_AAA_LORA), D ** -0.5),
        'rw_a2': nrm((N_ODD, 2, R_AAA_LORA, D), 0.1 * R_AAA_LORA ** -0.5),
        'rw_v0': 1.0 + nrm((N_ODD - 1, D), 0.1),
        'rw_v1': nrm((N_ODD - 1, D, R_MV_LORA), D ** -0.5),
        'rw_v2': nrm((N_ODD - 1, R_MV_LORA, D), 0.1 * R_MV_LORA ** -0.5),
        'rw_g1': nrm((N_ODD, D, R_GATE_LORA), D ** -0.5),
        'rw_g2': nrm((N_ODD, R_GATE_LORA, D), R_GATE_LORA ** -0.5),
        'rw_k_k': 0.85 + nrm((N_ODD, D), 0.02),
        'rw_k_a': 1.0 + nrm((N_ODD, D), 0.02),
        'rw_r_k': nrm((N_ODD, R_HEADS, R_HEAD), 0.1),
        'rw_ln_w': 1.0 + nrm((N_ODD, D), 0.02),
        'rw_ln_b': nrm((N_ODD, D), 0.02),
    }


def reference(x, c, ctx, c_ctx, ada_w, ada_b, norm_mix, norm_ffn, ffn_w_up, ffn_conv_w, ffn_conv_b, ffn_w_down,
              norm_final, ev_w_in, ev_b_in, ev_w_out, s5_lam_re, s5_lam_im, s5_log_step, s5_b_re, s5_b_im,
              s5_c_re, s5_c_im, s5_d, s5_w_glu, s5_b_glu, ml_conv_w, ml_conv_b, ml_norm, rw_mu, rw_w_r, rw_w_k,
              rw_w_v, rw_w_o, rw_w0, rw_w1, rw_w2, rw_a0, rw_a1, rw_a2, rw_v0, rw_v1, rw_v2, rw_g1, rw_g2,
              rw_k_k, rw_k_a, rw_r_k, rw_ln_w, rw_ln_b):
    v_first_ctx = None
    v_first_lat = None
    for l in range(DEPTH):
        need_ctx = l < DEPTH - 1
        j = l // 2
        mod_lat = (jax.nn.silu(c) @ ada_w[l] + ada_b[l])[:, None, :]
        mod_ctx = jax.nn.silu(c_ctx) @ ada_w[l] + ada_b[l]
        sh1, sc1, g1, sh2, sc2, g2 = jnp.split(mod_lat, 6, axis=-1)
        csh1, csc1, cg1, csh2, csc2, cg2 = jnp.split(mod_ctx, 6, axis=-1)
        h_lat = rmsnorm(x, norm_mix[l]) * (1.0 + sc1) + sh1
        h_ctx = rmsnorm(ctx, norm_mix[l]) * (1.0 + csc1) + csh1
        if l % 2 == 0:
            p = dict(w_in=ev_w_in[j], b_in=ev_b_in[j], w_out=ev_w_out[j], lam_re=s5_lam_re[j], lam_im=s5_lam_im[j],
                     log_step=s5_log_step[j], b_re=s5_b_re[j], b_im=s5_b_im[j], c_re=s5_c_re[j], c_im=s5_c_im[j],
                     d=s5_d[j], w_glu=s5_w_glu[j], b_glu=s5_b_glu[j], conv_w=ml_conv_w[j], conv_b=ml_conv_b[j],
                     norm=ml_norm[j])
            o_ctx, o_lat = even_mixer(h_ctx, h_lat, p, need_ctx)
        else:
            p = dict(mu=rw_mu[j], w_r=rw_w_r[j], w_k=rw_w_k[j], w_v=rw_w_v[j], w_o=rw_w_o[j], w0=rw_w0[j],
                     w1=rw_w1[j], w2=rw_w2[j], a0=rw_a0[j], a1=rw_a1[j], a2=rw_a2[j], g1=rw_g1[j], g2=rw_g2[j],
                     k_k=rw_k_k[j], k_a=rw_k_a[j], r_k=rw_r_k[j], ln_w=rw_ln_w[j], ln_b=rw_ln_b[j])
            if j > 0:
                p.update(v0=rw_v0[j - 1], v1=rw_v1[j - 1], v2=rw_v2[j - 1])
            o_ctx, o_lat, v_ctx, v_lat = rwkv7_mixer(h_ctx, h_lat, p, v_first_ctx, v_first_lat, need_ctx)
            if j == 0:
                v_first_ctx, v_first_lat = v_ctx, v_lat
        x = x + g1 * o_lat
        x = x + g2 * conv_ffn(rmsnorm(x, norm_ffn[l]) * (1.0 + sc2) + sh2,
                              ffn_w_up[l], ffn_conv_w[l], ffn_conv_b[l], ffn_w_down[l])
        if need_ctx:
            ctx = ctx + cg1 * o_ctx
            ctx = ctx + cg2 * conv_ffn(rmsnorm(ctx, norm_ffn[l]) * (1.0 + csc2) + csh2,
                                       ffn_w_up[l], ffn_conv_w[l], ffn_conv_b[l], ffn_w_down[l])
    return rmsnorm(x, norm_final)
```

```python
import contextlib
import numpy as np
import concourse.bass as bass
import concourse.mybir as mybir
from concourse.bass_utils import run_bass_kernel_spmd

F32 = mybir.dt.float32
F32R = mybir.dt.float32r
import os as _os0
USE_R = _os0.environ.get("K_F32R", "1") == "1"


def rnd(ap):
    return ap.bitcast(F32R) if USE_R else ap
I32 = mybir.dt.int32
AF = mybir.ActivationFunctionType
ALU = mybir.AluOpType
AX = mybir.AxisListType

D = 2048
DC = D // 128
TC = 256
TL = 4096
T = TC + TL
DEPTH = 4
DFF = 5632
EPS = 1e-6
S5W = 1024
MW = 1024
DIN = S5W + 4 * MW + 16
PI = float(np.pi)


class Res:
    __slots__ = ("w", "r")

    def __init__(self):
        self.w = None
        self.r = {}


class TileW(Res):
    __slots__ = ("t", "sub")

    def __init__(self, t):
        super().__init__()
        self.t = t
        self.sub = None

    def __getitem__(self, k):
        return self.t[k]


class Builder:
    SEM_LIMIT = 20000

    def __init__(self):
        self.nc = bass.Bass("TRN2", target_bir_lowering=False)
        nc = self.nc
        self.es = contextlib.ExitStack()
        self.eng = {"pe": nc.tensor, "act": nc.scalar, "dve": nc.vector, "pool": nc.gpsimd, "sp": nc.sync}
        self.sem = {}
        self.cnt = {}
        self.nsem = 0
        for e in self.eng:
            self._new_sem(e)
        self.seen = {e: {} for e in self.eng}
        self.dma_sems = []
        for i in range(12):
            s = self.es.enter_context(nc.semaphore("dq%d" % i))
            self.dma_sems.append([s, 0])
        self.dma_rr = 0
        self.all_res = []
        self.ninst = 0

    def _new_sem(self, e):
        self.nsem += 1
        self.sem[e] = self.es.enter_context(self.nc.semaphore("s_%s_%d" % (e, self.nsem)))
        self.cnt[e] = 0

    def sb(self, stack, name, shape, dt=F32):
        self.nalloc = getattr(self, "nalloc", 0) + 1
        name = "sb%d_%s" % (self.nalloc, name)
        t = stack.enter_context(self.nc.sbuf_tensor(name, list(shape), dt))
        return TileW(t)

    def ps(self, stack, name, shape, dt=F32):
        self.nalloc = getattr(self, "nalloc", 0) + 1
        name = "ps%d_%s" % (self.nalloc, name)
        t = stack.enter_context(self.nc.psum_tensor(name, list(shape), dt))
        return TileW(t)

    def _wait(self, e, tok):
        if tok is None:
            return
        sem, val = tok
        k = id(sem)
        cur = self.seen[e].get(k)
        if cur is not None and cur[1] >= val:
            return
        self.eng[e].wait_ge(sem, val)
        self.seen[e][k] = (sem, val)

    def _deps(self, e, reads, writes, pe_acc=False):
        for r in reads:
            self._wait(e, r.w)
        for w in writes:
            if not (pe_acc and e == "pe"):
                self._wait(e, w.w)
            for oe, tok in w.r.items():
                self._wait(e, tok)

    def _mark(self, tok, e, reads, writes):
        for r in reads:
            r.r[(e, id(tok[0]))] = tok
        for w in writes:
            w.w = tok
            w.r = {}

    def op(self, e, fn, reads=(), writes=(), pe_acc=False):
        self._deps(e, reads, writes, pe_acc)
        if self.cnt[e] >= self.SEM_LIMIT:
            self._new_sem(e)
        ins = fn(self.eng[e])
        self.cnt[e] += 1
        ins.then_inc(self.sem[e], 1)
        tok = (self.sem[e], self.cnt[e])
        self._mark(tok, e, reads, writes)
        self.ninst += 1
        return tok

    def dma(self, out, in_, reads=(), writes=(), q="sp"):
        self._deps(q, reads, writes)
        ent = self.dma_sems[self.dma_rr]
        self.dma_rr = (self.dma_rr + 1) % len(self.dma_sems)
        if ent[1] >= self.SEM_LIMIT:
            self._wait(q, (ent[0], ent[1]))
            ent[0] = self.es.enter_context(self.nc.semaphore("dq_n%d" % self.ninst))
            ent[1] = 0
        self._wait(q, (ent[0], ent[1]))
        self.eng[q].dma_start(out=out, in_=in_).then_inc(ent[0], 16)
        ent[1] += 16
        tok = (ent[0], ent[1])
        self._mark(tok, "dma", reads, writes)
        self.ninst += 1
        return tok

    def barrier(self):
        toks = [(self.sem[e], self.cnt[e]) for e in self.eng if self.cnt[e] > 0]
        toks += [(s, v) for s, v in self.dma_sems if v > 0]
        for e in self.eng:
            for tok in toks:
                self._wait(e, tok)

    def finish(self):
        self.barrier()


def tiles_tokens():
    out = [(0, TC, 1)]
    for i in range(TL // 512):
        out.append((TC + i * 512, 512, 0))
    return out


class Prog:
    def __init__(self, layers=DEPTH, mixers=True, debug=False):
        self.debug = debug
        self.b = Builder()
        self.nc = self.b.nc
        self.layers = layers
        self.mixers = mixers
        self.inputs = {}
        self.build()

    def din(self, name, shape):
        self.inputs[name] = tuple(shape)
        return self.nc.dram_tensor(name, list(shape), F32, kind="ExternalInput").ap()

    def dscr(self, name, shape):
        return self.nc.dram_tensor(name, list(shape), F32, kind="Internal").ap()

    def build(self):
        b, nc = self.b, self.nc
        self.xT = self.din("xT", [D, T])
        self.cT = self.din("cT", [128, DC, 2])
        self.ada_w = self.din("ada_w", [self.layers, D, 6 * D])
        self.ada_bT = self.din("ada_bT", [128, DEPTH, 48 * 2])
        self.nmixT = self.din("nmixT", [128, DEPTH, DC])
        self.nffnT = self.din("nffnT", [128, DEPTH, DC])
        self.nfinT = self.din("nfinT", [128, DC])
        self.w_up = self.din("w_up", [self.layers, 88, 128, 16, 128])
        self.convwT = self.din("convwT", [128, DEPTH, 3, 88])
        self.convbT = self.din("convbT", [128, DEPTH, 88])
        self.w_down = self.din("w_down", [self.layers, 16, 128, 44, 128])
        self.ones_in = self.din("ones", [128, 128])
        self.out = self.nc.dram_tensor("outT", [D, TL], F32, kind="ExternalOutput").ap()
        NE = (self.layers + 1) // 2
        self.NE = NE
        if self.mixers and NE > 0:
            self.w_in = self.din("w_in", [NE, D, DIN])
            self.binT = self.din("binT", [128, 2, 40])
            self.bin_row = self.din("bin_row", [1, 2, DIN])
            self.w_out = self.din("w_out", [NE, 16, 128, 16, 128])
            self.w_in_r = self.din("w_in_r", [NE, 24, 128, 16, 128])
            self.mcwT = self.din("mcwT", [128, 2, 3, 16])
            self.mcbT = self.din("mcbT", [128, 2, 16])
            self.mlnorm = self.din("mlnorm", [128, 2, 1024])
            self.w_glu = self.din("w_glu", [NE, 8, 128, 8, 128])
            self.bgluT = self.din("bgluT", [128, 2, 8])
            self.s5lam = self.din("s5lam", [128, 2, 2, 2, 64])
            self.s5ls = self.din("s5ls", [128, 2, 2, 64])
            self.s5BX = self.din("s5BX", [128, 2, 2, 64, 16])
            self.s5BY = self.din("s5BY", [128, 2, 2, 64, 16])
            self.s5CX = self.din("s5CX", [128, 2, 2, 64, 16])
            self.s5sgn = self.din("s5sgn", [128, 2])
            self.s5gmask = self.din("s5gmask", [128, 8])
            self.s5psw = self.din("s5psw", [128, 128])
            self.s5dT = self.din("s5dT", [128, 2, 8])
            self.masks_in = self.din("masks", [128, 4, 128])
            self.ident_in = self.din("ident", [128, 128])
            self.uT = self.dscr("uT", [1024, T])
            self.qkT = self.dscr("qkT", [2048, T])
            self.qkcT = self.dscr("qkcT", [2048, T])
            self.vtok = self.dscr("vtok", [T, 1024])
            self.otok = self.dscr("otok", [T, 1024])
            self.hd = [self.dscr("hd%d" % i, [T, 1024]) for i in range(2)]
            self.mixT = self.dscr("mixT", [D, T])
        NO = self.layers // 2
        self.NO = NO
        if self.mixers and NO > 0:
            self.muT = self.din("muT", [128, 2, 6, 16])
            self.w_r = self.din("w_r", [NO, 16, 128, 16, 128])
            self.w_k = self.din("w_k", [NO, 16, 128, 16, 128])
            self.w_v = self.din("w_v", [NO, 16, 128, 16, 128])
            self.w_o = self.din("w_o", [NO, 16, 128, 16, 128])
            self.w0T = self.din("w0T", [128, 2, 2, 16])
            self.w1 = self.din("rw_w1", [2, 2, D, 96])
            self.w2 = self.din("rw_w2", [2, 2, 96, D])
            self.a0T = self.din("a0T", [128, 2, 2, 16])
            self.a1 = self.din("rw_a1", [2, 2, D, 96])
            self.a2 = self.din("rw_a2", [2, 2, 96, D])
            self.v0T = self.din("v0T", [128, 1, 16])
            self.v1 = self.din("rw_v1", [1, D, 64])
            self.v2 = self.din("rw_v2", [1, 64, D])
            self.g1 = self.din("rw_g1", [2, D, 256])
            self.g2 = self.din("rw_g2", [2, 256, D])
            self.kkwT = self.din("kkwT", [128, 2, 16])
            self.kawT = self.din("kawT", [128, 2, 16])
            self.rkT = self.din("rkT", [128, 2, 16])
            self.lnwT = self.din("lnwT", [128, 2, 16])
            self.lnbT = self.din("lnbT", [128, 2, 16])
            self.blk64 = self.din("blk64", [128, 128])
            self.hT = self.dscr("hT", [D, T])
            self.rT = self.dscr("rT", [D, T])
            self.kkT = self.dscr("kkT", [D, T])
            self.vT = self.dscr("vT", [D, T])
            self.vfT = self.dscr("vfT", [D, T])
            self.gT = self.dscr("gT", [D, T])
            self.lwT = [self.dscr("lwT%d" % i, [D, T]) for i in range(2)]
            self.kdT = [self.dscr("kdT%d" % i, [D, T]) for i in range(2)]
            self.bdT = [self.dscr("bdT%d" % i, [D, T]) for i in range(2)]
            self.yT = [self.dscr("yT%d" % i, [D, T]) for i in range(2)]
        self.xs = self.dscr("xs", [D, T])
        self.upT = self.dscr("upT", [2 * DFF, T])
        self.actT = self.dscr("actT", [DFF, T])

        with contextlib.ExitStack() as glob:
            self.g = glob
            self.ones = b.sb(glob, "ones", [128, 128])
            b.dma(self.ones[:], self.ones_in[:, :], writes=[self.ones])
            self.mod = b.sb(glob, "mod", [128, DEPTH, 96, 2])
            self.scl = b.sb(glob, "scl", [128, DEPTH, 2, DC, 2])
            if self.mixers:
                self.masks = b.sb(glob, "masks", [128, 4, 128])
                self.ident = b.sb(glob, "ident", [128, 128])
                b.dma(self.masks[:], self.masks_in[:, :, :], writes=[self.masks])
                b.dma(self.ident[:], self.ident_in[:, :], writes=[self.ident])
            self.stage_mod()
            for l in range(self.layers):
                src = self.xT if l == 0 else self.xs
                if self.mixers:
                    if l % 2 == 0:
                        self.stage_even(l, src)
                    else:
                        self.stage_odd(l, src)
                    src = self.xs
                self.stage_ffn(l, src)
            self.stage_final(self.xT if self.layers == 0 else self.xs)
            if getattr(self, "debug", False):
                self.debug_dump()
            b.finish()

    def debug_dump(self):
        b = self.b
        def dout(name, shape):
            return self.nc.dram_tensor(name, list(shape), F32, kind="ExternalOutput").ap()
        d1 = dout("dbg_mod", [128, DEPTH * 96 * 2])
        b.dma(d1[:, :], self.mod[:].rearrange("p l c t -> p (l c t)"), reads=[self.mod])
        d2 = dout("dbg_up", [256, T])
        b.dma(d2[0:128, :], self.upT[0:128, :])
        b.dma(d2[128:256, :], self.upT[DFF:DFF + 128, :])
        d3 = dout("dbg_act", [128, T])
        b.dma(d3[:, :], self.actT[0:128, :])
        d4 = dout("dbg_xs", [128, T])
        b.dma(d4[:, :], self.xs[0:128, :])

    def stage_mod(self):
        b = self.b
        with contextlib.ExitStack() as st:
            ct = b.sb(st, "ct", [128, DC, 2])
            sg = b.sb(st, "sg", [128, DC, 2])
            nm = b.sb(st, "nm", [128, DEPTH, DC])
            nf = b.sb(st, "nf", [128, DEPTH, DC])
            b.dma(ct[:], self.cT[:, :, :], writes=[ct])
            b.dma(nm[:], self.nmixT[:, :, :], writes=[nm])
            b.dma(nf[:], self.nffnT[:, :, :], writes=[nf])
            b.op("act", lambda e: e.activation(out=sg[:], in_=ct[:], func=AF.Sigmoid), reads=[ct], writes=[sg])
            b.op("dve", lambda e: e.tensor_tensor(out=sg[:], in0=sg[:], in1=ct[:], op=ALU.mult), reads=[ct, sg], writes=[sg])
            wts = [b.sb(st, "mw%d" % i, [128, DC, 512]) for i in range(3)]
            pss = [b.ps(st, "mp%d" % i, [128, 4, 2]) for i in range(2)]
            abt = b.sb(st, "abt", [128, DEPTH, 96])
            b.dma(abt[:], self.ada_bT[:, :, 0:96], writes=[abt])
            it = 0
            for l in range(self.layers):
                for nb in range(6 * D // 512):
                    wt = wts[it % 3]
                    ps = pss[it % 2]
                    it += 1
                    b.dma(wt[:], self.ada_w[l, :, nb * 512:(nb + 1) * 512].rearrange("(kc p) n -> p kc n", p=128),
                          writes=[wt])
                    for j in range(4):
                        for kc in range(DC):
                            b.op("pe", lambda e, j=j, kc=kc: e.matmul(ps[:, j, :], wt[:, kc, j * 128:(j + 1) * 128],
                                                                      sg[:, kc, :], start=(kc == 0), stop=(kc == DC - 1)),
                                 reads=[wt, sg], writes=[ps], pe_acc=True)
                    for col in range(2):
                        b.op("dve", lambda e, col=col: e.tensor_tensor(
                            out=self.mod[:, l, nb * 4:(nb + 1) * 4, col], in0=ps[:, :, col],
                            in1=abt[:, l, nb * 4:(nb + 1) * 4], op=ALU.add), reads=[ps, abt], writes=[self.mod])
                for which, (nw, c0) in enumerate(((nm, 16), (nf, 64))):
                    for col in range(2):
                        b.op("dve", lambda e, which=which, nw=nw, c0=c0, col=col: e.scalar_tensor_tensor(
                            out=self.scl[:, l, which, :, col], in0=self.mod[:, l, c0:c0 + DC, col], scalar=1.0,
                            in1=nw[:, l, :], op0=ALU.add, op1=ALU.mult), reads=[self.mod, nw], writes=[self.scl])
            b.barrier()

    def normmod(self, st, xt, ht, tmp, pss, rstd, n, scale_ap, shift_ap, res_extra=()):
        b = self.b
        for c in range(DC):
            b.op("act", lambda e, c=c: e.activation(out=tmp[:, c % 2, 0:n], in_=xt[:, c, 0:n], func=AF.Square),
                 reads=[xt], writes=[tmp.sub[c % 2]])
            b.op("pe", lambda e, c=c: e.matmul(pss[:, 0:n], self.ones[:, :], tmp[:, c % 2, 0:n], start=(c == 0),
                                               stop=(c == DC - 1)), reads=[tmp.sub[c % 2], self.ones], writes=[pss],
                 pe_acc=True)
        b.op("act", lambda e: e.activation(out=rstd[:, 0:n], in_=pss[:, 0:n], func=AF.Sqrt, scale=1.0 / D, bias=self.epsb[:, 0:1]),
             reads=[pss, self.epsb], writes=[rstd])
        b.op("dve", lambda e: e.reciprocal(out=rstd[:, 0:n], in_=rstd[:, 0:n]), reads=[rstd], writes=[rstd])
        for c in range(DC):
            b.op("dve", lambda e, c=c: e.tensor_tensor(out=xt[:, c, 0:n], in0=xt[:, c, 0:n], in1=rstd[:, 0:n], op=ALU.mult),
                 reads=[xt, rstd], writes=[xt])
            b.op("act", lambda e, c=c: e.activation(out=rnd(ht[:, c, 0:n]), in_=xt[:, c, 0:n], func=AF.Identity,
                                                    scale=scale_ap(c), bias=shift_ap(c)),
                 reads=[xt] + list(res_extra), writes=[ht.sub[c]])

    def subres(self, tl, n):
        tl.sub = [Res() for _ in range(n)]
        return tl

    def gemm(self, st, W, K_chunks, n_chunks, rhs_fn, rhs_res, n, evac, wts, pss, n_col0=0, ksub=16, use_r=True):
        b = self.b
        cast = (lambda ap: rnd(ap)) if use_r else (lambda ap: ap)
        for j in range(n_chunks):
            ps = pss[self.psi % len(pss)]
            self.psi += 1
            nk = (K_chunks + ksub - 1) // ksub
            for kb in range(nk):
                k0 = kb * ksub
                kn = min(ksub, K_chunks - k0)
                wt = wts[self.wi % len(wts)]
                self.wi += 1
                b.dma(cast(wt[:, 0:kn, :]), W[n_col0 // 128 + j, :, k0:k0 + kn, :], writes=[wt],
                      q=("pool" if (use_r and USE_R) else "sp"))
                for kk in range(kn):
                    kc = k0 + kk
                    b.op("pe", lambda e, kk=kk, kc=kc, wt=wt, ps=ps: e.matmul(
                        ps[:, 0:n], cast(wt[:, kk, :]), cast(rhs_fn(kc)), start=(kc == 0), stop=(kc == K_chunks - 1)),
                        reads=[wt] + list(rhs_res(kc)), writes=[ps], pe_acc=True)
            evac(j, ps)

    def stage_ffn(self, l, src):
        b = self.b
        self.psi = 0
        self.wi = 0
        with contextlib.ExitStack() as st:
            self.epsb = b.sb(st, "epsb", [128, 1])
            b.op("dve", lambda e: e.memset(self.epsb[:], EPS), writes=[self.epsb])
            xt = b.sb(st, "xt", [128, DC, 512])
            ht = self.subres(b.sb(st, "ht", [128, DC, 512]), DC)
            tmp = self.subres(b.sb(st, "tmp", [128, 2, 512]), 2)
            rstd = b.sb(st, "rstd", [128, 512])
            psn = b.ps(st, "psn", [128, 512])
            wts = [b.sb(st, "w%d" % i, [128, 16, 128]) for i in range(4)]
            pss = [b.ps(st, "pg%d" % i, [128, 512]) for i in range(4)]
            obs = [b.sb(st, "ob%d" % i, [128, 512]) for i in range(4)]
            oi = [0]
            for (t0, n, isctx) in tiles_tokens():
                b.dma(xt[:, :, 0:n], src[:, t0:t0 + n].rearrange("(c p) t -> p c t", p=128), writes=[xt])
                self.normmod(st, xt, ht, tmp, psn, rstd, n,
                             lambda c: self.scl[:, l, 1, c, isctx:isctx + 1],
                             lambda c: self.mod[:, l, 48 + c, isctx:isctx + 1], res_extra=[self.scl, self.mod])

                def evac(j, ps, t0=t0, n=n):
                    ob = obs[oi[0] % 4]
                    oi[0] += 1
                    b.op("act", lambda e: e.copy(out=ob[:, 0:n], in_=ps[:, 0:n]), reads=[ps], writes=[ob])
                    b.dma(self.upT[j * 128:(j + 1) * 128, t0:t0 + n], ob[:, 0:n], reads=[ob], q="pool")

                self.gemm(st, self.w_up[l], DC, 88, lambda kc: ht[:, kc, 0:n], lambda kc: [ht.sub[kc]], n, evac, wts, pss)
            b.barrier()
        with contextlib.ExitStack() as st:
            cw = b.sb(st, "cw", [128, 3, 88])
            cb = b.sb(st, "cb", [128, 88])
            b.dma(cw[:], self.convwT[:, l, :, :], writes=[cw])
            b.dma(cb[:], self.convbT[:, l, :], writes=[cb])
            NT = 2048
            ua = [b.sb(st, "ua%d" % i, [128, NT + 2]) for i in range(2)]
            ug = [b.sb(st, "ug%d" % i, [128, NT + 2]) for i in range(2)]
            ca = [b.sb(st, "ca%d" % i, [128, NT]) for i in range(2)]
            cg = [b.sb(st, "cg%d" % i, [128, NT]) for i in range(2)]
            segs = [(0, TC, 0, TC), (TC, NT, TC, T), (TC + NT, NT, TC, T)]
            it = 0
            for j in range(44):
                for (t0, n, lo, hi) in segs:
                    A, G, CA, CG = ua[it % 2], ug[it % 2], ca[it % 2], cg[it % 2]
                    it += 1
                    for (U, row) in ((A, j), (G, 44 + j)):
                        a0 = max(t0 - 1, lo)
                        a1 = min(t0 + n + 1, hi)
                        if t0 - 1 < lo:
                            b.op("dve", lambda e, U=U: e.memset(U[:, 0:1], 0.0), writes=[U])
                        if t0 + n + 1 > hi:
                            b.op("dve", lambda e, U=U, n=n: e.memset(U[:, n + 1:n + 2], 0.0), writes=[U])
                        b.dma(U[:, a0 - (t0 - 1):a1 - (t0 - 1)], self.upT[row * 128:(row + 1) * 128, a0:a1], writes=[U])
                    for (U, C, col) in ((A, CA, j), (G, CG, 44 + j)):
                        b.op("dve", lambda e, U=U, C=C, col=col, n=n: e.tensor_scalar(
                            out=C[:, 0:n], in0=U[:, 1:n + 1], scalar1=cw[:, 1, col:col + 1], scalar2=cb[:, col:col + 1],
                            op0=ALU.mult, op1=ALU.add), reads=[U, cw, cb], writes=[C])
                        b.op("dve", lambda e, U=U, C=C, col=col, n=n: e.scalar_tensor_tensor(
                            out=C[:, 0:n], in0=U[:, 0:n], scalar=cw[:, 0, col:col + 1], in1=C[:, 0:n],
                            op0=ALU.mult, op1=ALU.add), reads=[U, cw, C], writes=[C])
                        b.op("dve", lambda e, U=U, C=C, col=col, n=n: e.scalar_tensor_tensor(
                            out=C[:, 0:n], in0=U[:, 2:n + 2], scalar=cw[:, 2, col:col + 1], in1=C[:, 0:n],
                            op0=ALU.mult, op1=ALU.add), reads=[U, cw, C], writes=[C])
                    b.op("act", lambda e, G=G, CG=CG, n=n: e.activation(out=G[:, 0:n], in_=CG[:, 0:n], func=AF.Sigmoid),
                         reads=[CG], writes=[G])
                    b.op("pool", lambda e, G=G, CG=CG, n=n: e.tensor_tensor(out=CG[:, 0:n], in0=CG[:, 0:n], in1=G[:, 0:n], op=ALU.mult),
                         reads=[CG, G], writes=[CG])
                    b.op("pool", lambda e, CA=CA, CG=CG, n=n: e.tensor_tensor(out=CA[:, 0:n], in0=CA[:, 0:n], in1=CG[:, 0:n], op=ALU.mult),
                         reads=[CA, CG], writes=[CA])
                    b.dma(self.actT[j * 128:(j + 1) * 128, t0:t0 + n], CA[:, 0:n], reads=[CA], q="pool")
            b.barrier()
        with contextlib.ExitStack() as st:
            at = self.subres(b.sb(st, "at", [128, 44, 512]), 44)
            xt = b.sb(st, "xt", [128, DC, 512])
            wts = [b.sb(st, "w%d" % i, [128, 11, 128]) for i in range(4)]
            pss = [b.ps(st, "pg%d" % i, [128, 512]) for i in range(4)]
            for (t0, n, isctx) in tiles_tokens():
                b.dma(xt[:, :, 0:n], src[:, t0:t0 + n].rearrange("(c p) t -> p c t", p=128), writes=[xt])
                for q4 in range(4):
                    b.dma(at[:, q4 * 11:(q4 + 1) * 11, 0:n],
                          self.actT[q4 * 11 * 128:(q4 + 1) * 11 * 128, t0:t0 + n].rearrange("(c p) t -> p c t", p=128),
                          writes=[at.sub[c] for c in range(q4 * 11, (q4 + 1) * 11)])

                def evac(j, ps, t0=t0, n=n, isctx=isctx):
                    b.op("dve", lambda e: e.scalar_tensor_tensor(
                        out=xt[:, j, 0:n], in0=ps[:, 0:n], scalar=self.mod[:, l, 80 + j, isctx:isctx + 1], in1=xt[:, j, 0:n],
                        op0=ALU.mult, op1=ALU.add), reads=[ps, self.mod, xt], writes=[xt])

                self.gemm(st, self.w_down[l], 44, DC, lambda kc: at[:, kc, 0:n], lambda kc: [at.sub[kc]], n, evac, wts, pss,
                          ksub=11)
                b.dma(self.xs[:, t0:t0 + n].rearrange("(c p) t -> p c t", p=128), xt[:, :, 0:n], reads=[xt], q="pool")
            b.barrier()

    def stage_final(self, src):
        b = self.b
        with contextlib.ExitStack() as st:
            self.epsb = b.sb(st, "epsb", [128, 1])
            b.op("dve", lambda e: e.memset(self.epsb[:], EPS), writes=[self.epsb])
            nf = b.sb(st, "nfin", [128, DC])
            zero = b.sb(st, "zero", [128, 1])
            b.op("dve", lambda e: e.memset(zero[:], 0.0), writes=[zero])
            b.dma(nf[:], self.nfinT[:, :], writes=[nf])
            xt = b.sb(st, "xt", [128, DC, 512])
            ht = self.subres(b.sb(st, "ht", [128, DC, 512]), DC)
            tmp = self.subres(b.sb(st, "tmp", [128, 2, 512]), 2)
            rstd = b.sb(st, "rstd", [128, 512])
            psn = b.ps(st, "psn", [128, 512])
            for (t0, n, isctx) in tiles_tokens():
                if isctx:
                    continue
                b.dma(xt[:, :, 0:n], src[:, t0:t0 + n].rearrange("(c p) t -> p c t", p=128), writes=[xt])
                self.normmod(st, xt, ht, tmp, psn, rstd, n, lambda c: nf[:, c:c + 1], lambda c: zero[:, 0:1],
                             res_extra=[nf, zero])
                self.out_tok = b.dma(self.out[:, t0 - TC:t0 - TC + n].rearrange("(c p) t -> p c t", p=128), ht[:, :, 0:n],
                                     reads=ht.sub, q="pool")
            b.barrier()


    def conv_pass(self, srcT, dstT, nch, cw, cb, silu, NT=2048):
        b = self.b
        with contextlib.ExitStack() as st:
            us = [b.sb(st, "cu%d" % i, [128, NT + 2]) for i in range(2)]
            cs = [b.sb(st, "cc%d" % i, [128, NT]) for i in range(2)]
            sg = [b.sb(st, "cs%d" % i, [128, NT]) for i in range(2)]
            segs = [(0, TC, 0, TC)] + [(TC + i * NT, NT, TC, T) for i in range(TL // NT)]
            it = 0
            for j in range(nch):
                for (t0, n, lo, hi) in segs:
                    U, C, S = us[it % 2], cs[it % 2], sg[it % 2]
                    it += 1
                    a0 = max(t0 - 1, lo)
                    a1 = min(t0 + n + 1, hi)
                    if t0 - 1 < lo:
                        b.op("dve", lambda e, U=U: e.memset(U[:, 0:1], 0.0), writes=[U])
                    if t0 + n + 1 > hi:
                        b.op("dve", lambda e, U=U, n=n: e.memset(U[:, n + 1:n + 2], 0.0), writes=[U])
                    b.dma(U[:, a0 - (t0 - 1):a1 - (t0 - 1)], srcT[j * 128:(j + 1) * 128, a0:a1], writes=[U])
                    b.op("dve", lambda e, U=U, C=C, j=j, n=n: e.tensor_scalar(
                        out=C[:, 0:n], in0=U[:, 1:n + 1], scalar1=cw[:, 1, j:j + 1], scalar2=cb[:, j:j + 1],
                        op0=ALU.mult, op1=ALU.add), reads=[U, cw, cb], writes=[C])
                    b.op("dve", lambda e, U=U, C=C, j=j, n=n: e.scalar_tensor_tensor(
                        out=C[:, 0:n], in0=U[:, 0:n], scalar=cw[:, 0, j:j + 1], in1=C[:, 0:n],
                        op0=ALU.mult, op1=ALU.add), reads=[U, cw, C], writes=[C])
                    b.op("dve", lambda e, U=U, C=C, j=j, n=n: e.scalar_tensor_tensor(
                        out=C[:, 0:n], in0=U[:, 2:n + 2], scalar=cw[:, 2, j:j + 1], in1=C[:, 0:n],
                        op0=ALU.mult, op1=ALU.add), reads=[U, cw, C], writes=[C])
                    if silu:
                        b.op("act", lambda e, S=S, C=C, n=n: e.activation(out=S[:, 0:n], in_=C[:, 0:n], func=AF.Sigmoid),
                             reads=[C], writes=[S])
                        b.op("pool", lambda e, S=S, C=C, n=n: e.tensor_tensor(out=C[:, 0:n], in0=C[:, 0:n], in1=S[:, 0:n], op=ALU.mult),
                             reads=[C, S], writes=[C])
                    b.dma(dstT[j * 128:(j + 1) * 128, t0:t0 + n], C[:, 0:n], reads=[C], q="pool")
            b.barrier()

    def even_inproj(self, l, j, src):
        b = self.b
        self.psi = 0
        self.wi = 0
        with contextlib.ExitStack() as st:
            self.epsb = b.sb(st, "epsb", [128, 1])
            b.op("dve", lambda e: e.memset(self.epsb[:], EPS), writes=[self.epsb])
            xt = b.sb(st, "xt", [128, DC, 512])
            ht = self.subres(b.sb(st, "ht", [128, DC, 512]), DC)
            tmp = self.subres(b.sb(st, "tmp", [128, 2, 512]), 2)
            rstd = b.sb(st, "rstd", [128, 512])
            psn = b.ps(st, "psn", [128, 512])
            wts = [b.sb(st, "w%d" % i, [128, 16, 128]) for i in range(3)]
            pss = [b.ps(st, "pg%d" % i, [128, 512]) for i in range(3)]
            obs = [b.sb(st, "ob%d" % i, [128, 512]) for i in range(2)]
            wtok = [b.sb(st, "wk%d" % i, [128, 16, 256]) for i in range(2)]
            brow = b.sb(st, "brow", [1, 2064])
            bint = b.sb(st, "bint", [128, 40])
            pst = [b.ps(st, "pt%d" % i, [128, 256]) for i in range(2)]
            otb = [b.sb(st, "otb%d" % i, [128, 256]) for i in range(2)]
            b.dma(brow[:], self.bin_row[0:1, j, 3072:5136], writes=[brow])
            b.dma(bint[:], self.binT[:, j, :], writes=[bint])
            oi = [0]
            ti = 0
            for (t0, n, isctx) in tiles_tokens():
                b.dma(xt[:, :, 0:n], src[:, t0:t0 + n].rearrange("(c p) t -> p c t", p=128), writes=[xt])
                self.normmod(st, xt, ht, tmp, psn, rstd, n,
                             lambda c: self.scl[:, l, 0, c, isctx:isctx + 1],
                             lambda c: self.mod[:, l, c, isctx:isctx + 1], res_extra=[self.scl, self.mod])

                def evac(jc, ps, t0=t0, n=n):
                    ob = obs[oi[0] % 2]
                    oi[0] += 1
                    b.op("act", lambda e: e.activation(out=ob[:, 0:n], in_=ps[:, 0:n], func=AF.Identity,
                                                       bias=bint[:, jc:jc + 1]), reads=[ps, bint], writes=[ob])
                    dst = self.uT[jc * 128:(jc + 1) * 128, t0:t0 + n] if jc < 8 else \
                        self.qkT[(jc - 8) * 128:(jc - 7) * 128, t0:t0 + n]
                    b.dma(dst, ob[:, 0:n], reads=[ob], q="pool")

                self.gemm(st, self.w_in_r[j], DC, 24, lambda kc: ht[:, kc, 0:n], lambda kc: [ht.sub[kc]], n, evac, wts, pss)
                for blk in range(9):
                    c0 = 3072 + blk * 256
                    ncol = 256 if blk < 8 else 16
                    wt = wtok[ti % 2]
                    b.dma(wt[:, :, 0:ncol], self.w_in[j][:, c0:c0 + ncol].rearrange("(kc p) n -> p kc n", p=128), writes=[wt])
                    for ts in range(n // 128):
                        ps = pst[ti % 2]
                        ob = otb[ti % 2]
                        ti += 1
                        for kc in range(DC):
                            b.op("pe", lambda e, kc=kc, ts=ts, ps=ps, wt=wt, ncol=ncol: e.matmul(
                                ps[:, 0:ncol], ht[:, kc, ts * 128:(ts + 1) * 128], wt[:, kc, 0:ncol], start=(kc == 0), stop=False),
                                reads=[wt, ht.sub[kc]], writes=[ps], pe_acc=True)
                        b.op("pe", lambda e, ps=ps, ncol=ncol, c0=c0: e.matmul(
                            ps[:, 0:ncol], self.ones[0:1, 0:128], brow[0:1, c0 - 3072:c0 - 3072 + ncol], start=False, stop=True),
                            reads=[self.ones, brow], writes=[ps], pe_acc=True)
                        tt0 = t0 + ts * 128
                        if blk < 8:
                            b.op("act", lambda e, ob=ob, ps=ps: e.copy(out=ob[:, 0:256], in_=ps[:, 0:256]), reads=[ps], writes=[ob])
                            dst = (self.vtok if blk < 4 else self.otok)[tt0:tt0 + 128, (blk % 4) * 256:(blk % 4 + 1) * 256]
                            b.dma(dst, ob[:, 0:256], reads=[ob], q="pool")
                        else:
                            b.op("act", lambda e, ps=ps, tt0=tt0: e.copy(out=self.Gt[:, tt0 // 128, :], in_=ps[:, 0:16]),
                                 reads=[ps], writes=[self.Gt])
            b.barrier()

    def mlstm(self, j):
        b = self.b
        NCH = T // 128
        with contextlib.ExitStack() as st:
            LF = b.sb(st, "LF", [128, 2, NCH, 4])
            IG = b.sb(st, "IG", [128, 2, NCH, 4])
            Bc = b.sb(st, "Bc", [128, 2, NCH, 4])
            AL = b.sb(st, "AL", [128, 2, NCH, 4])
            BE = b.sb(st, "BE", [128, 2, NCH, 4])
            ALL = b.sb(st, "ALL", [128, 2, NCH, 4])
            gst = contextlib.ExitStack()
            psg = b.ps(gst, "psg", [128, 2, 256])
            psl = b.ps(gst, "psl", [128, 2, 256])
            for d in range(2):
                b.op("act", lambda e, d=d: e.activation(out=LF[:, d], in_=self.Gt[:, :, d * 8 + 4:d * 8 + 8], func=AF.Exp, scale=-1.0),
                     reads=[self.Gt], writes=[LF])
                b.op("dve", lambda e, d=d: e.tensor_copy(out=IG[:, d], in_=self.Gt[:, :, d * 8:d * 8 + 4]), reads=[self.Gt], writes=[IG])
            b.op("dve", lambda e: e.tensor_scalar_add(out=LF[:], in0=LF[:], scalar1=1.0), reads=[LF], writes=[LF])
            b.op("act", lambda e: e.activation(out=LF[:], in_=LF[:], func=AF.Ln), reads=[LF], writes=[LF])
            b.op("dve", lambda e: e.tensor_scalar_mul(out=LF[:], in0=LF[:], scalar1=-1.0), reads=[LF], writes=[LF])
            for d in range(2):
                b.op("pe", lambda e, d=d: e.matmul(psg[:, d, 0:NCH * 4], self.masks[:, d, :], LF[:, d].rearrange("p c h -> p (c h)"),
                                                   start=True, stop=True), reads=[self.masks, LF], writes=[psg], pe_acc=True)
                b.op("pe", lambda e, d=d: e.matmul(psl[:, d, 0:NCH * 4], self.ones[:, :], LF[:, d].rearrange("p c h -> p (c h)"),
                                                   start=True, stop=True), reads=[self.ones, LF], writes=[psl], pe_acc=True)
            b.op("dve", lambda e: e.tensor_copy(out=Bc[:].rearrange("p d c h -> p d (c h)"), in_=psg[:, :, 0:NCH * 4]), reads=[psg], writes=[Bc])
            b.op("act", lambda e: e.activation(out=AL[:], in_=Bc[:], func=AF.Exp), reads=[Bc], writes=[AL])
            b.op("act", lambda e: e.activation(out=ALL[:].rearrange("p d c h -> p d (c h)"), in_=psl[:, :, 0:NCH * 4], func=AF.Exp), reads=[psl], writes=[ALL])
            b.op("dve", lambda e: e.tensor_tensor(out=BE[:], in0=IG[:], in1=Bc[:], op=ALU.subtract), reads=[IG, Bc], writes=[BE])
            b.op("act", lambda e: e.activation(out=BE[:], in_=BE[:], func=AF.Exp), reads=[BE], writes=[BE])
            b.op("dve", lambda e: e.tensor_scalar_mul(out=BE[:], in0=BE[:], scalar1=1.0 / 16.0), reads=[BE], writes=[BE])
            b.barrier()
            gst.close()
            Cst = [[b.sb(st, "C%d%d" % (d, h), [128, 2, 257]) for h in range(4)] for d in range(2)]
            for d in range(2):
                for h in range(4):
                    b.op("pool", lambda e, d=d, h=h: e.memset(Cst[d][h][:], 0.0), writes=[Cst[d][h]])
            qt = [b.sb(st, "q%d" % i, [128, 8, 128]) for i in range(2)]
            kt = [b.sb(st, "k%d" % i, [128, 8, 128]) for i in range(2)]
            ktok = [b.sb(st, "kk%d" % i, [128, 1024]) for i in range(2)]
            Vp = [b.sb(st, "V%d" % i, [128, 4, 257]) for i in range(2)]
            for i in range(2):
                b.op("pool", lambda e, i=i: e.memset(Vp[i][:, :, 256:257], 1.0), writes=[Vp[i]])
            pkt = b.ps(st, "pkt", [128, 2, 512])
            pst_ = [b.ps(st, "pst%d" % i, [128, 512]) for i in range(2)]
            ppp = [b.ps(st, "ppp%d" % i, [128, 512]) for i in range(2)]
            pcc = [b.ps(st, "pcc%d" % i, [128, 512]) for i in range(2)]
            ST = [b.sb(st, "ST%d" % i, [128, 128]) for i in range(2)]
            V2 = [b.sb(st, "V2%d" % i, [128, 257]) for i in range(2)]
            sm = [b.sb(st, "sm%d" % i, [128, 4]) for i in range(2)]
            Hb = [b.sb(st, "Hb%d" % i, [128, 1024]) for i in range(2)]
            tC = b.sb(st, "tC", [128, 2, 257])
            order = [list(range(NCH)), [1, 0] + list(range(NCH - 1, 1, -1))]
            it = 0
            for s_ in range(NCH):
                for d in range(2):
                    c = order[d][s_]
                    t0 = c * 128
                    Q, K_, KT, V, H = qt[it % 2], kt[it % 2], ktok[it % 2], Vp[it % 2], Hb[it % 2]
                    it += 1
                    b.dma(Q[:], self.qkcT[0:1024, t0:t0 + 128].rearrange("(c p) t -> p c t", p=128), writes=[Q])
                    b.dma(K_[:], self.qkcT[1024:2048, t0:t0 + 128].rearrange("(c p) t -> p c t", p=128), writes=[K_])
                    b.dma(V[:, :, 0:256], self.vtok[t0:t0 + 128, :].rearrange("t (h e) -> t h e", h=4), writes=[V])
                    for fc in range(8):
                        b.op("pe", lambda e, fc=fc, K_=K_: e.transpose(pkt[:, fc // 4, (fc % 4) * 128:(fc % 4 + 1) * 128], K_[:, fc, :],
                                                                      self.ident[:, :]), reads=[K_, self.ident], writes=[pkt], pe_acc=True)
                    b.op("act", lambda e, KT=KT: e.copy(out=KT[:].rearrange("p (a x) -> p a x", a=2), in_=pkt[:]), reads=[pkt], writes=[KT])
                    for h in range(4):
                        i2 = (it * 4 + h) % 2
                        pS, pP, S_, V2_, sm_ = pst_[i2], ppp[i2], ST[i2], V2[i2], sm[i2]
                        for dc in range(2):
                            b.op("pe", lambda e, dc=dc, h=h, pS=pS, K_=K_, Q=Q: e.matmul(pS[:, 0:128], K_[:, 2 * h + dc, :], Q[:, 2 * h + dc, :],
                                                                                   start=(dc == 0), stop=(dc == 1)),
                                 reads=[K_, Q], writes=[pS], pe_acc=True)
                        b.op("dve", lambda e, pS=pS, S_=S_, d=d, c=c, h=h: e.scalar_tensor_tensor(
                            out=S_[:], in0=pS[:, 0:128], scalar=BE[:, d, c, h:h + 1], in1=self.masks[:, d, :], op0=ALU.mult, op1=ALU.mult),
                            reads=[pS, BE, self.masks], writes=[S_])
                        b.op("pe", lambda e, pP=pP, S_=S_, V=V, h=h: e.matmul(pP[:, 0:257], S_[:, :], V[:, h, :], start=True, stop=False),
                             reads=[S_, V], writes=[pP], pe_acc=True)
                        for dc in range(2):
                            b.op("pe", lambda e, pP=pP, Q=Q, dc=dc, h=h, d=d: e.matmul(pP[:, 0:257], Q[:, 2 * h + dc, :], Cst[d][h][:, dc, :],
                                                                                 start=False, stop=(dc == 1)),
                                 reads=[Q, Cst[d][h]], writes=[pP], pe_acc=True)
                        b.op("dve", lambda e, pP=pP, sm_=sm_, d=d, c=c, h=h: e.tensor_scalar(
                            out=sm_[:, 3:4], in0=pP[:, 256:257], scalar1=AL[:, d, c, h:h + 1], scalar2=None, op0=ALU.mult),
                            reads=[pP, AL], writes=[sm_])
                        b.op("act", lambda e, sm_=sm_: e.activation(out=sm_[:, 0:1], in_=sm_[:, 3:4], func=AF.Abs),
                             reads=[sm_], writes=[sm_])
                        b.op("dve", lambda e, sm_=sm_: e.tensor_scalar_max(out=sm_[:, 0:1], in0=sm_[:, 0:1], scalar1=1.0),
                             reads=[sm_], writes=[sm_])
                        b.op("dve", lambda e, sm_=sm_: e.reciprocal(out=sm_[:, 1:2], in_=sm_[:, 0:1]), reads=[sm_], writes=[sm_])
                        b.op("dve", lambda e, sm_=sm_, d=d, c=c, h=h: e.tensor_tensor(out=sm_[:, 2:3], in0=sm_[:, 1:2], in1=AL[:, d, c, h:h + 1], op=ALU.mult),
                             reads=[sm_, AL], writes=[sm_])
                        b.op("dve", lambda e, pP=pP, sm_=sm_, H=H, h=h: e.tensor_scalar(
                            out=H[:, h * 256:(h + 1) * 256], in0=pP[:, 0:256], scalar1=sm_[:, 2:3], scalar2=None, op0=ALU.mult),
                             reads=[pP, sm_], writes=[H])
                        b.op("pool", lambda e, V2_=V2_, V=V, h=h, d=d, c=c: e.tensor_scalar(
                            out=V2_[:], in0=V[:, h, :], scalar1=BE[:, d, c, h:h + 1], scalar2=None, op0=ALU.mult),
                            reads=[V, BE], writes=[V2_])
                        for dc in range(2):
                            b.op("pe", lambda e, dc=dc, h=h, KT=KT, V2_=V2_: e.matmul(pcc[dc][:, 0:257], KT[:, (2 * h + dc) * 128:(2 * h + dc + 1) * 128],
                                                                                  V2_[:], start=True, stop=True),
                                 reads=[KT, V2_], writes=[pcc[dc]], pe_acc=True)
                            b.op("dve", lambda e, d=d, h=h, dc=dc: e.tensor_tensor(out=tC[:, dc, :], in0=pcc[dc][:, 0:257], in1=Cst[d][h][:, dc, :], op=ALU.add),
                                 reads=[pcc[dc], Cst[d][h]], writes=[tC])
                        b.op("pool", lambda e, d=d, h=h, c=c: e.tensor_scalar(
                            out=Cst[d][h][:], in0=tC[:], scalar1=ALL[:, d, c, h:h + 1], scalar2=None, op0=ALU.mult),
                            reads=[tC, ALL], writes=[Cst[d][h]])
                    b.dma(self.hd[d][t0:t0 + 128, :], H[:], reads=[H], q="pool")
            b.barrier()
        with contextlib.ExitStack() as st:
            self.epsb = b.sb(st, "epsb", [128, 1])
            b.op("dve", lambda e: e.memset(self.epsb[:], EPS), writes=[self.epsb])
            nw = b.sb(st, "nw", [128, 1024])
            b.dma(nw[:], self.mlnorm[:, j, :], writes=[nw])
            h0 = [b.sb(st, "h0%d" % i, [128, 1024]) for i in range(2)]
            h1 = [b.sb(st, "h1%d" % i, [128, 1024]) for i in range(2)]
            og = [b.sb(st, "og%d" % i, [128, 1024]) for i in range(2)]
            junk = b.sb(st, "junk", [128, 256])
            ss = [b.sb(st, "ss%d" % i, [128, 4]) for i in range(2)]
            ptr = b.ps(st, "ptr", [128, 2, 512])
            ob = [b.sb(st, "fo%d" % i, [128, 1024]) for i in range(2)]
            for c in range(NCH):
                t0 = c * 128
                A, B_, O, SS, OB = h0[c % 2], h1[c % 2], og[c % 2], ss[c % 2], ob[c % 2]
                b.dma(A[:], self.hd[0][t0:t0 + 128, :], writes=[A])
                b.dma(B_[:], self.hd[1][t0:t0 + 128, :], writes=[B_])
                b.dma(O[:], self.otok[t0:t0 + 128, :], writes=[O])
                b.op("dve", lambda e, A=A, B_=B_: e.tensor_tensor(out=A[:], in0=A[:], in1=B_[:], op=ALU.add), reads=[A, B_], writes=[A])
                for h in range(4):
                    b.op("act", lambda e, A=A, SS=SS, h=h: e.activation(out=junk[:], in_=A[:, h * 256:(h + 1) * 256], func=AF.Square,
                                                                      accum_out=SS[:, h:h + 1]), reads=[A], writes=[junk, SS])
                b.op("act", lambda e, SS=SS: e.activation(out=SS[:], in_=SS[:], func=AF.Sqrt, scale=1.0 / 256, bias=self.epsb[:, 0:1]),
                     reads=[SS, self.epsb], writes=[SS])
                b.op("dve", lambda e, SS=SS: e.reciprocal(out=SS[:], in_=SS[:]), reads=[SS], writes=[SS])
                b.op("act", lambda e, O=O: e.activation(out=O[:], in_=O[:], func=AF.Sigmoid), reads=[O], writes=[O])
                b.op("pool", lambda e, O=O: e.tensor_tensor(out=O[:], in0=O[:], in1=nw[:], op=ALU.mult), reads=[O, nw], writes=[O])
                for h in range(4):
                    b.op("dve", lambda e, A=A, O=O, SS=SS, h=h: e.scalar_tensor_tensor(
                        out=A[:, h * 256:(h + 1) * 256], in0=A[:, h * 256:(h + 1) * 256], scalar=SS[:, h:h + 1],
                        in1=O[:, h * 256:(h + 1) * 256], op0=ALU.mult, op1=ALU.mult), reads=[A, O, SS], writes=[A])
                for fc in range(8):
                    b.op("pe", lambda e, A=A, fc=fc: e.transpose(ptr[:, fc // 4, (fc % 4) * 128:(fc % 4 + 1) * 128], A[:, fc * 128:(fc + 1) * 128],
                                                                  self.ident[:, :]), reads=[A, self.ident], writes=[ptr], pe_acc=True)
                b.op("act", lambda e, OB=OB: e.copy(out=OB[:].rearrange("p (a x) -> p a x", a=2), in_=ptr[:]), reads=[ptr], writes=[OB])
                b.dma(self.mixT[1024:2048, t0:t0 + 128].rearrange("(c p) t -> p c t", p=128), OB[:].rearrange("p (c t) -> p c t", c=8),
                      reads=[OB], q="pool")
            b.barrier()

    def even_out(self, l, j, src):
        b = self.b
        self.psi = 0
        self.wi = 0
        with contextlib.ExitStack() as st:
            bg = b.sb(st, "bg", [128, 8])
            b.dma(bg[:], self.bgluT[:, j, :], writes=[bg])
            mt = self.subres(b.sb(st, "mt", [128, DC, 512]), DC)
            mg = self.subres(b.sb(st, "mg", [128, 8, 512]), 8)
            xt = b.sb(st, "xt", [128, DC, 512])
            gt = [b.sb(st, "gt%d" % i, [128, 512]) for i in range(2)]
            wts = [b.sb(st, "w%d" % i, [128, 16, 128]) for i in range(4)]
            pss = [b.ps(st, "pg%d" % i, [128, 512]) for i in range(4)]
            gi = [0]
            for (t0, n, isctx) in tiles_tokens():
                b.dma(xt[:, :, 0:n], src[:, t0:t0 + n].rearrange("(c p) t -> p c t", p=128), writes=[xt])
                b.dma(mt[:, :, 0:n], self.mixT[:, t0:t0 + n].rearrange("(c p) t -> p c t", p=128), writes=mt.sub)

                def evac_glu(jc, ps, n=n):
                    g = gt[gi[0] % 2]
                    gi[0] += 1
                    b.op("act", lambda e: e.activation(out=g[:, 0:n], in_=ps[:, 0:n], func=AF.Sigmoid, bias=bg[:, jc:jc + 1]),
                         reads=[ps, bg], writes=[g])
                    b.op("dve", lambda e: e.tensor_tensor(out=rnd(mg[:, jc, 0:n]), in0=mt[:, jc, 0:n], in1=g[:, 0:n], op=ALU.mult),
                         reads=[g, mt.sub[jc]], writes=[mg.sub[jc]])

                self.gemm(st, self.w_glu[j], 8, 8, lambda kc: mt[:, kc, 0:n], lambda kc: [mt.sub[kc]], n, evac_glu, wts, pss, ksub=8)

                def evac(jc, ps, n=n, isctx=isctx):
                    b.op("dve", lambda e: e.scalar_tensor_tensor(
                        out=xt[:, jc, 0:n], in0=ps[:, 0:n], scalar=self.mod[:, l, 32 + jc, isctx:isctx + 1], in1=xt[:, jc, 0:n],
                        op0=ALU.mult, op1=ALU.add), reads=[ps, self.mod, xt], writes=[xt])

                self.gemm(st, self.w_out[j], DC, DC, lambda kc: (mg[:, kc, 0:n] if kc < 8 else mt[:, kc, 0:n]),
                          lambda kc: [mg.sub[kc] if kc < 8 else mt.sub[kc]], n, evac, wts, pss)
                b.dma(self.xs[:, t0:t0 + n].rearrange("(c p) t -> p c t", p=128), xt[:, :, 0:n], reads=[xt], q="pool")
            b.barrier()

    def s5(self, j):
        b = self.b
        TWO_PI = 2.0 * PI
        CH = 256
        NCHK = T // CH
        with contextlib.ExitStack() as st:
            def t4(name):
                return b.sb(st, name, [128, 2, 64, 1])
            LR, LI, STP, RHO, TH, THR, M_, SINT, COST, ABR, ABI, DEN, T2, AM1, CR, CI, CIS, CRS = [
                t4(n) for n in ("LR", "LI", "STP", "RHO", "TH", "THR", "M_", "SINT", "COST", "ABR", "ABI", "DEN", "T2", "AM1",
                                "CR", "CI", "CIS", "CRS")]
            halfpi = b.sb(st, "halfpi", [128, 1])
            sgn = b.sb(st, "sgn", [128, 2])
            gmask = b.sb(st, "gmask", [128, 8])
            psw = b.sb(st, "psw", [128, 128])
            s5d = b.sb(st, "s5d", [128, 8])
            b.op("dve", lambda e: e.memset(halfpi[:], PI / 2), writes=[halfpi])
            b.dma(sgn[:], self.s5sgn[:, :], writes=[sgn])
            b.dma(gmask[:], self.s5gmask[:, :], writes=[gmask])
            b.dma(psw[:], self.s5psw[:, :], writes=[psw])
            b.dma(s5d[:], self.s5dT[:, j, :], writes=[s5d])
            b.dma(LR[:], self.s5lam[:, 0, j, :, :].unsqueeze(3), writes=[LR])
            b.dma(LI[:], self.s5lam[:, 1, j, :, :].unsqueeze(3), writes=[LI])
            b.dma(STP[:], self.s5ls[:, j, :, :].unsqueeze(3), writes=[STP])

            def tt(out, a, c, op, eng="dve"):
                b.op(eng, lambda e: e.tensor_tensor(out=out[:], in0=a[:], in1=c[:], op=op), reads=[a, c], writes=[out])

            def act(out, a, func, **kw):
                extra = [kw["bias"].tile] if hasattr(kw.get("bias", None), "tile") else []
                b.op("act", lambda e: e.activation(out=out[:], in_=a[:], func=func, **kw), reads=[a], writes=[out])

            act(STP, STP, AF.Exp)
            tt(T2, LR, STP, ALU.mult)
            act(RHO, T2, AF.Exp)
            tt(TH, LI, STP, ALU.mult)
            b.op("dve", lambda e: e.tensor_copy(out=THR[:], in_=TH[:]), reads=[TH], writes=[THR])
            for k in range(4):
                thr = (2 * k + 1) * PI
                b.op("dve", lambda e, thr=thr: e.tensor_scalar(out=M_[:], in0=TH[:], scalar1=thr, scalar2=None, op0=ALU.is_ge),
                     reads=[TH], writes=[M_])
                b.op("dve", lambda e: e.scalar_tensor_tensor(out=THR[:], in0=M_[:], scalar=-TWO_PI, in1=THR[:], op0=ALU.mult, op1=ALU.add),
                     reads=[M_, THR], writes=[THR])
            act(SINT, THR, AF.Sin)
            act(T2, THR, AF.Abs)
            b.op("act", lambda e: e.activation(out=COST[:], in_=T2[:], func=AF.Sin, scale=-1.0, bias=halfpi[:, 0:1]),
                 reads=[T2, halfpi], writes=[COST])
            tt(ABR, RHO, COST, ALU.mult)
            tt(ABI, RHO, SINT, ALU.mult)
            tt(DEN, LR, LR, ALU.mult)
            tt(T2, LI, LI, ALU.mult)
            tt(DEN, DEN, T2, ALU.add)
            b.op("dve", lambda e: e.reciprocal(out=DEN[:], in_=DEN[:]), reads=[DEN], writes=[DEN])
            b.op("dve", lambda e: e.tensor_scalar_add(out=AM1[:], in0=ABR[:], scalar1=-1.0), reads=[ABR], writes=[AM1])
            tt(CR, AM1, LR, ALU.mult)
            tt(T2, ABI, LI, ALU.mult)
            tt(CR, CR, T2, ALU.add)
            tt(CR, CR, DEN, ALU.mult)
            tt(CI, ABI, LR, ALU.mult)
            tt(T2, AM1, LI, ALU.mult)
            tt(CI, CI, T2, ALU.subtract)
            tt(CI, CI, DEN, ALU.mult)
            b.op("dve", lambda e: e.tensor_scalar(out=CIS[:], in0=CI[:], scalar1=sgn[:, 0:1], scalar2=None, op0=ALU.mult), reads=[CI, sgn], writes=[CIS])
            b.op("dve", lambda e: e.tensor_scalar(out=CRS[:], in0=CR[:], scalar1=sgn[:, 1:2], scalar2=None, op0=ALU.mult), reads=[CR, sgn], writes=[CRS])
            BTA = b.sb(st, "BTA", [128, 2, 8, 128])
            BTB = b.sb(st, "BTB", [128, 2, 8, 128])
            CP = b.sb(st, "CP", [128, 2, 64, 16])
            CPADS = [b.sb(st, "CPAD%d" % d, [128, 8, 128]) for d in range(2)]
            for d in range(2):
                b.op("pool", lambda e, d=d: e.memset(CPADS[d][:], 0.0), writes=[CPADS[d]])
            b.dma(CP[:], self.s5CX[:, j, :, :, :], writes=[CP])
            b.op("dve", lambda e: e.tensor_scalar(out=CP[:], in0=CP[:], scalar1=sgn[:, 1:2], scalar2=None, op0=ALU.mult), reads=[CP, sgn], writes=[CP])
            with contextlib.ExitStack() as s2:
                BX = b.sb(s2, "BX", [128, 2, 64, 16])
                BY = b.sb(s2, "BY", [128, 2, 64, 16])
                SA = b.sb(s2, "SA", [128, 2, 64, 16])
                SB = b.sb(s2, "SB", [128, 2, 64, 16])
                TM = b.sb(s2, "TM", [128, 2, 64, 16])
                ptr = b.ps(s2, "ptr5", [128, 512])
                b.dma(BX[:], self.s5BX[:, j, :, :, :], writes=[BX])
                b.dma(BY[:], self.s5BY[:, j, :, :, :], writes=[BY])
                shp = [128, 2, 64, 16]

                def ttb(out, a, col, op):
                    b.op("dve", lambda e: e.tensor_tensor(out=out[:], in0=a[:], in1=col[:].to_broadcast(shp), op=op), reads=[a, col], writes=[out])
                ttb(SA, BX, CR, ALU.mult)
                ttb(TM, BY, CIS, ALU.mult)
                tt(SA, SA, TM, ALU.add)
                ttb(SB, BY, CRS, ALU.mult)
                ttb(TM, BX, CI, ALU.mult)
                tt(SB, SB, TM, ALU.add)
                for (S_, BT_) in ((SA, BTA), (SB, BTB)):
                    for d in range(2):
                        for half in range(2):
                            for k in range(4):
                                fc = half * 4 + k
                                b.op("pe", lambda e, S_=S_, d=d, fc=fc, k=k: e.transpose(
                                    ptr[:, k * 128:(k + 1) * 128], S_[:, d, fc * 8:(fc + 1) * 8, :].rearrange("p g c -> p (g c)"),
                                    self.ident[:, :]), reads=[S_, self.ident], writes=[ptr], pe_acc=True)
                            b.op("act", lambda e, BT_=BT_, d=d, half=half: e.copy(
                                out=BT_[:, d, half * 4:(half + 1) * 4, :].rearrange("p a x -> p (a x)"), in_=ptr[:, :]),
                                reads=[ptr], writes=[BT_])
                b.barrier()
            UT = b.sb(st, "UT", [128, T])
            Y = b.sb(st, "Y", [128, T])
            ER = b.sb(st, "ER", [128, 8, CH])
            EI = b.sb(st, "EI", [128, 8, CH])
            T1 = b.sb(st, "T1", [128, 8, CH // 2])
            T2b = b.sb(st, "T2b", [128, 8, CH // 2])
            RHOT = b.sb(st, "RHOT", [128, 8, CH])
            BPAD = b.sb(st, "BPAD", [128, 2, 8, 128])
            XCAR = b.sb(st, "XCAR", [128, 8])
            BTl = [b.sb(st, "BTl%d" % i, [128, CH]) for i in range(2)]
            TTl = [b.sb(st, "TTl%d" % i, [128, CH]) for i in range(2)]
            Gl = [b.sb(st, "Gl%d" % i, [128, CH]) for i in range(2)]
            Xl = [b.sb(st, "Xl%d" % i, [128, CH]) for i in range(2)]
            pbu = [b.ps(st, "pbu%d" % i, [128, 2, CH]) for i in range(2)]
            psw_ps = [b.ps(st, "psw%d" % i, [128, 512]) for i in range(2)]
            pyy = [b.ps(st, "pyy%d" % i, [128, 512]) for i in range(2)]
            GE = b.sb(st, "GE", [128, T])
            order = [list(range(NCHK)), [0] + list(range(NCHK - 1, 0, -1))]
            it = 0
            yi = 0
            for fc in range(8):
                b.dma(UT[:], self.uT[fc * 128:(fc + 1) * 128, :], writes=[UT])
                for d in range(2):
                    g0 = fc * 8
                    b.op("dve", lambda e, d=d, g0=g0: e.tensor_copy(out=ER[:, :, 0:1], in_=COST[:, d, g0:g0 + 8, :]), reads=[COST], writes=[ER])
                    b.op("dve", lambda e, d=d, g0=g0: e.tensor_copy(out=EI[:, :, 0:1], in_=SINT[:, d, g0:g0 + 8, :]), reads=[SINT], writes=[EI])
                    n = 1
                    while n < CH:
                        bs = [128, 8, n]
                        b.op("dve", lambda e, n=n, bs=bs: e.tensor_tensor(out=T1[:, :, 0:n], in0=ER[:, :, 0:n], in1=ER[:, :, n - 1:n].to_broadcast(bs), op=ALU.mult),
                             reads=[ER], writes=[T1])
                        b.op("pool", lambda e, n=n, bs=bs: e.tensor_tensor(out=T2b[:, :, 0:n], in0=EI[:, :, 0:n], in1=EI[:, :, n - 1:n].to_broadcast(bs), op=ALU.mult),
                             reads=[EI], writes=[T2b])
                        b.op("dve", lambda e, n=n: e.tensor_tensor(out=ER[:, :, n:2 * n], in0=T1[:, :, 0:n], in1=T2b[:, :, 0:n], op=ALU.subtract),
                             reads=[T1, T2b, EI], writes=[ER])
                        b.op("dve", lambda e, n=n, bs=bs: e.tensor_tensor(out=T1[:, :, 0:n], in0=ER[:, :, 0:n], in1=EI[:, :, n - 1:n].to_broadcast(bs), op=ALU.mult),
                             reads=[ER, EI], writes=[T1])
                        b.op("pool", lambda e, n=n, bs=bs: e.tensor_tensor(out=T2b[:, :, 0:n], in0=EI[:, :, 0:n], in1=ER[:, :, n - 1:n].to_broadcast(bs), op=ALU.mult),
                             reads=[EI, ER], writes=[T2b])
                        b.op("dve", lambda e, n=n: e.tensor_tensor(out=EI[:, :, n:2 * n], in0=T1[:, :, 0:n], in1=T2b[:, :, 0:n], op=ALU.add),
                             reads=[T1, T2b, ER], writes=[EI])
                        n *= 2
                    b.op("dve", lambda e, d=d, g0=g0: e.tensor_copy(out=RHOT[:], in_=RHO[:, d, g0:g0 + 8, :].to_broadcast([128, 8, CH])),
                         reads=[RHO], writes=[RHOT])
                    for gp in range(8):
                        b.op("pool", lambda e, gp=gp, d=d, fc=fc: e.tensor_scalar(out=BPAD[:, 0, gp, :], in0=BTA[:, d, fc, :], scalar1=gmask[:, gp:gp + 1],
                                                                                  scalar2=None, op0=ALU.mult), reads=[BTA, gmask], writes=[BPAD])
                        b.op("pool", lambda e, gp=gp, d=d, fc=fc: e.tensor_scalar(out=BPAD[:, 1, gp, :], in0=BTB[:, d, fc, :], scalar1=gmask[:, gp:gp + 1],
                                                                                  scalar2=None, op0=ALU.mult), reads=[BTB, gmask], writes=[BPAD])
                        b.op("dve", lambda e, gp=gp, d=d, g0=g0: e.tensor_copy(out=CPADS[d][:, gp, gp * 16:(gp + 1) * 16], in_=CP[:, d, g0 + gp, :]),
                             reads=[CP], writes=[CPADS[d]])
                    b.op("dve", lambda e: e.memset(XCAR[:], 0.0), writes=[XCAR])
                    for ck in order[d]:
                        c0 = ck * CH
                        py = pyy[yi % 2]
                        yi += 1
                        for gp in range(8):
                            BT_, TT_, G_, X_, pb, pw = BTl[it % 2], TTl[it % 2], Gl[it % 2], Xl[it % 2], pbu[it % 2], psw_ps[it % 2]
                            it += 1
                            for ab in range(2):
                                b.op("pe", lambda e, ab=ab, gp=gp, pb=pb, c0=c0: e.matmul(pb[:, ab, :], BPAD[:, ab, gp, :], UT[:, c0:c0 + CH], start=True, stop=True),
                                     reads=[BPAD, UT], writes=[pb], pe_acc=True)
                            if d == 0:
                                cosv, sinv = ER[:, gp, :], EI[:, gp, :]
                                rv = lambda ap: ap
                            else:
                                cosv, sinv = ER[:, gp, ::-1], EI[:, gp, ::-1]
                                rv = lambda ap: ap[:, ::-1]
                            b.op("dve", lambda e, BT_=BT_, pb=pb, cosv=cosv: e.tensor_tensor(out=BT_[:], in0=pb[:, 0, :], in1=cosv, op=ALU.mult),
                                 reads=[pb, ER], writes=[BT_])
                            b.op("dve", lambda e, TT_=TT_, pb=pb, sinv=sinv: e.tensor_tensor(out=TT_[:], in0=pb[:, 1, :], in1=sinv, op=ALU.mult),
                                 reads=[pb, EI], writes=[TT_])
                            b.op("pool", lambda e, BT_=BT_, TT_=TT_: e.tensor_tensor(out=BT_[:], in0=BT_[:], in1=TT_[:], op=ALU.add),
                                 reads=[BT_, TT_], writes=[BT_])
                            b.op("dve", lambda e, G_=G_, BT_=BT_, gp=gp, rv=rv: e.tensor_tensor_scan(
                                rv(G_[:]), RHOT[:, gp, :], rv(BT_[:]), XCAR[:, gp:gp + 1], ALU.mult, ALU.add),
                                reads=[RHOT, BT_, XCAR], writes=[G_])
                            b.op("pe", lambda e, pw=pw, G_=G_: e.matmul(pw[:, 0:CH], psw[:, :], G_[:], start=True, stop=True),
                                 reads=[psw, G_], writes=[pw], pe_acc=True)
                            b.op("pool", lambda e, X_=X_, G_=G_, cosv=cosv: e.tensor_tensor(out=X_[:], in0=G_[:], in1=cosv, op=ALU.mult),
                                 reads=[G_, ER], writes=[X_])
                            b.op("dve", lambda e, TT_=TT_, pw=pw, sinv=sinv: e.tensor_tensor(out=TT_[:], in0=pw[:, 0:CH], in1=sinv, op=ALU.mult),
                                 reads=[pw, EI], writes=[TT_])
                            b.op("pool", lambda e, X_=X_, TT_=TT_: e.tensor_tensor(out=X_[:], in0=X_[:], in1=TT_[:], op=ALU.subtract),
                                 reads=[X_, TT_], writes=[X_])
                            last = CH - 1 if d == 0 else 0
                            b.op("act", lambda e, X_=X_, gp=gp, last=last: e.copy(out=XCAR[:, gp:gp + 1], in_=X_[:, last:last + 1]),
                                 reads=[X_], writes=[XCAR])
                            b.op("pe", lambda e, py=py, X_=X_, gp=gp, d=d: e.matmul(py[:, 0:CH], CPADS[d][:, gp, :], X_[:], start=(gp == 0), stop=(gp == 7)),
                                 reads=[CPADS[d], X_], writes=[py], pe_acc=True)
                        if d == 0:
                            b.op("act", lambda e, py=py, c0=c0: e.copy(out=Y[:, c0:c0 + CH], in_=py[:, 0:CH]), reads=[py], writes=[Y])
                        else:
                            b.op("dve", lambda e, py=py, c0=c0: e.tensor_tensor(out=Y[:, c0:c0 + CH], in0=py[:, 0:CH], in1=Y[:, c0:c0 + CH], op=ALU.add),
                                 reads=[py, Y], writes=[Y])
                b.op("dve", lambda e, fc=fc: e.scalar_tensor_tensor(out=Y[:], in0=UT[:], scalar=s5d[:, fc:fc + 1], in1=Y[:], op0=ALU.mult, op1=ALU.add),
                     reads=[UT, s5d, Y], writes=[Y])
                b.op("pool", lambda e: e.tensor_tensor(out=GE[:], in0=Y[:], in1=Y[:], op=ALU.mult), reads=[Y], writes=[GE])
                b.op("dve", lambda e: e.tensor_scalar(out=GE[:], in0=GE[:], scalar1=0.044715, scalar2=1.0, op0=ALU.mult, op1=ALU.add),
                     reads=[GE], writes=[GE])
                b.op("pool", lambda e: e.tensor_tensor(out=GE[:], in0=GE[:], in1=Y[:], op=ALU.mult), reads=[GE, Y], writes=[GE])
                b.op("act", lambda e: e.activation(out=GE[:], in_=GE[:], func=AF.Tanh, scale=0.7978845608028654), reads=[GE], writes=[GE])
                b.op("dve", lambda e: e.scalar_tensor_tensor(out=GE[:], in0=GE[:], scalar=1.0, in1=Y[:], op0=ALU.add, op1=ALU.mult),
                     reads=[GE, Y], writes=[GE])
                b.op("pool", lambda e: e.tensor_scalar(out=GE[:], in0=GE[:], scalar1=0.5, scalar2=None, op0=ALU.mult), reads=[GE], writes=[GE])
                b.dma(self.mixT[fc * 128:(fc + 1) * 128, :], GE[:], reads=[GE], q="pool")
            b.barrier()

    def odd_norm(self, l, src):
        b = self.b
        with contextlib.ExitStack() as st:
            self.epsb = b.sb(st, "epsb", [128, 1])
            b.op("dve", lambda e: e.memset(self.epsb[:], EPS), writes=[self.epsb])
            xt = b.sb(st, "xt", [128, DC, 512])
            ht = self.subres(b.sb(st, "ht", [128, DC, 512]), DC)
            tmp = self.subres(b.sb(st, "tmp", [128, 2, 512]), 2)
            rstd = b.sb(st, "rstd", [128, 512])
            psn = b.ps(st, "psn", [128, 512])
            for (t0, n, isctx) in tiles_tokens():
                b.dma(xt[:, :, 0:n], src[:, t0:t0 + n].rearrange("(c p) t -> p c t", p=128), writes=[xt])
                self.normmod(st, xt, ht, tmp, psn, rstd, n,
                             lambda c: self.scl[:, l, 0, c, isctx:isctx + 1],
                             lambda c: self.mod[:, l, c, isctx:isctx + 1], res_extra=[self.scl, self.mod])
                b.dma(self.hT[:, t0:t0 + n].rearrange("(c p) t -> p c t", p=128), ht[:, :, 0:n], reads=ht.sub, q="pool")
            b.barrier()

    def odd_proj(self, l, j):
        b = self.b
        self.psi = 0
        self.wi = 0
        N = 256
        with contextlib.ExitStack() as st:
            mu = b.sb(st, "mu", [128, 6, 16])
            kkw = b.sb(st, "kkw", [128, 16])
            kaw = b.sb(st, "kaw", [128, 16])
            oma = b.sb(st, "oma", [128, 16])
            nw0 = b.sb(st, "nw0", [128, 2, 16])
            a0 = b.sb(st, "a0", [128, 2, 16])
            v0 = b.sb(st, "v0", [128, 16])
            blk = b.sb(st, "blk", [128, 128])
            mhalf = b.sb(st, "mhalf", [128, 1])
            tiny = b.sb(st, "tiny", [128, 1])
            b.dma(mu[:], self.muT[:, j, :, :], writes=[mu])
            b.dma(kkw[:], self.kkwT[:, j, :], writes=[kkw])
            b.dma(kaw[:], self.kawT[:, j, :], writes=[kaw])
            b.dma(nw0[:], self.w0T[:, j, :, :], writes=[nw0])
            b.dma(a0[:], self.a0T[:, j, :, :], writes=[a0])
            b.dma(blk[:], self.blk64[:, :], writes=[blk])
            if j > 0:
                b.dma(v0[:], self.v0T[:, j - 1, :], writes=[v0])
            b.op("dve", lambda e: e.tensor_scalar_mul(out=nw0[:], in0=nw0[:], scalar1=-1.0), reads=[nw0], writes=[nw0])
            b.op("dve", lambda e: e.tensor_scalar(out=oma[:], in0=kaw[:], scalar1=-1.0, scalar2=1.0, op0=ALU.mult, op1=ALU.add), reads=[kaw], writes=[oma])
            b.op("dve", lambda e: e.memset(mhalf[:], -0.5), writes=[mhalf])
            b.op("dve", lambda e: e.memset(tiny[:], 0.0), writes=[tiny])
            H = b.sb(st, "H", [128, DC, N])
            DX = b.sb(st, "DX", [128, DC, N])
            X = self.subres(b.sb(st, "X", [128, DC, N]), DC)
            Kt = self.subres(b.sb(st, "Kt", [128, DC, N]), DC)
            KK = self.subres(b.sb(st, "KK", [128, DC, N]), DC)
            wts = [b.sb(st, "w%d" % i, [128, 16, 128]) for i in range(3)]
            pss = [b.ps(st, "pg%d" % i, [128, 512]) for i in range(3)]
            obs = [b.sb(st, "ob%d" % i, [128, N]) for i in range(4)]
            sq = [b.sb(st, "sq%d" % i, [128, N]) for i in range(2)]
            pl = [b.ps(st, "pl%d" % i, [128, 512]) for i in range(2)]
            pq = [b.ps(st, "pq%d" % i, [128, 512]) for i in range(2)]
            L1 = [b.sb(st, "L1%d" % i, [128, 2, N]) for i in range(2)]
            w1t = [b.sb(st, "w1t%d" % i, [128, 16, 256]) for i in range(1)]
            w2t = [b.sb(st, "w2t%d" % i, [128, 2, 2048]) for i in range(2)]
            vft = [b.sb(st, "vft%d" % i, [128, N]) for i in range(2)]
            cnt = {"o": 0, "s": 0, "l": 0, "w": 0, "q": 0, "v": 0}

            def nxt(lst, key):
                t_ = lst[cnt[key] % len(lst)]
                cnt[key] += 1
                return t_

            def store(dst, rows0, t0, ob, n):
                b.dma(dst[rows0:rows0 + 128, t0:t0 + n], ob[:, 0:n], reads=[ob], q="pool")

            tiles = [(0, TC, 1)] + [(TC + i * N, N, 0) for i in range(TL // N)]
            import os as _os
            SUB = int(_os.environ.get("ODD_SUB", "99"))
            tiles = tiles[:int(_os.environ.get("ODD_TILES", "99"))]
            for (t0, n, isctx) in tiles:
                b.dma(H[:], self.hT[:, t0:t0 + n].rearrange("(c p) t -> p c t", p=128), writes=[H])
                hv = lambda c0, c1, a, e_: self.hT[c0 * 128:c1 * 128, a:e_].rearrange("(c p) t -> p c t", p=128)
                if isctx:
                    b.op("pool", lambda e: e.memset(DX[:, 0:8, 0:1], 0.0), writes=[DX])
                    b.op("pool", lambda e: e.memset(DX[:, 8:16, n - 1:n], 0.0), writes=[DX])
                    b.dma(DX[:, 0:8, 1:n], hv(0, 8, 0, n - 1), writes=[DX])
                    b.dma(DX[:, 8:16, 0:n - 1], hv(8, 16, 1, n), writes=[DX])
                else:
                    first = (t0 == TC)
                    lastt = (t0 + n == T)
                    b.dma(DX[:, 0:4, :], hv(0, 4, t0 - 1, t0 + n - 1), writes=[DX])
                    if lastt:
                        b.dma(DX[:, 4:8, 0:n - 1], hv(4, 8, t0 + 1, t0 + n), writes=[DX])
                    else:
                        b.dma(DX[:, 4:8, :], hv(4, 8, t0 + 1, t0 + n + 1), writes=[DX])
                    b.dma(DX[:, 8:12, :], hv(8, 12, t0 - 64, t0 + n - 64), writes=[DX])
                    if lastt:
                        b.dma(DX[:, 12:16, 0:n - 64], hv(12, 16, t0 + 64, t0 + n), writes=[DX])
                        b.op("pool", lambda e: e.memset(DX[:, 12:16, n - 64:n], 0.0), writes=[DX])
                    else:
                        b.dma(DX[:, 12:16, :], hv(12, 16, t0 + 64, t0 + n + 64), writes=[DX])
                    if first:
                        b.op("pool", lambda e: e.memset(DX[:, 8:12, 0:64], 0.0), writes=[DX])
                    for c in range(4):
                        b.op("pool", lambda e, c=c: e.memset(DX[:, c, :].rearrange("p (r w) -> p r w", w=64)[:, :, 0:1], 0.0), writes=[DX])
                        b.op("pool", lambda e, c=c: e.memset(DX[:, 4 + c, :].rearrange("p (r w) -> p r w", w=64)[:, :, 63:64], 0.0), writes=[DX])
                b.op("dve", lambda e: e.tensor_tensor(out=DX[:], in0=DX[:], in1=H[:], op=ALU.subtract), reads=[DX, H], writes=[DX])

                def mix(i):
                    for c in range(DC):
                        b.op("dve", lambda e, c=c: e.scalar_tensor_tensor(
                            out=rnd(X[:, c, 0:n]), in0=DX[:, c, 0:n], scalar=mu[:, i, c:c + 1], in1=H[:, c, 0:n], op0=ALU.mult, op1=ALU.add),
                            reads=[DX, mu, H], writes=[X.sub[c]])

                xr = lambda kc: X[:, kc, 0:n]
                xres = lambda kc: [X.sub[kc]]

                def lora1(W1, r_, func, li_):
                    wt = nxt(w1t, "w")
                    b.dma(wt[:, :, 0:r_], W1.rearrange("(kc p) r -> p kc r", p=128), writes=[wt])
                    Lt = L1[li_]
                    for m0 in range(0, r_, 128):
                        mm = min(128, r_ - m0)
                        ps = nxt(pl, "l")
                        for kc in range(DC):
                            b.op("pe", lambda e, kc=kc, ps=ps, wt=wt, m0=m0, mm=mm: e.matmul(ps[0:mm, 0:n], wt[:, kc, m0:m0 + mm], X[:, kc, 0:n],
                                                                                       start=(kc == 0), stop=(kc == DC - 1)),
                                 reads=[wt, X.sub[kc]], writes=[ps], pe_acc=True)
                        b.op("act", lambda e, ps=ps, Lt=Lt, m0=m0, mm=mm: e.activation(out=Lt[0:mm, m0 // 128, 0:n], in_=ps[0:mm, 0:n], func=func),
                             reads=[ps], writes=[Lt])
                    return Lt

                def lora2_load(W2, r_):
                    wt = nxt(w2t, "q")
                    for m0 in range(0, r_, 128):
                        mm = min(128, r_ - m0)
                        b.dma(wt[0:mm, m0 // 128, :], W2[m0:m0 + mm, :], writes=[wt])
                    return wt

                def lora2(ps, wt, Lt, r_, jc):
                    nk = (r_ + 127) // 128
                    for ki in range(nk):
                        mm = min(128, r_ - ki * 128)
                        b.op("pe", lambda e, ki=ki, mm=mm: e.matmul(ps[:, 0:n], wt[0:mm, ki, jc * 128:(jc + 1) * 128], Lt[0:mm, ki, 0:n],
                                                                      start=(ki == 0), stop=(ki == nk - 1)),
                             reads=[wt, Lt], writes=[ps], pe_acc=True)

                mix(0)

                def ev_r(jc, ps):
                    ob = nxt(obs, "o")
                    b.op("act", lambda e: e.copy(out=ob[:, 0:n], in_=ps[:, 0:n]), reads=[ps], writes=[ob])
                    store(self.rT, jc * 128, t0, ob, n)
                self.gemm(st, self.w_r[j], DC, DC, xr, xres, n, ev_r, wts, pss)
                if SUB < 2:
                    continue
                mix(2)

                def ev_k(jc, ps):
                    b.op("act", lambda e: e.copy(out=Kt[:, jc, 0:n], in_=ps[:, 0:n]), reads=[ps], writes=[Kt.sub[jc]])

                def kk_post():
                    for jc in range(DC):
                        b.op("dve", lambda e, jc=jc: e.tensor_scalar(out=KK[:, jc, 0:n], in0=Kt[:, jc, 0:n], scalar1=kkw[:, jc:jc + 1], scalar2=None, op0=ALU.mult),
                             reads=[Kt.sub[jc], kkw], writes=[KK.sub[jc]])
                        sq_ = nxt(sq, "s")
                        b.op("act", lambda e, jc=jc, sq_=sq_: e.activation(out=sq_[:, 0:n], in_=KK[:, jc, 0:n], func=AF.Square), reads=[KK.sub[jc]], writes=[sq_])
                        pq_ = nxt(pq, "v")
                        b.op("pe", lambda e, sq_=sq_, pq_=pq_: e.matmul(pq_[:, 0:n], blk[:, :], sq_[:, 0:n], start=True, stop=True), reads=[blk, sq_], writes=[pq_], pe_acc=True)
                        b.op("dve", lambda e, sq_=sq_, pq_=pq_: e.tensor_scalar_max(out=sq_[:, 0:n], in0=pq_[:, 0:n], scalar1=1e-24), reads=[pq_], writes=[sq_])
                        b.op("act", lambda e, sq_=sq_: e.activation(out=sq_[:, 0:n], in_=sq_[:, 0:n], func=AF.Sqrt), reads=[sq_], writes=[sq_])
                        b.op("dve", lambda e, sq_=sq_: e.reciprocal(out=sq_[:, 0:n], in_=sq_[:, 0:n]), reads=[sq_], writes=[sq_])
                        b.op("dve", lambda e, jc=jc, sq_=sq_: e.tensor_tensor(out=KK[:, jc, 0:n], in0=KK[:, jc, 0:n], in1=sq_[:, 0:n], op=ALU.mult),
                             reads=[KK.sub[jc], sq_], writes=[KK.sub[jc]])
                        b.dma(self.kkT[jc * 128:(jc + 1) * 128, t0:t0 + n], KK[:, jc, 0:n], reads=[KK.sub[jc]], q="pool")
                self.gemm(st, self.w_k[j], DC, DC, xr, xres, n, ev_k, wts, pss)
                kk_post()
                if SUB < 3:
                    continue
                mix(3)
                if j > 0:
                    Lv = lora1(self.v1[j - 1], 64, AF.Copy, 0)
                    wv2 = lora2_load(self.v2[j - 1], 64)

                def ev_v(jc, ps):
                    ob = nxt(obs, "o")
                    if j == 0:
                        b.op("act", lambda e: e.copy(out=ob[:, 0:n], in_=ps[:, 0:n]), reads=[ps], writes=[ob])
                        store(self.vfT, jc * 128, t0, ob, n)
                    else:
                        pq_ = nxt(pq, "v")
                        lora2(pq_, wv2, Lv, 64, jc)
                        sg_ = nxt(sq, "s")
                        vf_ = nxt(vft, "v")
                        b.dma(vf_[:, 0:n], self.vfT[jc * 128:(jc + 1) * 128, t0:t0 + n], writes=[vf_])
                        b.op("act", lambda e: e.activation(out=sg_[:, 0:n], in_=pq_[:, 0:n], func=AF.Sigmoid, bias=v0[:, jc:jc + 1]),
                             reads=[pq_, v0], writes=[sg_])
                        b.op("dve", lambda e: e.tensor_tensor(out=vf_[:, 0:n], in0=vf_[:, 0:n], in1=ps[:, 0:n], op=ALU.subtract), reads=[vf_, ps], writes=[vf_])
                        b.op("pool", lambda e: e.tensor_tensor(out=vf_[:, 0:n], in0=vf_[:, 0:n], in1=sg_[:, 0:n], op=ALU.mult), reads=[vf_, sg_], writes=[vf_])
                        b.op("dve", lambda e: e.tensor_tensor(out=ob[:, 0:n], in0=vf_[:, 0:n], in1=ps[:, 0:n], op=ALU.add), reads=[vf_, ps], writes=[ob])
                        store(self.vT, jc * 128, t0, ob, n)
                self.gemm(st, self.w_v[j], DC, DC, xr, xres, n, ev_v, wts, pss)
                if SUB < 4:
                    continue
                mix(1)
                for d in range(2):
                    Lw = lora1(self.w1[j, d], 96, AF.Tanh, d)
                    ww2 = lora2_load(self.w2[j, d], 96)
                    for jc in range(DC):
                        pq_ = nxt(pq, "v")
                        lora2(pq_, ww2, Lw, 96, jc)
                        ob = nxt(obs, "o")
                        b.op("act", lambda e: e.activation(out=ob[:, 0:n], in_=pq_[:, 0:n], func=AF.Exp, scale=-1.0, bias=nw0[:, d, jc:jc + 1]),
                             reads=[pq_, nw0], writes=[ob])
                        b.op("dve", lambda e: e.tensor_scalar_add(out=ob[:, 0:n], in0=ob[:, 0:n], scalar1=1.0), reads=[ob], writes=[ob])
                        b.op("act", lambda e: e.activation(out=ob[:, 0:n], in_=ob[:, 0:n], func=AF.Ln), reads=[ob], writes=[ob])
                        b.op("act", lambda e: e.activation(out=ob[:, 0:n], in_=ob[:, 0:n], func=AF.Exp, scale=-1.0, bias=mhalf[:, 0:1]),
                             reads=[ob, mhalf], writes=[ob])
                        b.op("dve", lambda e: e.tensor_scalar_mul(out=ob[:, 0:n], in0=ob[:, 0:n], scalar1=-1.0), reads=[ob], writes=[ob])
                        store(self.lwT[d], jc * 128, t0, ob, n)
                if SUB < 5:
                    continue
                mix(4)
                for d in range(2):
                    La = lora1(self.a1[j, d], 96, AF.Copy, d)
                    wa2 = lora2_load(self.a2[j, d], 96)
                    for jc in range(DC):
                        pq_ = nxt(pq, "v")
                        lora2(pq_, wa2, La, 96, jc)
                        sa_ = nxt(sq, "s")
                        b.op("act", lambda e: e.activation(out=sa_[:, 0:n], in_=pq_[:, 0:n], func=AF.Sigmoid, bias=a0[:, d, jc:jc + 1]),
                             reads=[pq_, a0], writes=[sa_])
                        ob = nxt(obs, "o")
                        b.op("dve", lambda e: e.tensor_tensor(out=ob[:, 0:n], in0=KK[:, jc, 0:n], in1=sa_[:, 0:n], op=ALU.mult),
                             reads=[KK.sub[jc], sa_], writes=[ob])
                        store(self.bdT[d], jc * 128, t0, ob, n)
                        ob2 = nxt(obs, "o")
                        b.op("dve", lambda e: e.tensor_scalar(out=ob2[:, 0:n], in0=sa_[:, 0:n], scalar1=kaw[:, jc:jc + 1], scalar2=oma[:, jc:jc + 1],
                                                              op0=ALU.mult, op1=ALU.add), reads=[sa_, kaw, oma], writes=[ob2])
                        b.op("pool", lambda e: e.tensor_tensor(out=ob2[:, 0:n], in0=ob2[:, 0:n], in1=Kt[:, jc, 0:n], op=ALU.mult),
                             reads=[ob2, Kt.sub[jc]], writes=[ob2])
                        store(self.kdT[d], jc * 128, t0, ob2, n)
                if SUB < 6:
                    continue
                mix(5)
                Lg = lora1(self.g1[j], 256, AF.Sigmoid, 0)
                wg2 = lora2_load(self.g2[j], 256)
                for jc in range(DC):
                    pq_ = nxt(pq, "v")
                    lora2(pq_, wg2, Lg, 256, jc)
                    ob = nxt(obs, "o")
                    b.op("act", lambda e: e.copy(out=ob[:, 0:n], in_=pq_[:, 0:n]), reads=[pq_], writes=[ob])
                    store(self.gT, jc * 128, t0, ob, n)
            b.barrier()

    def rwkv_scan(self, j):
        b = self.b
        L = 128
        NCH = T // L
        vsrc = self.vfT if j == 0 else self.vT
        with contextlib.ExitStack() as st:
            onesL = b.sb(st, "onesL", [128, L])
            b.op("dve", lambda e: e.memset(onesL[:], 1.0), writes=[onesL])
            MK = [b.sb(st, "MK%d" % d, [128, 2, 2 * L]) for d in range(2)]
            MN = [b.sb(st, "MN%d" % d, [128, 2, L]) for d in range(2)]
            for hh in range(2):
                b.op("dve", lambda e, hh=hh: e.tensor_copy(out=MK[0][:, hh, 0:L], in_=self.masks[:, 2, :]), reads=[self.masks], writes=[MK[0]])
                b.op("dve", lambda e, hh=hh: e.tensor_copy(out=MK[0][:, hh, L:2 * L], in_=self.masks[:, 0, :]), reads=[self.masks], writes=[MK[0]])
                b.op("dve", lambda e, hh=hh: e.tensor_copy(out=MK[1][:, hh, 0:L], in_=self.masks[:, 3, :]), reads=[self.masks], writes=[MK[1]])
                b.op("dve", lambda e, hh=hh: e.tensor_copy(out=MK[1][:, hh, L:2 * L], in_=self.masks[:, 1, :]), reads=[self.masks], writes=[MK[1]])
                b.op("dve", lambda e, hh=hh: e.tensor_copy(out=MN[0][:, hh, :], in_=self.masks[:, 3, :]), reads=[self.masks], writes=[MN[0]])
                b.op("dve", lambda e, hh=hh: e.tensor_copy(out=MN[1][:, hh, :], in_=self.masks[:, 2, :]), reads=[self.masks], writes=[MN[1]])
            ST_ = [b.sb(st, "ST%d" % d, [128, 64]) for d in range(2)]
            def dbl(name, shape):
                return [b.sb(st, "%s%d" % (name, i), shape) for i in range(2)]
            Rt, LWt, KDt, BDt, KKt, Vt = [dbl(nm, [128, L]) for nm in ("Rt", "LWt", "KDt", "BDt", "KKt", "Vt")]
            CS, EP, EM, EA = [dbl(nm, [128, L]) for nm in ("CS", "EP", "EM", "EA")]
            ART = dbl("ART", [128, 2 * L])
            KH, BH = dbl("KH", [128, L]), dbl("BH", [128, L])
            VT, KHT, BHT = dbl("VT", [128, L]), dbl("KHT", [128, L]), dbl("BHT", [128, L])
            AKR, NRB = dbl("AKR", [128, 2, 2 * L]), dbl("NRB", [128, 2, 2 * L])
            PP = dbl("PP", [128, 2, 2 * L])
            XX = dbl("XX", [128, 2, 64])
            YO = dbl("YO", [128, L])
            ptr = b.ps(st, "ptr", [128, 512])
            pm1 = b.ps(st, "pm1", [128, 2, 2 * L])
            pm2 = b.ps(st, "pm2", [128, 2, 2 * L])
            pm3 = b.ps(st, "pm3", [128, 2, 2 * L])
            px = [b.ps(st, "px%d" % i, [128, 512]) for i in range(2)]
            ppp = b.ps(st, "ppq", [128, 2, 2 * L])
            pys = b.ps(st, "pys", [128, 512])
            order = [list(range(NCH)), [1, 0] + list(range(NCH - 1, 1, -1))]
            it = 0
            for fc in range(DC):
                rows = slice(fc * 128, (fc + 1) * 128)
                for d in range(2):
                    S_ = ST_[d]
                    b.op("pool", lambda e, S_=S_: e.memset(S_[:], 0.0), writes=[S_])
                    rv = (lambda ap: ap) if d == 0 else (lambda ap: ap[:, ::-1])
                    lastc = L - 1 if d == 0 else 0
                    for c in order[d]:
                        i2 = it % 2
                        it += 1
                        t0 = c * L
                        R_, LW_, KD_, BD_, KK_, V_ = Rt[i2], LWt[i2], KDt[i2], BDt[i2], KKt[i2], Vt[i2]
                        for (tl_, src_) in ((R_, self.rT), (LW_, self.lwT[d]), (KD_, self.kdT[d]), (BD_, self.bdT[d]), (KK_, self.kkT), (V_, vsrc)):
                            b.dma(tl_[:], src_[rows, t0:t0 + L], writes=[tl_])
                        cs, ep, em, ea, art, kh, bh = CS[i2], EP[i2], EM[i2], EA[i2], ART[i2], KH[i2], BH[i2]
                        b.op("dve", lambda e, cs=cs, LW_=LW_: e.tensor_tensor_scan(rv(cs[:]), onesL[:], rv(LW_[:]), 0.0, ALU.mult, ALU.add),
                             reads=[onesL, LW_], writes=[cs])
                        b.op("act", lambda e, ep=ep, cs=cs: e.activation(out=ep[:], in_=cs[:], func=AF.Exp), reads=[cs], writes=[ep])
                        b.op("act", lambda e, em=em, cs=cs: e.activation(out=em[:], in_=cs[:], func=AF.Exp, scale=-1.0), reads=[cs], writes=[em])
                        b.op("pool", lambda e, ea=ea, cs=cs, LW_=LW_: e.tensor_tensor(out=ea[:], in0=cs[:], in1=LW_[:], op=ALU.subtract), reads=[cs, LW_], writes=[ea])
                        b.op("act", lambda e, ea=ea: e.activation(out=ea[:], in_=ea[:], func=AF.Exp), reads=[ea], writes=[ea])
                        b.op("dve", lambda e, art=art, KK_=KK_, ea=ea: e.scalar_tensor_tensor(out=art[:, 0:L], in0=KK_[:], scalar=-1.0, in1=ea[:], op0=ALU.mult, op1=ALU.mult),
                             reads=[KK_, ea], writes=[art])
                        b.op("pool", lambda e, art=art, R_=R_, ep=ep: e.tensor_tensor(out=art[:, L:2 * L], in0=R_[:], in1=ep[:], op=ALU.mult), reads=[R_, ep], writes=[art])
                        b.op("dve", lambda e, kh=kh, KD_=KD_, em=em: e.tensor_tensor(out=kh[:], in0=KD_[:], in1=em[:], op=ALU.mult), reads=[KD_, em], writes=[kh])
                        b.op("pool", lambda e, bh=bh, BD_=BD_, em=em: e.tensor_tensor(out=bh[:], in0=BD_[:], in1=em[:], op=ALU.mult), reads=[BD_, em], writes=[bh])
                        vt, kht, bht = VT[i2], KHT[i2], BHT[i2]
                        for qi, srcq in enumerate((V_, kh, bh)):
                            b.op("pe", lambda e, qi=qi, srcq=srcq: e.transpose(ptr[:, qi * L:(qi + 1) * L], srcq[:], self.ident[:, :]),
                                 reads=[srcq, self.ident], writes=[ptr], pe_acc=True)
                        b.op("act", lambda e, vt=vt: e.copy(out=vt[:], in_=ptr[:, 0:L]), reads=[ptr], writes=[vt])
                        b.op("act", lambda e, kht=kht: e.copy(out=kht[:], in_=ptr[:, L:2 * L]), reads=[ptr], writes=[kht])
                        b.op("act", lambda e, bht=bht: e.copy(out=bht[:], in_=ptr[:, 2 * L:3 * L]), reads=[ptr], writes=[bht])
                        akr, nrb, pp = AKR[i2], NRB[i2], PP[i2]
                        for hh in range(2):
                            hs = slice(hh * 64, (hh + 1) * 64)
                            b.op("pe", lambda e, hh=hh, hs=hs: e.matmul(pm1[:, hh, :], kh[hs, :], art[hs, :], start=True, stop=True),
                                 reads=[kh, art], writes=[pm1], pe_acc=True)
                            b.op("pe", lambda e, hh=hh, hs=hs: e.matmul(pm2[:, hh, :], bh[hs, :], art[hs, :], start=True, stop=True),
                                 reads=[bh, art], writes=[pm2], pe_acc=True)
                            b.op("pe", lambda e, hh=hh, hs=hs: e.matmul(pm3[:, hh, 0:L], art[hs, 0:L], bh[hs, :], start=True, stop=True),
                                 reads=[bh, art], writes=[pm3], pe_acc=True)
                        b.op("dve", lambda e, akr=akr: e.tensor_tensor(out=akr[:], in0=pm1[:], in1=MK[d][:], op=ALU.mult), reads=[pm1, MK[d]], writes=[akr])
                        b.op("dve", lambda e, nrb=nrb: e.tensor_tensor(out=nrb[:], in0=pm2[:], in1=MK[d][:], op=ALU.mult), reads=[pm2, MK[d]], writes=[nrb])
                        b.op("dve", lambda e, pp=pp: e.tensor_tensor(out=pp[:, :, 0:L], in0=pm3[:, :, 0:L], in1=MN[d][:], op=ALU.mult), reads=[pm3, MN[d]], writes=[pp])
                        b.op("pool", lambda e, pp=pp, nrb=nrb: e.tensor_copy(out=pp[:, :, L:2 * L], in_=nrb[:, :, 0:L]), reads=[nrb], writes=[pp])
                        pxi = px[0]
                        for hh in range(2):
                            hs = slice(hh * 64, (hh + 1) * 64)
                            b.op("pe", lambda e, hh=hh, hs=hs: e.matmul(pxi[:, hh * 64:(hh + 1) * 64], art[hs, 0:L], S_[hs, :], start=True, stop=False),
                                 reads=[art, S_], writes=[pxi], pe_acc=True)
                            b.op("pe", lambda e, hh=hh, hs=hs: e.matmul(pxi[:, hh * 64:(hh + 1) * 64], akr[:, hh, 0:L], vt[:, hs], start=False, stop=True),
                                 reads=[akr, vt], writes=[pxi], pe_acc=True)
                        xc = XX[0]
                        b.op("act", lambda e, xc=xc, pxi=pxi: e.copy(out=xc[:].rearrange("p h v -> p (h v)"), in_=pxi[:, 0:128]), reads=[pxi], writes=[xc])
                        cur = pp
                        for lev in range(7):
                            pxi = px[(lev + 1) % 2]
                            for hh in range(2):
                                b.op("pe", lambda e, hh=hh, cur=cur, xc=xc, pxi=pxi: e.matmul(pxi[:, hh * 64:(hh + 1) * 64], cur[:, hh, L:2 * L], xc[:, hh, :],
                                                                                         start=True, stop=True), reads=[cur, xc], writes=[pxi], pe_acc=True)
                            xn = XX[(lev + 1) % 2]
                            b.op("dve", lambda e, xn=xn, xc=xc, pxi=pxi: e.tensor_tensor(out=xn[:].rearrange("p h v -> p (h v)"), in0=pxi[:, 0:128],
                                                                                       in1=xc[:].rearrange("p h v -> p (h v)"), op=ALU.add),
                                 reads=[pxi, xc], writes=[xn])
                            if lev < 6:
                                for hh in range(2):
                                    b.op("pe", lambda e, hh=hh, cur=cur: e.matmul(ppp[:, hh, 0:L], cur[:, hh, L:2 * L], cur[:, hh, 0:L], start=True, stop=True),
                                         reads=[cur], writes=[ppp], pe_acc=True)
                                    b.op("pe", lambda e, hh=hh, cur=cur: e.matmul(ppp[:, hh, L:2 * L], cur[:, hh, 0:L], cur[:, hh, L:2 * L], start=True, stop=True),
                                         reads=[cur], writes=[ppp], pe_acc=True)
                                nxt_ = PP[(i2 + lev + 1) % 2] if False else (PP[1 - i2] if cur is pp else pp)
                                b.op("act", lambda e, nxt_=nxt_: e.copy(out=nxt_[:], in_=ppp[:]), reads=[ppp], writes=[nxt_])
                                cur = nxt_
                            xc = xn
                        U = xc
                        for hh in range(2):
                            hs = slice(hh * 64, (hh + 1) * 64)
                            b.op("pe", lambda e, hs=hs: e.matmul(pys[hs, 0:L], S_[hs, :], art[hs, L:2 * L], start=True, stop=False),
                                 reads=[S_, art], writes=[pys], pe_acc=True)
                            b.op("pe", lambda e, hs=hs, hh=hh: e.matmul(pys[hs, 0:L], vt[:, hs], akr[:, hh, L:2 * L], start=False, stop=False),
                                 reads=[vt, akr], writes=[pys], pe_acc=True)
                            b.op("pe", lambda e, hs=hs, hh=hh, U=U: e.matmul(pys[hs, 0:L], U[:, hh, :], nrb[:, hh, L:2 * L], start=False, stop=True),
                                 reads=[U, nrb], writes=[pys], pe_acc=True)
                        yo = YO[i2]
                        b.op("act", lambda e, yo=yo: e.copy(out=yo[:], in_=pys[:, 0:L]), reads=[pys], writes=[yo])
                        b.dma(self.yT[d][rows, t0:t0 + L], yo[:], reads=[yo], q="pool")
                        for hh in range(2):
                            hs = slice(hh * 64, (hh + 1) * 64)
                            b.op("pe", lambda e, hs=hs: e.matmul(pys[hs, 256:320], self.ident[hs, hs], S_[hs, :], start=True, stop=False),
                                 reads=[self.ident, S_], writes=[pys], pe_acc=True)
                            b.op("pe", lambda e, hs=hs: e.matmul(pys[hs, 256:320], kht[:, hs], vt[:, hs], start=False, stop=False),
                                 reads=[kht, vt], writes=[pys], pe_acc=True)
                            b.op("pe", lambda e, hs=hs, hh=hh, U=U: e.matmul(pys[hs, 256:320], bht[:, hs], U[:, hh, :], start=False, stop=True),
                                 reads=[bht, U], writes=[pys], pe_acc=True)
                        b.op("dve", lambda e, ep=ep, S_=S_: e.tensor_scalar(out=S_[:], in0=pys[:, 256:320], scalar1=ep[:, lastc:lastc + 1], scalar2=None, op0=ALU.mult),
                             reads=[pys, ep], writes=[S_])
            b.barrier()

    def odd_out(self, l, j, src):
        b = self.b
        self.psi = 0
        self.wi = 0
        N = 256
        with contextlib.ExitStack() as st:
            lnw = b.sb(st, "lnw", [128, 16])
            lnb = b.sb(st, "lnb", [128, 16])
            rk = b.sb(st, "rk", [128, 16])
            blk = b.sb(st, "blk", [128, 128])
            epsl = b.sb(st, "epsl", [128, 1])
            b.dma(lnw[:], self.lnwT[:, j, :], writes=[lnw])
            b.dma(lnb[:], self.lnbT[:, j, :], writes=[lnb])
            b.dma(rk[:], self.rkT[:, j, :], writes=[rk])
            b.dma(blk[:], self.blk64[:, :], writes=[blk])
            b.op("dve", lambda e: e.memset(epsl[:], 64e-5), writes=[epsl])
            vsrc = self.vfT if j == 0 else self.vT
            MT = self.subres(b.sb(st, "MT", [128, DC, N]), DC)
            xt = b.sb(st, "xt", [128, DC, N])
            def dbl(name):
                return [b.sb(st, "%s%d" % (name, i), [128, N]) for i in range(2)]
            Y0, Y1, RR, K0, K1, VV, GG, TA, TB = [dbl(nm) for nm in ("Y0", "Y1", "RR", "K0", "K1", "VV", "GG", "TA", "TB")]
            pq = [b.ps(st, "pq%d" % i, [128, 512]) for i in range(3)]
            wts = [b.sb(st, "w%d" % i, [128, 16, 128]) for i in range(3)]
            pss = [b.ps(st, "pg%d" % i, [128, 512]) for i in range(3)]
            tiles = [(0, TC, 1)] + [(TC + i * N, N, 0) for i in range(TL // N)]
            it = 0
            qi = 0
            for (t0, n, isctx) in tiles:
                b.dma(xt[:], src[:, t0:t0 + n].rearrange("(c p) t -> p c t", p=128), writes=[xt])
                for c in range(DC):
                    i2 = it % 2
                    it += 1
                    rows = slice(c * 128, (c + 1) * 128)
                    y0, y1, rr, k0, k1, vv, gg, ta, tb = Y0[i2], Y1[i2], RR[i2], K0[i2], K1[i2], VV[i2], GG[i2], TA[i2], TB[i2]
                    for (tl_, src_) in ((y0, self.yT[0]), (y1, self.yT[1]), (rr, self.rT), (k0, self.kdT[0]), (k1, self.kdT[1]), (vv, vsrc), (gg, self.gT)):
                        b.dma(tl_[:], src_[rows, t0:t0 + n], writes=[tl_])
                    b.op("dve", lambda e, y0=y0, y1=y1: e.tensor_tensor(out=y0[:], in0=y0[:], in1=y1[:], op=ALU.add), reads=[y0, y1], writes=[y0])
                    p1 = pq[qi % 3]; qi += 1
                    b.op("pe", lambda e, p1=p1, y0=y0: e.matmul(p1[:, 0:n], blk[:, :], y0[:], start=True, stop=True), reads=[blk, y0], writes=[p1], pe_acc=True)
                    b.op("dve", lambda e, p1=p1, y0=y0: e.scalar_tensor_tensor(out=y0[:], in0=p1[:, 0:n], scalar=-1.0 / 64, in1=y0[:], op0=ALU.mult, op1=ALU.add),
                         reads=[p1, y0], writes=[y0])
                    b.op("act", lambda e, ta=ta, y0=y0: e.activation(out=ta[:], in_=y0[:], func=AF.Square), reads=[y0], writes=[ta])
                    p2 = pq[qi % 3]; qi += 1
                    b.op("pe", lambda e, p2=p2, ta=ta: e.matmul(p2[:, 0:n], blk[:, :], ta[:], start=True, stop=True), reads=[blk, ta], writes=[p2], pe_acc=True)
                    b.op("act", lambda e, ta=ta, p2=p2: e.activation(out=ta[:], in_=p2[:, 0:n], func=AF.Sqrt, scale=1.0 / 64, bias=epsl[:, 0:1]),
                         reads=[p2, epsl], writes=[ta])
                    b.op("dve", lambda e, ta=ta: e.reciprocal(out=ta[:], in_=ta[:]), reads=[ta], writes=[ta])
                    b.op("dve", lambda e, ta=ta, y0=y0: e.tensor_tensor(out=y0[:], in0=y0[:], in1=ta[:], op=ALU.mult), reads=[y0, ta], writes=[y0])
                    b.op("act", lambda e, y0=y0, c=c: e.activation(out=y0[:], in_=y0[:], func=AF.Identity, scale=lnw[:, c:c + 1], bias=lnb[:, c:c + 1]),
                         reads=[y0, lnw, lnb], writes=[y0])
                    b.op("pool", lambda e, k0=k0, k1=k1: e.tensor_tensor(out=k0[:], in0=k0[:], in1=k1[:], op=ALU.add), reads=[k0, k1], writes=[k0])
                    b.op("pool", lambda e, k0=k0, rr=rr: e.tensor_tensor(out=k0[:], in0=k0[:], in1=rr[:], op=ALU.mult), reads=[k0, rr], writes=[k0])
                    b.op("pool", lambda e, k0=k0, c=c: e.tensor_scalar(out=k0[:], in0=k0[:], scalar1=rk[:, c:c + 1], scalar2=None, op0=ALU.mult), reads=[k0, rk], writes=[k0])
                    p3 = pq[qi % 3]; qi += 1
                    b.op("pe", lambda e, p3=p3, k0=k0: e.matmul(p3[:, 0:n], blk[:, :], k0[:], start=True, stop=True), reads=[blk, k0], writes=[p3], pe_acc=True)
                    b.op("dve", lambda e, tb=tb, p3=p3, vv=vv: e.tensor_tensor(out=tb[:], in0=p3[:, 0:n], in1=vv[:], op=ALU.mult), reads=[p3, vv], writes=[tb])
                    b.op("dve", lambda e, tb=tb, y0=y0: e.tensor_tensor(out=tb[:], in0=tb[:], in1=y0[:], op=ALU.add), reads=[tb, y0], writes=[tb])
                    b.op("pool", lambda e, tb=tb, gg=gg, c=c: e.tensor_tensor(out=rnd(MT[:, c, :]), in0=tb[:], in1=gg[:], op=ALU.mult), reads=[tb, gg], writes=[MT.sub[c]])

                def evac(jc, ps, isctx=isctx):
                    b.op("dve", lambda e: e.scalar_tensor_tensor(
                        out=xt[:, jc, :], in0=ps[:, 0:n], scalar=self.mod[:, l, 32 + jc, isctx:isctx + 1], in1=xt[:, jc, :],
                        op0=ALU.mult, op1=ALU.add), reads=[ps, self.mod, xt], writes=[xt])
                self.gemm(st, self.w_o[j], DC, DC, lambda kc: MT[:, kc, :], lambda kc: [MT.sub[kc]], n, evac, wts, pss)
                b.dma(self.xs[:, t0:t0 + n].rearrange("(c p) t -> p c t", p=128), xt[:], reads=[xt], q="pool")
            b.barrier()

    def stage_odd(self, l, src):
        j = l // 2
        self.odd_norm(l, src)
        self.odd_proj(l, j)
        self.rwkv_scan(j)
        self.odd_out(l, j, src)

    def stage_even(self, l, src):
        b = self.b
        j = l // 2
        with contextlib.ExitStack() as lst:
            self.Gt = b.sb(lst, "Gt", [128, T // 128, 16])
            self.even_inproj(l, j, src)
            with contextlib.ExitStack() as st:
                cw = b.sb(st, "mcw", [128, 3, 16])
                cb = b.sb(st, "mcb", [128, 16])
                b.dma(cw[:], self.mcwT[:, j, :, :], writes=[cw])
                b.dma(cb[:], self.mcbT[:, j, :], writes=[cb])
                self.conv_pass(self.qkT, self.qkcT, 16, cw, cb, True)
            self.s5(j)
            self.mlstm(j)
        self.even_out(l, j, src)

def relay(w):
    w = np.asarray(w, np.float32)
    lead = w.shape[:-2]
    K_, N_ = w.shape[-2:]
    w = w.reshape(lead + (K_ // 128, 128, N_ // 128, 128))
    nd = len(lead)
    w = np.transpose(w, tuple(range(nd)) + (nd + 2, nd + 1, nd + 0, nd + 3))
    return np.ascontiguousarray(w)


def fm(v):
    v = np.asarray(v, np.float32)
    lead = v.shape[:-1]
    c = v.shape[-1] // 128
    return np.ascontiguousarray(np.moveaxis(v.reshape(lead + (c, 128)), -1, 0))


_PROG = {}


def get_prog(layers=DEPTH, mixers=True):
    key = (layers, mixers)
    if key not in _PROG:
        _PROG[key] = Prog(layers, mixers)
    return _PROG[key]


def make_inputs(p, x, c, ctx, c_ctx, ada_w, ada_b, norm_mix, norm_ffn, ffn_w_up, ffn_conv_w, ffn_conv_b, ffn_w_down,
                norm_final, **kw):
    B = x.shape[0]
    shared = {}
    shared["ada_w"] = np.ascontiguousarray(ada_w[:p.layers], np.float32)
    ab = fm(ada_b)
    shared["ada_bT"] = ab
    shared["nmixT"] = fm(norm_mix)
    shared["nffnT"] = fm(norm_ffn)
    shared["nfinT"] = fm(norm_final)
    shared["w_up"] = relay(ffn_w_up[:p.layers])
    shared["convwT"] = fm(ffn_conv_w)
    shared["convbT"] = fm(ffn_conv_b)
    shared["w_down"] = relay(ffn_w_down[:p.layers])
    shared["ones"] = np.ones((128, 128), np.float32)
    if p.mixers:
        NE = p.NE
        shared["w_in"] = np.ascontiguousarray(kw["ev_w_in"][:NE], np.float32)
        shared["binT"] = fm(kw["ev_b_in"][:, :5120])
        shared["bin_row"] = np.ascontiguousarray(kw["ev_b_in"][None], np.float32)
        shared["w_out"] = relay(kw["ev_w_out"][:NE])
        shared["w_in_r"] = relay(kw["ev_w_in"][:NE, :, :3072])
        shared["mcwT"] = fm(kw["ml_conv_w"])
        shared["mcbT"] = fm(kw["ml_conv_b"])
        shared["mlnorm"] = np.ascontiguousarray(np.broadcast_to(kw["ml_norm"][None], (128, 2, 1024)), np.float32)
        shared["w_glu"] = relay(kw["s5_w_glu"][:NE])
        shared["bgluT"] = fm(kw["s5_b_glu"])
        if p.NO > 0:
            NO = p.NO
            f32 = lambda a: np.ascontiguousarray(a, np.float32)
            shared["muT"] = fm(kw["rw_mu"])
            shared["w_r"] = relay(kw["rw_w_r"][:NO]); shared["w_k"] = relay(kw["rw_w_k"][:NO])
            shared["w_v"] = relay(kw["rw_w_v"][:NO]); shared["w_o"] = relay(kw["rw_w_o"][:NO])
            shared["w0T"] = fm(kw["rw_w0"]); shared["a0T"] = fm(kw["rw_a0"]); shared["v0T"] = fm(kw["rw_v0"])
            for nm in ("rw_w1", "rw_w2", "rw_a1", "rw_a2", "rw_v1", "rw_v2", "rw_g1", "rw_g2"):
                shared[nm] = f32(kw[nm])
            shared["kkwT"] = fm(kw["rw_k_k"]); shared["kawT"] = fm(kw["rw_k_a"])
            shared["rkT"] = fm(kw["rw_r_k"].reshape(2, 2048))
            shared["lnwT"] = fm(kw["rw_ln_w"]); shared["lnbT"] = fm(kw["rw_ln_b"])
            bb_ = np.zeros((128, 128), np.float32)
            bb_[:64, :64] = 1.0
            bb_[64:, 64:] = 1.0
            shared["blk64"] = bb_
        dup = lambda a: np.concatenate([a, a], axis=0)
        lr = np.transpose(kw["s5_lam_re"], (3, 0, 1, 2))
        li = np.transpose(kw["s5_lam_im"], (3, 0, 1, 2))
        shared["s5lam"] = np.ascontiguousarray(np.stack([dup(lr), dup(li)], axis=1), np.float32)
        shared["s5ls"] = np.ascontiguousarray(np.broadcast_to(kw["s5_log_step"][None], (128, 2, 2, 64)), np.float32)
        bre = np.transpose(kw["s5_b_re"], (3, 0, 1, 2, 4))
        bim = np.transpose(kw["s5_b_im"], (3, 0, 1, 2, 4))
        shared["s5BX"] = np.ascontiguousarray(np.concatenate([bre, bim], axis=0), np.float32)
        shared["s5BY"] = np.ascontiguousarray(np.concatenate([bim, bre], axis=0), np.float32)
        cre = np.transpose(kw["s5_c_re"], (4, 0, 1, 2, 3))
        cim = np.transpose(kw["s5_c_im"], (4, 0, 1, 2, 3))
        shared["s5CX"] = np.ascontiguousarray(np.concatenate([cre, cim], axis=0), np.float32)
        sg = np.ones((128, 2), np.float32)
        sg[:64, 0] = -1.0
        sg[64:, 1] = -1.0
        shared["s5sgn"] = sg
        shared["s5gmask"] = (np.arange(128)[:, None] // 16 == np.arange(8)[None, :]).astype(np.float32)
        psw = np.zeros((128, 128), np.float32)
        for m in range(64):
            psw[m + 64, m] = 1.0
            psw[m, m + 64] = -1.0
        shared["s5psw"] = psw
        shared["s5dT"] = fm(kw["s5_d"])
        ii = np.arange(128)
        up = (ii[:, None] <= ii[None, :]).astype(np.float32)
        shared["masks"] = np.ascontiguousarray(np.stack([up, up.T, (ii[:, None] < ii[None, :]).astype(np.float32),
                                                         (ii[:, None] > ii[None, :]).astype(np.float32)], axis=1))
        shared["ident"] = np.eye(128, dtype=np.float32)
    maps = []
    for bi in range(B):
        m = dict(shared)
        xt = np.concatenate([ctx[bi], x[bi]], axis=0).T
        m["xT"] = np.ascontiguousarray(xt, np.float32)
        cc = np.stack([c[bi], c_ctx], axis=-1)
        m["cT"] = np.ascontiguousarray(cc.reshape(DC, 128, 2).transpose(1, 0, 2), np.float32)
        for k in p.inputs:
            assert tuple(m[k].shape) == tuple(p.inputs[k]), (k, m[k].shape, p.inputs[k])
        maps.append({k: m[k] for k in p.inputs})
    return maps


def kernel(**inputs):
    inputs = {k: np.asarray(v) for k, v in inputs.items()}
    p = get_prog()
    maps = make_inputs(p, **inputs)
    B = len(maps)
    res = run_bass_kernel_spmd(p.nc, maps, core_ids=list(range(B)))
    outs = [np.asarray(r["outT"]).T for r in res.results]
    return np.ascontiguousarray(np.stack(outs, axis=0).astype(np.float32))
```

```python
import contextlib
import numpy as np
import concourse.bass as bass
import concourse.mybir as mybir
from concourse.bass_utils import run_bass_kernel_spmd

F32 = mybir.dt.float32
F32R = mybir.dt.float32r
import os as _os0
USE_R = _os0.environ.get("K_F32R", "1") == "1"


def rnd(ap):
    return ap.bitcast(F32R) if USE_R else ap
I32 = mybir.dt.int32
AF = mybir.ActivationFunctionType
ALU = mybir.AluOpType
AX = mybir.AxisListType

D = 2048
DC = D // 128
TC = 256
TL = 4096
T = TC + TL
DEPTH = 4
DFF = 5632
EPS = 1e-6
S5W = 1024
MW = 1024
DIN = S5W + 4 * MW + 16
PI = float(np.pi)


class Res:
    __slots__ = ("w", "r")

    def __init__(self):
        self.w = None
        self.r = {}


class TileW(Res):
    __slots__ = ("t", "sub")

    def __init__(self, t):
        super().__init__()
        self.t = t
        self.sub = None

    def __getitem__(self, k):
        return self.t[k]


class Builder:
    SEM_LIMIT = 20000

    def __init__(self):
        self.nc = bass.Bass("TRN2", target_bir_lowering=False)
        nc = self.nc
        self.es = contextlib.ExitStack()
        self.eng = {"pe": nc.tensor, "act": nc.scalar, "dve": nc.vector, "pool": nc.gpsimd, "sp": nc.sync}
        self.sem = {}
        self.cnt = {}
        self.nsem = 0
        for e in self.eng:
            self._new_sem(e)
        self.seen = {e: {} for e in self.eng}
        self.dma_sems = []
        for i in range(12):
            s = self.es.enter_context(nc.semaphore("dq%d" % i))
            self.dma_sems.append([s, 0])
        self.dma_rr = 0
        self.all_res = []
        self.ninst = 0

    def _new_sem(self, e):
        self.nsem += 1
        self.sem[e] = self.es.enter_context(self.nc.semaphore("s_%s_%d" % (e, self.nsem)))
        self.cnt[e] = 0

    def sb(self, stack, name, shape, dt=F32):
        self.nalloc = getattr(self, "nalloc", 0) + 1
        name = "sb%d_%s" % (self.nalloc, name)
        t = stack.enter_context(self.nc.sbuf_tensor(name, list(shape), dt))
        return TileW(t)

    def ps(self, stack, name, shape, dt=F32):
        self.nalloc = getattr(self, "nalloc", 0) + 1
        name = "ps%d_%s" % (self.nalloc, name)
        t = stack.enter_context(self.nc.psum_tensor(name, list(shape), dt))
        return TileW(t)

    def _wait(self, e, tok):
        if tok is None:
            return
        sem, val = tok
        k = id(sem)
        cur = self.seen[e].get(k)
        if cur is not None and cur[1] >= val:
            return
        self.eng[e].wait_ge(sem, val)
        self.seen[e][k] = (sem, val)

    def _deps(self, e, reads, writes, pe_acc=False):
        for r in reads:
            self._wait(e, r.w)
        for w in writes:
            if not (pe_acc and e == "pe"):
                self._wait(e, w.w)
            for oe, tok in w.r.items():
                self._wait(e, tok)

    def _mark(self, tok, e, reads, writes):
        for r in reads:
            r.r[(e, id(tok[0]))] = tok
        for w in writes:
            w.w = tok
            w.r = {}

    def op(self, e, fn, reads=(), writes=(), pe_acc=False):
        self._deps(e, reads, writes, pe_acc)
        if self.cnt[e] >= self.SEM_LIMIT:
            self._new_sem(e)
        ins = fn(self.eng[e])
        self.cnt[e] += 1
        ins.then_inc(self.sem[e], 1)
        tok = (self.sem[e], self.cnt[e])
        self._mark(tok, e, reads, writes)
        self.ninst += 1
        return tok

    def dma(self, out, in_, reads=(), writes=(), q="sp"):
        self._deps(q, reads, writes)
        ent = self.dma_sems[self.dma_rr]
        self.dma_rr = (self.dma_rr + 1) % len(self.dma_sems)
        if ent[1] >= self.SEM_LIMIT:
            self._wait(q, (ent[0], ent[1]))
            ent[0] = self.es.enter_context(self.nc.semaphore("dq_n%d" % self.ninst))
            ent[1] = 0
        self._wait(q, (ent[0], ent[1]))
        self.eng[q].dma_start(out=out, in_=in_).then_inc(ent[0], 16)
        ent[1] += 16
        tok = (ent[0], ent[1])
        self._mark(tok, "dma", reads, writes)
        self.ninst += 1
        return tok

    def barrier(self):
        toks = [(self.sem[e], self.cnt[e]) for e in self.eng if self.cnt[e] > 0]
        toks += [(s, v) for s, v in self.dma_sems if v > 0]
        for e in self.eng:
            for tok in toks:
                self._wait(e, tok)

    def finish(self):
        self.barrier()


def tiles_tokens():
    out = [(0, TC, 1)]
    for i in range(TL // 512):
        out.append((TC + i * 512, 512, 0))
    return out


class Prog:
    def __init__(self, layers=DEPTH, mixers=True, debug=False):
        self.debug = debug
        self.b = Builder()
        self.nc = self.b.nc
        self.layers = layers
        self.mixers = mixers
        self.inputs = {}
        self.build()

    def din(self, name, shape):
        self.inputs[name] = tuple(shape)
        return self.nc.dram_tensor(name, list(shape), F32, kind="ExternalInput").ap()

    def dscr(self, name, shape):
        return self.nc.dram_tensor(name, list(shape), F32, kind="Internal").ap()

    def build(self):
        b, nc = self.b, self.nc
        self.xT = self.din("xT", [D, T])
        self.cT = self.din("cT", [128, DC, 2])
        self.ada_w = self.din("ada_w", [self.layers, D, 6 * D])
        self.ada_bT = self.din("ada_bT", [128, DEPTH, 48 * 2])
        self.nmixT = self.din("nmixT", [128, DEPTH, DC])
        self.nffnT = self.din("nffnT", [128, DEPTH, DC])
        self.nfinT = self.din("nfinT", [128, DC])
        self.w_up = self.din("w_up", [self.layers, 88, 128, 16, 128])
        self.convwT = self.din("convwT", [128, DEPTH, 3, 88])
        self.convbT = self.din("convbT", [128, DEPTH, 88])
        self.w_down = self.din("w_down", [self.layers, 16, 128, 44, 128])
        self.ones_in = self.din("ones", [128, 128])
        self.out = self.nc.dram_tensor("outT", [D, TL], F32, kind="ExternalOutput").ap()
        NE = (self.layers + 1) // 2
        self.NE = NE
        if self.mixers and NE > 0:
            self.w_in = self.din("w_in", [NE, D, DIN])
            self.binT = self.din("binT", [128, 2, 40])
            self.bin_row = self.din("bin_row", [1, 2, DIN])
            self.w_out = self.din("w_out", [NE, 16, 128, 16, 128])
            self.w_in_r = self.din("w_in_r", [NE, 24, 128, 16, 128])
            self.mcwT = self.din("mcwT", [128, 2, 3, 16])
            self.mcbT = self.din("mcbT", [128, 2, 16])
            self.mlnorm = self.din("mlnorm", [128, 2, 1024])
            self.w_glu = self.din("w_glu", [NE, 8, 128, 8, 128])
            self.bgluT = self.din("bgluT", [128, 2, 8])
            self.s5lam = self.din("s5lam", [128, 2, 2, 2, 64])
            self.s5ls = self.din("s5ls", [128, 2, 2, 64])
            self.s5BX = self.din("s5BX", [128, 2, 2, 64, 16])
            self.s5BY = self.din("s5BY", [128, 2, 2, 64, 16])
            self.s5CX = self.din("s5CX", [128, 2, 2, 64, 16])
            self.s5sgn = self.din("s5sgn", [128, 2])
            self.s5gmask = self.din("s5gmask", [128, 8])
            self.s5psw = self.din("s5psw", [128, 128])
            self.s5dT = self.din("s5dT", [128, 2, 8])
            self.masks_in = self.din("masks", [128, 4, 128])
            self.ident_in = self.din("ident", [128, 128])
            self.uT = self.dscr("uT", [1024, T])
            self.qkT = self.dscr("qkT", [2048, T])
            self.qkcT = self.dscr("qkcT", [2048, T])
            self.vtok = self.dscr("vtok", [T, 1024])
            self.otok = self.dscr("otok", [T, 1024])
            self.hd = [self.dscr("hd%d" % i, [T, 1024]) for i in range(2)]
            self.mixT = self.dscr("mixT", [D, T])
        NO = self.layers // 2
        self.NO = NO
        if self.mixers and NO > 0:
            self.muT = self.din("muT", [128, 2, 6, 16])
            self.w_r = self.din("w_r", [NO, 16, 128, 16, 128])
            self.w_k = self.din("w_k", [NO, 16, 128, 16, 128])
            self.w_v = self.din("w_v", [NO, 16, 128, 16, 128])
            self.w_o = self.din("w_o", [NO, 16, 128, 16, 128])
            self.w0T = self.din("w0T", [128, 2, 2, 16])
            self.w1 = self.din("rw_w1", [2, 2, D, 96])
            self.w2 = self.din("rw_w2", [2, 2, 96, D])
            self.a0T = self.din("a0T", [128, 2, 2, 16])
            self.a1 = self.din("rw_a1", [2, 2, D, 96])
            self.a2 = self.din("rw_a2", [2, 2, 96, D])
            self.v0T = self.din("v0T", [128, 1, 16])
            self.v1 = self.din("rw_v1", [1, D, 64])
            self.v2 = self.din("rw_v2", [1, 64, D])
            self.g1 = self.din("rw_g1", [2, D, 256])
            self.g2 = self.din("rw_g2", [2, 256, D])
            self.kkwT = self.din("kkwT", [128, 2, 16])
            self.kawT = self.din("kawT", [128, 2, 16])
            self.rkT = self.din("rkT", [128, 2, 16])
            self.lnwT = self.din("lnwT", [128, 2, 16])
            self.lnbT = self.din("lnbT", [128, 2, 16])
            self.blk64 = self.din("blk64", [128, 128])
            self.hT = self.dscr("hT", [D, T])
            self.rT = self.dscr("rT", [D, T])
            self.kkT = self.dscr("kkT", [D, T])
            self.vT = self.dscr("vT", [D, T])
            self.vfT = self.dscr("vfT", [D, T])
            self.gT = self.dscr("gT", [D, T])
            self.lwT = [self.dscr("lwT%d" % i, [D, T]) for i in range(2)]
            self.kdT = [self.dscr("kdT%d" % i, [D, T]) for i in range(2)]
            self.bdT = [self.dscr("bdT%d" % i, [D, T]) for i in range(2)]
            self.yT = [self.dscr("yT%d" % i, [D, T]) for i in range(2)]
        self.xs = self.dscr("xs", [D, T])
        self.upT = self.dscr("upT", [2 * DFF, T])
        self.actT = self.dscr("actT", [DFF, T])

        with contextlib.ExitStack() as glob:
            self.g = glob
            self.ones = b.sb(glob, "ones", [128, 128])
            b.dma(self.ones[:], self.ones_in[:, :], writes=[self.ones])
            self.mod = b.sb(glob, "mod", [128, DEPTH, 96, 2])
            self.scl = b.sb(glob, "scl", [128, DEPTH, 2, DC, 2])
            if self.mixers:
                self.masks = b.sb(glob, "masks", [128, 4, 128])
                self.ident = b.sb(glob, "ident", [128, 128])
                b.dma(self.masks[:], self.masks_in[:, :, :], writes=[self.masks])
                b.dma(self.ident[:], self.ident_in[:, :], writes=[self.ident])
            self.stage_mod()
            for l in range(self.layers):
                src = self.xT if l == 0 else self.xs
                if self.mixers:
                    if l % 2 == 0:
                        self.stage_even(l, src)
                    else:
                        self.stage_odd(l, src)
                    src = self.xs
                self.stage_ffn(l, src)
            self.stage_final(self.xT if self.layers == 0 else self.xs)
            if getattr(self, "debug", False):
                self.debug_dump()
            b.finish()

    def debug_dump(self):
        b = self.b
        def dout(name, shape):
            return self.nc.dram_tensor(name, list(shape), F32, kind="ExternalOutput").ap()
        d1 = dout("dbg_mod", [128, DEPTH * 96 * 2])
        b.dma(d1[:, :], self.mod[:].rearrange("p l c t -> p (l c t)"), reads=[self.mod])
        d2 = dout("dbg_up", [256, T])
        b.dma(d2[0:128, :], self.upT[0:128, :])
        b.dma(d2[128:256, :], self.upT[DFF:DFF + 128, :])
        d3 = dout("dbg_act", [128, T])
        b.dma(d3[:, :], self.actT[0:128, :])
        d4 = dout("dbg_xs", [128, T])
        b.dma(d4[:, :], self.xs[0:128, :])

    def stage_mod(self):
        b = self.b
        with contextlib.ExitStack() as st:
            ct = b.sb(st, "ct", [128, DC, 2])
            sg = b.sb(st, "sg", [128, DC, 2])
            nm = b.sb(st, "nm", [128, DEPTH, DC])
            nf = b.sb(st, "nf", [128, DEPTH, DC])
            b.dma(ct[:], self.cT[:, :, :], writes=[ct])
            b.dma(nm[:], self.nmixT[:, :, :], writes=[nm])
            b.dma(nf[:], self.nffnT[:, :, :], writes=[nf])
            b.op("act", lambda e: e.activation(out=sg[:], in_=ct[:], func=AF.Sigmoid), reads=[ct], writes=[sg])
            b.op("dve", lambda e: e.tensor_tensor(out=sg[:], in0=sg[:], in1=ct[:], op=ALU.mult), reads=[ct, sg], writes=[sg])
            wts = [b.sb(st, "mw%d" % i, [128, DC, 512]) for i in range(3)]
            pss = [b.ps(st, "mp%d" % i, [128, 4, 2]) for i in range(2)]
            abt = b.sb(st, "abt", [128, DEPTH, 96])
            b.dma(abt[:], self.ada_bT[:, :, 0:96], writes=[abt])
            it = 0
            for l in range(self.layers):
                for nb in range(6 * D // 512):
                    wt = wts[it % 3]
                    ps = pss[it % 2]
                    it += 1
                    b.dma(wt[:], self.ada_w[l, :, nb * 512:(nb + 1) * 512].rearrange("(kc p) n -> p kc n", p=128),
                          writes=[wt])
                    for j in range(4):
                        for kc in range(DC):
                            b.op("pe", lambda e, j=j, kc=kc: e.matmul(ps[:, j, :], wt[:, kc, j * 128:(j + 1) * 128],
                                                                      sg[:, kc, :], start=(kc == 0), stop=(kc == DC - 1)),
                                 reads=[wt, sg], writes=[ps], pe_acc=True)
                    for col in range(2):
                        b.op("dve", lambda e, col=col: e.tensor_tensor(
                            out=self.mod[:, l, nb * 4:(nb + 1) * 4, col], in0=ps[:, :, col],
                            in1=abt[:, l, nb * 4:(nb + 1) * 4], op=ALU.add), reads=[ps, abt], writes=[self.mod])
                for which, (nw, c0) in enumerate(((nm, 16), (nf, 64))):
                    for col in range(2):
                        b.op("dve", lambda e, which=which, nw=nw, c0=c0, col=col: e.scalar_tensor_tensor(
                            out=self.scl[:, l, which, :, col], in0=self.mod[:, l, c0:c0 + DC, col], scalar=1.0,
                            in1=nw[:, l, :], op0=ALU.add, op1=ALU.mult), reads=[self.mod, nw], writes=[self.scl])
            b.barrier()

    def normmod(self, st, xt, ht, tmp, pss, rstd, n, scale_ap, shift_ap, res_extra=()):
        b = self.b
        for c in range(DC):
            b.op("act", lambda e, c=c: e.activation(out=tmp[:, c % 2, 0:n], in_=xt[:, c, 0:n], func=AF.Square),
                 reads=[xt], writes=[tmp.sub[c % 2]])
            b.op("pe", lambda e, c=c: e.matmul(pss[:, 0:n], self.ones[:, :], tmp[:, c % 2, 0:n], start=(c == 0),
                                               stop=(c == DC - 1)), reads=[tmp.sub[c % 2], self.ones], writes=[pss],
                 pe_acc=True)
        b.op("act", lambda e: e.activation(out=rstd[:, 0:n], in_=pss[:, 0:n], func=AF.Sqrt, scale=1.0 / D, bias=self.epsb[:, 0:1]),
             reads=[pss, self.epsb], writes=[rstd])
        b.op("dve", lambda e: e.reciprocal(out=rstd[:, 0:n], in_=rstd[:, 0:n]), reads=[rstd], writes=[rstd])
        for c in range(DC):
            b.op("dve", lambda e, c=c: e.tensor_tensor(out=xt[:, c, 0:n], in0=xt[:, c, 0:n], in1=rstd[:, 0:n], op=ALU.mult),
                 reads=[xt, rstd], writes=[xt])
            b.op("act", lambda e, c=c: e.activation(out=rnd(ht[:, c, 0:n]), in_=xt[:, c, 0:n], func=AF.Identity,
                                                    scale=scale_ap(c), bias=shift_ap(c)),
                 reads=[xt] + list(res_extra), writes=[ht.sub[c]])

    def subres(self, tl, n):
        tl.sub = [Res() for _ in range(n)]
        return tl

    def gemm(self, st, W, K_chunks, n_chunks, rhs_fn, rhs_res, n, evac, wts, pss, n_col0=0, ksub=16, use_r=True):
        b = self.b
        cast = (lambda ap: rnd(ap)) if use_r else (lambda ap: ap)
        for j in range(n_chunks):
            ps = pss[self.psi % len(pss)]
            self.psi += 1
            nk = (K_chunks + ksub - 1) // ksub
            for kb in range(nk):
                k0 = kb * ksub
                kn = min(ksub, K_chunks - k0)
                wt = wts[self.wi % len(wts)]
                self.wi += 1
                b.dma(cast(wt[:, 0:kn, :]), W[n_col0 // 128 + j, :, k0:k0 + kn, :], writes=[wt],
                      q=("pool" if (use_r and USE_R) else "sp"))
                for kk in range(kn):
                    kc = k0 + kk
                    b.op("pe", lambda e, kk=kk, kc=kc, wt=wt, ps=ps: e.matmul(
                        ps[:, 0:n], cast(wt[:, kk, :]), cast(rhs_fn(kc)), start=(kc == 0), stop=(kc == K_chunks - 1)),
                        reads=[wt] + list(rhs_res(kc)), writes=[ps], pe_acc=True)
            evac(j, ps)

    def stage_ffn(self, l, src):
        b = self.b
        self.psi = 0
        self.wi = 0
        with contextlib.ExitStack() as st:
            self.epsb = b.sb(st, "epsb", [128, 1])
            b.op("dve", lambda e: e.memset(self.epsb[:], EPS), writes=[self.epsb])
            xt = b.sb(st, "xt", [128, DC, 512])
            ht = self.subres(b.sb(st, "ht", [128, DC, 512]), DC)
            tmp = self.subres(b.sb(st, "tmp", [128, 2, 512]), 2)
            rstd = b.sb(st, "rstd", [128, 512])
            psn = b.ps(st, "psn", [128, 512])
            wts = [b.sb(st, "w%d" % i, [128, 16, 128]) for i in range(4)]
            pss = [b.ps(st, "pg%d" % i, [128, 512]) for i in range(4)]
            obs = [b.sb(st, "ob%d" % i, [128, 512]) for i in range(4)]
            oi = [0]
            for (t0, n, isctx) in tiles_tokens():
                b.dma(xt[:, :, 0:n], src[:, t0:t0 + n].rearrange("(c p) t -> p c t", p=128), writes=[xt])
                self.normmod(st, xt, ht, tmp, psn, rstd, n,
                             lambda c: self.scl[:, l, 1, c, isctx:isctx + 1],
                             lambda c: self.mod[:, l, 48 + c, isctx:isctx + 1], res_extra=[self.scl, self.mod])

                def evac(j, ps, t0=t0, n=n):
                    ob = obs[oi[0] % 4]
                    oi[0] += 1
                    b.op("act", lambda e: e.copy(out=ob[:, 0:n], in_=ps[:, 0:n]), reads=[ps], writes=[ob])
                    b.dma(self.upT[j * 128:(j + 1) * 128, t0:t0 + n], ob[:, 0:n], reads=[ob], q="pool")

                self.gemm(st, self.w_up[l], DC, 88, lambda kc: ht[:, kc, 0:n], lambda kc: [ht.sub[kc]], n, evac, wts, pss)
            b.barrier()
        with contextlib.ExitStack() as st:
            cw = b.sb(st, "cw", [128, 3, 88])
            cb = b.sb(st, "cb", [128, 88])
            b.dma(cw[:], self.convwT[:, l, :, :], writes=[cw])
            b.dma(cb[:], self.convbT[:, l, :], writes=[cb])
            NT = 2048
            ua = [b.sb(st, "ua%d" % i, [128, NT + 2]) for i in range(2)]
            ug = [b.sb(st, "ug%d" % i, [128, NT + 2]) for i in range(2)]
            ca = [b.sb(st, "ca%d" % i, [128, NT]) for i in range(2)]
            cg = [b.sb(st, "cg%d" % i, [128, NT]) for i in range(2)]
            segs = [(0, TC, 0, TC), (TC, NT, TC, T), (TC + NT, NT, TC, T)]
            it = 0
            for j in range(44):
                for (t0, n, lo, hi) in segs:
                    A, G, CA, CG = ua[it % 2], ug[it % 2], ca[it % 2], cg[it % 2]
                    it += 1
                    for (U, row) in ((A, j), (G, 44 + j)):
                        a0 = max(t0 - 1, lo)
                        a1 = min(t0 + n + 1, hi)
                        if t0 - 1 < lo:
                            b.op("dve", lambda e, U=U: e.memset(U[:, 0:1], 0.0), writes=[U])
                        if t0 + n + 1 > hi:
                            b.op("dve", lambda e, U=U, n=n: e.memset(U[:, n + 1:n + 2], 0.0), writes=[U])
                        b.dma(U[:, a0 - (t0 - 1):a1 - (t0 - 1)], self.upT[row * 128:(row + 1) * 128, a0:a1], writes=[U])
                    for (U, C, col) in ((A, CA, j), (G, CG, 44 + j)):
                        b.op("dve", lambda e, U=U, C=C, col=col, n=n: e.tensor_scalar(
                            out=C[:, 0:n], in0=U[:, 1:n + 1], scalar1=cw[:, 1, col:col + 1], scalar2=cb[:, col:col + 1],
                            op0=ALU.mult, op1=ALU.add), reads=[U, cw, cb], writes=[C])
                        b.op("dve", lambda e, U=U, C=C, col=col, n=n: e.scalar_tensor_tensor(
                            out=C[:, 0:n], in0=U[:, 0:n], scalar=cw[:, 0, col:col + 1], in1=C[:, 0:n],
                            op0=ALU.mult, op1=ALU.add), reads=[U, cw, C], writes=[C])
                        b.op("dve", lambda e, U=U, C=C, col=col, n=n: e.scalar_tensor_tensor(
                            out=C[:, 0:n], in0=U[:, 2:n + 2], scalar=cw[:, 2, col:col + 1], in1=C[:, 0:n],
                            op0=ALU.mult, op1=ALU.add), reads=[U, cw, C], writes=[C])
                    b.op("act", lambda e, G=G, CG=CG, n=n: e.activation(out=G[:, 0:n], in_=CG[:, 0:n], func=AF.Sigmoid),
                         reads=[CG], writes=[G])
                    b.op("pool", lambda e, G=G, CG=CG, n=n: e.tensor_tensor(out=CG[:, 0:n], in0=CG[:, 0:n], in1=G[:, 0:n], op=ALU.mult),
                         reads=[CG, G], writes=[CG])
                    b.op("pool", lambda e, CA=CA, CG=CG, n=n: e.tensor_tensor(out=CA[:, 0:n], in0=CA[:, 0:n], in1=CG[:, 0:n], op=ALU.mult),
                         reads=[CA, CG], writes=[CA])
                    b.dma(self.actT[j * 128:(j + 1) * 128, t0:t0 + n], CA[:, 0:n], reads=[CA], q="pool")
            b.barrier()
        with contextlib.ExitStack() as st:
            at = self.subres(b.sb(st, "at", [128, 44, 512]), 44)
            xt = b.sb(st, "xt", [128, DC, 512])
            wts = [b.sb(st, "w%d" % i, [128, 11, 128]) for i in range(4)]
            pss = [b.ps(st, "pg%d" % i, [128, 512]) for i in range(4)]
            for (t0, n, isctx) in tiles_tokens():
                b.dma(xt[:, :, 0:n], src[:, t0:t0 + n].rearrange("(c p) t -> p c t", p=128), writes=[xt])
                for q4 in range(4):
                    b.dma(at[:, q4 * 11:(q4 + 1) * 11, 0:n],
                          self.actT[q4 * 11 * 128:(q4 + 1) * 11 * 128, t0:t0 + n].rearrange("(c p) t -> p c t", p=128),
                          writes=[at.sub[c] for c in range(q4 * 11, (q4 + 1) * 11)])

                def evac(j, ps, t0=t0, n=n, isctx=isctx):
                    b.op("dve", lambda e: e.scalar_tensor_tensor(
                        out=xt[:, j, 0:n], in0=ps[:, 0:n], scalar=self.mod[:, l, 80 + j, isctx:isctx + 1], in1=xt[:, j, 0:n],
                        op0=ALU.mult, op1=ALU.add), reads=[ps, self.mod, xt], writes=[xt])

                self.gemm(st, self.w_down[l], 44, DC, lambda kc: at[:, kc, 0:n], lambda kc: [at.sub[kc]], n, evac, wts, pss,
                          ksub=11)
                b.dma(self.xs[:, t0:t0 + n].rearrange("(c p) t -> p c t", p=128), xt[:, :, 0:n], reads=[xt], q="pool")
            b.barrier()

    def stage_final(self, src):
        b = self.b
        with contextlib.ExitStack() as st:
            self.epsb = b.sb(st, "epsb", [128, 1])
            b.op("dve", lambda e: e.memset(self.epsb[:], EPS), writes=[self.epsb])
            nf = b.sb(st, "nfin", [128, DC])
            zero = b.sb(st, "zero", [128, 1])
            b.op("dve", lambda e: e.memset(zero[:], 0.0), writes=[zero])
            b.dma(nf[:], self.nfinT[:, :], writes=[nf])
            xt = b.sb(st, "xt", [128, DC, 512])
            ht = self.subres(b.sb(st, "ht", [128, DC, 512]), DC)
            tmp = self.subres(b.sb(st, "tmp", [128, 2, 512]), 2)
            rstd = b.sb(st, "rstd", [128, 512])
            psn = b.ps(st, "psn", [128, 512])
            for (t0, n, isctx) in tiles_tokens():
                if isctx:
                    continue
                b.dma(xt[:, :, 0:n], src[:, t0:t0 + n].rearrange("(c p) t -> p c t", p=128), writes=[xt])
                self.normmod(st, xt, ht, tmp, psn, rstd, n, lambda c: nf[:, c:c + 1], lambda c: zero[:, 0:1],
                             res_extra=[nf, zero])
                self.out_tok = b.dma(self.out[:, t0 - TC:t0 - TC + n].rearrange("(c p) t -> p c t", p=128), ht[:, :, 0:n],
                                     reads=ht.sub, q="pool")
            b.barrier()


    def conv_pass(self, srcT, dstT, nch, cw, cb, silu, NT=2048):
        b = self.b
        with contextlib.ExitStack() as st:
            us = [b.sb(st, "cu%d" % i, [128, NT + 2]) for i in range(2)]
            cs = [b.sb(st, "cc%d" % i, [128, NT]) for i in range(2)]
            sg = [b.sb(st, "cs%d" % i, [128, NT]) for i in range(2)]
            segs = [(0, TC, 0, TC)] + [(TC + i * NT, NT, TC, T) for i in range(TL // NT)]
            it = 0
            for j in range(nch):
                for (t0, n, lo, hi) in segs:
                    U, C, S = us[it % 2], cs[it % 2], sg[it % 2]
                    it += 1
                    a0 = max(t0 - 1, lo)
                    a1 = min(t0 + n + 1, hi)
                    if t0 - 1 < lo:
                        b.op("dve", lambda e, U=U: e.memset(U[:, 0:1], 0.0), writes=[U])
                    if t0 + n + 1 > hi:
                        b.op("dve", lambda e, U=U, n=n: e.memset(U[:, n + 1:n + 2], 0.0), writes=[U])
                    b.dma(U[:, a0 - (t0 - 1):a1 - (t0 - 1)], srcT[j * 128:(j + 1) * 128, a0:a1], writes=[U])
                    b.op("dve", lambda e, U=U, C=C, j=j, n=n: e.tensor_scalar(
                        out=C[:, 0:n], in0=U[:, 1:n + 1], scalar1=cw[:, 1, j:j + 1], scalar2=cb[:, j:j + 1],
                        op0=ALU.mult, op1=ALU.add), reads=[U, cw, cb], writes=[C])
                    b.op("dve", lambda e, U=U, C=C, j=j, n=n: e.scalar_tensor_tensor(
                        out=C[:, 0:n], in0=U[:, 0:n], scalar=cw[:, 0, j:j + 1], in1=C[:, 0:n],
                        op0=ALU.mult, op1=ALU.add), reads=[U, cw, C], writes=[C])
                    b.op("dve", lambda e, U=U, C=C, j=j, n=n: e.scalar_tensor_tensor(
                        out=C[:, 0:n], in0=U[:, 2:n + 2], scalar=cw[:, 2, j:j + 1], in1=C[:, 0:n],
                        op0=ALU.mult, op1=ALU.add), reads=[U, cw, C], writes=[C])
                    if silu:
                        b.op("act", lambda e, S=S, C=C, n=n: e.activation(out=S[:, 0:n], in_=C[:, 0:n], func=AF.Sigmoid),
                             reads=[C], writes=[S])
                        b.op("pool", lambda e, S=S, C=C, n=n: e.tensor_tensor(out=C[:, 0:n], in0=C[:, 0:n], in1=S[:, 0:n], op=ALU.mult),
                             reads=[C, S], writes=[C])
                    b.dma(dstT[j * 128:(j + 1) * 128, t0:t0 + n], C[:, 0:n], reads=[C], q="pool")
            b.barrier()

    def even_inproj(self, l, j, src):
        b = self.b
        self.psi = 0
        self.wi = 0
        with contextlib.ExitStack() as st:
            self.epsb = b.sb(st, "epsb", [128, 1])
            b.op("dve", lambda e: e.memset(self.epsb[:], EPS), writes=[self.epsb])
            xt = b.sb(st, "xt", [128, DC, 512])
            ht = self.subres(b.sb(st, "ht", [128, DC, 512]), DC)
            tmp = self.subres(b.sb(st, "tmp", [128, 2, 512]), 2)
            rstd = b.sb(st, "rstd", [128, 512])
            psn = b.ps(st, "psn", [128, 512])
            wts = [b.sb(st, "w%d" % i, [128, 16, 128]) for i in range(3)]
            pss = [b.ps(st, "pg%d" % i, [128, 512]) for i in range(3)]
            obs = [b.sb(st, "ob%d" % i, [128, 512]) for i in range(2)]
            wtok = [b.sb(st, "wk%d" % i, [128, 16, 256]) for i in range(2)]
            brow = b.sb(st, "brow", [1, 2064])
            bint = b.sb(st, "bint", [128, 40])
            pst = [b.ps(st, "pt%d" % i, [128, 256]) for i in range(2)]
            otb = [b.sb(st, "otb%d" % i, [128, 256]) for i in range(2)]
            b.dma(brow[:], self.bin_row[0:1, j, 3072:5136], writes=[brow])
            b.dma(bint[:], self.binT[:, j, :], writes=[bint])
            oi = [0]
            ti = 0
            for (t0, n, isctx) in tiles_tokens():
                b.dma(xt[:, :, 0:n], src[:, t0:t0 + n].rearrange("(c p) t -> p c t", p=128), writes=[xt])
                self.normmod(st, xt, ht, tmp, psn, rstd, n,
                             lambda c: self.scl[:, l, 0, c, isctx:isctx + 1],
                             lambda c: self.mod[:, l, c, isctx:isctx + 1], res_extra=[self.scl, self.mod])

                def evac(jc, ps, t0=t0, n=n):
                    ob = obs[oi[0] % 2]
                    oi[0] += 1
                    b.op("act", lambda e: e.activation(out=ob[:, 0:n], in_=ps[:, 0:n], func=AF.Identity,
                                                       bias=bint[:, jc:jc + 1]), reads=[ps, bint], writes=[ob])
                    dst = self.uT[jc * 128:(jc + 1) * 128, t0:t0 + n] if jc < 8 else \
                        self.qkT[(jc - 8) * 128:(jc - 7) * 128, t0:t0 + n]
                    b.dma(dst, ob[:, 0:n], reads=[ob], q="pool")

                self.gemm(st, self.w_in_r[j], DC, 24, lambda kc: ht[:, kc, 0:n], lambda kc: [ht.sub[kc]], n, evac, wts, pss)
                for blk in range(9):
                    c0 = 3072 + blk * 256
                    ncol = 256 if blk < 8 else 16
                    wt = wtok[ti % 2]
                    b.dma(wt[:, :, 0:ncol], self.w_in[j][:, c0:c0 + ncol].rearrange("(kc p) n -> p kc n", p=128), writes=[wt])
                    for ts in range(n // 128):
                        ps = pst[ti % 2]
                        ob = otb[ti % 2]
                        ti += 1
                        for kc in range(DC):
                            b.op("pe", lambda e, kc=kc, ts=ts, ps=ps, wt=wt, ncol=ncol: e.matmul(
                                ps[:, 0:ncol], ht[:, kc, ts * 128:(ts + 1) * 128], wt[:, kc, 0:ncol], start=(kc == 0), stop=False),
                                reads=[wt, ht.sub[kc]], writes=[ps], pe_acc=True)
                        b.op("pe", lambda e, ps=ps, ncol=ncol, c0=c0: e.matmul(
                            ps[:, 0:ncol], self.ones[0:1, 0:128], brow[0:1, c0 - 3072:c0 - 3072 + ncol], start=False, stop=True),
                            reads=[self.ones, brow], writes=[ps], pe_acc=True)
                        tt0 = t0 + ts * 128
                        if blk < 8:
                            b.op("act", lambda e, ob=ob, ps=ps: e.copy(out=ob[:, 0:256], in_=ps[:, 0:256]), reads=[ps], writes=[ob])
                            dst = (self.vtok if blk < 4 else self.otok)[tt0:tt0 + 128, (blk % 4) * 256:(blk % 4 + 1) * 256]
                            b.dma(dst, ob[:, 0:256], reads=[ob], q="pool")
                        else:
                            b.op("act", lambda e, ps=ps, tt0=tt0: e.copy(out=self.Gt[:, tt0 // 128, :], in_=ps[:, 0:16]),
                                 reads=[ps], writes=[self.Gt])
            b.barrier()

    def mlstm(self, j):
        b = self.b
        NCH = T // 128
        with contextlib.ExitStack() as st:
            LF = b.sb(st, "LF", [128, 2, NCH, 4])
            IG = b.sb(st, "IG", [128, 2, NCH, 4])
            Bc = b.sb(st, "Bc", [128, 2, NCH, 4])
            AL = b.sb(st, "AL", [128, 2, NCH, 4])
            BE = b.sb(st, "BE", [128, 2, NCH, 4])
            ALL = b.sb(st, "ALL", [128, 2, NCH, 4])
            gst = contextlib.ExitStack()
            psg = b.ps(gst, "psg", [128, 2, 256])
            psl = b.ps(gst, "psl", [128, 2, 256])
            for d in range(2):
                b.op("act", lambda e, d=d: e.activation(out=LF[:, d], in_=self.Gt[:, :, d * 8 + 4:d * 8 + 8], func=AF.Exp, scale=-1.0),
                     reads=[self.Gt], writes=[LF])
                b.op("dve", lambda e, d=d: e.tensor_copy(out=IG[:, d], in_=self.Gt[:, :, d * 8:d * 8 + 4]), reads=[self.Gt], writes=[IG])
            b.op("dve", lambda e: e.tensor_scalar_add(out=LF[:], in0=LF[:], scalar1=1.0), reads=[LF], writes=[LF])
            b.op("act", lambda e: e.activation(out=LF[:], in_=LF[:], func=AF.Ln), reads=[LF], writes=[LF])
            b.op("dve", lambda e: e.tensor_scalar_mul(out=LF[:], in0=LF[:], scalar1=-1.0), reads=[LF], writes=[LF])
            for d in range(2):
                b.op("pe", lambda e, d=d: e.matmul(psg[:, d, 0:NCH * 4], self.masks[:, d, :], LF[:, d].rearrange("p c h -> p (c h)"),
                                                   start=True, stop=True), reads=[self.masks, LF], writes=[psg], pe_acc=True)
                b.op("pe", lambda e, d=d: e.matmul(psl[:, d, 0:NCH * 4], self.ones[:, :], LF[:, d].rearrange("p c h -> p (c h)"),
                                                   start=True, stop=True), reads=[self.ones, LF], writes=[psl], pe_acc=True)
            b.op("dve", lambda e: e.tensor_copy(out=Bc[:].rearrange("p d c h -> p d (c h)"), in_=psg[:, :, 0:NCH * 4]), reads=[psg], writes=[Bc])
            b.op("act", lambda e: e.activation(out=AL[:], in_=Bc[:], func=AF.Exp), reads=[Bc], writes=[AL])
            b.op("act", lambda e: e.activation(out=ALL[:].rearrange("p d c h -> p d (c h)"), in_=psl[:, :, 0:NCH * 4], func=AF.Exp), reads=[psl], writes=[ALL])
            b.op("dve", lambda e: e.tensor_tensor(out=BE[:], in0=IG[:], in1=Bc[:], op=ALU.subtract), reads=[IG, Bc], writes=[BE])
            b.op("act", lambda e: e.activation(out=BE[:], in_=BE[:], func=AF.Exp), reads=[BE], writes=[BE])
            b.op("dve", lambda e: e.tensor_scalar_mul(out=BE[:], in0=BE[:], scalar1=1.0 / 16.0), reads=[BE], writes=[BE])
            b.barrier()
            gst.close()
            Cst = [[b.sb(st, "C%d%d" % (d, h), [128, 2, 257]) for h in range(4)] for d in range(2)]
            for d in range(2):
                for h in range(4):
                    b.op("pool", lambda e, d=d, h=h: e.memset(Cst[d][h][:], 0.0), writes=[Cst[d][h]])
            qt = [b.sb(st, "q%d" % i, [128, 8, 128]) for i in range(2)]
            kt = [b.sb(st, "k%d" % i, [128, 8, 128]) for i in range(2)]
            ktok = [b.sb(st, "kk%d" % i, [128, 1024]) for i in range(2)]
            Vp = [b.sb(st, "V%d" % i, [128, 4, 257]) for i in range(2)]
            for i in range(2):
                b.op("pool", lambda e, i=i: e.memset(Vp[i][:, :, 256:257], 1.0), writes=[Vp[i]])
            pkt = b.ps(st, "pkt", [128, 2, 512])
            pst_ = [b.ps(st, "pst%d" % i, [128, 512]) for i in range(2)]
            ppp = [b.ps(st, "ppp%d" % i, [128, 512]) for i in range(2)]
            pcc = [b.ps(st, "pcc%d" % i, [128, 512]) for i in range(2)]
            ST = [b.sb(st, "ST%d" % i, [128, 128]) for i in range(2)]
            V2 = [b.sb(st, "V2%d" % i, [128, 257]) for i in range(2)]
            sm = [b.sb(st, "sm%d" % i, [128, 4]) for i in range(2)]
            Hb = [b.sb(st, "Hb%d" % i, [128, 1024]) for i in range(2)]
            tC = b.sb(st, "tC", [128, 2, 257])
            order = [list(range(NCH)), [1, 0] + list(range(NCH - 1, 1, -1))]
            it = 0
            for s_ in range(NCH):
                for d in range(2):
                    c = order[d][s_]
                    t0 = c * 128
                    Q, K_, KT, V, H = qt[it % 2], kt[it % 2], ktok[it % 2], Vp[it % 2], Hb[it % 2]
                    it += 1
                    b.dma(Q[:], self.qkcT[0:1024, t0:t0 + 128].rearrange("(c p) t -> p c t", p=128), writes=[Q])
                    b.dma(K_[:], self.qkcT[1024:2048, t0:t0 + 128].rearrange("(c p) t -> p c t", p=128), writes=[K_])
                    b.dma(V[:, :, 0:256], self.vtok[t0:t0 + 128, :].rearrange("t (h e) -> t h e", h=4), writes=[V])
                    for fc in range(8):
                        b.op("pe", lambda e, fc=fc, K_=K_: e.transpose(pkt[:, fc // 4, (fc % 4) * 128:(fc % 4 + 1) * 128], K_[:, fc, :],
                                                                      self.ident[:, :]), reads=[K_, self.ident], writes=[pkt], pe_acc=True)
                    b.op("act", lambda e, KT=KT: e.copy(out=KT[:].rearrange("p (a x) -> p a x", a=2), in_=pkt[:]), reads=[pkt], writes=[KT])
                    for h in range(4):
                        i2 = (it * 4 + h) % 2
                        pS, pP, S_, V2_, sm_ = pst_[i2], ppp[i2], ST[i2], V2[i2], sm[i2]
                        for dc in range(2):
                            b.op("pe", lambda e, dc=dc, h=h, pS=pS, K_=K_, Q=Q: e.matmul(pS[:, 0:128], K_[:, 2 * h + dc, :], Q[:, 2 * h + dc, :],
                                                                                   start=(dc == 0), stop=(dc == 1)),
                                 reads=[K_, Q], writes=[pS], pe_acc=True)
                        b.op("dve", lambda e, pS=pS, S_=S_, d=d, c=c, h=h: e.scalar_tensor_tensor(
                            out=S_[:], in0=pS[:, 0:128], scalar=BE[:, d, c, h:h + 1], in1=self.masks[:, d, :], op0=ALU.mult, op1=ALU.mult),
                            reads=[pS, BE, self.masks], writes=[S_])
                        b.op("pe", lambda e, pP=pP, S_=S_, V=V, h=h: e.matmul(pP[:, 0:257], S_[:, :], V[:, h, :], start=True, stop=False),
                             reads=[S_, V], writes=[pP], pe_acc=True)
                        for dc in range(2):
                            b.op("pe", lambda e, pP=pP, Q=Q, dc=dc, h=h, d=d: e.matmul(pP[:, 0:257], Q[:, 2 * h + dc, :], Cst[d][h][:, dc, :],
                                                                                 start=False, stop=(dc == 1)),
                                 reads=[Q, Cst[d][h]], writes=[pP], pe_acc=True)
                        b.op("dve", lambda e, pP=pP, sm_=sm_, d=d, c=c, h=h: e.tensor_scalar(
                            out=sm_[:, 3:4], in0=pP[:, 256:257], scalar1=AL[:, d, c, h:h + 1], scalar2=None, op0=ALU.mult),
                            reads=[pP, AL], writes=[sm_])
                        b.op("act", lambda e, sm_=sm_: e.activation(out=sm_[:, 0:1], in_=sm_[:, 3:4], func=AF.Abs),
                             reads=[sm_], writes=[sm_])
                        b.op("dve", lambda e, sm_=sm_: e.tensor_scalar_max(out=sm_[:, 0:1], in0=sm_[:, 0:1], scalar1=1.0),
                             reads=[sm_], writes=[sm_])
                        b.op("dve", lambda e, sm_=sm_: e.reciprocal(out=sm_[:, 1:2], in_=sm_[:, 0:1]), reads=[sm_], writes=[sm_])
                        b.op("dve", lambda e, sm_=sm_, d=d, c=c, h=h: e.tensor_tensor(out=sm_[:, 2:3], in0=sm_[:, 1:2], in1=AL[:, d, c, h:h + 1], op=ALU.mult),
                             reads=[sm_, AL], writes=[sm_])
                        b.op("dve", lambda e, pP=pP, sm_=sm_, H=H, h=h: e.tensor_scalar(
                            out=H[:, h * 256:(h + 1) * 256], in0=pP[:, 0:256], scalar1=sm_[:, 2:3], scalar2=None, op0=ALU.mult),
                             reads=[pP, sm_], writes=[H])
                        b.op("pool", lambda e, V2_=V2_, V=V, h=h, d=d, c=c: e.tensor_scalar(
                            out=V2_[:], in0=V[:, h, :], scalar1=BE[:, d, c, h:h + 1], scalar2=None, op0=ALU.mult),
                            reads=[V, BE], writes=[V2_])
                        for dc in range(2):
                            b.op("pe", lambda e, dc=dc, h=h, KT=KT, V2_=V2_: e.matmul(pcc[dc][:, 0:257], KT[:, (2 * h + dc) * 128:(2 * h + dc + 1) * 128],
                                                                                  V2_[:], start=True, stop=True),
                                 reads=[KT, V2_], writes=[pcc[dc]], pe_acc=True)
                            b.op("dve", lambda e, d=d, h=h, dc=dc: e.tensor_tensor(out=tC[:, dc, :], in0=pcc[dc][:, 0:257], in1=Cst[d][h][:, dc, :], op=ALU.add),
                                 reads=[pcc[dc], Cst[d][h]], writes=[tC])
                        b.op("pool", lambda e, d=d, h=h, c=c: e.tensor_scalar(
                            out=Cst[d][h][:], in0=tC[:], scalar1=ALL[:, d, c, h:h + 1], scalar2=None, op0=ALU.mult),
                            reads=[tC, ALL], writes=[Cst[d][h]])
                    b.dma(self.hd[d][t0:t0 + 128, :], H[:], reads=[H], q="pool")
            b.barrier()
        with contextlib.ExitStack() as st:
            self.epsb = b.sb(st, "epsb", [128, 1])
            b.op("dve", lambda e: e.memset(self.epsb[:], EPS), writes=[self.epsb])
            nw = b.sb(st, "nw", [128, 1024])
            b.dma(nw[:], self.mlnorm[:, j, :], writes=[nw])
            h0 = [b.sb(st, "h0%d" % i, [128, 1024]) for i in range(2)]
            h1 = [b.sb(st, "h1%d" % i, [128, 1024]) for i in range(2)]
            og = [b.sb(st, "og%d" % i, [128, 1024]) for i in range(2)]
            junk = b.sb(st, "junk", [128, 256])
            ss = [b.sb(st, "ss%d" % i, [128, 4]) for i in range(2)]
            ptr = b.ps(st, "ptr", [128, 2, 512])
            ob = [b.sb(st, "fo%d" % i, [128, 1024]) for i in range(2)]
            for c in range(NCH):
                t0 = c * 128
                A, B_, O, SS, OB = h0[c % 2], h1[c % 2], og[c % 2], ss[c % 2], ob[c % 2]
                b.dma(A[:], self.hd[0][t0:t0 + 128, :], writes=[A])
                b.dma(B_[:], self.hd[1][t0:t0 + 128, :], writes=[B_])
                b.dma(O[:], self.otok[t0:t0 + 128, :], writes=[O])
                b.op("dve", lambda e, A=A, B_=B_: e.tensor_tensor(out=A[:], in0=A[:], in1=B_[:], op=ALU.add), reads=[A, B_], writes=[A])
                for h in range(4):
                    b.op("act", lambda e, A=A, SS=SS, h=h: e.activation(out=junk[:], in_=A[:, h * 256:(h + 1) * 256], func=AF.Square,
                                                                      accum_out=SS[:, h:h + 1]), reads=[A], writes=[junk, SS])
                b.op("act", lambda e, SS=SS: e.activation(out=SS[:], in_=SS[:], func=AF.Sqrt, scale=1.0 / 256, bias=self.epsb[:, 0:1]),
                     reads=[SS, self.epsb], writes=[SS])
                b.op("dve", lambda e, SS=SS: e.reciprocal(out=SS[:], in_=SS[:]), reads=[SS], writes=[SS])
                b.op("act", lambda e, O=O: e.activation(out=O[:], in_=O[:], func=AF.Sigmoid), reads=[O], writes=[O])
                b.op("pool", lambda e, O=O: e.tensor_tensor(out=O[:], in0=O[:], in1=nw[:], op=ALU.mult), reads=[O, nw], writes=[O])
                for h in range(4):
                    b.op("dve", lambda e, A=A, O=O, SS=SS, h=h: e.scalar_tensor_tensor(
                        out=A[:, h * 256:(h + 1) * 256], in0=A[:, h * 256:(h + 1) * 256], scalar=SS[:, h:h + 1],
                        in1=O[:, h * 256:(h + 1) * 256], op0=ALU.mult, op1=ALU.mult), reads=[A, O, SS], writes=[A])
                for fc in range(8):
                    b.op("pe", lambda e, A=A, fc=fc: e.transpose(ptr[:, fc // 4, (fc % 4) * 128:(fc % 4 + 1) * 128], A[:, fc * 128:(fc + 1) * 128],
                                                                  self.ident[:, :]), reads=[A, self.ident], writes=[ptr], pe_acc=True)
                b.op("act", lambda e, OB=OB: e.copy(out=OB[:].rearrange("p (a x) -> p a x", a=2), in_=ptr[:]), reads=[ptr], writes=[OB])
                b.dma(self.mixT[1024:2048, t0:t0 + 128].rearrange("(c p) t -> p c t", p=128), OB[:].rearrange("p (c t) -> p c t", c=8),
                      reads=[OB], q="pool")
            b.barrier()

    def even_out(self, l, j, src):
        b = self.b
        self.psi = 0
        self.wi = 0
        with contextlib.ExitStack() as st:
            bg = b.sb(st, "bg", [128, 8])
            b.dma(bg[:], self.bgluT[:, j, :], writes=[bg])
            mt = self.subres(b.sb(st, "mt", [128, DC, 512]), DC)
            mg = self.subres(b.sb(st, "mg", [128, 8, 512]), 8)
            xt = b.sb(st, "xt", [128, DC, 512])
            gt = [b.sb(st, "gt%d" % i, [128, 512]) for i in range(2)]
            wts = [b.sb(st, "w%d" % i, [128, 16, 128]) for i in range(4)]
            pss = [b.ps(st, "pg%d" % i, [128, 512]) for i in range(4)]
            gi = [0]
            for (t0, n, isctx) in tiles_tokens():
                b.dma(xt[:, :, 0:n], src[:, t0:t0 + n].rearrange("(c p) t -> p c t", p=128), writes=[xt])
                b.dma(mt[:, :, 0:n], self.mixT[:, t0:t0 + n].rearrange("(c p) t -> p c t", p=128), writes=mt.sub)

                def evac_glu(jc, ps, n=n):
                    g = gt[gi[0] % 2]
                    gi[0] += 1
                    b.op("act", lambda e: e.activation(out=g[:, 0:n], in_=ps[:, 0:n], func=AF.Sigmoid, bias=bg[:, jc:jc + 1]),
                         reads=[ps, bg], writes=[g])
                    b.op("dve", lambda e: e.tensor_tensor(out=rnd(mg[:, jc, 0:n]), in0=mt[:, jc, 0:n], in1=g[:, 0:n], op=ALU.mult),
                         reads=[g, mt.sub[jc]], writes=[mg.sub[jc]])

                self.gemm(st, self.w_glu[j], 8, 8, lambda kc: mt[:, kc, 0:n], lambda kc: [mt.sub[kc]], n, evac_glu, wts, pss, ksub=8)

                def evac(jc, ps, n=n, isctx=isctx):
                    b.op("dve", lambda e: e.scalar_tensor_tensor(
                        out=xt[:, jc, 0:n], in0=ps[:, 0:n], scalar=self.mod[:, l, 32 + jc, isctx:isctx + 1], in1=xt[:, jc, 0:n],
                        op0=ALU.mult, op1=ALU.add), reads=[ps, self.mod, xt], writes=[xt])

                self.gemm(st, self.w_out[j], DC, DC, lambda kc: (mg[:, kc, 0:n] if kc < 8 else mt[:, kc, 0:n]),
                          lambda kc: [mg.sub[kc] if kc < 8 else mt.sub[kc]], n, evac, wts, pss)
                b.dma(self.xs[:, t0:t0 + n].rearrange("(c p) t -> p c t", p=128), xt[:, :, 0:n], reads=[xt], q="pool")
            b.barrier()

    def s5(self, j):
        b = self.b
        TWO_PI = 2.0 * PI
        CH = 256
        NCHK = T // CH
        with contextlib.ExitStack() as st:
            def t4(name):
                return b.sb(st, name, [128, 2, 64, 1])
            LR, LI, STP, RHO, TH, THR, M_, SINT, COST, ABR, ABI, DEN, T2, AM1, CR, CI, CIS, CRS = [
                t4(n) for n in ("LR", "LI", "STP", "RHO", "TH", "THR", "M_", "SINT", "COST", "ABR", "ABI", "DEN", "T2", "AM1",
                                "CR", "CI", "CIS", "CRS")]
            halfpi = b.sb(st, "halfpi", [128, 1])
            sgn = b.sb(st, "sgn", [128, 2])
            gmask = b.sb(st, "gmask", [128, 8])
            psw = b.sb(st, "psw", [128, 128])
            s5d = b.sb(st, "s5d", [128, 8])
            b.op("dve", lambda e: e.memset(halfpi[:], PI / 2), writes=[halfpi])
            b.dma(sgn[:], self.s5sgn[:, :], writes=[sgn])
            b.dma(gmask[:], self.s5gmask[:, :], writes=[gmask])
            b.dma(psw[:], self.s5psw[:, :], writes=[psw])
            b.dma(s5d[:], self.s5dT[:, j, :], writes=[s5d])
            b.dma(LR[:], self.s5lam[:, 0, j, :, :].unsqueeze(3), writes=[LR])
            b.dma(LI[:], self.s5lam[:, 1, j, :, :].unsqueeze(3), writes=[LI])
            b.dma(STP[:], self.s5ls[:, j, :, :].unsqueeze(3), writes=[STP])

            def tt(out, a, c, op, eng="dve"):
                b.op(eng, lambda e: e.tensor_tensor(out=out[:], in0=a[:], in1=c[:], op=op), reads=[a, c], writes=[out])

            def act(out, a, func, **kw):
                extra = [kw["bias"].tile] if hasattr(kw.get("bias", None), "tile") else []
                b.op("act", lambda e: e.activation(out=out[:], in_=a[:], func=func, **kw), reads=[a], writes=[out])

            act(STP, STP, AF.Exp)
            tt(T2, LR, STP, ALU.mult)
            act(RHO, T2, AF.Exp)
            tt(TH, LI, STP, ALU.mult)
            b.op("dve", lambda e: e.tensor_copy(out=THR[:], in_=TH[:]), reads=[TH], writes=[THR])
            for k in range(4):
                thr = (2 * k + 1) * PI
                b.op("dve", lambda e, thr=thr: e.tensor_scalar(out=M_[:], in0=TH[:], scalar1=thr, scalar2=None, op0=ALU.is_ge),
                     reads=[TH], writes=[M_])
                b.op("dve", lambda e: e.scalar_tensor_tensor(out=THR[:], in0=M_[:], scalar=-TWO_PI, in1=THR[:], op0=ALU.mult, op1=ALU.add),
                     reads=[M_, THR], writes=[THR])
            act(SINT, THR, AF.Sin)
            act(T2, THR, AF.Abs)
            b.op("act", lambda e: e.activation(out=COST[:], in_=T2[:], func=AF.Sin, scale=-1.0, bias=halfpi[:, 0:1]),
                 reads=[T2, halfpi], writes=[COST])
            tt(ABR, RHO, COST, ALU.mult)
            tt(ABI, RHO, SINT, ALU.mult)
            tt(DEN, LR, LR, ALU.mult)
            tt(T2, LI, LI, ALU.mult)
            tt(DEN, DEN, T2, ALU.add)
            b.op("dve", lambda e: e.reciprocal(out=DEN[:], in_=DEN[:]), reads=[DEN], writes=[DEN])
            b.op("dve", lambda e: e.tensor_scalar_add(out=AM1[:], in0=ABR[:], scalar1=-1.0), reads=[ABR], writes=[AM1])
            tt(CR, AM1, LR, ALU.mult)
            tt(T2, ABI, LI, ALU.mult)
            tt(CR, CR, T2, ALU.add)
            tt(CR, CR, DEN, ALU.mult)
            tt(CI, ABI, LR, ALU.mult)
            tt(T2, AM1, LI, ALU.mult)
            tt(CI, CI, T2, ALU.subtract)
            tt(CI, CI, DEN, ALU.mult)
            b.op("dve", lambda e: e.tensor_scalar(out=CIS[:], in0=CI[:], scalar1=sgn[:, 0:1], scalar2=None, op0=ALU.mult), reads=[CI, sgn], writes=[CIS])
            b.op("dve", lambda e: e.tensor_scalar(out=CRS[:], in0=CR[:], scalar1=sgn[:, 1:2], scalar2=None, op0=ALU.mult), reads=[CR, sgn], writes=[CRS])
            BTA = b.sb(st, "BTA", [128, 2, 8, 128])
            BTB = b.sb(st, "BTB", [128, 2, 8, 128])
            CP = b.sb(st, "CP", [128, 2, 64, 16])
            CPADS = [b.sb(st, "CPAD%d" % d, [128, 8, 128]) for d in range(2)]
            for d in range(2):
                b.op("pool", lambda e, d=d: e.memset(CPADS[d][:], 0.0), writes=[CPADS[d]])
            b.dma(CP[:], self.s5CX[:, j, :, :, :], writes=[CP])
            b.op("dve", lambda e: e.tensor_scalar(out=CP[:], in0=CP[:], scalar1=sgn[:, 1:2], scalar2=None, op0=ALU.mult), reads=[CP, sgn], writes=[CP])
            with contextlib.ExitStack() as s2:
                BX = b.sb(s2, "BX", [128, 2, 64, 16])
                BY = b.sb(s2, "BY", [128, 2, 64, 16])
                SA = b.sb(s2, "SA", [128, 2, 64, 16])
                SB = b.sb(s2, "SB", [128, 2, 64, 16])
                TM = b.sb(s2, "TM", [128, 2, 64, 16])
                ptr = b.ps(s2, "ptr5", [128, 512])
                b.dma(BX[:], self.s5BX[:, j, :, :, :], writes=[BX])
                b.dma(BY[:], self.s5BY[:, j, :, :, :], writes=[BY])
                shp = [128, 2, 64, 16]

                def ttb(out, a, col, op):
                    b.op("dve", lambda e: e.tensor_tensor(out=out[:], in0=a[:], in1=col[:].to_broadcast(shp), op=op), reads=[a, col], writes=[out])
                ttb(SA, BX, CR, ALU.mult)
                ttb(TM, BY, CIS, ALU.mult)
                tt(SA, SA, TM, ALU.add)
                ttb(SB, BY, CRS, ALU.mult)
                ttb(TM, BX, CI, ALU.mult)
                tt(SB, SB, TM, ALU.add)
                for (S_, BT_) in ((SA, BTA), (SB, BTB)):
                    for d in range(2):
                        for half in range(2):
                            for k in range(4):
                                fc = half * 4 + k
                                b.op("pe", lambda e, S_=S_, d=d, fc=fc, k=k: e.transpose(
                                    ptr[:, k * 128:(k + 1) * 128], S_[:, d, fc * 8:(fc + 1) * 8, :].rearrange("p g c -> p (g c)"),
                                    self.ident[:, :]), reads=[S_, self.ident], writes=[ptr], pe_acc=True)
                            b.op("act", lambda e, BT_=BT_, d=d, half=half: e.copy(
                                out=BT_[:, d, half * 4:(half + 1) * 4, :].rearrange("p a x -> p (a x)"), in_=ptr[:, :]),
                                reads=[ptr], writes=[BT_])
                b.barrier()
            UT = b.sb(st, "UT", [128, T])
            Y = b.sb(st, "Y", [128, T])
            ER = b.sb(st, "ER", [128, 8, CH])
            EI = b.sb(st, "EI", [128, 8, CH])
            T1 = b.sb(st, "T1", [128, 8, CH // 2])
            T2b = b.sb(st, "T2b", [128, 8, CH // 2])
            RHOT = b.sb(st, "RHOT", [128, 8, CH])
            BPAD = b.sb(st, "BPAD", [128, 2, 8, 128])
            XCAR = b.sb(st, "XCAR", [128, 8])
            BTl = [b.sb(st, "BTl%d" % i, [128, CH]) for i in range(2)]
            TTl = [b.sb(st, "TTl%d" % i, [128, CH]) for i in range(2)]
            Gl = [b.sb(st, "Gl%d" % i, [128, CH]) for i in range(2)]
            Xl = [b.sb(st, "Xl%d" % i, [128, CH]) for i in range(2)]
            pbu = [b.ps(st, "pbu%d" % i, [128, 2, CH]) for i in range(2)]
            psw_ps = [b.ps(st, "psw%d" % i, [128, 512]) for i in range(2)]
            pyy = [b.ps(st, "pyy%d" % i, [128, 512]) for i in range(2)]
            GE = b.sb(st, "GE", [128, T])
            order = [list(range(NCHK)), [0] + list(range(NCHK - 1, 0, -1))]
            it = 0
            yi = 0
            for fc in range(8):
                b.dma(UT[:], self.uT[fc * 128:(fc + 1) * 128, :], writes=[UT])
                for d in range(2):
                    g0 = fc * 8
                    b.op("dve", lambda e, d=d, g0=g0: e.tensor_copy(out=ER[:, :, 0:1], in_=COST[:, d, g0:g0 + 8, :]), reads=[COST], writes=[ER])
                    b.op("dve", lambda e, d=d, g0=g0: e.tensor_copy(out=EI[:, :, 0:1], in_=SINT[:, d, g0:g0 + 8, :]), reads=[SINT], writes=[EI])
                    n = 1
                    while n < CH:
                        bs = [128, 8, n]
                        b.op("dve", lambda e, n=n, bs=bs: e.tensor_tensor(out=T1[:, :, 0:n], in0=ER[:, :, 0:n], in1=ER[:, :, n - 1:n].to_broadcast(bs), op=ALU.mult),
                             reads=[ER], writes=[T1])
                        b.op("pool", lambda e, n=n, bs=bs: e.tensor_tensor(out=T2b[:, :, 0:n], in0=EI[:, :, 0:n], in1=EI[:, :, n - 1:n].to_broadcast(bs), op=ALU.mult),
                             reads=[EI], writes=[T2b])
                        b.op("dve", lambda e, n=n: e.tensor_tensor(out=ER[:, :, n:2 * n], in0=T1[:, :, 0:n], in1=T2b[:, :, 0:n], op=ALU.subtract),
                             reads=[T1, T2b, EI], writes=[ER])
                        b.op("dve", lambda e, n=n, bs=bs: e.tensor_tensor(out=T1[:, :, 0:n], in0=ER[:, :, 0:n], in1=EI[:, :, n - 1:n].to_broadcast(bs), op=ALU.mult),
                             reads=[ER, EI], writes=[T1])
                        b.op("pool", lambda e, n=n, bs=bs: e.tensor_tensor(out=T2b[:, :, 0:n], in0=EI[:, :, 0:n], in1=ER[:, :, n - 1:n].to_broadcast(bs), op=ALU.mult),
                             reads=[EI, ER], writes=[T2b])
                        b.op("dve", lambda e, n=n: e.tensor_tensor(out=EI[:, :, n:2 * n], in0=T1[:, :, 0:n], in1=T2b[:, :, 0:n], op=ALU.add),
                             reads=[T1, T2b, ER], writes=[EI])
                        n *= 2
                    b.op("dve", lambda e, d=d, g0=g0: e.tensor_copy(out=RHOT[:], in_=RHO[:, d, g0:g0 + 8, :].to_broadcast([128, 8, CH])),
                         reads=[RHO], writes=[RHOT])
                    for gp in range(8):
                        b.op("pool", lambda e, gp=gp, d=d, fc=fc: e.tensor_scalar(out=BPAD[:, 0, gp, :], in0=BTA[:, d, fc, :], scalar1=gmask[:, gp:gp + 1],
                                                                                  scalar2=None, op0=ALU.mult), reads=[BTA, gmask], writes=[BPAD])
                        b.op("pool", lambda e, gp=gp, d=d, fc=fc: e.tensor_scalar(out=BPAD[:, 1, gp, :], in0=BTB[:, d, fc, :], scalar1=gmask[:, gp:gp + 1],
                                                                                  scalar2=None, op0=ALU.mult), reads=[BTB, gmask], writes=[BPAD])
                        b.op("dve", lambda e, gp=gp, d=d, g0=g0: e.tensor_copy(out=CPADS[d][:, gp, gp * 16:(gp + 1) * 16], in_=CP[:, d, g0 + gp, :]),
                             reads=[CP], writes=[CPADS[d]])
                    b.op("dve", lambda e: e.memset(XCAR[:], 0.0), writes=[XCAR])
                    for ck in order[d]:
                        c0 = ck * CH
                        py = pyy[yi % 2]
                        yi += 1
                        for gp in range(8):
                            BT_, TT_, G_, X_, pb, pw = BTl[it % 2], TTl[it % 2], Gl[it % 2], Xl[it % 2], pbu[it % 2], psw_ps[it % 2]
                            it += 1
                            for ab in range(2):
                                b.op("pe", lambda e, ab=ab, gp=gp, pb=pb, c0=c0: e.matmul(pb[:, ab, :], BPAD[:, ab, gp, :], UT[:, c0:c0 + CH], start=True, stop=True),
                                     reads=[BPAD, UT], writes=[pb], pe_acc=True)
                            if d == 0:
                                cosv, sinv = ER[:, gp, :], EI[:, gp, :]
                                rv = lambda ap: ap
                            else:
                                cosv, sinv = ER[:, gp, ::-1], EI[:, gp, ::-1]
                                rv = lambda ap: ap[:, ::-1]
                            b.op("dve", lambda e, BT_=BT_, pb=pb, cosv=cosv: e.tensor_tensor(out=BT_[:], in0=pb[:, 0, :], in1=cosv, op=ALU.mult),
                                 reads=[pb, ER], writes=[BT_])
                            b.op("dve", lambda e, TT_=TT_, pb=pb, sinv=sinv: e.tensor_tensor(out=TT_[:], in0=pb[:, 1, :], in1=sinv, op=ALU.mult),
                                 reads=[pb, EI], writes=[TT_])
                            b.op("pool", lambda e, BT_=BT_, TT_=TT_: e.tensor_tensor(out=BT_[:], in0=BT_[:], in1=TT_[:], op=ALU.add),
                                 reads=[BT_, TT_], writes=[BT_])
                            b.op("dve", lambda e, G_=G_, BT_=BT_, gp=gp, rv=rv: e.tensor_tensor_scan(
                                rv(G_[:]), RHOT[:, gp, :], rv(BT_[:]), XCAR[:, gp:gp + 1], ALU.mult, ALU.add),
                                reads=[RHOT, BT_, XCAR], writes=[G_])
                            b.op("pe", lambda e, pw=pw, G_=G_: e.matmul(pw[:, 0:CH], psw[:, :], G_[:], start=True, stop=True),
                                 reads=[psw, G_], writes=[pw], pe_acc=True)
                            b.op("pool", lambda e, X_=X_, G_=G_, cosv=cosv: e.tensor_tensor(out=X_[:], in0=G_[:], in1=cosv, op=ALU.mult),
                                 reads=[G_, ER], writes=[X_])
                            b.op("dve", lambda e, TT_=TT_, pw=pw, sinv=sinv: e.tensor_tensor(out=TT_[:], in0=pw[:, 0:CH], in1=sinv, op=ALU.mult),
                                 reads=[pw, EI], writes=[TT_])
                            b.op("pool", lambda e, X_=X_, TT_=TT_: e.tensor_tensor(out=X_[:], in0=X_[:], in1=TT_[:], op=ALU.subtract),
                                 reads=[X_, TT_], writes=[X_])
                            last = CH - 1 if d == 0 else 0
                            b.op("act", lambda e, X_=X_, gp=gp, last=last: e.copy(out=XCAR[:, gp:gp + 1], in_=X_[:, last:last + 1]),
                                 reads=[X_], writes=[XCAR])
                            b.op("pe", lambda e, py=py, X_=X_, gp=gp, d=d: e.matmul(py[:, 0:CH], CPADS[d][:, gp, :], X_[:], start=(gp == 0), stop=(gp == 7)),
                                 reads=[CPADS[d], X_], writes=[py], pe_acc=True)
                        if d == 0:
                            b.op("act", lambda e, py=py, c0=c0: e.copy(out=Y[:, c0:c0 + CH], in_=py[:, 0:CH]), reads=[py], writes=[Y])
                        else:
                            b.op("dve", lambda e, py=py, c0=c0: e.tensor_tensor(out=Y[:, c0:c0 + CH], in0=py[:, 0:CH], in1=Y[:, c0:c0 + CH], op=ALU.add),
                                 reads=[py, Y], writes=[Y])
                b.op("dve", lambda e, fc=fc: e.scalar_tensor_tensor(out=Y[:], in0=UT[:], scalar=s5d[:, fc:fc + 1], in1=Y[:], op0=ALU.mult, op1=ALU.add),
                     reads=[UT, s5d, Y], writes=[Y])
                b.op("pool", lambda e: e.tensor_tensor(out=GE[:], in0=Y[:], in1=Y[:], op=ALU.mult), reads=[Y], writes=[GE])
                b.op("dve", lambda e: e.tensor_scalar(out=GE[:], in0=GE[:], scalar1=0.044715, scalar2=1.0, op0=ALU.mult, op1=ALU.add),
                     reads=[GE], writes=[GE])
                b.op("pool", lambda e: e.tensor_tensor(out=GE[:], in0=GE[:], in1=Y[:], op=ALU.mult), reads=[GE, Y], writes=[GE])
                b.op("act", lambda e: e.activation(out=GE[:], in_=GE[:], func=AF.Tanh, scale=0.7978845608028654), reads=[GE], writes=[GE])
                b.op("dve", lambda e: e.scalar_tensor_tensor(out=GE[:], in0=GE[:], scalar=1.0, in1=Y[:], op0=ALU.add, op1=ALU.mult),
                     reads=[GE, Y], writes=[GE])
                b.op("pool", lambda e: e.tensor_scalar(out=GE[:], in0=GE[:], scalar1=0.5, scalar2=None, op0=ALU.mult), reads=[GE], writes=[GE])
                b.dma(self.mixT[fc * 128:(fc + 1) * 128, :], GE[:], reads=[GE], q="pool")
            b.barrier()

    def odd_norm(self, l, src):
        b = self.b
        with contextlib.ExitStack() as st:
            self.epsb = b.sb(st, "epsb", [128, 1])
            b.op("dve", lambda e: e.memset(self.epsb[:], EPS), writes=[self.epsb])
            xt = b.sb(st, "xt", [128, DC, 512])
            ht = self.subres(b.sb(st, "ht", [128, DC, 512]), DC)
            tmp = self.subres(b.sb(st, "tmp", [128, 2, 512]), 2)
            rstd = b.sb(st, "rstd", [128, 512])
            psn = b.ps(st, "psn", [128, 512])
            for (t0, n, isctx) in tiles_tokens():
                b.dma(xt[:, :, 0:n], src[:, t0:t0 + n].rearrange("(c p) t -> p c t", p=128), writes=[xt])
                self.normmod(st, xt, ht, tmp, psn, rstd, n,
                             lambda c: self.scl[:, l, 0, c, isctx:isctx + 1],
                             lambda c: self.mod[:, l, c, isctx:isctx + 1], res_extra=[self.scl, self.mod])
                b.dma(self.hT[:, t0:t0 + n].rearrange("(c p) t -> p c t", p=128), ht[:, :, 0:n], reads=ht.sub, q="pool")
            b.barrier()

    def odd_proj(self, l, j):
        b = self.b
        self.psi = 0
        self.wi = 0
        N = 256
        with contextlib.ExitStack() as st:
            mu = b.sb(st, "mu", [128, 6, 16])
            kkw = b.sb(st, "kkw", [128, 16])
            kaw = b.sb(st, "kaw", [128, 16])
            oma = b.sb(st, "oma", [128, 16])
            nw0 = b.sb(st, "nw0", [128, 2, 16])
            a0 = b.sb(st, "a0", [128, 2, 16])
            v0 = b.sb(st, "v0", [128, 16])
            blk = b.sb(st, "blk", [128, 128])
            mhalf = b.sb(st, "mhalf", [128, 1])
            tiny = b.sb(st, "tiny", [128, 1])
            b.dma(mu[:], self.muT[:, j, :, :], writes=[mu])
            b.dma(kkw[:], self.kkwT[:, j, :], writes=[kkw])
            b.dma(kaw[:], self.kawT[:, j, :], writes=[kaw])
            b.dma(nw0[:], self.w0T[:, j, :, :], writes=[nw0])
            b.dma(a0[:], self.a0T[:, j, :, :], writes=[a0])
            b.dma(blk[:], self.blk64[:, :], writes=[blk])
            if j > 0:
                b.dma(v0[:], self.v0T[:, j - 1, :], writes=[v0])
            b.op("dve", lambda e: e.tensor_scalar_mul(out=nw0[:], in0=nw0[:], scalar1=-1.0), reads=[nw0], writes=[nw0])
            b.op("dve", lambda e: e.tensor_scalar(out=oma[:], in0=kaw[:], scalar1=-1.0, scalar2=1.0, op0=ALU.mult, op1=ALU.add), reads=[kaw], writes=[oma])
            b.op("dve", lambda e: e.memset(mhalf[:], -0.5), writes=[mhalf])
            b.op("dve", lambda e: e.memset(tiny[:], 0.0), writes=[tiny])
            H = b.sb(st, "H", [128, DC, N])
            DX = b.sb(st, "DX", [128, DC, N])
            X = self.subres(b.sb(st, "X", [128, DC, N]), DC)
            Kt = self.subres(b.sb(st, "Kt", [128, DC, N]), DC)
            KK = self.subres(b.sb(st, "KK", [128, DC, N]), DC)
            wts = [b.sb(st, "w%d" % i, [128, 16, 128]) for i in range(3)]
            pss = [b.ps(st, "pg%d" % i, [128, 512]) for i in range(3)]
            obs = [b.sb(st, "ob%d" % i, [128, N]) for i in range(4)]
            sq = [b.sb(st, "sq%d" % i, [128, N]) for i in range(2)]
            pl = [b.ps(st, "pl%d" % i, [128, 512]) for i in range(2)]
            pq = [b.ps(st, "pq%d" % i, [128, 512]) for i in range(2)]
            L1 = [b.sb(st, "L1%d" % i, [128, 2, N]) for i in range(2)]
            w1t = [b.sb(st, "w1t%d" % i, [128, 16, 256]) for i in range(1)]
            w2t = [b.sb(st, "w2t%d" % i, [128, 2, 2048]) for i in range(2)]
            vft = [b.sb(st, "vft%d" % i, [128, N]) for i in range(2)]
            cnt = {"o": 0, "s": 0, "l": 0, "w": 0, "q": 0, "v": 0}

            def nxt(lst, key):
                t_ = lst[cnt[key] % len(lst)]
                cnt[key] += 1
                return t_

            def store(dst, rows0, t0, ob, n):
                b.dma(dst[rows0:rows0 + 128, t0:t0 + n], ob[:, 0:n], reads=[ob], q="pool")

            tiles = [(0, TC, 1)] + [(TC + i * N, N, 0) for i in range(TL // N)]
            import os as _os
            SUB = int(_os.environ.get("ODD_SUB", "99"))
            tiles = tiles[:int(_os.environ.get("ODD_TILES", "99"))]
            for (t0, n, isctx) in tiles:
                b.dma(H[:], self.hT[:, t0:t0 + n].rearrange("(c p) t -> p c t", p=128), writes=[H])
                hv = lambda c0, c1, a, e_: self.hT[c0 * 128:c1 * 128, a:e_].rearrange("(c p) t -> p c t", p=128)
                if isctx:
                    b.op("pool", lambda e: e.memset(DX[:, 0:8, 0:1], 0.0), writes=[DX])
                    b.op("pool", lambda e: e.memset(DX[:, 8:16, n - 1:n], 0.0), writes=[DX])
                    b.dma(DX[:, 0:8, 1:n], hv(0, 8, 0, n - 1), writes=[DX])
                    b.dma(DX[:, 8:16, 0:n - 1], hv(8, 16, 1, n), writes=[DX])
                else:
                    first = (t0 == TC)
                    lastt = (t0 + n == T)
                    b.dma(DX[:, 0:4, :], hv(0, 4, t0 - 1, t0 + n - 1), writes=[DX])
                    if lastt:
                        b.dma(DX[:, 4:8, 0:n - 1], hv(4, 8, t0 + 1, t0 + n), writes=[DX])
                    else:
                        b.dma(DX[:, 4:8, :], hv(4, 8, t0 + 1, t0 + n + 1), writes=[DX])
                    b.dma(DX[:, 8:12, :], hv(8, 12, t0 - 64, t0 + n - 64), writes=[DX])
                    if lastt:
                        b.dma(DX[:, 12:16, 0:n - 64], hv(12, 16, t0 + 64, t0 + n), writes=[DX])
                        b.op("pool", lambda e: e.memset(DX[:, 12:16, n - 64:n], 0.0), writes=[DX])
                    else:
                        b.dma(DX[:, 12:16, :], hv(12, 16, t0 + 64, t0 + n + 64), writes=[DX])
                    if first:
                        b.op("pool", lambda e: e.memset(DX[:, 8:12, 0:64], 0.0), writes=[DX])
                    for c in range(4):
                        b.op("pool", lambda e, c=c: e.memset(DX[:, c, :].rearrange("p (r w) -> p r w", w=64)[:, :, 0:1], 0.0), writes=[DX])
                        b.op("pool", lambda e, c=c: e.memset(DX[:, 4 + c, :].rearrange("p (r w) -> p r w", w=64)[:, :, 63:64], 0.0), writes=[DX])
                b.op("dve", lambda e: e.tensor_tensor(out=DX[:], in0=DX[:], in1=H[:], op=ALU.subtract), reads=[DX, H], writes=[DX])

                def mix(i):
                    for c in range(DC):
                        b.op("dve", lambda e, c=c: e.scalar_tensor_tensor(
                            out=rnd(X[:, c, 0:n]), in0=DX[:, c, 0:n], scalar=mu[:, i, c:c + 1], in1=H[:, c, 0:n], op0=ALU.mult, op1=ALU.add),
                            reads=[DX, mu, H], writes=[X.sub[c]])

                xr = lambda kc: X[:, kc, 0:n]
                xres = lambda kc: [X.sub[kc]]

                def lora1(W1, r_, func, li_):
                    wt = nxt(w1t, "w")
                    b.dma(wt[:, :, 0:r_], W1.rearrange("(kc p) r -> p kc r", p=128), writes=[wt])
                    Lt = L1[li_]
                    for m0 in range(0, r_, 128):
                        mm = min(128, r_ - m0)
                        ps = nxt(pl, "l")
                        for kc in range(DC):
                            b.op("pe", lambda e, kc=kc, ps=ps, wt=wt, m0=m0, mm=mm: e.matmul(ps[0:mm, 0:n], wt[:, kc, m0:m0 + mm], X[:, kc, 0:n],
                                                                                       start=(kc == 0), stop=(kc == DC - 1)),
                                 reads=[wt, X.sub[kc]], writes=[ps], pe_acc=True)
                        b.op("act", lambda e, ps=ps, Lt=Lt, m0=m0, mm=mm: e.activation(out=Lt[0:mm, m0 // 128, 0:n], in_=ps[0:mm, 0:n], func=func),
                             reads=[ps], writes=[Lt])
                    return Lt

                def lora2_load(W2, r_):
                    wt = nxt(w2t, "q")
                    for m0 in range(0, r_, 128):
                        mm = min(128, r_ - m0)
                        b.dma(wt[0:mm, m0 // 128, :], W2[m0:m0 + mm, :], writes=[wt])
                    return wt

                def lora2(ps, wt, Lt, r_, jc):
                    nk = (r_ + 127) // 128
                    for ki in range(nk):
                        mm = min(128, r_ - ki * 128)
                        b.op("pe", lambda e, ki=ki, mm=mm: e.matmul(ps[:, 0:n], wt[0:mm, ki, jc * 128:(jc + 1) * 128], Lt[0:mm, ki, 0:n],
                                                                      start=(ki == 0), stop=(ki == nk - 1)),
                             reads=[wt, Lt], writes=[ps], pe_acc=True)

                mix(0)

                def ev_r(jc, ps):
                    ob = nxt(obs, "o")
                    b.op("act", lambda e: e.copy(out=ob[:, 0:n], in_=ps[:, 0:n]), reads=[ps], writes=[ob])
                    store(self.rT, jc * 128, t0, ob, n)
                self.gemm(st, self.w_r[j], DC, DC, xr, xres, n, ev_r, wts, pss)
                if SUB < 2:
                    continue
                mix(2)

                def ev_k(jc, ps):
                    b.op("act", lambda e: e.copy(out=Kt[:, jc, 0:n], in_=ps[:, 0:n]), reads=[ps], writes=[Kt.sub[jc]])

                def kk_post():
                    for jc in range(DC):
                        b.op("dve", lambda e, jc=jc: e.tensor_scalar(out=KK[:, jc, 0:n], in0=Kt[:, jc, 0:n], scalar1=kkw[:, jc:jc + 1], scalar2=None, op0=ALU.mult),
                             reads=[Kt.sub[jc], kkw], writes=[KK.sub[jc]])
                        sq_ = nxt(sq, "s")
                        b.op("act", lambda e, jc=jc, sq_=sq_: e.activation(out=sq_[:, 0:n], in_=KK[:, jc, 0:n], func=AF.Square), reads=[KK.sub[jc]], writes=[sq_])
                        pq_ = nxt(pq, "v")
                        b.op("pe", lambda e, sq_=sq_, pq_=pq_: e.matmul(pq_[:, 0:n], blk[:, :], sq_[:, 0:n], start=True, stop=True), reads=[blk, sq_], writes=[pq_], pe_acc=True)
                        b.op("dve", lambda e, sq_=sq_, pq_=pq_: e.tensor_scalar_max(out=sq_[:, 0:n], in0=pq_[:, 0:n], scalar1=1e-24), reads=[pq_], writes=[sq_])
                        b.op("act", lambda e, sq_=sq_: e.activation(out=sq_[:, 0:n], in_=sq_[:, 0:n], func=AF.Sqrt), reads=[sq_], writes=[sq_])
                        b.op("dve", lambda e, sq_=sq_: e.reciprocal(out=sq_[:, 0:n], in_=sq_[:, 0:n]), reads=[sq_], writes=[sq_])
                        b.op("dve", lambda e, jc=jc, sq_=sq_: e.tensor_tensor(out=KK[:, jc, 0:n], in0=KK[:, jc, 0:n], in1=sq_[:, 0:n], op=ALU.mult),
                             reads=[KK.sub[jc], sq_], writes=[KK.sub[jc]])
                        b.dma(self.kkT[jc * 128:(jc + 1) * 128, t0:t0 + n], KK[:, jc, 0:n], reads=[KK.sub[jc]], q="pool")
                self.gemm(st, self.w_k[j], DC, DC, xr, xres, n, ev_k, wts, pss)
                kk_post()
                if SUB < 3:
                    continue
                mix(3)
                if j > 0:
                    Lv = lora1(self.v1[j - 1], 64, AF.Copy, 0)
                    wv2 = lora2_load(self.v2[j - 1], 64)

                def ev_v(jc, ps):
                    ob = nxt(obs, "o")
                    if j == 0:
                        b.op("act", lambda e: e.copy(out=ob[:, 0:n], in_=ps[:, 0:n]), reads=[ps], writes=[ob])
                        store(self.vfT, jc * 128, t0, ob, n)
                    else:
                        pq_ = nxt(pq, "v")
                        lora2(pq_, wv2, Lv, 64, jc)
                        sg_ = nxt(sq, "s")
                        vf_ = nxt(vft, "v")
                        b.dma(vf_[:, 0:n], self.vfT[jc * 128:(jc + 1) * 128, t0:t0 + n], writes=[vf_])
                        b.op("act", lambda e: e.activation(out=sg_[:, 0:n], in_=pq_[:, 0:n], func=AF.Sigmoid, bias=v0[:, jc:jc + 1]),
                             reads=[pq_, v0], writes=[sg_])
                        b.op("dve", lambda e: e.tensor_tensor(out=vf_[:, 0:n], in0=vf_[:, 0:n], in1=ps[:, 0:n], op=ALU.subtract), reads=[vf_, ps], writes=[vf_])
                        b.op("pool", lambda e: e.tensor_tensor(out=vf_[:, 0:n], in0=vf_[:, 0:n], in1=sg_[:, 0:n], op=ALU.mult), reads=[vf_, sg_], writes=[vf_])
                        b.op("dve", lambda e: e.tensor_tensor(out=ob[:, 0:n], in0=vf_[:, 0:n], in1=ps[:, 0:n], op=ALU.add), reads=[vf_, ps], writes=[ob])
                        store(self.vT, jc * 128, t0, ob, n)
                self.gemm(st, self.w_v[j], DC, DC, xr, xres, n, ev_v, wts, pss)
                if SUB < 4:
                    continue
                mix(1)
                for d in range(2):
                    Lw = lora1(self.w1[j, d], 96, AF.Tanh, d)
                    ww2 = lora2_load(self.w2[j, d], 96)
                    for jc in range(DC):
                        pq_ = nxt(pq, "v")
                        lora2(pq_, ww2, Lw, 96, jc)
                        ob = nxt(obs, "o")
                        b.op("act", lambda e: e.activation(out=ob[:, 0:n], in_=pq_[:, 0:n], func=AF.Exp, scale=-1.0, bias=nw0[:, d, jc:jc + 1]),
                             reads=[pq_, nw0], writes=[ob])
                        b.op("dve", lambda e: e.tensor_scalar_add(out=ob[:, 0:n], in0=ob[:, 0:n], scalar1=1.0), reads=[ob], writes=[ob])
                        b.op("act", lambda e: e.activation(out=ob[:, 0:n], in_=ob[:, 0:n], func=AF.Ln), reads=[ob], writes=[ob])
                        b.op("act", lambda e: e.activation(out=ob[:, 0:n], in_=ob[:, 0:n], func=AF.Exp, scale=-1.0, bias=mhalf[:, 0:1]),
                             reads=[ob, mhalf], writes=[ob])
                        b.op("dve", lambda e: e.tensor_scalar_mul(out=ob[:, 0:n], in0=ob[:, 0:n], scalar1=-1.0), reads=[ob], writes=[ob])
                        store(self.lwT[d], jc * 128, t0, ob, n)
                if SUB < 5:
                    continue
                mix(4)
                for d in range(2):
                    La = lora1(self.a1[j, d], 96, AF.Copy, d)
                    wa2 = lora2_load(self.a2[j, d], 96)
                    for jc in range(DC):
                        pq_ = nxt(pq, "v")
                        lora2(pq_, wa2, La, 96, jc)
                        sa_ = nxt(sq, "s")
                        b.op("act", lambda e: e.activation(out=sa_[:, 0:n], in_=pq_[:, 0:n], func=AF.Sigmoid, bias=a0[:, d, jc:jc + 1]),
                             reads=[pq_, a0], writes=[sa_])
                        ob = nxt(obs, "o")
                        b.op("dve", lambda e: e.tensor_tensor(out=ob[:, 0:n], in0=KK[:, jc, 0:n], in1=sa_[:, 0:n], op=ALU.mult),
                             reads=[KK.sub[jc], sa_], writes=[ob])
                        store(self.bdT[d], jc * 128, t0, ob, n)
                        ob2 = nxt(obs, "o")
                        b.op("dve", lambda e: e.tensor_scalar(out=ob2[:, 0:n], in0=sa_[:, 0:n], scalar1=kaw[:, jc:jc + 1], scalar2=oma[:, jc:jc + 1],
                                                              op0=ALU.mult, op1=ALU.add), reads=[sa_, kaw, oma], writes=[ob2])
                        b.op("pool", lambda e: e.tensor_tensor(out=ob2[:, 0:n], in0=ob2[:, 0:n], in1=Kt[:, jc, 0:n], op=ALU.mult),
                             reads=[ob2, Kt.sub[jc]], writes=[ob2])
                        store(self.kdT[d], jc * 128, t0, ob2, n)
                if SUB < 6:
                    continue
                mix(5)
                Lg = lora1(self.g1[j], 256, AF.Sigmoid, 0)
                wg2 = lora2_load(self.g2[j], 256)
                for jc in range(DC):
                    pq_ = nxt(pq, "v")
                    lora2(pq_, wg2, Lg, 256, jc)
                    ob = nxt(obs, "o")
                    b.op("act", lambda e: e.copy(out=ob[:, 0:n], in_=pq_[:, 0:n]), reads=[pq_], writes=[ob])
                    store(self.gT, jc * 128, t0, ob, n)
            b.barrier()

    def rwkv_scan(self, j):
        b = self.b
        L = 128
        NCH = T // L
        vsrc = self.vfT if j == 0 else self.vT
        with contextlib.ExitStack() as st:
            onesL = b.sb(st, "onesL", [128, L])
            b.op("dve", lambda e: e.memset(onesL[:], 1.0), writes=[onesL])
            MK = [b.sb(st, "MK%d" % d, [128, 2, 2 * L]) for d in range(2)]
            MN = [b.sb(st, "MN%d" % d, [128, 2, L]) for d in range(2)]
            for hh in range(2):
                b.op("dve", lambda e, hh=hh: e.tensor_copy(out=MK[0][:, hh, 0:L], in_=self.masks[:, 2, :]), reads=[self.masks], writes=[MK[0]])
                b.op("dve", lambda e, hh=hh: e.tensor_copy(out=MK[0][:, hh, L:2 * L], in_=self.masks[:, 0, :]), reads=[self.masks], writes=[MK[0]])
                b.op("dve", lambda e, hh=hh: e.tensor_copy(out=MK[1][:, hh, 0:L], in_=self.masks[:, 3, :]), reads=[self.masks], writes=[MK[1]])
                b.op("dve", lambda e, hh=hh: e.tensor_copy(out=MK[1][:, hh, L:2 * L], in_=self.masks[:, 1, :]), reads=[self.masks], writes=[MK[1]])
                b.op("dve", lambda e, hh=hh: e.tensor_copy(out=MN[0][:, hh, :], in_=self.masks[:, 3, :]), reads=[self.masks], writes=[MN[0]])
                b.op("dve", lambda e, hh=hh: e.tensor_copy(out=MN[1][:, hh, :], in_=self.masks[:, 2, :]), reads=[self.masks], writes=[MN[1]])
            ST_ = [b.sb(st, "ST%d" % d, [128, 64]) for d in range(2)]

            def per_d(name, shape):
                return [b.sb(st, "%s%d" % (name, i), shape) for i in range(2)]
            Rt, LWt, KDt, BDt, KKt, Vt = [per_d(nm, [128, L]) for nm in ("Rt", "LWt", "KDt", "BDt", "KKt", "Vt")]
            CS, EP, EM, EA = [per_d(nm, [128, L]) for nm in ("CS", "EP", "EM", "EA")]
            ART = per_d("ART", [128, 2 * L])
            KH, BH = per_d("KH", [128, L]), per_d("BH", [128, L])
            VT, KHT, BHT = per_d("VT", [128, L]), per_d("KHT", [128, L]), per_d("BHT", [128, L])
            AKR, NRB = per_d("AKR", [128, 2, 2 * L]), per_d("NRB", [128, 2, 2 * L])
            PPa, PPb = per_d("PPa", [128, 2, 2 * L]), per_d("PPb", [128, 2, 2 * L])
            XXa, XXb = per_d("XXa", [128, 2, 64]), per_d("XXb", [128, 2, 64])
            YO = per_d("YO", [128, L])
            BA = [b.ps(st, "bA%d" % d, [128, 512]) for d in range(2)]
            BB = [b.ps(st, "bB%d" % d, [128, 2, 2 * L]) for d in range(2)]
            BC = [b.ps(st, "bC%d" % d, [128, 512]) for d in range(2)]
            BD = [b.ps(st, "bD%d" % d, [128, 512]) for d in range(2)]
            order = [list(range(NCH)), [1, 0] + list(range(NCH - 1, 1, -1))]

            def stream(fc, d):
                rows = slice(fc * 128, (fc + 1) * 128)
                S_ = ST_[d]
                bA, bB, bC, bD = BA[d], BB[d], BC[d], BD[d]
                b.op("pool", lambda e: e.memset(S_[:], 0.0), writes=[S_])
                rv = (lambda ap: ap) if d == 0 else (lambda ap: ap[:, ::-1])
                lastc = L - 1 if d == 0 else 0
                R_, LW_, KD_, BD_, KK_, V_ = Rt[d], LWt[d], KDt[d], BDt[d], KKt[d], Vt[d]
                cs, ep, em, ea, art, kh, bh = CS[d], EP[d], EM[d], EA[d], ART[d], KH[d], BH[d]
                vt, kht, bht = VT[d], KHT[d], BHT[d]
                akr, nrb = AKR[d], NRB[d]
                pxs = [(bA, 384), (bD, 256)]
                for c in order[d]:
                    t0 = c * L
                    for (tl_, src_) in ((R_, self.rT), (LW_, self.lwT[d]), (KD_, self.kdT[d]), (BD_, self.bdT[d]), (KK_, self.kkT), (V_, vsrc)):
                        b.dma(tl_[:], src_[rows, t0:t0 + L], writes=[tl_])
                    yield
                    b.op("dve", lambda e: e.tensor_tensor_scan(rv(cs[:]), onesL[:], rv(LW_[:]), 0.0, ALU.mult, ALU.add),
                         reads=[onesL, LW_], writes=[cs])
                    b.op("act", lambda e: e.activation(out=ep[:], in_=cs[:], func=AF.Exp), reads=[cs], writes=[ep])
                    b.op("act", lambda e: e.activation(out=em[:], in_=cs[:], func=AF.Exp, scale=-1.0), reads=[cs], writes=[em])
                    b.op("pool", lambda e: e.tensor_tensor(out=ea[:], in0=cs[:], in1=LW_[:], op=ALU.subtract), reads=[cs, LW_], writes=[ea])
                    b.op("act", lambda e: e.activation(out=ea[:], in_=ea[:], func=AF.Exp), reads=[ea], writes=[ea])
                    yield
                    b.op("dve", lambda e: e.scalar_tensor_tensor(out=art[:, 0:L], in0=KK_[:], scalar=-1.0, in1=ea[:], op0=ALU.mult, op1=ALU.mult),
                         reads=[KK_, ea], writes=[art])
                    b.op("pool", lambda e: e.tensor_tensor(out=art[:, L:2 * L], in0=R_[:], in1=ep[:], op=ALU.mult), reads=[R_, ep], writes=[art])
                    b.op("dve", lambda e: e.tensor_tensor(out=kh[:], in0=KD_[:], in1=em[:], op=ALU.mult), reads=[KD_, em], writes=[kh])
                    b.op("pool", lambda e: e.tensor_tensor(out=bh[:], in0=BD_[:], in1=em[:], op=ALU.mult), reads=[BD_, em], writes=[bh])
                    yield
                    for qi, srcq in enumerate((V_, kh, bh)):
                        b.op("pe", lambda e, qi=qi, srcq=srcq: e.transpose(bA[:, qi * L:(qi + 1) * L], srcq[:], self.ident[:, :]),
                             reads=[srcq, self.ident], writes=[bA], pe_acc=True)
                    b.op("act", lambda e: e.copy(out=vt[:], in_=bA[:, 0:L]), reads=[bA], writes=[vt])
                    b.op("act", lambda e: e.copy(out=kht[:], in_=bA[:, L:2 * L]), reads=[bA], writes=[kht])
                    b.op("act", lambda e: e.copy(out=bht[:], in_=bA[:, 2 * L:3 * L]), reads=[bA], writes=[bht])
                    for hh in range(2):
                        hs = slice(hh * 64, (hh + 1) * 64)
                        b.op("pe", lambda e, hh=hh, hs=hs: e.matmul(bB[:, hh, :], kh[hs, :], art[hs, :], start=True, stop=True),
                             reads=[kh, art], writes=[bB], pe_acc=True)
                        b.op("pe", lambda e, hh=hh, hs=hs: e.matmul(bC[:, hh * 256:(hh + 1) * 256], bh[hs, :], art[hs, :], start=True, stop=True),
                             reads=[bh, art], writes=[bC], pe_acc=True)
                        b.op("pe", lambda e, hh=hh, hs=hs: e.matmul(bD[:, hh * L:(hh + 1) * L], art[hs, 0:L], bh[hs, :], start=True, stop=True),
                             reads=[bh, art], writes=[bD], pe_acc=True)
                    yield
                    pp = PPa[d]
                    b.op("dve", lambda e: e.tensor_tensor(out=akr[:], in0=bB[:], in1=MK[d][:], op=ALU.mult), reads=[bB, MK[d]], writes=[akr])
                    b.op("dve", lambda e: e.tensor_tensor(out=nrb[:], in0=bC[:].rearrange("p (h x) -> p h x", h=2), in1=MK[d][:], op=ALU.mult),
                         reads=[bC, MK[d]], writes=[nrb])
                    b.op("dve", lambda e: e.tensor_tensor(out=pp[:, :, 0:L], in0=bD[:, 0:2 * L].rearrange("p (h x) -> p h x", h=2), in1=MN[d][:], op=ALU.mult),
                         reads=[bD, MN[d]], writes=[pp])
                    b.op("pool", lambda e: e.tensor_copy(out=pp[:, :, L:2 * L], in_=nrb[:, :, 0:L]), reads=[nrb], writes=[pp])
                    yield
                    pt_, po_ = pxs[0]
                    for hh in range(2):
                        hs = slice(hh * 64, (hh + 1) * 64)
                        b.op("pe", lambda e, hh=hh, hs=hs: e.matmul(pt_[:, po_ + hh * 64:po_ + (hh + 1) * 64], art[hs, 0:L], S_[hs, :], start=True, stop=False),
                             reads=[art, S_], writes=[pt_], pe_acc=True)
                        b.op("pe", lambda e, hh=hh, hs=hs: e.matmul(pt_[:, po_ + hh * 64:po_ + (hh + 1) * 64], akr[:, hh, 0:L], vt[:, hs], start=False, stop=True),
                             reads=[akr, vt], writes=[pt_], pe_acc=True)
                    xc = XXa[d]
                    b.op("act", lambda e: e.copy(out=xc[:].rearrange("p h v -> p (h v)"), in_=pt_[:, po_:po_ + 128]), reads=[pt_], writes=[xc])
                    yield
                    cur = pp
                    for lev in range(7):
                        pt_, po_ = pxs[(lev + 1) % 2]
                        for hh in range(2):
                            b.op("pe", lambda e, hh=hh, cur=cur, xc=xc, pt_=pt_, po_=po_: e.matmul(pt_[:, po_ + hh * 64:po_ + (hh + 1) * 64], cur[:, hh, L:2 * L], xc[:, hh, :],
                                                                                     start=True, stop=True), reads=[cur, xc], writes=[pt_], pe_acc=True)
                        if lev < 6:
                            for hh in range(2):
                                b.op("pe", lambda e, hh=hh, cur=cur: e.matmul(bB[:, hh, 0:L], cur[:, hh, L:2 * L], cur[:, hh, 0:L], start=True, stop=True),
                                     reads=[cur], writes=[bB], pe_acc=True)
                                b.op("pe", lambda e, hh=hh, cur=cur: e.matmul(bB[:, hh, L:2 * L], cur[:, hh, 0:L], cur[:, hh, L:2 * L], start=True, stop=True),
                                     reads=[cur], writes=[bB], pe_acc=True)
                        yield
                        xn = XXb[d] if xc is XXa[d] else XXa[d]
                        b.op("dve", lambda e, xn=xn, xc=xc, pt_=pt_, po_=po_: e.tensor_tensor(out=xn[:].rearrange("p h v -> p (h v)"), in0=pt_[:, po_:po_ + 128],
                                                                                   in1=xc[:].rearrange("p h v -> p (h v)"), op=ALU.add),
                             reads=[pt_, xc], writes=[xn])
                        if lev < 6:
                            nxt_ = PPb[d] if cur is PPa[d] else PPa[d]
                            b.op("act", lambda e, nxt_=nxt_: e.copy(out=nxt_[:], in_=bB[:]), reads=[bB], writes=[nxt_])
                            cur = nxt_
                        xc = xn
                        yield
                    U = xc
                    for hh in range(2):
                        hs = slice(hh * 64, (hh + 1) * 64)
                        b.op("pe", lambda e, hs=hs: e.matmul(bC[hs, 0:L], S_[hs, :], art[hs, L:2 * L], start=True, stop=False),
                             reads=[S_, art], writes=[bC], pe_acc=True)
                        b.op("pe", lambda e, hs=hs, hh=hh: e.matmul(bC[hs, 0:L], vt[:, hs], akr[:, hh, L:2 * L], start=False, stop=False),
                             reads=[vt, akr], writes=[bC], pe_acc=True)
                        b.op("pe", lambda e, hs=hs, hh=hh, U=U: e.matmul(bC[hs, 0:L], U[:, hh, :], nrb[:, hh, L:2 * L], start=False, stop=True),
                             reads=[U, nrb], writes=[bC], pe_acc=True)
                    yield
                    yo = YO[d]
                    b.op("act", lambda e: e.copy(out=yo[:], in_=bC[:, 0:L]), reads=[bC], writes=[yo])
                    b.dma(self.yT[d][rows, t0:t0 + L], yo[:], reads=[yo], q="pool")
                    for hh in range(2):
                        hs = slice(hh * 64, (hh + 1) * 64)
                        b.op("pe", lambda e, hs=hs: e.matmul(bC[hs, 256:320], self.ident[hs, hs], S_[hs, :], start=True, stop=False),
                             reads=[self.ident, S_], writes=[bC], pe_acc=True)
                        b.op("pe", lambda e, hs=hs: e.matmul(bC[hs, 256:320], kht[:, hs], vt[:, hs], start=False, stop=False),
                             reads=[kht, vt], writes=[bC], pe_acc=True)
                        b.op("pe", lambda e, hs=hs, hh=hh, U=U: e.matmul(bC[hs, 256:320], bht[:, hs], U[:, hh, :], start=False, stop=True),
                             reads=[bht, U], writes=[bC], pe_acc=True)
                    yield
                    b.op("dve", lambda e: e.tensor_scalar(out=S_[:], in0=bC[:, 256:320], scalar1=ep[:, lastc:lastc + 1], scalar2=None, op0=ALU.mult),
                         reads=[bC, ep], writes=[S_])
                    yield

            for fc in range(DC):
                gens = [stream(fc, 0), stream(fc, 1)]
                while gens:
                    for g in list(gens):
                        try:
                            next(g)
                        except StopIteration:
                            gens.remove(g)
            b.barrier()

    def odd_out(self, l, j, src):
        b = self.b
        self.psi = 0
        self.wi = 0
        N = 256
        with contextlib.ExitStack() as st:
            lnw = b.sb(st, "lnw", [128, 16])
            lnb = b.sb(st, "lnb", [128, 16])
            rk = b.sb(st, "rk", [128, 16])
            blk = b.sb(st, "blk", [128, 128])
            epsl = b.sb(st, "epsl", [128, 1])
            b.dma(lnw[:], self.lnwT[:, j, :], writes=[lnw])
            b.dma(lnb[:], self.lnbT[:, j, :], writes=[lnb])
            b.dma(rk[:], self.rkT[:, j, :], writes=[rk])
            b.dma(blk[:], self.blk64[:, :], writes=[blk])
            b.op("dve", lambda e: e.memset(epsl[:], 64e-5), writes=[epsl])
            vsrc = self.vfT if j == 0 else self.vT
            MT = self.subres(b.sb(st, "MT", [128, DC, N]), DC)
            xt = b.sb(st, "xt", [128, DC, N])
            def dbl(name):
                return [b.sb(st, "%s%d" % (name, i), [128, N]) for i in range(2)]
            Y0, Y1, RR, K0, K1, VV, GG, TA, TB = [dbl(nm) for nm in ("Y0", "Y1", "RR", "K0", "K1", "VV", "GG", "TA", "TB")]
            pq = [b.ps(st, "pq%d" % i, [128, 512]) for i in range(3)]
            wts = [b.sb(st, "w%d" % i, [128, 16, 128]) for i in range(3)]
            pss = [b.ps(st, "pg%d" % i, [128, 512]) for i in range(3)]
            tiles = [(0, TC, 1)] + [(TC + i * N, N, 0) for i in range(TL // N)]
            it = 0
            qi = 0
            for (t0, n, isctx) in tiles:
                b.dma(xt[:], src[:, t0:t0 + n].rearrange("(c p) t -> p c t", p=128), writes=[xt])
                for c in range(DC):
                    i2 = it % 2
                    it += 1
                    rows = slice(c * 128, (c + 1) * 128)
                    y0, y1, rr, k0, k1, vv, gg, ta, tb = Y0[i2], Y1[i2], RR[i2], K0[i2], K1[i2], VV[i2], GG[i2], TA[i2], TB[i2]
                    for (tl_, src_) in ((y0, self.yT[0]), (y1, self.yT[1]), (rr, self.rT), (k0, self.kdT[0]), (k1, self.kdT[1]), (vv, vsrc), (gg, self.gT)):
                        b.dma(tl_[:], src_[rows, t0:t0 + n], writes=[tl_])
                    b.op("dve", lambda e, y0=y0, y1=y1: e.tensor_tensor(out=y0[:], in0=y0[:], in1=y1[:], op=ALU.add), reads=[y0, y1], writes=[y0])
                    p1 = pq[qi % 3]; qi += 1
                    b.op("pe", lambda e, p1=p1, y0=y0: e.matmul(p1[:, 0:n], blk[:, :], y0[:], start=True, stop=True), reads=[blk, y0], writes=[p1], pe_acc=True)
                    b.op("dve", lambda e, p1=p1, y0=y0: e.scalar_tensor_tensor(out=y0[:], in0=p1[:, 0:n], scalar=-1.0 / 64, in1=y0[:], op0=ALU.mult, op1=ALU.add),
                         reads=[p1, y0], writes=[y0])
                    b.op("act", lambda e, ta=ta, y0=y0: e.activation(out=ta[:], in_=y0[:], func=AF.Square), reads=[y0], writes=[ta])
                    p2 = pq[qi % 3]; qi += 1
                    b.op("pe", lambda e, p2=p2, ta=ta: e.matmul(p2[:, 0:n], blk[:, :], ta[:], start=True, stop=True), reads=[blk, ta], writes=[p2], pe_acc=True)
                    b.op("act", lambda e, ta=ta, p2=p2: e.activation(out=ta[:], in_=p2[:, 0:n], func=AF.Sqrt, scale=1.0 / 64, bias=epsl[:, 0:1]),
                         reads=[p2, epsl], writes=[ta])
                    b.op("dve", lambda e, ta=ta: e.reciprocal(out=ta[:], in_=ta[:]), reads=[ta], writes=[ta])
                    b.op("dve", lambda e, ta=ta, y0=y0: e.tensor_tensor(out=y0[:], in0=y0[:], in1=ta[:], op=ALU.mult), reads=[y0, ta], writes=[y0])
                    b.op("act", lambda e, y0=y0, c=c: e.activation(out=y0[:], in_=y0[:], func=AF.Identity, scale=lnw[:, c:c + 1], bias=lnb[:, c:c + 1]),
                         reads=[y0, lnw, lnb], writes=[y0])
                    b.op("pool", lambda e, k0=k0, k1=k1: e.tensor_tensor(out=k0[:], in0=k0[:], in1=k1[:], op=ALU.add), reads=[k0, k1], writes=[k0])
                    b.op("pool", lambda e, k0=k0, rr=rr: e.tensor_tensor(out=k0[:], in0=k0[:], in1=rr[:], op=ALU.mult), reads=[k0, rr], writes=[k0])
                    b.op("pool", lambda e, k0=k0, c=c: e.tensor_scalar(out=k0[:], in0=k0[:], scalar1=rk[:, c:c + 1], scalar2=None, op0=ALU.mult), reads=[k0, rk], writes=[k0])
                    p3 = pq[qi % 3]; qi += 1
                    b.op("pe", lambda e, p3=p3, k0=k0: e.matmul(p3[:, 0:n], blk[:, :], k0[:], start=True, stop=True), reads=[blk, k0], writes=[p3], pe_acc=True)
                    b.op("dve", lambda e, tb=tb, p3=p3, vv=vv: e.tensor_tensor(out=tb[:], in0=p3[:, 0:n], in1=vv[:], op=ALU.mult), reads=[p3, vv], writes=[tb])
                    b.op("dve", lambda e, tb=tb, y0=y0: e.tensor_tensor(out=tb[:], in0=tb[:], in1=y0[:], op=ALU.add), reads=[tb, y0], writes=[tb])
                    b.op("pool", lambda e, tb=tb, gg=gg, c=c: e.tensor_tensor(out=rnd(MT[:, c, :]), in0=tb[:], in1=gg[:], op=ALU.mult), reads=[tb, gg], writes=[MT.sub[c]])

                def evac(jc, ps, isctx=isctx):
                    b.op("dve", lambda e: e.scalar_tensor_tensor(
                        out=xt[:, jc, :], in0=ps[:, 0:n], scalar=self.mod[:, l, 32 + jc, isctx:isctx + 1], in1=xt[:, jc, :],
                        op0=ALU.mult, op1=ALU.add), reads=[ps, self.mod, xt], writes=[xt])
                self.gemm(st, self.w_o[j], DC, DC, lambda kc: MT[:, kc, :], lambda kc: [MT.sub[kc]], n, evac, wts, pss)
                b.dma(self.xs[:, t0:t0 + n].rearrange("(c p) t -> p c t", p=128), xt[:], reads=[xt], q="pool")
            b.barrier()

    def stage_odd(self, l, src):
        j = l // 2
        self.odd_norm(l, src)
        self.odd_proj(l, j)
        self.rwkv_scan(j)
        self.odd_out(l, j, src)

    def stage_even(self, l, src):
        b = self.b
        j = l // 2
        with contextlib.ExitStack() as lst:
            self.Gt = b.sb(lst, "Gt", [128, T // 128, 16])
            self.even_inproj(l, j, src)
            with contextlib.ExitStack() as st:
                cw = b.sb(st, "mcw", [128, 3, 16])
                cb = b.sb(st, "mcb", [128, 16])
                b.dma(cw[:], self.mcwT[:, j, :, :], writes=[cw])
                b.dma(cb[:], self.mcbT[:, j, :], writes=[cb])
                self.conv_pass(self.qkT, self.qkcT, 16, cw, cb, True)
            self.s5(j)
            self.mlstm(j)
        self.even_out(l, j, src)

def relay(w):
    w = np.asarray(w, np.float32)
    lead = w.shape[:-2]
    K_, N_ = w.shape[-2:]
    w = w.reshape(lead + (K_ // 128, 128, N_ // 128, 128))
    nd = len(lead)
    w = np.transpose(w, tuple(range(nd)) + (nd + 2, nd + 1, nd + 0, nd + 3))
    return np.ascontiguousarray(w)


def fm(v):
    v = np.asarray(v, np.float32)
    lead = v.shape[:-1]
    c = v.shape[-1] // 128
    return np.ascontiguousarray(np.moveaxis(v.reshape(lead + (c, 128)), -1, 0))


_PROG = {}


def get_prog(layers=DEPTH, mixers=True):
    key = (layers, mixers)
    if key not in _PROG:
        _PROG[key] = Prog(layers, mixers)
    return _PROG[key]


def make_inputs(p, x, c, ctx, c_ctx, ada_w, ada_b, norm_mix, norm_ffn, ffn_w_up, ffn_conv_w, ffn_conv_b, ffn_w_down,
                norm_final, **kw):
    B = x.shape[0]
    shared = {}
    shared["ada_w"] = np.ascontiguousarray(ada_w[:p.layers], np.float32)
    ab = fm(ada_b)
    shared["ada_bT"] = ab
    shared["nmixT"] = fm(norm_mix)
    shared["nffnT"] = fm(norm_ffn)
    shared["nfinT"] = fm(norm_final)
    shared["w_up"] = relay(ffn_w_up[:p.layers])
    shared["convwT"] = fm(ffn_conv_w)
    shared["convbT"] = fm(ffn_conv_b)
    shared["w_down"] = relay(ffn_w_down[:p.layers])
    shared["ones"] = np.ones((128, 128), np.float32)
    if p.mixers:
        NE = p.NE
        shared["w_in"] = np.ascontiguousarray(kw["ev_w_in"][:NE], np.float32)
        shared["binT"] = fm(kw["ev_b_in"][:, :5120])
        shared["bin_row"] = np.ascontiguousarray(kw["ev_b_in"][None], np.float32)
        shared["w_out"] = relay(kw["ev_w_out"][:NE])
        shared["w_in_r"] = relay(kw["ev_w_in"][:NE, :, :3072])
        shared["mcwT"] = fm(kw["ml_conv_w"])
        shared["mcbT"] = fm(kw["ml_conv_b"])
        shared["mlnorm"] = np.ascontiguousarray(np.broadcast_to(kw["ml_norm"][None], (128, 2, 1024)), np.float32)
        shared["w_glu"] = relay(kw["s5_w_glu"][:NE])
        shared["bgluT"] = fm(kw["s5_b_glu"])
        if p.NO > 0:
            NO = p.NO
            f32 = lambda a: np.ascontiguousarray(a, np.float32)
            shared["muT"] = fm(kw["rw_mu"])
            shared["w_r"] = relay(kw["rw_w_r"][:NO]); shared["w_k"] = relay(kw["rw_w_k"][:NO])
            shared["w_v"] = relay(kw["rw_w_v"][:NO]); shared["w_o"] = relay(kw["rw_w_o"][:NO])
            shared["w0T"] = fm(kw["rw_w0"]); shared["a0T"] = fm(kw["rw_a0"]); shared["v0T"] = fm(kw["rw_v0"])
            for nm in ("rw_w1", "rw_w2", "rw_a1", "rw_a2", "rw_v1", "rw_v2", "rw_g1", "rw_g2"):
                shared[nm] = f32(kw[nm])
            shared["kkwT"] = fm(kw["rw_k_k"]); shared["kawT"] = fm(kw["rw_k_a"])
            shared["rkT"] = fm(kw["rw_r_k"].reshape(2, 2048))
            shared["lnwT"] = fm(kw["rw_ln_w"]); shared["lnbT"] = fm(kw["rw_ln_b"])
            bb_ = np.zeros((128, 128), np.float32)
            bb_[:64, :64] = 1.0
            bb_[64:, 64:] = 1.0
            shared["blk64"] = bb_
        dup = lambda a: np.concatenate([a, a], axis=0)
        lr = np.transpose(kw["s5_lam_re"], (3, 0, 1, 2))
        li = np.transpose(kw["s5_lam_im"], (3, 0, 1, 2))
        shared["s5lam"] = np.ascontiguousarray(np.stack([dup(lr), dup(li)], axis=1), np.float32)
        shared["s5ls"] = np.ascontiguousarray(np.broadcast_to(kw["s5_log_step"][None], (128, 2, 2, 64)), np.float32)
        bre = np.transpose(kw["s5_b_re"], (3, 0, 1, 2, 4))
        bim = np.transpose(kw["s5_b_im"], (3, 0, 1, 2, 4))
        shared["s5BX"] = np.ascontiguousarray(np.concatenate([bre, bim], axis=0), np.float32)
        shared["s5BY"] = np.ascontiguousarray(np.concatenate([bim, bre], axis=0), np.float32)
        cre = np.transpose(kw["s5_c_re"], (4, 0, 1, 2, 3))
        cim = np.transpose(kw["s5_c_im"], (4, 0, 1, 2, 3))
        shared["s5CX"] = np.ascontiguousarray(np.concatenate([cre, cim], axis=0), np.float32)
        sg = np.ones((128, 2), np.float32)
        sg[:64, 0] = -1.0
        sg[64:, 1] = -1.0
        shared["s5sgn"] = sg
        shared["s5gmask"] = (np.arange(128)[:, None] // 16 == np.arange(8)[None, :]).astype(np.float32)
        psw = np.zeros((128, 128), np.float32)
        for m in range(64):
            psw[m + 64, m] = 1.0
            psw[m, m + 64] = -1.0
        shared["s5psw"] = psw
        shared["s5dT"] = fm(kw["s5_d"])
        ii = np.arange(128)
        up = (ii[:, None] <= ii[None, :]).astype(np.float32)
        shared["masks"] = np.ascontiguousarray(np.stack([up, up.T, (ii[:, None] < ii[None, :]).astype(np.float32),
                                                         (ii[:, None] > ii[None, :]).astype(np.float32)], axis=1))
        shared["ident"] = np.eye(128, dtype=np.float32)
    maps = []
    for bi in range(B):
        m = dict(shared)
        xt = np.concatenate([ctx[bi], x[bi]], axis=0).T
        m["xT"] = np.ascontiguousarray(xt, np.float32)
        cc = np.stack([c[bi], c_ctx], axis=-1)
        m["cT"] = np.ascontiguousarray(cc.reshape(DC, 128, 2).transpose(1, 0, 2), np.float32)
        for k in p.inputs:
            assert tuple(m[k].shape) == tuple(p.inputs[k]), (k, m[k].shape, p.inputs[k])
        maps.append({k: m[k] for k in p.inputs})
    return maps


def kernel(**inputs):
    inputs = {k: np.asarray(v) for k, v in inputs.items()}
    p = get_prog()
    maps = make_inputs(p, **inputs)
    B = len(maps)
    res = run_bass_kernel_spmd(p.nc, maps, core_ids=list(range(B)))
    outs = [np.asarray(r["outT"]).T for r in res.results]
    return np.ascontiguousarray(np.stack(outs, axis=0).astype(np.float32))
```

```python
import contextlib
import numpy as np
import concourse.bass as bass
import concourse.mybir as mybir
from concourse.bass_utils import run_bass_kernel_spmd

F32 = mybir.dt.float32
F32R = mybir.dt.float32r
import os as _os0
USE_R = _os0.environ.get("K_F32R", "1") == "1"


def rnd(ap):
    return ap.bitcast(F32R) if USE_R else ap
I32 = mybir.dt.int32
AF = mybir.ActivationFunctionType
ALU = mybir.AluOpType
AX = mybir.AxisListType

D = 2048
DC = D // 128
TC = 256
TL = 4096
T = TC + TL
DEPTH = 4
DFF = 5632
EPS = 1e-6
S5W = 1024
MW = 1024
DIN = S5W + 4 * MW + 16
PI = float(np.pi)


class Res:
    __slots__ = ("w", "r")

    def __init__(self):
        self.w = None
        self.r = {}


class TileW(Res):
    __slots__ = ("t", "sub")

    def __init__(self, t):
        super().__init__()
        self.t = t
        self.sub = None

    def __getitem__(self, k):
        return self.t[k]


class Builder:
    SEM_LIMIT = 20000

    def __init__(self):
        self.nc = bass.Bass("TRN2", target_bir_lowering=False)
        nc = self.nc
        self.es = contextlib.ExitStack()
        self.eng = {"pe": nc.tensor, "act": nc.scalar, "dve": nc.vector, "pool": nc.gpsimd, "sp": nc.sync}
        self.sem = {}
        self.cnt = {}
        self.nsem = 0
        for e in self.eng:
            self._new_sem(e)
        self.seen = {e: {} for e in self.eng}
        self.dma_sems = []
        for i in range(12):
            s = self.es.enter_context(nc.semaphore("dq%d" % i))
            self.dma_sems.append([s, 0])
        self.dma_rr = 0
        self.all_res = []
        self.ninst = 0

    def _new_sem(self, e):
        self.nsem += 1
        self.sem[e] = self.es.enter_context(self.nc.semaphore("s_%s_%d" % (e, self.nsem)))
        self.cnt[e] = 0

    def sb(self, stack, name, shape, dt=F32):
        self.nalloc = getattr(self, "nalloc", 0) + 1
        name = "sb%d_%s" % (self.nalloc, name)
        t = stack.enter_context(self.nc.sbuf_tensor(name, list(shape), dt))
        return TileW(t)

    def ps(self, stack, name, shape, dt=F32):
        self.nalloc = getattr(self, "nalloc", 0) + 1
        name = "ps%d_%s" % (self.nalloc, name)
        t = stack.enter_context(self.nc.psum_tensor(name, list(shape), dt))
        return TileW(t)

    def _wait(self, e, tok):
        if tok is None:
            return
        sem, val = tok
        k = id(sem)
        cur = self.seen[e].get(k)
        if cur is not None and cur[1] >= val:
            return
        self.eng[e].wait_ge(sem, val)
        self.seen[e][k] = (sem, val)

    def _deps(self, e, reads, writes, pe_acc=False):
        for r in reads:
            self._wait(e, r.w)
        for w in writes:
            if not (pe_acc and e == "pe"):
                self._wait(e, w.w)
            for oe, tok in w.r.items():
                self._wait(e, tok)

    def _mark(self, tok, e, reads, writes):
        for r in reads:
            r.r[(e, id(tok[0]))] = tok
        for w in writes:
            w.w = tok
            w.r = {}

    def op(self, e, fn, reads=(), writes=(), pe_acc=False):
        self._deps(e, reads, writes, pe_acc)
        if self.cnt[e] >= self.SEM_LIMIT:
            self._new_sem(e)
        ins = fn(self.eng[e])
        self.cnt[e] += 1
        ins.then_inc(self.sem[e], 1)
        tok = (self.sem[e], self.cnt[e])
        self._mark(tok, e, reads, writes)
        self.ninst += 1
        return tok

    def dma(self, out, in_, reads=(), writes=(), q="sp"):
        self._deps(q, reads, writes)
        ent = self.dma_sems[self.dma_rr]
        self.dma_rr = (self.dma_rr + 1) % len(self.dma_sems)
        if ent[1] >= self.SEM_LIMIT:
            self._wait(q, (ent[0], ent[1]))
            ent[0] = self.es.enter_context(self.nc.semaphore("dq_n%d" % self.ninst))
            ent[1] = 0
        self._wait(q, (ent[0], ent[1]))
        self.eng[q].dma_start(out=out, in_=in_).then_inc(ent[0], 16)
        ent[1] += 16
        tok = (ent[0], ent[1])
        self._mark(tok, "dma", reads, writes)
        self.ninst += 1
        return tok

    def barrier(self):
        toks = [(self.sem[e], self.cnt[e]) for e in self.eng if self.cnt[e] > 0]
        toks += [(s, v) for s, v in self.dma_sems if v > 0]
        for e in self.eng:
            for tok in toks:
                self._wait(e, tok)

    def finish(self):
        self.barrier()


def tiles_tokens():
    out = [(0, TC, 1)]
    for i in range(TL // 512):
        out.append((TC + i * 512, 512, 0))
    return out


class Prog:
    def __init__(self, layers=DEPTH, mixers=True, debug=False):
        self.debug = debug
        self.b = Builder()
        self.nc = self.b.nc
        self.layers = layers
        self.mixers = mixers
        self.inputs = {}
        self.build()

    def din(self, name, shape):
        self.inputs[name] = tuple(shape)
        return self.nc.dram_tensor(name, list(shape), F32, kind="ExternalInput").ap()

    def dscr(self, name, shape):
        return self.nc.dram_tensor(name, list(shape), F32, kind="Internal").ap()

    def build(self):
        b, nc = self.b, self.nc
        self.xT = self.din("xT", [D, T])
        self.cT = self.din("cT", [128, DC, 2])
        self.ada_w = self.din("ada_w", [self.layers, D, 6 * D])
        self.ada_bT = self.din("ada_bT", [128, DEPTH, 48 * 2])
        self.nmixT = self.din("nmixT", [128, DEPTH, DC])
        self.nffnT = self.din("nffnT", [128, DEPTH, DC])
        self.nfinT = self.din("nfinT", [128, DC])
        self.w_up = self.din("w_up", [self.layers, 88, 128, 16, 128])
        self.convwT = self.din("convwT", [128, DEPTH, 3, 88])
        self.convbT = self.din("convbT", [128, DEPTH, 88])
        self.w_down = self.din("w_down", [self.layers, 16, 128, 44, 128])
        self.ones_in = self.din("ones", [128, 128])
        self.out = self.nc.dram_tensor("outT", [D, TL], F32, kind="ExternalOutput").ap()
        NE = (self.layers + 1) // 2
        self.NE = NE
        if self.mixers and NE > 0:
            self.w_in = self.din("w_in", [NE, D, DIN])
            self.binT = self.din("binT", [128, 2, 40])
            self.bin_row = self.din("bin_row", [1, 2, DIN])
            self.w_out = self.din("w_out", [NE, 16, 128, 16, 128])
            self.w_in_r = self.din("w_in_r", [NE, 24, 128, 16, 128])
            self.mcwT = self.din("mcwT", [128, 2, 3, 16])
            self.mcbT = self.din("mcbT", [128, 2, 16])
            self.mlnorm = self.din("mlnorm", [128, 2, 1024])
            self.w_glu = self.din("w_glu", [NE, 8, 128, 8, 128])
            self.bgluT = self.din("bgluT", [128, 2, 8])
            self.s5lam = self.din("s5lam", [128, 2, 2, 2, 64])
            self.s5ls = self.din("s5ls", [128, 2, 2, 64])
            self.s5BX = self.din("s5BX", [128, 2, 2, 64, 16])
            self.s5BY = self.din("s5BY", [128, 2, 2, 64, 16])
            self.s5CX = self.din("s5CX", [128, 2, 2, 64, 16])
            self.s5sgn = self.din("s5sgn", [128, 2])
            self.s5gmask = self.din("s5gmask", [128, 8])
            self.s5psw = self.din("s5psw", [128, 128])
            self.s5dT = self.din("s5dT", [128, 2, 8])
            self.masks_in = self.din("masks", [128, 4, 128])
            self.ident_in = self.din("ident", [128, 128])
            self.uT = self.dscr("uT", [1024, T])
            self.qkT = self.dscr("qkT", [2048, T])
            self.qkcT = self.dscr("qkcT", [2048, T])
            self.vtok = self.dscr("vtok", [T, 1024])
            self.otok = self.dscr("otok", [T, 1024])
            self.hd = [self.dscr("hd%d" % i, [T, 1024]) for i in range(2)]
            self.mixT = self.dscr("mixT", [D, T])
        NO = self.layers // 2
        self.NO = NO
        if self.mixers and NO > 0:
            self.muT = self.din("muT", [128, 2, 6, 16])
            self.w_r = self.din("w_r", [NO, 16, 128, 16, 128])
            self.w_k = self.din("w_k", [NO, 16, 128, 16, 128])
            self.w_v = self.din("w_v", [NO, 16, 128, 16, 128])
            self.w_o = self.din("w_o", [NO, 16, 128, 16, 128])
            self.w0T = self.din("w0T", [128, 2, 2, 16])
            self.w1 = self.din("rw_w1", [2, 2, D, 96])
            self.w2 = self.din("rw_w2", [2, 2, 96, D])
            self.a0T = self.din("a0T", [128, 2, 2, 16])
            self.a1 = self.din("rw_a1", [2, 2, D, 96])
            self.a2 = self.din("rw_a2", [2, 2, 96, D])
            self.v0T = self.din("v0T", [128, 1, 16])
            self.v1 = self.din("rw_v1", [1, D, 64])
            self.v2 = self.din("rw_v2", [1, 64, D])
            self.g1 = self.din("rw_g1", [2, D, 256])
            self.g2 = self.din("rw_g2", [2, 256, D])
            self.kkwT = self.din("kkwT", [128, 2, 16])
            self.kawT = self.din("kawT", [128, 2, 16])
            self.rkT = self.din("rkT", [128, 2, 16])
            self.lnwT = self.din("lnwT", [128, 2, 16])
            self.lnbT = self.din("lnbT", [128, 2, 16])
            self.blk64 = self.din("blk64", [128, 128])
            self.hT = self.dscr("hT", [D, T])
            self.rT = self.dscr("rT", [D, T])
            self.kkT = self.dscr("kkT", [D, T])
            self.vT = self.dscr("vT", [D, T])
            self.vfT = self.dscr("vfT", [D, T])
            self.gT = self.dscr("gT", [D, T])
            self.lwT = [self.dscr("lwT%d" % i, [D, T]) for i in range(2)]
            self.kdT = [self.dscr("kdT%d" % i, [D, T]) for i in range(2)]
            self.bdT = [self.dscr("bdT%d" % i, [D, T]) for i in range(2)]
            self.yT = [self.dscr("yT%d" % i, [D, T]) for i in range(2)]
        self.xs = self.dscr("xs", [D, T])
        self.upT = self.dscr("upT", [2 * DFF, T])
        self.actT = self.dscr("actT", [DFF, T])

        with contextlib.ExitStack() as glob:
            self.g = glob
            self.ones = b.sb(glob, "ones", [128, 128])
            b.dma(self.ones[:], self.ones_in[:, :], writes=[self.ones])
            self.mod = b.sb(glob, "mod", [128, DEPTH, 96, 2])
            self.scl = b.sb(glob, "scl", [128, DEPTH, 2, DC, 2])
            if self.mixers:
                self.masks = b.sb(glob, "masks", [128, 4, 128])
                self.ident = b.sb(glob, "ident", [128, 128])
                b.dma(self.masks[:], self.masks_in[:, :, :], writes=[self.masks])
                b.dma(self.ident[:], self.ident_in[:, :], writes=[self.ident])
            self.stage_mod()
            for l in range(self.layers):
                src = self.xT if l == 0 else self.xs
                if self.mixers:
                    if l % 2 == 0:
                        self.stage_even(l, src)
                    else:
                        self.stage_odd(l, src)
                    src = self.xs
                self.stage_ffn(l, src)
            self.stage_final(self.xT if self.layers == 0 else self.xs)
            if getattr(self, "debug", False):
                self.debug_dump()
            b.finish()

    def debug_dump(self):
        b = self.b
        def dout(name, shape):
            return self.nc.dram_tensor(name, list(shape), F32, kind="ExternalOutput").ap()
        d1 = dout("dbg_mod", [128, DEPTH * 96 * 2])
        b.dma(d1[:, :], self.mod[:].rearrange("p l c t -> p (l c t)"), reads=[self.mod])
        d2 = dout("dbg_up", [256, T])
        b.dma(d2[0:128, :], self.upT[0:128, :])
        b.dma(d2[128:256, :], self.upT[DFF:DFF + 128, :])
        d3 = dout("dbg_act", [128, T])
        b.dma(d3[:, :], self.actT[0:128, :])
        d4 = dout("dbg_xs", [128, T])
        b.dma(d4[:, :], self.xs[0:128, :])

    def stage_mod(self):
        b = self.b
        with contextlib.ExitStack() as st:
            ct = b.sb(st, "ct", [128, DC, 2])
            sg = b.sb(st, "sg", [128, DC, 2])
            nm = b.sb(st, "nm", [128, DEPTH, DC])
            nf = b.sb(st, "nf", [128, DEPTH, DC])
            b.dma(ct[:], self.cT[:, :, :], writes=[ct])
            b.dma(nm[:], self.nmixT[:, :, :], writes=[nm])
            b.dma(nf[:], self.nffnT[:, :, :], writes=[nf])
            b.op("act", lambda e: e.activation(out=sg[:], in_=ct[:], func=AF.Sigmoid), reads=[ct], writes=[sg])
            b.op("dve", lambda e: e.tensor_tensor(out=sg[:], in0=sg[:], in1=ct[:], op=ALU.mult), reads=[ct, sg], writes=[sg])
            wts = [b.sb(st, "mw%d" % i, [128, DC, 512]) for i in range(3)]
            pss = [b.ps(st, "mp%d" % i, [128, 4, 2]) for i in range(2)]
            abt = b.sb(st, "abt", [128, DEPTH, 96])
            b.dma(abt[:], self.ada_bT[:, :, 0:96], writes=[abt])
            it = 0
            for l in range(self.layers):
                for nb in range(6 * D // 512):
                    wt = wts[it % 3]
                    ps = pss[it % 2]
                    it += 1
                    b.dma(wt[:], self.ada_w[l, :, nb * 512:(nb + 1) * 512].rearrange("(kc p) n -> p kc n", p=128),
                          writes=[wt])
                    for j in range(4):
                        for kc in range(DC):
                            b.op("pe", lambda e, j=j, kc=kc: e.matmul(ps[:, j, :], wt[:, kc, j * 128:(j + 1) * 128],
                                                                      sg[:, kc, :], start=(kc == 0), stop=(kc == DC - 1)),
                                 reads=[wt, sg], writes=[ps], pe_acc=True)
                    for col in range(2):
                        b.op("dve", lambda e, col=col: e.tensor_tensor(
                            out=self.mod[:, l, nb * 4:(nb + 1) * 4, col], in0=ps[:, :, col],
                            in1=abt[:, l, nb * 4:(nb + 1) * 4], op=ALU.add), reads=[ps, abt], writes=[self.mod])
                for which, (nw, c0) in enumerate(((nm, 16), (nf, 64))):
                    for col in range(2):
                        b.op("dve", lambda e, which=which, nw=nw, c0=c0, col=col: e.scalar_tensor_tensor(
                            out=self.scl[:, l, which, :, col], in0=self.mod[:, l, c0:c0 + DC, col], scalar=1.0,
                            in1=nw[:, l, :], op0=ALU.add, op1=ALU.mult), reads=[self.mod, nw], writes=[self.scl])
            b.barrier()

    def normmod(self, st, xt, ht, tmp, pss, rstd, n, scale_ap, shift_ap, res_extra=()):
        b = self.b
        for c in range(DC):
            b.op("act", lambda e, c=c: e.activation(out=tmp[:, c % 2, 0:n], in_=xt[:, c, 0:n], func=AF.Square),
                 reads=[xt], writes=[tmp.sub[c % 2]])
            b.op("pe", lambda e, c=c: e.matmul(pss[:, 0:n], self.ones[:, :], tmp[:, c % 2, 0:n], start=(c == 0),
                                               stop=(c == DC - 1)), reads=[tmp.sub[c % 2], self.ones], writes=[pss],
                 pe_acc=True)
        b.op("act", lambda e: e.activation(out=rstd[:, 0:n], in_=pss[:, 0:n], func=AF.Sqrt, scale=1.0 / D, bias=self.epsb[:, 0:1]),
             reads=[pss, self.epsb], writes=[rstd])
        b.op("dve", lambda e: e.reciprocal(out=rstd[:, 0:n], in_=rstd[:, 0:n]), reads=[rstd], writes=[rstd])
        for c in range(DC):
            b.op("dve", lambda e, c=c: e.tensor_tensor(out=xt[:, c, 0:n], in0=xt[:, c, 0:n], in1=rstd[:, 0:n], op=ALU.mult),
                 reads=[xt, rstd], writes=[xt])
            b.op("act", lambda e, c=c: e.activation(out=rnd(ht[:, c, 0:n]), in_=xt[:, c, 0:n], func=AF.Identity,
                                                    scale=scale_ap(c), bias=shift_ap(c)),
                 reads=[xt] + list(res_extra), writes=[ht.sub[c]])

    def subres(self, tl, n):
        tl.sub = [Res() for _ in range(n)]
        return tl

    def gemm(self, st, W, K_chunks, n_chunks, rhs_fn, rhs_res, n, evac, wts, pss, n_col0=0, ksub=16, use_r=True):
        b = self.b
        cast = (lambda ap: rnd(ap)) if use_r else (lambda ap: ap)
        for j in range(n_chunks):
            ps = pss[self.psi % len(pss)]
            self.psi += 1
            nk = (K_chunks + ksub - 1) // ksub
            for kb in range(nk):
                k0 = kb * ksub
                kn = min(ksub, K_chunks - k0)
                wt = wts[self.wi % len(wts)]
                self.wi += 1
                b.dma(cast(wt[:, 0:kn, :]), W[n_col0 // 128 + j, :, k0:k0 + kn, :], writes=[wt],
                      q=("pool" if (use_r and USE_R) else "sp"))
                for kk in range(kn):
                    kc = k0 + kk
                    b.op("pe", lambda e, kk=kk, kc=kc, wt=wt, ps=ps: e.matmul(
                        ps[:, 0:n], cast(wt[:, kk, :]), cast(rhs_fn(kc)), start=(kc == 0), stop=(kc == K_chunks - 1)),
                        reads=[wt] + list(rhs_res(kc)), writes=[ps], pe_acc=True)
            evac(j, ps)

    def stage_ffn(self, l, src):
        b = self.b
        self.psi = 0
        self.wi = 0
        with contextlib.ExitStack() as st:
            self.epsb = b.sb(st, "epsb", [128, 1])
            b.op("dve", lambda e: e.memset(self.epsb[:], EPS), writes=[self.epsb])
            xt = b.sb(st, "xt", [128, DC, 512])
            ht = self.subres(b.sb(st, "ht", [128, DC, 512]), DC)
            tmp = self.subres(b.sb(st, "tmp", [128, 2, 512]), 2)
            rstd = b.sb(st, "rstd", [128, 512])
            psn = b.ps(st, "psn", [128, 512])
            wts = [b.sb(st, "w%d" % i, [128, 16, 128]) for i in range(4)]
            pss = [b.ps(st, "pg%d" % i, [128, 512]) for i in range(4)]
            obs = [b.sb(st, "ob%d" % i, [128, 512]) for i in range(4)]
            oi = [0]
            for (t0, n, isctx) in tiles_tokens():
                b.dma(xt[:, :, 0:n], src[:, t0:t0 + n].rearrange("(c p) t -> p c t", p=128), writes=[xt])
                self.normmod(st, xt, ht, tmp, psn, rstd, n,
                             lambda c: self.scl[:, l, 1, c, isctx:isctx + 1],
                             lambda c: self.mod[:, l, 48 + c, isctx:isctx + 1], res_extra=[self.scl, self.mod])

                def evac(j, ps, t0=t0, n=n):
                    ob = obs[oi[0] % 4]
                    oi[0] += 1
                    b.op("act", lambda e: e.copy(out=ob[:, 0:n], in_=ps[:, 0:n]), reads=[ps], writes=[ob])
                    b.dma(self.upT[j * 128:(j + 1) * 128, t0:t0 + n], ob[:, 0:n], reads=[ob], q="pool")

                self.gemm(st, self.w_up[l], DC, 88, lambda kc: ht[:, kc, 0:n], lambda kc: [ht.sub[kc]], n, evac, wts, pss)
            b.barrier()
        with contextlib.ExitStack() as st:
            cw = b.sb(st, "cw", [128, 3, 88])
            cb = b.sb(st, "cb", [128, 88])
            b.dma(cw[:], self.convwT[:, l, :, :], writes=[cw])
            b.dma(cb[:], self.convbT[:, l, :], writes=[cb])
            NT = 2048
            ua = [b.sb(st, "ua%d" % i, [128, NT + 2]) for i in range(2)]
            ug = [b.sb(st, "ug%d" % i, [128, NT + 2]) for i in range(2)]
            ca = [b.sb(st, "ca%d" % i, [128, NT]) for i in range(2)]
            cg = [b.sb(st, "cg%d" % i, [128, NT]) for i in range(2)]
            segs = [(0, TC, 0, TC), (TC, NT, TC, T), (TC + NT, NT, TC, T)]
            it = 0
            for j in range(44):
                for (t0, n, lo, hi) in segs:
                    A, G, CA, CG = ua[it % 2], ug[it % 2], ca[it % 2], cg[it % 2]
                    it += 1
                    for (U, row) in ((A, j), (G, 44 + j)):
                        a0 = max(t0 - 1, lo)
                        a1 = min(t0 + n + 1, hi)
                        if t0 - 1 < lo:
                            b.op("dve", lambda e, U=U: e.memset(U[:, 0:1], 0.0), writes=[U])
                        if t0 + n + 1 > hi:
                            b.op("dve", lambda e, U=U, n=n: e.memset(U[:, n + 1:n + 2], 0.0), writes=[U])
                        b.dma(U[:, a0 - (t0 - 1):a1 - (t0 - 1)], self.upT[row * 128:(row + 1) * 128, a0:a1], writes=[U])
                    for (U, C, col) in ((A, CA, j), (G, CG, 44 + j)):
                        b.op("dve", lambda e, U=U, C=C, col=col, n=n: e.tensor_scalar(
                            out=C[:, 0:n], in0=U[:, 1:n + 1], scalar1=cw[:, 1, col:col + 1], scalar2=cb[:, col:col + 1],
                            op0=ALU.mult, op1=ALU.add), reads=[U, cw, cb], writes=[C])
                        b.op("dve", lambda e, U=U, C=C, col=col, n=n: e.scalar_tensor_tensor(
                            out=C[:, 0:n], in0=U[:, 0:n], scalar=cw[:, 0, col:col + 1], in1=C[:, 0:n],
                            op0=ALU.mult, op1=ALU.add), reads=[U, cw, C], writes=[C])
                        b.op("dve", lambda e, U=U, C=C, col=col, n=n: e.scalar_tensor_tensor(
                            out=C[:, 0:n], in0=U[:, 2:n + 2], scalar=cw[:, 2, col:col + 1], in1=C[:, 0:n],
                            op0=ALU.mult, op1=ALU.add), reads=[U, cw, C], writes=[C])
                    b.op("act", lambda e, G=G, CG=CG, n=n: e.activation(out=G[:, 0:n], in_=CG[:, 0:n], func=AF.Sigmoid),
                         reads=[CG], writes=[G])
                    b.op("pool", lambda e, G=G, CG=CG, n=n: e.tensor_tensor(out=CG[:, 0:n], in0=CG[:, 0:n], in1=G[:, 0:n], op=ALU.mult),
                         reads=[CG, G], writes=[CG])
                    b.op("pool", lambda e, CA=CA, CG=CG, n=n: e.tensor_tensor(out=CA[:, 0:n], in0=CA[:, 0:n], in1=CG[:, 0:n], op=ALU.mult),
                         reads=[CA, CG], writes=[CA])
                    b.dma(self.actT[j * 128:(j + 1) * 128, t0:t0 + n], CA[:, 0:n], reads=[CA], q="pool")
            b.barrier()
        with contextlib.ExitStack() as st:
            at = self.subres(b.sb(st, "at", [128, 44, 512]), 44)
            xt = b.sb(st, "xt", [128, DC, 512])
            wts = [b.sb(st, "w%d" % i, [128, 11, 128]) for i in range(4)]
            pss = [b.ps(st, "pg%d" % i, [128, 512]) for i in range(4)]
            for (t0, n, isctx) in tiles_tokens():
                b.dma(xt[:, :, 0:n], src[:, t0:t0 + n].rearrange("(c p) t -> p c t", p=128), writes=[xt])
                for q4 in range(4):
                    b.dma(at[:, q4 * 11:(q4 + 1) * 11, 0:n],
                          self.actT[q4 * 11 * 128:(q4 + 1) * 11 * 128, t0:t0 + n].rearrange("(c p) t -> p c t", p=128),
                          writes=[at.sub[c] for c in range(q4 * 11, (q4 + 1) * 11)])

                def evac(j, ps, t0=t0, n=n, isctx=isctx):
                    b.op("dve", lambda e: e.scalar_tensor_tensor(
                        out=xt[:, j, 0:n], in0=ps[:, 0:n], scalar=self.mod[:, l, 80 + j, isctx:isctx + 1], in1=xt[:, j, 0:n],
                        op0=ALU.mult, op1=ALU.add), reads=[ps, self.mod, xt], writes=[xt])

                self.gemm(st, self.w_down[l], 44, DC, lambda kc: at[:, kc, 0:n], lambda kc: [at.sub[kc]], n, evac, wts, pss,
                          ksub=11)
                b.dma(self.xs[:, t0:t0 + n].rearrange("(c p) t -> p c t", p=128), xt[:, :, 0:n], reads=[xt], q="pool")
            b.barrier()

    def stage_final(self, src):
        b = self.b
        with contextlib.ExitStack() as st:
            self.epsb = b.sb(st, "epsb", [128, 1])
            b.op("dve", lambda e: e.memset(self.epsb[:], EPS), writes=[self.epsb])
            nf = b.sb(st, "nfin", [128, DC])
            zero = b.sb(st, "zero", [128, 1])
            b.op("dve", lambda e: e.memset(zero[:], 0.0), writes=[zero])
            b.dma(nf[:], self.nfinT[:, :], writes=[nf])
            xt = b.sb(st, "xt", [128, DC, 512])
            ht = self.subres(b.sb(st, "ht", [128, DC, 512]), DC)
            tmp = self.subres(b.sb(st, "tmp", [128, 2, 512]), 2)
            rstd = b.sb(st, "rstd", [128, 512])
            psn = b.ps(st, "psn", [128, 512])
            for (t0, n, isctx) in tiles_tokens():
                if isctx:
                    continue
                b.dma(xt[:, :, 0:n], src[:, t0:t0 + n].rearrange("(c p) t -> p c t", p=128), writes=[xt])
                self.normmod(st, xt, ht, tmp, psn, rstd, n, lambda c: nf[:, c:c + 1], lambda c: zero[:, 0:1],
                             res_extra=[nf, zero])
                self.out_tok = b.dma(self.out[:, t0 - TC:t0 - TC + n].rearrange("(c p) t -> p c t", p=128), ht[:, :, 0:n],
                                     reads=ht.sub, q="pool")
            b.barrier()


    def conv_pass(self, srcT, dstT, nch, cw, cb, silu, NT=2048):
        b = self.b
        with contextlib.ExitStack() as st:
            us = [b.sb(st, "cu%d" % i, [128, NT + 2]) for i in range(2)]
            cs = [b.sb(st, "cc%d" % i, [128, NT]) for i in range(2)]
            sg = [b.sb(st, "cs%d" % i, [128, NT]) for i in range(2)]
            segs = [(0, TC, 0, TC)] + [(TC + i * NT, NT, TC, T) for i in range(TL // NT)]
            it = 0
            for j in range(nch):
                for (t0, n, lo, hi) in segs:
                    U, C, S = us[it % 2], cs[it % 2], sg[it % 2]
                    it += 1
                    a0 = max(t0 - 1, lo)
                    a1 = min(t0 + n + 1, hi)
                    if t0 - 1 < lo:
                        b.op("dve", lambda e, U=U: e.memset(U[:, 0:1], 0.0), writes=[U])
                    if t0 + n + 1 > hi:
                        b.op("dve", lambda e, U=U, n=n: e.memset(U[:, n + 1:n + 2], 0.0), writes=[U])
                    b.dma(U[:, a0 - (t0 - 1):a1 - (t0 - 1)], srcT[j * 128:(j + 1) * 128, a0:a1], writes=[U])
                    b.op("dve", lambda e, U=U, C=C, j=j, n=n: e.tensor_scalar(
                        out=C[:, 0:n], in0=U[:, 1:n + 1], scalar1=cw[:, 1, j:j + 1], scalar2=cb[:, j:j + 1],
                        op0=ALU.mult, op1=ALU.add), reads=[U, cw, cb], writes=[C])
                    b.op("dve", lambda e, U=U, C=C, j=j, n=n: e.scalar_tensor_tensor(
                        out=C[:, 0:n], in0=U[:, 0:n], scalar=cw[:, 0, j:j + 1], in1=C[:, 0:n],
                        op0=ALU.mult, op1=ALU.add), reads=[U, cw, C], writes=[C])
                    b.op("dve", lambda e, U=U, C=C, j=j, n=n: e.scalar_tensor_tensor(
                        out=C[:, 0:n], in0=U[:, 2:n + 2], scalar=cw[:, 2, j:j + 1], in1=C[:, 0:n],
                        op0=ALU.mult, op1=ALU.add), reads=[U, cw, C], writes=[C])
                    if silu:
                        b.op("act", lambda e, S=S, C=C, n=n: e.activation(out=S[:, 0:n], in_=C[:, 0:n], func=AF.Sigmoid),
                             reads=[C], writes=[S])
                        b.op("pool", lambda e, S=S, C=C, n=n: e.tensor_tensor(out=C[:, 0:n], in0=C[:, 0:n], in1=S[:, 0:n], op=ALU.mult),
                             reads=[C, S], writes=[C])
                    b.dma(dstT[j * 128:(j + 1) * 128, t0:t0 + n], C[:, 0:n], reads=[C], q="pool")
            b.barrier()

    def even_inproj(self, l, j, src):
        b = self.b
        self.psi = 0
        self.wi = 0
        with contextlib.ExitStack() as st:
            self.epsb = b.sb(st, "epsb", [128, 1])
            b.op("dve", lambda e: e.memset(self.epsb[:], EPS), writes=[self.epsb])
            xt = b.sb(st, "xt", [128, DC, 512])
            ht = self.subres(b.sb(st, "ht", [128, DC, 512]), DC)
            tmp = self.subres(b.sb(st, "tmp", [128, 2, 512]), 2)
            rstd = b.sb(st, "rstd", [128, 512])
            psn = b.ps(st, "psn", [128, 512])
            wts = [b.sb(st, "w%d" % i, [128, 16, 128]) for i in range(3)]
            pss = [b.ps(st, "pg%d" % i, [128, 512]) for i in range(3)]
            obs = [b.sb(st, "ob%d" % i, [128, 512]) for i in range(2)]
            wtok = [b.sb(st, "wk%d" % i, [128, 16, 256]) for i in range(2)]
            brow = b.sb(st, "brow", [1, 2064])
            bint = b.sb(st, "bint", [128, 40])
            pst = [b.ps(st, "pt%d" % i, [128, 256]) for i in range(2)]
            otb = [b.sb(st, "otb%d" % i, [128, 256]) for i in range(2)]
            b.dma(brow[:], self.bin_row[0:1, j, 3072:5136], writes=[brow])
            b.dma(bint[:], self.binT[:, j, :], writes=[bint])
            oi = [0]
            ti = 0
            for (t0, n, isctx) in tiles_tokens():
                b.dma(xt[:, :, 0:n], src[:, t0:t0 + n].rearrange("(c p) t -> p c t", p=128), writes=[xt])
                self.normmod(st, xt, ht, tmp, psn, rstd, n,
                             lambda c: self.scl[:, l, 0, c, isctx:isctx + 1],
                             lambda c: self.mod[:, l, c, isctx:isctx + 1], res_extra=[self.scl, self.mod])

                def evac(jc, ps, t0=t0, n=n):
                    ob = obs[oi[0] % 2]
                    oi[0] += 1
                    b.op("act", lambda e: e.activation(out=ob[:, 0:n], in_=ps[:, 0:n], func=AF.Identity,
                                                       bias=bint[:, jc:jc + 1]), reads=[ps, bint], writes=[ob])
                    dst = self.uT[jc * 128:(jc + 1) * 128, t0:t0 + n] if jc < 8 else \
                        self.qkT[(jc - 8) * 128:(jc - 7) * 128, t0:t0 + n]
                    b.dma(dst, ob[:, 0:n], reads=[ob], q="pool")

                self.gemm(st, self.w_in_r[j], DC, 24, lambda kc: ht[:, kc, 0:n], lambda kc: [ht.sub[kc]], n, evac, wts, pss)
                for blk in range(9):
                    c0 = 3072 + blk * 256
                    ncol = 256 if blk < 8 else 16
                    wt = wtok[ti % 2]
                    b.dma(wt[:, :, 0:ncol], self.w_in[j][:, c0:c0 + ncol].rearrange("(kc p) n -> p kc n", p=128), writes=[wt])
                    for ts in range(n // 128):
                        ps = pst[ti % 2]
                        ob = otb[ti % 2]
                        ti += 1
                        for kc in range(DC):
                            b.op("pe", lambda e, kc=kc, ts=ts, ps=ps, wt=wt, ncol=ncol: e.matmul(
                                ps[:, 0:ncol], ht[:, kc, ts * 128:(ts + 1) * 128], wt[:, kc, 0:ncol], start=(kc == 0), stop=False),
                                reads=[wt, ht.sub[kc]], writes=[ps], pe_acc=True)
                        b.op("pe", lambda e, ps=ps, ncol=ncol, c0=c0: e.matmul(
                            ps[:, 0:ncol], self.ones[0:1, 0:128], brow[0:1, c0 - 3072:c0 - 3072 + ncol], start=False, stop=True),
                            reads=[self.ones, brow], writes=[ps], pe_acc=True)
                        tt0 = t0 + ts * 128
                        if blk < 8:
                            b.op("act", lambda e, ob=ob, ps=ps: e.copy(out=ob[:, 0:256], in_=ps[:, 0:256]), reads=[ps], writes=[ob])
                            dst = (self.vtok if blk < 4 else self.otok)[tt0:tt0 + 128, (blk % 4) * 256:(blk % 4 + 1) * 256]
                            b.dma(dst, ob[:, 0:256], reads=[ob], q="pool")
                        else:
                            b.op("act", lambda e, ps=ps, tt0=tt0: e.copy(out=self.Gt[:, tt0 // 128, :], in_=ps[:, 0:16]),
                                 reads=[ps], writes=[self.Gt])
            b.barrier()

    def mlstm(self, j):
        b = self.b
        NCH = T // 128
        with contextlib.ExitStack() as st:
            LF = b.sb(st, "LF", [128, 2, NCH, 4])
            IG = b.sb(st, "IG", [128, 2, NCH, 4])
            Bc = b.sb(st, "Bc", [128, 2, NCH, 4])
            AL = b.sb(st, "AL", [128, 2, NCH, 4])
            BE = b.sb(st, "BE", [128, 2, NCH, 4])
            ALL = b.sb(st, "ALL", [128, 2, NCH, 4])
            gst = contextlib.ExitStack()
            psg = b.ps(gst, "psg", [128, 2, 256])
            psl = b.ps(gst, "psl", [128, 2, 256])
            for d in range(2):
                b.op("act", lambda e, d=d: e.activation(out=LF[:, d], in_=self.Gt[:, :, d * 8 + 4:d * 8 + 8], func=AF.Exp, scale=-1.0),
                     reads=[self.Gt], writes=[LF])
                b.op("dve", lambda e, d=d: e.tensor_copy(out=IG[:, d], in_=self.Gt[:, :, d * 8:d * 8 + 4]), reads=[self.Gt], writes=[IG])
            b.op("dve", lambda e: e.tensor_scalar_add(out=LF[:], in0=LF[:], scalar1=1.0), reads=[LF], writes=[LF])
            b.op("act", lambda e: e.activation(out=LF[:], in_=LF[:], func=AF.Ln), reads=[LF], writes=[LF])
            b.op("dve", lambda e: e.tensor_scalar_mul(out=LF[:], in0=LF[:], scalar1=-1.0), reads=[LF], writes=[LF])
            for d in range(2):
                b.op("pe", lambda e, d=d: e.matmul(psg[:, d, 0:NCH * 4], self.masks[:, d, :], LF[:, d].rearrange("p c h -> p (c h)"),
                                                   start=True, stop=True), reads=[self.masks, LF], writes=[psg], pe_acc=True)
                b.op("pe", lambda e, d=d: e.matmul(psl[:, d, 0:NCH * 4], self.ones[:, :], LF[:, d].rearrange("p c h -> p (c h)"),
                                                   start=True, stop=True), reads=[self.ones, LF], writes=[psl], pe_acc=True)
            b.op("dve", lambda e: e.tensor_copy(out=Bc[:].rearrange("p d c h -> p d (c h)"), in_=psg[:, :, 0:NCH * 4]), reads=[psg], writes=[Bc])
            b.op("act", lambda e: e.activation(out=AL[:], in_=Bc[:], func=AF.Exp), reads=[Bc], writes=[AL])
            b.op("act", lambda e: e.activation(out=ALL[:].rearrange("p d c h -> p d (c h)"), in_=psl[:, :, 0:NCH * 4], func=AF.Exp), reads=[psl], writes=[ALL])
            b.op("dve", lambda e: e.tensor_tensor(out=BE[:], in0=IG[:], in1=Bc[:], op=ALU.subtract), reads=[IG, Bc], writes=[BE])
            b.op("act", lambda e: e.activation(out=BE[:], in_=BE[:], func=AF.Exp), reads=[BE], writes=[BE])
            b.op("dve", lambda e: e.tensor_scalar_mul(out=BE[:], in0=BE[:], scalar1=1.0 / 16.0), reads=[BE], writes=[BE])
            b.barrier()
            gst.close()
            Cst = [[b.sb(st, "C%d%d" % (d, h), [128, 2, 257]) for h in range(4)] for d in range(2)]
            for d in range(2):
                for h in range(4):
                    b.op("pool", lambda e, d=d, h=h: e.memset(Cst[d][h][:], 0.0), writes=[Cst[d][h]])
            qt = [b.sb(st, "q%d" % i, [128, 8, 128]) for i in range(2)]
            kt = [b.sb(st, "k%d" % i, [128, 8, 128]) for i in range(2)]
            ktok = [b.sb(st, "kk%d" % i, [128, 1024]) for i in range(2)]
            Vp = [b.sb(st, "V%d" % i, [128, 4, 257]) for i in range(2)]
            for i in range(2):
                b.op("pool", lambda e, i=i: e.memset(Vp[i][:, :, 256:257], 1.0), writes=[Vp[i]])
            pkt = b.ps(st, "pkt", [128, 2, 512])
            pst_ = [b.ps(st, "pst%d" % i, [128, 512]) for i in range(2)]
            ppp = [b.ps(st, "ppp%d" % i, [128, 512]) for i in range(2)]
            pcc = [b.ps(st, "pcc%d" % i, [128, 512]) for i in range(2)]
            ST = [b.sb(st, "ST%d" % i, [128, 128]) for i in range(2)]
            V2 = [b.sb(st, "V2%d" % i, [128, 257]) for i in range(2)]
            sm = [b.sb(st, "sm%d" % i, [128, 4]) for i in range(2)]
            Hb = [b.sb(st, "Hb%d" % i, [128, 1024]) for i in range(2)]
            tC = b.sb(st, "tC", [128, 2, 257])
            order = [list(range(NCH)), [1, 0] + list(range(NCH - 1, 1, -1))]
            it = 0
            for s_ in range(NCH):
                for d in range(2):
                    c = order[d][s_]
                    t0 = c * 128
                    Q, K_, KT, V, H = qt[it % 2], kt[it % 2], ktok[it % 2], Vp[it % 2], Hb[it % 2]
                    it += 1
                    b.dma(Q[:], self.qkcT[0:1024, t0:t0 + 128].rearrange("(c p) t -> p c t", p=128), writes=[Q])
                    b.dma(K_[:], self.qkcT[1024:2048, t0:t0 + 128].rearrange("(c p) t -> p c t", p=128), writes=[K_])
                    b.dma(V[:, :, 0:256], self.vtok[t0:t0 + 128, :].rearrange("t (h e) -> t h e", h=4), writes=[V])
                    for fc in range(8):
                        b.op("pe", lambda e, fc=fc, K_=K_: e.transpose(pkt[:, fc // 4, (fc % 4) * 128:(fc % 4 + 1) * 128], K_[:, fc, :],
                                                                      self.ident[:, :]), reads=[K_, self.ident], writes=[pkt], pe_acc=True)
                    b.op("act", lambda e, KT=KT: e.copy(out=KT[:].rearrange("p (a x) -> p a x", a=2), in_=pkt[:]), reads=[pkt], writes=[KT])
                    for h in range(4):
                        i2 = (it * 4 + h) % 2
                        pS, pP, S_, V2_, sm_ = pst_[i2], ppp[i2], ST[i2], V2[i2], sm[i2]
                        for dc in range(2):
                            b.op("pe", lambda e, dc=dc, h=h, pS=pS, K_=K_, Q=Q: e.matmul(pS[:, 0:128], K_[:, 2 * h + dc, :], Q[:, 2 * h + dc, :],
                                                                                   start=(dc == 0), stop=(dc == 1)),
                                 reads=[K_, Q], writes=[pS], pe_acc=True)
                        b.op("dve", lambda e, pS=pS, S_=S_, d=d, c=c, h=h: e.scalar_tensor_tensor(
                            out=S_[:], in0=pS[:, 0:128], scalar=BE[:, d, c, h:h + 1], in1=self.masks[:, d, :], op0=ALU.mult, op1=ALU.mult),
                            reads=[pS, BE, self.masks], writes=[S_])
                        b.op("pe", lambda e, pP=pP, S_=S_, V=V, h=h: e.matmul(pP[:, 0:257], S_[:, :], V[:, h, :], start=True, stop=False),
                             reads=[S_, V], writes=[pP], pe_acc=True)
                        for dc in range(2):
                            b.op("pe", lambda e, pP=pP, Q=Q, dc=dc, h=h, d=d: e.matmul(pP[:, 0:257], Q[:, 2 * h + dc, :], Cst[d][h][:, dc, :],
                                                                                 start=False, stop=(dc == 1)),
                                 reads=[Q, Cst[d][h]], writes=[pP], pe_acc=True)
                        b.op("dve", lambda e, pP=pP, sm_=sm_, d=d, c=c, h=h: e.tensor_scalar(
                            out=sm_[:, 3:4], in0=pP[:, 256:257], scalar1=AL[:, d, c, h:h + 1], scalar2=None, op0=ALU.mult),
                            reads=[pP, AL], writes=[sm_])
                        b.op("act", lambda e, sm_=sm_: e.activation(out=sm_[:, 0:1], in_=sm_[:, 3:4], func=AF.Abs),
                             reads=[sm_], writes=[sm_])
                        b.op("dve", lambda e, sm_=sm_: e.tensor_scalar_max(out=sm_[:, 0:1], in0=sm_[:, 0:1], scalar1=1.0),
                             reads=[sm_], writes=[sm_])
                        b.op("dve", lambda e, sm_=sm_: e.reciprocal(out=sm_[:, 1:2], in_=sm_[:, 0:1]), reads=[sm_], writes=[sm_])
                        b.op("dve", lambda e, sm_=sm_, d=d, c=c, h=h: e.tensor_tensor(out=sm_[:, 2:3], in0=sm_[:, 1:2], in1=AL[:, d, c, h:h + 1], op=ALU.mult),
                             reads=[sm_, AL], writes=[sm_])
                        b.op("dve", lambda e, pP=pP, sm_=sm_, H=H, h=h: e.tensor_scalar(
                            out=H[:, h * 256:(h + 1) * 256], in0=pP[:, 0:256], scalar1=sm_[:, 2:3], scalar2=None, op0=ALU.mult),
                             reads=[pP, sm_], writes=[H])
                        b.op("pool", lambda e, V2_=V2_, V=V, h=h, d=d, c=c: e.tensor_scalar(
                            out=V2_[:], in0=V[:, h, :], scalar1=BE[:, d, c, h:h + 1], scalar2=None, op0=ALU.mult),
                            reads=[V, BE], writes=[V2_])
                        for dc in range(2):
                            b.op("pe", lambda e, dc=dc, h=h, KT=KT, V2_=V2_: e.matmul(pcc[dc][:, 0:257], KT[:, (2 * h + dc) * 128:(2 * h + dc + 1) * 128],
                                                                                  V2_[:], start=True, stop=True),
                                 reads=[KT, V2_], writes=[pcc[dc]], pe_acc=True)
                            b.op("dve", lambda e, d=d, h=h, dc=dc: e.tensor_tensor(out=tC[:, dc, :], in0=pcc[dc][:, 0:257], in1=Cst[d][h][:, dc, :], op=ALU.add),
                                 reads=[pcc[dc], Cst[d][h]], writes=[tC])
                        b.op("pool", lambda e, d=d, h=h, c=c: e.tensor_scalar(
                            out=Cst[d][h][:], in0=tC[:], scalar1=ALL[:, d, c, h:h + 1], scalar2=None, op0=ALU.mult),
                            reads=[tC, ALL], writes=[Cst[d][h]])
                    b.dma(self.hd[d][t0:t0 + 128, :], H[:], reads=[H], q="pool")
            b.barrier()
        with contextlib.ExitStack() as st:
            self.epsb = b.sb(st, "epsb", [128, 1])
            b.op("dve", lambda e: e.memset(self.epsb[:], EPS), writes=[self.epsb])
            nw = b.sb(st, "nw", [128, 1024])
            b.dma(nw[:], self.mlnorm[:, j, :], writes=[nw])
            h0 = [b.sb(st, "h0%d" % i, [128, 1024]) for i in range(2)]
            h1 = [b.sb(st, "h1%d" % i, [128, 1024]) for i in range(2)]
            og = [b.sb(st, "og%d" % i, [128, 1024]) for i in range(2)]
            junk = b.sb(st, "junk", [128, 256])
            ss = [b.sb(st, "ss%d" % i, [128, 4]) for i in range(2)]
            ptr = b.ps(st, "ptr", [128, 2, 512])
            ob = [b.sb(st, "fo%d" % i, [128, 1024]) for i in range(2)]
            for c in range(NCH):
                t0 = c * 128
                A, B_, O, SS, OB = h0[c % 2], h1[c % 2], og[c % 2], ss[c % 2], ob[c % 2]
                b.dma(A[:], self.hd[0][t0:t0 + 128, :], writes=[A])
                b.dma(B_[:], self.hd[1][t0:t0 + 128, :], writes=[B_])
                b.dma(O[:], self.otok[t0:t0 + 128, :], writes=[O])
                b.op("dve", lambda e, A=A, B_=B_: e.tensor_tensor(out=A[:], in0=A[:], in1=B_[:], op=ALU.add), reads=[A, B_], writes=[A])
                for h in range(4):
                    b.op("act", lambda e, A=A, SS=SS, h=h: e.activation(out=junk[:], in_=A[:, h * 256:(h + 1) * 256], func=AF.Square,
                                                                      accum_out=SS[:, h:h + 1]), reads=[A], writes=[junk, SS])
                b.op("act", lambda e, SS=SS: e.activation(out=SS[:], in_=SS[:], func=AF.Sqrt, scale=1.0 / 256, bias=self.epsb[:, 0:1]),
                     reads=[SS, self.epsb], writes=[SS])
                b.op("dve", lambda e, SS=SS: e.reciprocal(out=SS[:], in_=SS[:]), reads=[SS], writes=[SS])
                b.op("act", lambda e, O=O: e.activation(out=O[:], in_=O[:], func=AF.Sigmoid), reads=[O], writes=[O])
                b.op("pool", lambda e, O=O: e.tensor_tensor(out=O[:], in0=O[:], in1=nw[:], op=ALU.mult), reads=[O, nw], writes=[O])
                for h in range(4):
                    b.op("dve", lambda e, A=A, O=O, SS=SS, h=h: e.scalar_tensor_tensor(
                        out=A[:, h * 256:(h + 1) * 256], in0=A[:, h * 256:(h + 1) * 256], scalar=SS[:, h:h + 1],
                        in1=O[:, h * 256:(h + 1) * 256], op0=ALU.mult, op1=ALU.mult), reads=[A, O, SS], writes=[A])
                for fc in range(8):
                    b.op("pe", lambda e, A=A, fc=fc: e.transpose(ptr[:, fc // 4, (fc % 4) * 128:(fc % 4 + 1) * 128], A[:, fc * 128:(fc + 1) * 128],
                                                                  self.ident[:, :]), reads=[A, self.ident], writes=[ptr], pe_acc=True)
                b.op("act", lambda e, OB=OB: e.copy(out=OB[:].rearrange("p (a x) -> p a x", a=2), in_=ptr[:]), reads=[ptr], writes=[OB])
                b.dma(self.mixT[1024:2048, t0:t0 + 128].rearrange("(c p) t -> p c t", p=128), OB[:].rearrange("p (c t) -> p c t", c=8),
                      reads=[OB], q="pool")
            b.barrier()

    def even_out(self, l, j, src):
        b = self.b
        self.psi = 0
        self.wi = 0
        with contextlib.ExitStack() as st:
            bg = b.sb(st, "bg", [128, 8])
            b.dma(bg[:], self.bgluT[:, j, :], writes=[bg])
            mt = self.subres(b.sb(st, "mt", [128, DC, 512]), DC)
            mg = self.subres(b.sb(st, "mg", [128, 8, 512]), 8)
            xt = b.sb(st, "xt", [128, DC, 512])
            gt = [b.sb(st, "gt%d" % i, [128, 512]) for i in range(2)]
            wts = [b.sb(st, "w%d" % i, [128, 16, 128]) for i in range(4)]
            pss = [b.ps(st, "pg%d" % i, [128, 512]) for i in range(4)]
            gi = [0]
            for (t0, n, isctx) in tiles_tokens():
                b.dma(xt[:, :, 0:n], src[:, t0:t0 + n].rearrange("(c p) t -> p c t", p=128), writes=[xt])
                b.dma(mt[:, :, 0:n], self.mixT[:, t0:t0 + n].rearrange("(c p) t -> p c t", p=128), writes=mt.sub)

                def evac_glu(jc, ps, n=n):
                    g = gt[gi[0] % 2]
                    gi[0] += 1
                    b.op("act", lambda e: e.activation(out=g[:, 0:n], in_=ps[:, 0:n], func=AF.Sigmoid, bias=bg[:, jc:jc + 1]),
                         reads=[ps, bg], writes=[g])
                    b.op("dve", lambda e: e.tensor_tensor(out=rnd(mg[:, jc, 0:n]), in0=mt[:, jc, 0:n], in1=g[:, 0:n], op=ALU.mult),
                         reads=[g, mt.sub[jc]], writes=[mg.sub[jc]])

                self.gemm(st, self.w_glu[j], 8, 8, lambda kc: mt[:, kc, 0:n], lambda kc: [mt.sub[kc]], n, evac_glu, wts, pss, ksub=8)

                def evac(jc, ps, n=n, isctx=isctx):
                    b.op("dve", lambda e: e.scalar_tensor_tensor(
                        out=xt[:, jc, 0:n], in0=ps[:, 0:n], scalar=self.mod[:, l, 32 + jc, isctx:isctx + 1], in1=xt[:, jc, 0:n],
                        op0=ALU.mult, op1=ALU.add), reads=[ps, self.mod, xt], writes=[xt])

                self.gemm(st, self.w_out[j], DC, DC, lambda kc: (mg[:, kc, 0:n] if kc < 8 else mt[:, kc, 0:n]),
                          lambda kc: [mg.sub[kc] if kc < 8 else mt.sub[kc]], n, evac, wts, pss)
                b.dma(self.xs[:, t0:t0 + n].rearrange("(c p) t -> p c t", p=128), xt[:, :, 0:n], reads=[xt], q="pool")
            b.barrier()

    def s5(self, j):
        b = self.b
        TWO_PI = 2.0 * PI
        CH = 256
        NCHK = T // CH
        with contextlib.ExitStack() as st:
            def t4(name):
                return b.sb(st, name, [128, 2, 64, 1])
            LR, LI, STP, RHO, TH, THR, M_, SINT, COST, ABR, ABI, DEN, T2, AM1, CR, CI, CIS, CRS = [
                t4(n) for n in ("LR", "LI", "STP", "RHO", "TH", "THR", "M_", "SINT", "COST", "ABR", "ABI", "DEN", "T2", "AM1",
                                "CR", "CI", "CIS", "CRS")]
            halfpi = b.sb(st, "halfpi", [128, 1])
            sgn = b.sb(st, "sgn", [128, 2])
            gmask = b.sb(st, "gmask", [128, 8])
            psw = b.sb(st, "psw", [128, 128])
            s5d = b.sb(st, "s5d", [128, 8])
            b.op("dve", lambda e: e.memset(halfpi[:], PI / 2), writes=[halfpi])
            b.dma(sgn[:], self.s5sgn[:, :], writes=[sgn])
            b.dma(gmask[:], self.s5gmask[:, :], writes=[gmask])
            b.dma(psw[:], self.s5psw[:, :], writes=[psw])
            b.dma(s5d[:], self.s5dT[:, j, :], writes=[s5d])
            b.dma(LR[:], self.s5lam[:, 0, j, :, :].unsqueeze(3), writes=[LR])
            b.dma(LI[:], self.s5lam[:, 1, j, :, :].unsqueeze(3), writes=[LI])
            b.dma(STP[:], self.s5ls[:, j, :, :].unsqueeze(3), writes=[STP])

            def tt(out, a, c, op, eng="dve"):
                b.op(eng, lambda e: e.tensor_tensor(out=out[:], in0=a[:], in1=c[:], op=op), reads=[a, c], writes=[out])

            def act(out, a, func, **kw):
                extra = [kw["bias"].tile] if hasattr(kw.get("bias", None), "tile") else []
                b.op("act", lambda e: e.activation(out=out[:], in_=a[:], func=func, **kw), reads=[a], writes=[out])

            act(STP, STP, AF.Exp)
            tt(T2, LR, STP, ALU.mult)
            act(RHO, T2, AF.Exp)
            tt(TH, LI, STP, ALU.mult)
            b.op("dve", lambda e: e.tensor_copy(out=THR[:], in_=TH[:]), reads=[TH], writes=[THR])
            for k in range(4):
                thr = (2 * k + 1) * PI
                b.op("dve", lambda e, thr=thr: e.tensor_scalar(out=M_[:], in0=TH[:], scalar1=thr, scalar2=None, op0=ALU.is_ge),
                     reads=[TH], writes=[M_])
                b.op("dve", lambda e: e.scalar_tensor_tensor(out=THR[:], in0=M_[:], scalar=-TWO_PI, in1=THR[:], op0=ALU.mult, op1=ALU.add),
                     reads=[M_, THR], writes=[THR])
            act(SINT, THR, AF.Sin)
            act(T2, THR, AF.Abs)
            b.op("act", lambda e: e.activation(out=COST[:], in_=T2[:], func=AF.Sin, scale=-1.0, bias=halfpi[:, 0:1]),
                 reads=[T2, halfpi], writes=[COST])
            tt(ABR, RHO, COST, ALU.mult)
            tt(ABI, RHO, SINT, ALU.mult)
            tt(DEN, LR, LR, ALU.mult)
            tt(T2, LI, LI, ALU.mult)
            tt(DEN, DEN, T2, ALU.add)
            b.op("dve", lambda e: e.reciprocal(out=DEN[:], in_=DEN[:]), reads=[DEN], writes=[DEN])
            b.op("dve", lambda e: e.tensor_scalar_add(out=AM1[:], in0=ABR[:], scalar1=-1.0), reads=[ABR], writes=[AM1])
            tt(CR, AM1, LR, ALU.mult)
            tt(T2, ABI, LI, ALU.mult)
            tt(CR, CR, T2, ALU.add)
            tt(CR, CR, DEN, ALU.mult)
            tt(CI, ABI, LR, ALU.mult)
            tt(T2, AM1, LI, ALU.mult)
            tt(CI, CI, T2, ALU.subtract)
            tt(CI, CI, DEN, ALU.mult)
            b.op("dve", lambda e: e.tensor_scalar(out=CIS[:], in0=CI[:], scalar1=sgn[:, 0:1], scalar2=None, op0=ALU.mult), reads=[CI, sgn], writes=[CIS])
            b.op("dve", lambda e: e.tensor_scalar(out=CRS[:], in0=CR[:], scalar1=sgn[:, 1:2], scalar2=None, op0=ALU.mult), reads=[CR, sgn], writes=[CRS])
            BTA = b.sb(st, "BTA", [128, 2, 8, 128])
            BTB = b.sb(st, "BTB", [128, 2, 8, 128])
            CP = b.sb(st, "CP", [128, 2, 64, 16])
            CPADS = [b.sb(st, "CPAD%d" % d, [128, 8, 128]) for d in range(2)]
            for d in range(2):
                b.op("pool", lambda e, d=d: e.memset(CPADS[d][:], 0.0), writes=[CPADS[d]])
            b.dma(CP[:], self.s5CX[:, j, :, :, :], writes=[CP])
            b.op("dve", lambda e: e.tensor_scalar(out=CP[:], in0=CP[:], scalar1=sgn[:, 1:2], scalar2=None, op0=ALU.mult), reads=[CP, sgn], writes=[CP])
            with contextlib.ExitStack() as s2:
                BX = b.sb(s2, "BX", [128, 2, 64, 16])
                BY = b.sb(s2, "BY", [128, 2, 64, 16])
                SA = b.sb(s2, "SA", [128, 2, 64, 16])
                SB = b.sb(s2, "SB", [128, 2, 64, 16])
                TM = b.sb(s2, "TM", [128, 2, 64, 16])
                ptr = b.ps(s2, "ptr5", [128, 512])
                b.dma(BX[:], self.s5BX[:, j, :, :, :], writes=[BX])
                b.dma(BY[:], self.s5BY[:, j, :, :, :], writes=[BY])
                shp = [128, 2, 64, 16]

                def ttb(out, a, col, op):
                    b.op("dve", lambda e: e.tensor_tensor(out=out[:], in0=a[:], in1=col[:].to_broadcast(shp), op=op), reads=[a, col], writes=[out])
                ttb(SA, BX, CR, ALU.mult)
                ttb(TM, BY, CIS, ALU.mult)
                tt(SA, SA, TM, ALU.add)
                ttb(SB, BY, CRS, ALU.mult)
                ttb(TM, BX, CI, ALU.mult)
                tt(SB, SB, TM, ALU.add)
                for (S_, BT_) in ((SA, BTA), (SB, BTB)):
                    for d in range(2):
                        for half in range(2):
                            for k in range(4):
                                fc = half * 4 + k
                                b.op("pe", lambda e, S_=S_, d=d, fc=fc, k=k: e.transpose(
                                    ptr[:, k * 128:(k + 1) * 128], S_[:, d, fc * 8:(fc + 1) * 8, :].rearrange("p g c -> p (g c)"),
                                    self.ident[:, :]), reads=[S_, self.ident], writes=[ptr], pe_acc=True)
                            b.op("act", lambda e, BT_=BT_, d=d, half=half: e.copy(
                                out=BT_[:, d, half * 4:(half + 1) * 4, :].rearrange("p a x -> p (a x)"), in_=ptr[:, :]),
                                reads=[ptr], writes=[BT_])
                b.barrier()
            UT = b.sb(st, "UT", [128, T])
            Y = b.sb(st, "Y", [128, T])
            ER = b.sb(st, "ER", [128, 8, CH])
            EI = b.sb(st, "EI", [128, 8, CH])
            T1 = b.sb(st, "T1", [128, 8, CH // 2])
            T2b = b.sb(st, "T2b", [128, 8, CH // 2])
            RHOT = b.sb(st, "RHOT", [128, 8, CH])
            BPAD = b.sb(st, "BPAD", [128, 2, 8, 128])
            XCAR = b.sb(st, "XCAR", [128, 8])
            BTl = [b.sb(st, "BTl%d" % i, [128, CH]) for i in range(4)]
            TTl = [b.sb(st, "TTl%d" % i, [128, CH]) for i in range(4)]
            Gl = [b.sb(st, "Gl%d" % i, [128, CH]) for i in range(4)]
            Xl = [b.sb(st, "Xl%d" % i, [128, CH]) for i in range(4)]
            pbu = [b.ps(st, "pbu%d" % i, [128, 2, CH]) for i in range(4)]
            psw_ps = [b.ps(st, "psw%d" % i, [128, 512]) for i in range(2)]
            pyy = [b.ps(st, "pyy%d" % i, [128, 512]) for i in range(2)]
            GE = b.sb(st, "GE", [128, T])
            order = [list(range(NCHK)), [0] + list(range(NCHK - 1, 0, -1))]
            it = 0
            yi = 0
            for fc in range(8):
                b.dma(UT[:], self.uT[fc * 128:(fc + 1) * 128, :], writes=[UT])
                for d in range(2):
                    g0 = fc * 8
                    b.op("dve", lambda e, d=d, g0=g0: e.tensor_copy(out=ER[:, :, 0:1], in_=COST[:, d, g0:g0 + 8, :]), reads=[COST], writes=[ER])
                    b.op("dve", lambda e, d=d, g0=g0: e.tensor_copy(out=EI[:, :, 0:1], in_=SINT[:, d, g0:g0 + 8, :]), reads=[SINT], writes=[EI])
                    n = 1
                    while n < CH:
                        bs = [128, 8, n]
                        b.op("dve", lambda e, n=n, bs=bs: e.tensor_tensor(out=T1[:, :, 0:n], in0=ER[:, :, 0:n], in1=ER[:, :, n - 1:n].to_broadcast(bs), op=ALU.mult),
                             reads=[ER], writes=[T1])
                        b.op("pool", lambda e, n=n, bs=bs: e.tensor_tensor(out=T2b[:, :, 0:n], in0=EI[:, :, 0:n], in1=EI[:, :, n - 1:n].to_broadcast(bs), op=ALU.mult),
                             reads=[EI], writes=[T2b])
                        b.op("dve", lambda e, n=n: e.tensor_tensor(out=ER[:, :, n:2 * n], in0=T1[:, :, 0:n], in1=T2b[:, :, 0:n], op=ALU.subtract),
                             reads=[T1, T2b, EI], writes=[ER])
                        b.op("dve", lambda e, n=n, bs=bs: e.tensor_tensor(out=T1[:, :, 0:n], in0=ER[:, :, 0:n], in1=EI[:, :, n - 1:n].to_broadcast(bs), op=ALU.mult),
                             reads=[ER, EI], writes=[T1])
                        b.op("pool", lambda e, n=n, bs=bs: e.tensor_tensor(out=T2b[:, :, 0:n], in0=EI[:, :, 0:n], in1=ER[:, :, n - 1:n].to_broadcast(bs), op=ALU.mult),
                             reads=[EI, ER], writes=[T2b])
                        b.op("dve", lambda e, n=n: e.tensor_tensor(out=EI[:, :, n:2 * n], in0=T1[:, :, 0:n], in1=T2b[:, :, 0:n], op=ALU.add),
                             reads=[T1, T2b, ER], writes=[EI])
                        n *= 2
                    b.op("dve", lambda e, d=d, g0=g0: e.tensor_copy(out=RHOT[:], in_=RHO[:, d, g0:g0 + 8, :].to_broadcast([128, 8, CH])),
                         reads=[RHO], writes=[RHOT])
                    for gp in range(8):
                        b.op("pool", lambda e, gp=gp, d=d, fc=fc: e.tensor_scalar(out=BPAD[:, 0, gp, :], in0=BTA[:, d, fc, :], scalar1=gmask[:, gp:gp + 1],
                                                                                  scalar2=None, op0=ALU.mult), reads=[BTA, gmask], writes=[BPAD])
                        b.op("pool", lambda e, gp=gp, d=d, fc=fc: e.tensor_scalar(out=BPAD[:, 1, gp, :], in0=BTB[:, d, fc, :], scalar1=gmask[:, gp:gp + 1],
                                                                                  scalar2=None, op0=ALU.mult), reads=[BTB, gmask], writes=[BPAD])
                        b.op("dve", lambda e, gp=gp, d=d, g0=g0: e.tensor_copy(out=CPADS[d][:, gp, gp * 16:(gp + 1) * 16], in_=CP[:, d, g0 + gp, :]),
                             reads=[CP], writes=[CPADS[d]])
                    b.op("dve", lambda e: e.memset(XCAR[:], 0.0), writes=[XCAR])
                    for ck in order[d]:
                        c0 = ck * CH
                        py = pyy[yi % 2]
                        yi += 1
                        if d == 0:
                            rv = lambda ap: ap
                        else:
                            rv = lambda ap: ap[:, ::-1]
                        last = CH - 1 if d == 0 else 0

                        def unit(gp, c0=c0, py=py, rv=rv, last=last, d=d):
                            k4 = gp % 4
                            BT_, TT_, G_, X_, pb = BTl[k4], TTl[k4], Gl[k4], Xl[k4], pbu[k4]
                            pwt = psw_ps[k4 // 2]
                            pw = pwt[:, (k4 % 2) * CH:(k4 % 2 + 1) * CH]
                            if d == 0:
                                cosv, sinv = ER[:, gp, :], EI[:, gp, :]
                            else:
                                cosv, sinv = ER[:, gp, ::-1], EI[:, gp, ::-1]
                            for ab in range(2):
                                b.op("pe", lambda e, ab=ab: e.matmul(pb[:, ab, :], BPAD[:, ab, gp, :], UT[:, c0:c0 + CH], start=True, stop=True),
                                     reads=[BPAD, UT], writes=[pb], pe_acc=True)
                            yield
                            b.op("dve", lambda e: e.tensor_tensor(out=BT_[:], in0=pb[:, 0, :], in1=cosv, op=ALU.mult), reads=[pb, ER], writes=[BT_])
                            b.op("dve", lambda e: e.tensor_tensor(out=TT_[:], in0=pb[:, 1, :], in1=sinv, op=ALU.mult), reads=[pb, EI], writes=[TT_])
                            yield
                            b.op("pool", lambda e: e.tensor_tensor(out=BT_[:], in0=BT_[:], in1=TT_[:], op=ALU.add), reads=[BT_, TT_], writes=[BT_])
                            yield
                            b.op("dve", lambda e: e.tensor_tensor_scan(rv(G_[:]), RHOT[:, gp, :], rv(BT_[:]), XCAR[:, gp:gp + 1], ALU.mult, ALU.add),
                                 reads=[RHOT, BT_, XCAR], writes=[G_])
                            yield
                            b.op("pe", lambda e: e.matmul(pw, psw[:, :], G_[:], start=True, stop=True), reads=[psw, G_], writes=[pwt], pe_acc=True)
                            b.op("pool", lambda e: e.tensor_tensor(out=X_[:], in0=G_[:], in1=cosv, op=ALU.mult), reads=[G_, ER], writes=[X_])
                            yield
                            b.op("dve", lambda e: e.tensor_tensor(out=TT_[:], in0=pw, in1=sinv, op=ALU.mult), reads=[pwt, EI], writes=[TT_])
                            yield
                            b.op("pool", lambda e: e.tensor_tensor(out=X_[:], in0=X_[:], in1=TT_[:], op=ALU.subtract), reads=[X_, TT_], writes=[X_])
                            yield
                            b.op("act", lambda e: e.copy(out=XCAR[:, gp:gp + 1], in_=X_[:, last:last + 1]), reads=[X_], writes=[XCAR])
                            b.op("pe", lambda e: e.matmul(py[:, 0:CH], CPADS[d][:, gp, :], X_[:], start=(gp == 0), stop=(gp == 7)),
                                 reads=[CPADS[d], X_], writes=[py], pe_acc=True)
                            yield

                        for half in range(2):
                            gens = [unit(half * 4 + q) for q in range(4)]
                            while gens:
                                for g_ in list(gens):
                                    try:
                                        next(g_)
                                    except StopIteration:
                                        gens.remove(g_)
                        if d == 0:
                            b.op("act", lambda e, py=py, c0=c0: e.copy(out=Y[:, c0:c0 + CH], in_=py[:, 0:CH]), reads=[py], writes=[Y])
                        else:
                            b.op("dve", lambda e, py=py, c0=c0: e.tensor_tensor(out=Y[:, c0:c0 + CH], in0=py[:, 0:CH], in1=Y[:, c0:c0 + CH], op=ALU.add),
                                 reads=[py, Y], writes=[Y])
                b.op("dve", lambda e, fc=fc: e.scalar_tensor_tensor(out=Y[:], in0=UT[:], scalar=s5d[:, fc:fc + 1], in1=Y[:], op0=ALU.mult, op1=ALU.add),
                     reads=[UT, s5d, Y], writes=[Y])
                b.op("pool", lambda e: e.tensor_tensor(out=GE[:], in0=Y[:], in1=Y[:], op=ALU.mult), reads=[Y], writes=[GE])
                b.op("dve", lambda e: e.tensor_scalar(out=GE[:], in0=GE[:], scalar1=0.044715, scalar2=1.0, op0=ALU.mult, op1=ALU.add),
                     reads=[GE], writes=[GE])
                b.op("pool", lambda e: e.tensor_tensor(out=GE[:], in0=GE[:], in1=Y[:], op=ALU.mult), reads=[GE, Y], writes=[GE])
                b.op("act", lambda e: e.activation(out=GE[:], in_=GE[:], func=AF.Tanh, scale=0.7978845608028654), reads=[GE], writes=[GE])
                b.op("dve", lambda e: e.scalar_tensor_tensor(out=GE[:], in0=GE[:], scalar=1.0, in1=Y[:], op0=ALU.add, op1=ALU.mult),
                     reads=[GE, Y], writes=[GE])
                b.op("pool", lambda e: e.tensor_scalar(out=GE[:], in0=GE[:], scalar1=0.5, scalar2=None, op0=ALU.mult), reads=[GE], writes=[GE])
                b.dma(self.mixT[fc * 128:(fc + 1) * 128, :], GE[:], reads=[GE], q="pool")
            b.barrier()

    def odd_norm(self, l, src):
        b = self.b
        with contextlib.ExitStack() as st:
            self.epsb = b.sb(st, "epsb", [128, 1])
            b.op("dve", lambda e: e.memset(self.epsb[:], EPS), writes=[self.epsb])
            xt = b.sb(st, "xt", [128, DC, 512])
            ht = self.subres(b.sb(st, "ht", [128, DC, 512]), DC)
            tmp = self.subres(b.sb(st, "tmp", [128, 2, 512]), 2)
            rstd = b.sb(st, "rstd", [128, 512])
            psn = b.ps(st, "psn", [128, 512])
            for (t0, n, isctx) in tiles_tokens():
                b.dma(xt[:, :, 0:n], src[:, t0:t0 + n].rearrange("(c p) t -> p c t", p=128), writes=[xt])
                self.normmod(st, xt, ht, tmp, psn, rstd, n,
                             lambda c: self.scl[:, l, 0, c, isctx:isctx + 1],
                             lambda c: self.mod[:, l, c, isctx:isctx + 1], res_extra=[self.scl, self.mod])
                b.dma(self.hT[:, t0:t0 + n].rearrange("(c p) t -> p c t", p=128), ht[:, :, 0:n], reads=ht.sub, q="pool")
            b.barrier()

    def odd_proj(self, l, j):
        b = self.b
        self.psi = 0
        self.wi = 0
        N = 256
        with contextlib.ExitStack() as st:
            mu = b.sb(st, "mu", [128, 6, 16])
            kkw = b.sb(st, "kkw", [128, 16])
            kaw = b.sb(st, "kaw", [128, 16])
            oma = b.sb(st, "oma", [128, 16])
            nw0 = b.sb(st, "nw0", [128, 2, 16])
            a0 = b.sb(st, "a0", [128, 2, 16])
            v0 = b.sb(st, "v0", [128, 16])
            blk = b.sb(st, "blk", [128, 128])
            mhalf = b.sb(st, "mhalf", [128, 1])
            tiny = b.sb(st, "tiny", [128, 1])
            b.dma(mu[:], self.muT[:, j, :, :], writes=[mu])
            b.dma(kkw[:], self.kkwT[:, j, :], writes=[kkw])
            b.dma(kaw[:], self.kawT[:, j, :], writes=[kaw])
            b.dma(nw0[:], self.w0T[:, j, :, :], writes=[nw0])
            b.dma(a0[:], self.a0T[:, j, :, :], writes=[a0])
            b.dma(blk[:], self.blk64[:, :], writes=[blk])
            if j > 0:
                b.dma(v0[:], self.v0T[:, j - 1, :], writes=[v0])
            b.op("dve", lambda e: e.tensor_scalar_mul(out=nw0[:], in0=nw0[:], scalar1=-1.0), reads=[nw0], writes=[nw0])
            b.op("dve", lambda e: e.tensor_scalar(out=oma[:], in0=kaw[:], scalar1=-1.0, scalar2=1.0, op0=ALU.mult, op1=ALU.add), reads=[kaw], writes=[oma])
            b.op("dve", lambda e: e.memset(mhalf[:], -0.5), writes=[mhalf])
            b.op("dve", lambda e: e.memset(tiny[:], 0.0), writes=[tiny])
            H = b.sb(st, "H", [128, DC, N])
            DX = b.sb(st, "DX", [128, DC, N])
            X = self.subres(b.sb(st, "X", [128, DC, N]), DC)
            Kt = self.subres(b.sb(st, "Kt", [128, DC, N]), DC)
            KK = self.subres(b.sb(st, "KK", [128, DC, N]), DC)
            wts = [b.sb(st, "w%d" % i, [128, 16, 128]) for i in range(3)]
            pss = [b.ps(st, "pg%d" % i, [128, 512]) for i in range(3)]
            obs = [b.sb(st, "ob%d" % i, [128, N]) for i in range(4)]
            sq = [b.sb(st, "sq%d" % i, [128, N]) for i in range(2)]
            pl = [b.ps(st, "pl%d" % i, [128, 512]) for i in range(2)]
            pq = [b.ps(st, "pq%d" % i, [128, 512]) for i in range(2)]
            L1 = [b.sb(st, "L1%d" % i, [128, 2, N]) for i in range(2)]
            w1t = [b.sb(st, "w1t%d" % i, [128, 16, 256]) for i in range(1)]
            w2t = [b.sb(st, "w2t%d" % i, [128, 2, 2048]) for i in range(2)]
            vft = [b.sb(st, "vft%d" % i, [128, N]) for i in range(2)]
            cnt = {"o": 0, "s": 0, "l": 0, "w": 0, "q": 0, "v": 0}

            def nxt(lst, key):
                t_ = lst[cnt[key] % len(lst)]
                cnt[key] += 1
                return t_

            def store(dst, rows0, t0, ob, n):
                b.dma(dst[rows0:rows0 + 128, t0:t0 + n], ob[:, 0:n], reads=[ob], q="pool")

            tiles = [(0, TC, 1)] + [(TC + i * N, N, 0) for i in range(TL // N)]
            import os as _os
            SUB = int(_os.environ.get("ODD_SUB", "99"))
            tiles = tiles[:int(_os.environ.get("ODD_TILES", "99"))]
            for (t0, n, isctx) in tiles:
                b.dma(H[:], self.hT[:, t0:t0 + n].rearrange("(c p) t -> p c t", p=128), writes=[H])
                hv = lambda c0, c1, a, e_: self.hT[c0 * 128:c1 * 128, a:e_].rearrange("(c p) t -> p c t", p=128)
                if isctx:
                    b.op("pool", lambda e: e.memset(DX[:, 0:8, 0:1], 0.0), writes=[DX])
                    b.op("pool", lambda e: e.memset(DX[:, 8:16, n - 1:n], 0.0), writes=[DX])
                    b.dma(DX[:, 0:8, 1:n], hv(0, 8, 0, n - 1), writes=[DX])
                    b.dma(DX[:, 8:16, 0:n - 1], hv(8, 16, 1, n), writes=[DX])
                else:
                    first = (t0 == TC)
                    lastt = (t0 + n == T)
                    b.dma(DX[:, 0:4, :], hv(0, 4, t0 - 1, t0 + n - 1), writes=[DX])
                    if lastt:
                        b.dma(DX[:, 4:8, 0:n - 1], hv(4, 8, t0 + 1, t0 + n), writes=[DX])
                    else:
                        b.dma(DX[:, 4:8, :], hv(4, 8, t0 + 1, t0 + n + 1), writes=[DX])
                    b.dma(DX[:, 8:12, :], hv(8, 12, t0 - 64, t0 + n - 64), writes=[DX])
                    if lastt:
                        b.dma(DX[:, 12:16, 0:n - 64], hv(12, 16, t0 + 64, t0 + n), writes=[DX])
                        b.op("pool", lambda e: e.memset(DX[:, 12:16, n - 64:n], 0.0), writes=[DX])
                    else:
                        b.dma(DX[:, 12:16, :], hv(12, 16, t0 + 64, t0 + n + 64), writes=[DX])
                    if first:
                        b.op("pool", lambda e: e.memset(DX[:, 8:12, 0:64], 0.0), writes=[DX])
                    for c in range(4):
                        b.op("pool", lambda e, c=c: e.memset(DX[:, c, :].rearrange("p (r w) -> p r w", w=64)[:, :, 0:1], 0.0), writes=[DX])
                        b.op("pool", lambda e, c=c: e.memset(DX[:, 4 + c, :].rearrange("p (r w) -> p r w", w=64)[:, :, 63:64], 0.0), writes=[DX])
                b.op("dve", lambda e: e.tensor_tensor(out=DX[:], in0=DX[:], in1=H[:], op=ALU.subtract), reads=[DX, H], writes=[DX])

                def mix(i):
                    for c in range(DC):
                        b.op("dve", lambda e, c=c: e.scalar_tensor_tensor(
                            out=rnd(X[:, c, 0:n]), in0=DX[:, c, 0:n], scalar=mu[:, i, c:c + 1], in1=H[:, c, 0:n], op0=ALU.mult, op1=ALU.add),
                            reads=[DX, mu, H], writes=[X.sub[c]])

                xr = lambda kc: X[:, kc, 0:n]
                xres = lambda kc: [X.sub[kc]]

                def lora1(W1, r_, func, li_):
                    wt = nxt(w1t, "w")
                    b.dma(wt[:, :, 0:r_], W1.rearrange("(kc p) r -> p kc r", p=128), writes=[wt])
                    Lt = L1[li_]
                    for m0 in range(0, r_, 128):
                        mm = min(128, r_ - m0)
                        ps = nxt(pl, "l")
                        for kc in range(DC):
                            b.op("pe", lambda e, kc=kc, ps=ps, wt=wt, m0=m0, mm=mm: e.matmul(ps[0:mm, 0:n], wt[:, kc, m0:m0 + mm], X[:, kc, 0:n],
                                                                                       start=(kc == 0), stop=(kc == DC - 1)),
                                 reads=[wt, X.sub[kc]], writes=[ps], pe_acc=True)
                        b.op("act", lambda e, ps=ps, Lt=Lt, m0=m0, mm=mm: e.activation(out=Lt[0:mm, m0 // 128, 0:n], in_=ps[0:mm, 0:n], func=func),
                             reads=[ps], writes=[Lt])
                    return Lt

                def lora2_load(W2, r_):
                    wt = nxt(w2t, "q")
                    for m0 in range(0, r_, 128):
                        mm = min(128, r_ - m0)
                        b.dma(wt[0:mm, m0 // 128, :], W2[m0:m0 + mm, :], writes=[wt])
                    return wt

                def lora2(ps, wt, Lt, r_, jc):
                    nk = (r_ + 127) // 128
                    for ki in range(nk):
                        mm = min(128, r_ - ki * 128)
                        b.op("pe", lambda e, ki=ki, mm=mm: e.matmul(ps[:, 0:n], wt[0:mm, ki, jc * 128:(jc + 1) * 128], Lt[0:mm, ki, 0:n],
                                                                      start=(ki == 0), stop=(ki == nk - 1)),
                             reads=[wt, Lt], writes=[ps], pe_acc=True)

                mix(0)

                def ev_r(jc, ps):
                    ob = nxt(obs, "o")
                    b.op("act", lambda e: e.copy(out=ob[:, 0:n], in_=ps[:, 0:n]), reads=[ps], writes=[ob])
                    store(self.rT, jc * 128, t0, ob, n)
                self.gemm(st, self.w_r[j], DC, DC, xr, xres, n, ev_r, wts, pss)
                if SUB < 2:
                    continue
                mix(2)

                def ev_k(jc, ps):
                    b.op("act", lambda e: e.copy(out=Kt[:, jc, 0:n], in_=ps[:, 0:n]), reads=[ps], writes=[Kt.sub[jc]])

                def kk_post():
                    for jc in range(DC):
                        b.op("dve", lambda e, jc=jc: e.tensor_scalar(out=KK[:, jc, 0:n], in0=Kt[:, jc, 0:n], scalar1=kkw[:, jc:jc + 1], scalar2=None, op0=ALU.mult),
                             reads=[Kt.sub[jc], kkw], writes=[KK.sub[jc]])
                        sq_ = nxt(sq, "s")
                        b.op("act", lambda e, jc=jc, sq_=sq_: e.activation(out=sq_[:, 0:n], in_=KK[:, jc, 0:n], func=AF.Square), reads=[KK.sub[jc]], writes=[sq_])
                        pq_ = nxt(pq, "v")
                        b.op("pe", lambda e, sq_=sq_, pq_=pq_: e.matmul(pq_[:, 0:n], blk[:, :], sq_[:, 0:n], start=True, stop=True), reads=[blk, sq_], writes=[pq_], pe_acc=True)
                        b.op("dve", lambda e, sq_=sq_, pq_=pq_: e.tensor_scalar_max(out=sq_[:, 0:n], in0=pq_[:, 0:n], scalar1=1e-24), reads=[pq_], writes=[sq_])
                        b.op("act", lambda e, sq_=sq_: e.activation(out=sq_[:, 0:n], in_=sq_[:, 0:n], func=AF.Sqrt), reads=[sq_], writes=[sq_])
                        b.op("dve", lambda e, sq_=sq_: e.reciprocal(out=sq_[:, 0:n], in_=sq_[:, 0:n]), reads=[sq_], writes=[sq_])
                        b.op("dve", lambda e, jc=jc, sq_=sq_: e.tensor_tensor(out=KK[:, jc, 0:n], in0=KK[:, jc, 0:n], in1=sq_[:, 0:n], op=ALU.mult),
                             reads=[KK.sub[jc], sq_], writes=[KK.sub[jc]])
                        b.dma(self.kkT[jc * 128:(jc + 1) * 128, t0:t0 + n], KK[:, jc, 0:n], reads=[KK.sub[jc]], q="pool")
                self.gemm(st, self.w_k[j], DC, DC, xr, xres, n, ev_k, wts, pss)
                kk_post()
                if SUB < 3:
                    continue
                mix(3)
                if j > 0:
                    Lv = lora1(self.v1[j - 1], 64, AF.Copy, 0)
                    wv2 = lora2_load(self.v2[j - 1], 64)

                def ev_v(jc, ps):
                    ob = nxt(obs, "o")
                    if j == 0:
                        b.op("act", lambda e: e.copy(out=ob[:, 0:n], in_=ps[:, 0:n]), reads=[ps], writes=[ob])
                        store(self.vfT, jc * 128, t0, ob, n)
                    else:
                        pq_ = nxt(pq, "v")
                        lora2(pq_, wv2, Lv, 64, jc)
                        sg_ = nxt(sq, "s")
                        vf_ = nxt(vft, "v")
                        b.dma(vf_[:, 0:n], self.vfT[jc * 128:(jc + 1) * 128, t0:t0 + n], writes=[vf_])
                        b.op("act", lambda e: e.activation(out=sg_[:, 0:n], in_=pq_[:, 0:n], func=AF.Sigmoid, bias=v0[:, jc:jc + 1]),
                             reads=[pq_, v0], writes=[sg_])
                        b.op("dve", lambda e: e.tensor_tensor(out=vf_[:, 0:n], in0=vf_[:, 0:n], in1=ps[:, 0:n], op=ALU.subtract), reads=[vf_, ps], writes=[vf_])
                        b.op("pool", lambda e: e.tensor_tensor(out=vf_[:, 0:n], in0=vf_[:, 0:n], in1=sg_[:, 0:n], op=ALU.mult), reads=[vf_, sg_], writes=[vf_])
                        b.op("dve", lambda e: e.tensor_tensor(out=ob[:, 0:n], in0=vf_[:, 0:n], in1=ps[:, 0:n], op=ALU.add), reads=[vf_, ps], writes=[ob])
                        store(self.vT, jc * 128, t0, ob, n)
                self.gemm(st, self.w_v[j], DC, DC, xr, xres, n, ev_v, wts, pss)
                if SUB < 4:
                    continue
                mix(1)
                for d in range(2):
                    Lw = lora1(self.w1[j, d], 96, AF.Tanh, d)
                    ww2 = lora2_load(self.w2[j, d], 96)
                    for jc in range(DC):
                        pq_ = nxt(pq, "v")
                        lora2(pq_, ww2, Lw, 96, jc)
                        ob = nxt(obs, "o")
                        b.op("act", lambda e: e.activation(out=ob[:, 0:n], in_=pq_[:, 0:n], func=AF.Exp, scale=-1.0, bias=nw0[:, d, jc:jc + 1]),
                             reads=[pq_, nw0], writes=[ob])
                        b.op("dve", lambda e: e.tensor_scalar_add(out=ob[:, 0:n], in0=ob[:, 0:n], scalar1=1.0), reads=[ob], writes=[ob])
                        b.op("act", lambda e: e.activation(out=ob[:, 0:n], in_=ob[:, 0:n], func=AF.Ln), reads=[ob], writes=[ob])
                        b.op("act", lambda e: e.activation(out=ob[:, 0:n], in_=ob[:, 0:n], func=AF.Exp, scale=-1.0, bias=mhalf[:, 0:1]),
                             reads=[ob, mhalf], writes=[ob])
                        b.op("dve", lambda e: e.tensor_scalar_mul(out=ob[:, 0:n], in0=ob[:, 0:n], scalar1=-1.0), reads=[ob], writes=[ob])
                        store(self.lwT[d], jc * 128, t0, ob, n)
                if SUB < 5:
                    continue
                mix(4)
                for d in range(2):
                    La = lora1(self.a1[j, d], 96, AF.Copy, d)
                    wa2 = lora2_load(self.a2[j, d], 96)
                    for jc in range(DC):
                        pq_ = nxt(pq, "v")
                        lora2(pq_, wa2, La, 96, jc)
                        sa_ = nxt(sq, "s")
                        b.op("act", lambda e: e.activation(out=sa_[:, 0:n], in_=pq_[:, 0:n], func=AF.Sigmoid, bias=a0[:, d, jc:jc + 1]),
                             reads=[pq_, a0], writes=[sa_])
                        ob = nxt(obs, "o")
                        b.op("dve", lambda e: e.tensor_tensor(out=ob[:, 0:n], in0=KK[:, jc, 0:n], in1=sa_[:, 0:n], op=ALU.mult),
                             reads=[KK.sub[jc], sa_], writes=[ob])
                        store(self.bdT[d], jc * 128, t0, ob, n)
                        ob2 = nxt(obs, "o")
                        b.op("dve", lambda e: e.tensor_scalar(out=ob2[:, 0:n], in0=sa_[:, 0:n], scalar1=kaw[:, jc:jc + 1], scalar2=oma[:, jc:jc + 1],
                                                              op0=ALU.mult, op1=ALU.add), reads=[sa_, kaw, oma], writes=[ob2])
                        b.op("pool", lambda e: e.tensor_tensor(out=ob2[:, 0:n], in0=ob2[:, 0:n], in1=Kt[:, jc, 0:n], op=ALU.mult),
                             reads=[ob2, Kt.sub[jc]], writes=[ob2])
                        store(self.kdT[d], jc * 128, t0, ob2, n)
                if SUB < 6:
                    continue
                mix(5)
                Lg = lora1(self.g1[j], 256, AF.Sigmoid, 0)
                wg2 = lora2_load(self.g2[j], 256)
                for jc in range(DC):
                    pq_ = nxt(pq, "v")
                    lora2(pq_, wg2, Lg, 256, jc)
                    ob = nxt(obs, "o")
                    b.op("act", lambda e: e.copy(out=ob[:, 0:n], in_=pq_[:, 0:n]), reads=[pq_], writes=[ob])
                    store(self.gT, jc * 128, t0, ob, n)
            b.barrier()

    def rwkv_scan(self, j):
        b = self.b
        L = 128
        NCH = T // L
        vsrc = self.vfT if j == 0 else self.vT
        with contextlib.ExitStack() as st:
            onesL = b.sb(st, "onesL", [128, L])
            b.op("dve", lambda e: e.memset(onesL[:], 1.0), writes=[onesL])
            MK = [b.sb(st, "MK%d" % d, [128, 2, 2 * L]) for d in range(2)]
            MN = [b.sb(st, "MN%d" % d, [128, 2, L]) for d in range(2)]
            for hh in range(2):
                b.op("dve", lambda e, hh=hh: e.tensor_copy(out=MK[0][:, hh, 0:L], in_=self.masks[:, 2, :]), reads=[self.masks], writes=[MK[0]])
                b.op("dve", lambda e, hh=hh: e.tensor_copy(out=MK[0][:, hh, L:2 * L], in_=self.masks[:, 0, :]), reads=[self.masks], writes=[MK[0]])
                b.op("dve", lambda e, hh=hh: e.tensor_copy(out=MK[1][:, hh, 0:L], in_=self.masks[:, 3, :]), reads=[self.masks], writes=[MK[1]])
                b.op("dve", lambda e, hh=hh: e.tensor_copy(out=MK[1][:, hh, L:2 * L], in_=self.masks[:, 1, :]), reads=[self.masks], writes=[MK[1]])
                b.op("dve", lambda e, hh=hh: e.tensor_copy(out=MN[0][:, hh, :], in_=self.masks[:, 3, :]), reads=[self.masks], writes=[MN[0]])
                b.op("dve", lambda e, hh=hh: e.tensor_copy(out=MN[1][:, hh, :], in_=self.masks[:, 2, :]), reads=[self.masks], writes=[MN[1]])
            ST_ = [b.sb(st, "ST%d" % d, [128, 64]) for d in range(2)]

            def per_d(name, shape):
                return [b.sb(st, "%s%d" % (name, i), shape) for i in range(2)]
            Rt, LWt, KDt, BDt, KKt, Vt = [per_d(nm, [128, L]) for nm in ("Rt", "LWt", "KDt", "BDt", "KKt", "Vt")]
            CS, EP, EM, EA = [per_d(nm, [128, L]) for nm in ("CS", "EP", "EM", "EA")]
            ART = per_d("ART", [128, 2 * L])
            KH, BH = per_d("KH", [128, L]), per_d("BH", [128, L])
            VT, KHT, BHT = per_d("VT", [128, L]), per_d("KHT", [128, L]), per_d("BHT", [128, L])
            AKR, NRB = per_d("AKR", [128, 2, 2 * L]), per_d("NRB", [128, 2, 2 * L])
            PPa, PPb = per_d("PPa", [128, 2, 2 * L]), per_d("PPb", [128, 2, 2 * L])
            XXa, XXb = per_d("XXa", [128, 2, 64]), per_d("XXb", [128, 2, 64])
            YO = per_d("YO", [128, L])
            BA = [b.ps(st, "bA%d" % d, [128, 512]) for d in range(2)]
            BB = [b.ps(st, "bB%d" % d, [128, 2, 2 * L]) for d in range(2)]
            BC = [b.ps(st, "bC%d" % d, [128, 512]) for d in range(2)]
            BD = [b.ps(st, "bD%d" % d, [128, 512]) for d in range(2)]
            order = [list(range(NCH)), [1, 0] + list(range(NCH - 1, 1, -1))]

            def stream(fc, d):
                rows = slice(fc * 128, (fc + 1) * 128)
                S_ = ST_[d]
                bA, bB, bC, bD = BA[d], BB[d], BC[d], BD[d]
                b.op("pool", lambda e: e.memset(S_[:], 0.0), writes=[S_])
                rv = (lambda ap: ap) if d == 0 else (lambda ap: ap[:, ::-1])
                lastc = L - 1 if d == 0 else 0
                R_, LW_, KD_, BD_, KK_, V_ = Rt[d], LWt[d], KDt[d], BDt[d], KKt[d], Vt[d]
                cs, ep, em, ea, art, kh, bh = CS[d], EP[d], EM[d], EA[d], ART[d], KH[d], BH[d]
                vt, kht, bht = VT[d], KHT[d], BHT[d]
                akr, nrb = AKR[d], NRB[d]
                pxs = [(bA, 384), (bD, 256)]
                for c in order[d]:
                    t0 = c * L
                    for (tl_, src_) in ((R_, self.rT), (LW_, self.lwT[d]), (KD_, self.kdT[d]), (BD_, self.bdT[d]), (KK_, self.kkT), (V_, vsrc)):
                        b.dma(tl_[:], src_[rows, t0:t0 + L], writes=[tl_])
                    yield
                    b.op("dve", lambda e: e.tensor_tensor_scan(rv(cs[:]), onesL[:], rv(LW_[:]), 0.0, ALU.mult, ALU.add),
                         reads=[onesL, LW_], writes=[cs])
                    b.op("act", lambda e: e.activation(out=ep[:], in_=cs[:], func=AF.Exp), reads=[cs], writes=[ep])
                    b.op("act", lambda e: e.activation(out=em[:], in_=cs[:], func=AF.Exp, scale=-1.0), reads=[cs], writes=[em])
                    b.op("pool", lambda e: e.tensor_tensor(out=ea[:], in0=cs[:], in1=LW_[:], op=ALU.subtract), reads=[cs, LW_], writes=[ea])
                    b.op("act", lambda e: e.activation(out=ea[:], in_=ea[:], func=AF.Exp), reads=[ea], writes=[ea])
                    yield
                    b.op("dve", lambda e: e.scalar_tensor_tensor(out=art[:, 0:L], in0=KK_[:], scalar=-1.0, in1=ea[:], op0=ALU.mult, op1=ALU.mult),
                         reads=[KK_, ea], writes=[art])
                    b.op("pool", lambda e: e.tensor_tensor(out=art[:, L:2 * L], in0=R_[:], in1=ep[:], op=ALU.mult), reads=[R_, ep], writes=[art])
                    b.op("dve", lambda e: e.tensor_tensor(out=kh[:], in0=KD_[:], in1=em[:], op=ALU.mult), reads=[KD_, em], writes=[kh])
                    b.op("pool", lambda e: e.tensor_tensor(out=bh[:], in0=BD_[:], in1=em[:], op=ALU.mult), reads=[BD_, em], writes=[bh])
                    yield
                    for qi, srcq in enumerate((V_, kh, bh)):
                        b.op("pe", lambda e, qi=qi, srcq=srcq: e.transpose(bA[:, qi * L:(qi + 1) * L], srcq[:], self.ident[:, :]),
                             reads=[srcq, self.ident], writes=[bA], pe_acc=True)
                    b.op("act", lambda e: e.copy(out=vt[:], in_=bA[:, 0:L]), reads=[bA], writes=[vt])
                    b.op("act", lambda e: e.copy(out=kht[:], in_=bA[:, L:2 * L]), reads=[bA], writes=[kht])
                    b.op("act", lambda e: e.copy(out=bht[:], in_=bA[:, 2 * L:3 * L]), reads=[bA], writes=[bht])
                    for hh in range(2):
                        hs = slice(hh * 64, (hh + 1) * 64)
                        b.op("pe", lambda e, hh=hh, hs=hs: e.matmul(bB[:, hh, :], kh[hs, :], art[hs, :], start=True, stop=True),
                             reads=[kh, art], writes=[bB], pe_acc=True)
                        b.op("pe", lambda e, hh=hh, hs=hs: e.matmul(bC[:, hh * 256:(hh + 1) * 256], bh[hs, :], art[hs, :], start=True, stop=True),
                             reads=[bh, art], writes=[bC], pe_acc=True)
                        b.op("pe", lambda e, hh=hh, hs=hs: e.matmul(bD[:, hh * L:(hh + 1) * L], art[hs, 0:L], bh[hs, :], start=True, stop=True),
                             reads=[bh, art], writes=[bD], pe_acc=True)
                    yield
                    pp = PPa[d]
                    b.op("dve", lambda e: e.tensor_tensor(out=akr[:], in0=bB[:], in1=MK[d][:], op=ALU.mult), reads=[bB, MK[d]], writes=[akr])
                    b.op("dve", lambda e: e.tensor_tensor(out=nrb[:], in0=bC[:].rearrange("p (h x) -> p h x", h=2), in1=MK[d][:], op=ALU.mult),
                         reads=[bC, MK[d]], writes=[nrb])
                    b.op("dve", lambda e: e.tensor_tensor(out=pp[:, :, 0:L], in0=bD[:, 0:2 * L].rearrange("p (h x) -> p h x", h=2), in1=MN[d][:], op=ALU.mult),
                         reads=[bD, MN[d]], writes=[pp])
                    b.op("pool", lambda e: e.tensor_copy(out=pp[:, :, L:2 * L], in_=nrb[:, :, 0:L]), reads=[nrb], writes=[pp])
                    yield
                    pt_, po_ = pxs[0]
                    for hh in range(2):
                        hs = slice(hh * 64, (hh + 1) * 64)
                        b.op("pe", lambda e, hh=hh, hs=hs: e.matmul(pt_[:, po_ + hh * 64:po_ + (hh + 1) * 64], art[hs, 0:L], S_[hs, :], start=True, stop=False),
                             reads=[art, S_], writes=[pt_], pe_acc=True)
                        b.op("pe", lambda e, hh=hh, hs=hs: e.matmul(pt_[:, po_ + hh * 64:po_ + (hh + 1) * 64], akr[:, hh, 0:L], vt[:, hs], start=False, stop=True),
                             reads=[akr, vt], writes=[pt_], pe_acc=True)
                    xc = XXa[d]
                    b.op("act", lambda e: e.copy(out=xc[:].rearrange("p h v -> p (h v)"), in_=pt_[:, po_:po_ + 128]), reads=[pt_], writes=[xc])
                    yield
                    cur = pp
                    for lev in range(7):
                        pt_, po_ = pxs[(lev + 1) % 2]
                        for hh in range(2):
                            b.op("pe", lambda e, hh=hh, cur=cur, xc=xc, pt_=pt_, po_=po_: e.matmul(pt_[:, po_ + hh * 64:po_ + (hh + 1) * 64], cur[:, hh, L:2 * L], xc[:, hh, :],
                                                                                     start=True, stop=True), reads=[cur, xc], writes=[pt_], pe_acc=True)
                        if lev < 6:
                            for hh in range(2):
                                b.op("pe", lambda e, hh=hh, cur=cur: e.matmul(bB[:, hh, 0:L], cur[:, hh, L:2 * L], cur[:, hh, 0:L], start=True, stop=True),
                                     reads=[cur], writes=[bB], pe_acc=True)
                                b.op("pe", lambda e, hh=hh, cur=cur: e.matmul(bB[:, hh, L:2 * L], cur[:, hh, 0:L], cur[:, hh, L:2 * L], start=True, stop=True),
                                     reads=[cur], writes=[bB], pe_acc=True)
                        yield
                        xn = XXb[d] if xc is XXa[d] else XXa[d]
                        b.op("dve", lambda e, xn=xn, xc=xc, pt_=pt_, po_=po_: e.tensor_tensor(out=xn[:].rearrange("p h v -> p (h v)"), in0=pt_[:, po_:po_ + 128],
                                                                                   in1=xc[:].rearrange("p h v -> p (h v)"), op=ALU.add),
                             reads=[pt_, xc], writes=[xn])
                        if lev < 6:
                            nxt_ = PPb[d] if cur is PPa[d] else PPa[d]
                            b.op("act", lambda e, nxt_=nxt_: e.copy(out=nxt_[:], in_=bB[:]), reads=[bB], writes=[nxt_])
                            cur = nxt_
                        xc = xn
                        yield
                    U = xc
                    for hh in range(2):
                        hs = slice(hh * 64, (hh + 1) * 64)
                        b.op("pe", lambda e, hs=hs: e.matmul(bC[hs, 0:L], S_[hs, :], art[hs, L:2 * L], start=True, stop=False),
                             reads=[S_, art], writes=[bC], pe_acc=True)
                        b.op("pe", lambda e, hs=hs, hh=hh: e.matmul(bC[hs, 0:L], vt[:, hs], akr[:, hh, L:2 * L], start=False, stop=False),
                             reads=[vt, akr], writes=[bC], pe_acc=True)
                        b.op("pe", lambda e, hs=hs, hh=hh, U=U: e.matmul(bC[hs, 0:L], U[:, hh, :], nrb[:, hh, L:2 * L], start=False, stop=True),
                             reads=[U, nrb], writes=[bC], pe_acc=True)
                    yield
                    yo = YO[d]
                    b.op("act", lambda e: e.copy(out=yo[:], in_=bC[:, 0:L]), reads=[bC], writes=[yo])
                    b.dma(self.yT[d][rows, t0:t0 + L], yo[:], reads=[yo], q="pool")
                    for hh in range(2):
                        hs = slice(hh * 64, (hh + 1) * 64)
                        b.op("pe", lambda e, hs=hs: e.matmul(bC[hs, 256:320], self.ident[hs, hs], S_[hs, :], start=True, stop=False),
                             reads=[self.ident, S_], writes=[bC], pe_acc=True)
                        b.op("pe", lambda e, hs=hs: e.matmul(bC[hs, 256:320], kht[:, hs], vt[:, hs], start=False, stop=False),
                             reads=[kht, vt], writes=[bC], pe_acc=True)
                        b.op("pe", lambda e, hs=hs, hh=hh, U=U: e.matmul(bC[hs, 256:320], bht[:, hs], U[:, hh, :], start=False, stop=True),
                             reads=[bht, U], writes=[bC], pe_acc=True)
                    yield
                    b.op("dve", lambda e: e.tensor_scalar(out=S_[:], in0=bC[:, 256:320], scalar1=ep[:, lastc:lastc + 1], scalar2=None, op0=ALU.mult),
                         reads=[bC, ep], writes=[S_])
                    yield

            for fc in range(DC):
                gens = [stream(fc, 0), stream(fc, 1)]
                while gens:
                    for g in list(gens):
                        try:
                            next(g)
                        except StopIteration:
                            gens.remove(g)
            b.barrier()

    def odd_out(self, l, j, src):
        b = self.b
        self.psi = 0
        self.wi = 0
        N = 256
        with contextlib.ExitStack() as st:
            lnw = b.sb(st, "lnw", [128, 16])
            lnb = b.sb(st, "lnb", [128, 16])
            rk = b.sb(st, "rk", [128, 16])
            blk = b.sb(st, "blk", [128, 128])
            epsl = b.sb(st, "epsl", [128, 1])
            b.dma(lnw[:], self.lnwT[:, j, :], writes=[lnw])
            b.dma(lnb[:], self.lnbT[:, j, :], writes=[lnb])
            b.dma(rk[:], self.rkT[:, j, :], writes=[rk])
            b.dma(blk[:], self.blk64[:, :], writes=[blk])
            b.op("dve", lambda e: e.memset(epsl[:], 64e-5), writes=[epsl])
            vsrc = self.vfT if j == 0 else self.vT
            MT = self.subres(b.sb(st, "MT", [128, DC, N]), DC)
            xt = b.sb(st, "xt", [128, DC, N])
            def dbl(name):
                return [b.sb(st, "%s%d" % (name, i), [128, N]) for i in range(2)]
            Y0, Y1, RR, K0, K1, VV, GG, TA, TB = [dbl(nm) for nm in ("Y0", "Y1", "RR", "K0", "K1", "VV", "GG", "TA", "TB")]
            pq = [b.ps(st, "pq%d" % i, [128, 512]) for i in range(3)]
            wts = [b.sb(st, "w%d" % i, [128, 16, 128]) for i in range(3)]
            pss = [b.ps(st, "pg%d" % i, [128, 512]) for i in range(3)]
            tiles = [(0, TC, 1)] + [(TC + i * N, N, 0) for i in range(TL // N)]
            it = 0
            qi = 0
            for (t0, n, isctx) in tiles:
                b.dma(xt[:], src[:, t0:t0 + n].rearrange("(c p) t -> p c t", p=128), writes=[xt])
                for c in range(DC):
                    i2 = it % 2
                    it += 1
                    rows = slice(c * 128, (c + 1) * 128)
                    y0, y1, rr, k0, k1, vv, gg, ta, tb = Y0[i2], Y1[i2], RR[i2], K0[i2], K1[i2], VV[i2], GG[i2], TA[i2], TB[i2]
                    for (tl_, src_) in ((y0, self.yT[0]), (y1, self.yT[1]), (rr, self.rT), (k0, self.kdT[0]), (k1, self.kdT[1]), (vv, vsrc), (gg, self.gT)):
                        b.dma(tl_[:], src_[rows, t0:t0 + n], writes=[tl_])
                    b.op("dve", lambda e, y0=y0, y1=y1: e.tensor_tensor(out=y0[:], in0=y0[:], in1=y1[:], op=ALU.add), reads=[y0, y1], writes=[y0])
                    p1 = pq[qi % 3]; qi += 1
                    b.op("pe", lambda e, p1=p1, y0=y0: e.matmul(p1[:, 0:n], blk[:, :], y0[:], start=True, stop=True), reads=[blk, y0], writes=[p1], pe_acc=True)
                    b.op("dve", lambda e, p1=p1, y0=y0: e.scalar_tensor_tensor(out=y0[:], in0=p1[:, 0:n], scalar=-1.0 / 64, in1=y0[:], op0=ALU.mult, op1=ALU.add),
                         reads=[p1, y0], writes=[y0])
                    b.op("act", lambda e, ta=ta, y0=y0: e.activation(out=ta[:], in_=y0[:], func=AF.Square), reads=[y0], writes=[ta])
                    p2 = pq[qi % 3]; qi += 1
                    b.op("pe", lambda e, p2=p2, ta=ta: e.matmul(p2[:, 0:n], blk[:, :], ta[:], start=True, stop=True), reads=[blk, ta], writes=[p2], pe_acc=True)
                    b.op("act", lambda e, ta=ta, p2=p2: e.activation(out=ta[:], in_=p2[:, 0:n], func=AF.Sqrt, scale=1.0 / 64, bias=epsl[:, 0:1]),
                         reads=[p2, epsl], writes=[ta])
                    b.op("dve", lambda e, ta=ta: e.reciprocal(out=ta[:], in_=ta[:]), reads=[ta], writes=[ta])
                    b.op("dve", lambda e, ta=ta, y0=y0: e.tensor_tensor(out=y0[:], in0=y0[:], in1=ta[:], op=ALU.mult), reads=[y0, ta], writes=[y0])
                    b.op("act", lambda e, y0=y0, c=c: e.activation(out=y0[:], in_=y0[:], func=AF.Identity, scale=lnw[:, c:c + 1], bias=lnb[:, c:c + 1]),
                         reads=[y0, lnw, lnb], writes=[y0])
                    b.op("pool", lambda e, k0=k0, k1=k1: e.tensor_tensor(out=k0[:], in0=k0[:], in1=k1[:], op=ALU.add), reads=[k0, k1], writes=[k0])
                    b.op("pool", lambda e, k0=k0, rr=rr: e.tensor_tensor(out=k0[:], in0=k0[:], in1=rr[:], op=ALU.mult), reads=[k0, rr], writes=[k0])
                    b.op("pool", lambda e, k0=k0, c=c: e.tensor_scalar(out=k0[:], in0=k0[:], scalar1=rk[:, c:c + 1], scalar2=None, op0=ALU.mult), reads=[k0, rk], writes=[k0])
                    p3 = pq[qi % 3]; qi += 1
                    b.op("pe", lambda e, p3=p3, k0=k0: e.matmul(p3[:, 0:n], blk[:, :], k0[:], start=True, stop=True), reads=[blk, k0], writes=[p3], pe_acc=True)
                    b.op("dve", lambda e, tb=tb, p3=p3, vv=vv: e.tensor_tensor(out=tb[:], in0=p3[:, 0:n], in1=vv[:], op=ALU.mult), reads=[p3, vv], writes=[tb])
                    b.op("dve", lambda e, tb=tb, y0=y0: e.tensor_tensor(out=tb[:], in0=tb[:], in1=y0[:], op=ALU.add), reads=[tb, y0], writes=[tb])
                    b.op("pool", lambda e, tb=tb, gg=gg, c=c: e.tensor_tensor(out=rnd(MT[:, c, :]), in0=tb[:], in1=gg[:], op=ALU.mult), reads=[tb, gg], writes=[MT.sub[c]])

                def evac(jc, ps, isctx=isctx):
                    b.op("dve", lambda e: e.scalar_tensor_tensor(
                        out=xt[:, jc, :], in0=ps[:, 0:n], scalar=self.mod[:, l, 32 + jc, isctx:isctx + 1], in1=xt[:, jc, :],
                        op0=ALU.mult, op1=ALU.add), reads=[ps, self.mod, xt], writes=[xt])
                self.gemm(st, self.w_o[j], DC, DC, lambda kc: MT[:, kc, :], lambda kc: [MT.sub[kc]], n, evac, wts, pss)
                b.dma(self.xs[:, t0:t0 + n].rearrange("(c p) t -> p c t", p=128), xt[:], reads=[xt], q="pool")
            b.barrier()

    def stage_odd(self, l, src):
        j = l // 2
        self.odd_norm(l, src)
        self.odd_proj(l, j)
        self.rwkv_scan(j)
        self.odd_out(l, j, src)

    def stage_even(self, l, src):
        b = self.b
        j = l // 2
        with contextlib.ExitStack() as lst:
            self.Gt = b.sb(lst, "Gt", [128, T // 128, 16])
            self.even_inproj(l, j, src)
            with contextlib.ExitStack() as st:
                cw = b.sb(st, "mcw", [128, 3, 16])
                cb = b.sb(st, "mcb", [128, 16])
                b.dma(cw[:], self.mcwT[:, j, :, :], writes=[cw])
                b.dma(cb[:], self.mcbT[:, j, :], writes=[cb])
                self.conv_pass(self.qkT, self.qkcT, 16, cw, cb, True)
            self.s5(j)
            self.mlstm(j)
        self.even_out(l, j, src)

def relay(w):
    w = np.asarray(w, np.float32)
    lead = w.shape[:-2]
    K_, N_ = w.shape[-2:]
    w = w.reshape(lead + (K_ // 128, 128, N_ // 128, 128))
    nd = len(lead)
    w = np.transpose(w, tuple(range(nd)) + (nd + 2, nd + 1, nd + 0, nd + 3))
    return np.ascontiguousarray(w)


def fm(v):
    v = np.asarray(v, np.float32)
    lead = v.shape[:-1]
    c = v.shape[-1] // 128
    return np.ascontiguousarray(np.moveaxis(v.reshape(lead + (c, 128)), -1, 0))


_PROG = {}


def get_prog(layers=DEPTH, mixers=True):
    key = (layers, mixers)
    if key not in _PROG:
        _PROG[key] = Prog(layers, mixers)
    return _PROG[key]


def make_inputs(p, x, c, ctx, c_ctx, ada_w, ada_b, norm_mix, norm_ffn, ffn_w_up, ffn_conv_w, ffn_conv_b, ffn_w_down,
                norm_final, **kw):
    B = x.shape[0]
    shared = {}
    shared["ada_w"] = np.ascontiguousarray(ada_w[:p.layers], np.float32)
    ab = fm(ada_b)
    shared["ada_bT"] = ab
    shared["nmixT"] = fm(norm_mix)
    shared["nffnT"] = fm(norm_ffn)
    shared["nfinT"] = fm(norm_final)
    shared["w_up"] = relay(ffn_w_up[:p.layers])
    shared["convwT"] = fm(ffn_conv_w)
    shared["convbT"] = fm(ffn_conv_b)
    shared["w_down"] = relay(ffn_w_down[:p.layers])
    shared["ones"] = np.ones((128, 128), np.float32)
    if p.mixers:
        NE = p.NE
        shared["w_in"] = np.ascontiguousarray(kw["ev_w_in"][:NE], np.float32)
        shared["binT"] = fm(kw["ev_b_in"][:, :5120])
        shared["bin_row"] = np.ascontiguousarray(kw["ev_b_in"][None], np.float32)
        shared["w_out"] = relay(kw["ev_w_out"][:NE])
        shared["w_in_r"] = relay(kw["ev_w_in"][:NE, :, :3072])
        shared["mcwT"] = fm(kw["ml_conv_w"])
        shared["mcbT"] = fm(kw["ml_conv_b"])
        shared["mlnorm"] = np.ascontiguousarray(np.broadcast_to(kw["ml_norm"][None], (128, 2, 1024)), np.float32)
        shared["w_glu"] = relay(kw["s5_w_glu"][:NE])
        shared["bgluT"] = fm(kw["s5_b_glu"])
        if p.NO > 0:
            NO = p.NO
            f32 = lambda a: np.ascontiguousarray(a, np.float32)
            shared["muT"] = fm(kw["rw_mu"])
            shared["w_r"] = relay(kw["rw_w_r"][:NO]); shared["w_k"] = relay(kw["rw_w_k"][:NO])
            shared["w_v"] = relay(kw["rw_w_v"][:NO]); shared["w_o"] = relay(kw["rw_w_o"][:NO])
            shared["w0T"] = fm(kw["rw_w0"]); shared["a0T"] = fm(kw["rw_a0"]); shared["v0T"] = fm(kw["rw_v0"])
            for nm in ("rw_w1", "rw_w2", "rw_a1", "rw_a2", "rw_v1", "rw_v2", "rw_g1", "rw_g2"):
                shared[nm] = f32(kw[nm])
            shared["kkwT"] = fm(kw["rw_k_k"]); shared["kawT"] = fm(kw["rw_k_a"])
            shared["rkT"] = fm(kw["rw_r_k"].reshape(2, 2048))
            shared["lnwT"] = fm(kw["rw_ln_w"]); shared["lnbT"] = fm(kw["rw_ln_b"])
            bb_ = np.zeros((128, 128), np.float32)
            bb_[:64, :64] = 1.0
            bb_[64:, 64:] = 1.0
            shared["blk64"] = bb_
        dup = lambda a: np.concatenate([a, a], axis=0)
        lr = np.transpose(kw["s5_lam_re"], (3, 0, 1, 2))
        li = np.transpose(kw["s5_lam_im"], (3, 0, 1, 2))
        shared["s5lam"] = np.ascontiguousarray(np.stack([dup(lr), dup(li)], axis=1), np.float32)
        shared["s5ls"] = np.ascontiguousarray(np.broadcast_to(kw["s5_log_step"][None], (128, 2, 2, 64)), np.float32)
        bre = np.transpose(kw["s5_b_re"], (3, 0, 1, 2, 4))
        bim = np.transpose(kw["s5_b_im"], (3, 0, 1, 2, 4))
        shared["s5BX"] = np.ascontiguousarray(np.concatenate([bre, bim], axis=0), np.float32)
        shared["s5BY"] = np.ascontiguousarray(np.concatenate([bim, bre], axis=0), np.float32)
        cre = np.transpose(kw["s5_c_re"], (4, 0, 1, 2, 3))
        cim = np.transpose(kw["s5_c_im"], (4, 0, 1, 2, 3))
        shared["s5CX"] = np.ascontiguousarray(np.concatenate([cre, cim], axis=0), np.float32)
        sg = np.ones((128, 2), np.float32)
        sg[:64, 0] = -1.0
        sg[64:, 1] = -1.0
        shared["s5sgn"] = sg
        shared["s5gmask"] = (np.arange(128)[:, None] // 16 == np.arange(8)[None, :]).astype(np.float32)
        psw = np.zeros((128, 128), np.float32)
        for m in range(64):
            psw[m + 64, m] = 1.0
            psw[m, m + 64] = -1.0
        shared["s5psw"] = psw
        shared["s5dT"] = fm(kw["s5_d"])
        ii = np.arange(128)
        up = (ii[:, None] <= ii[None, :]).astype(np.float32)
        shared["masks"] = np.ascontiguousarray(np.stack([up, up.T, (ii[:, None] < ii[None, :]).astype(np.float32),
                                                         (ii[:, None] > ii[None, :]).astype(np.float32)], axis=1))
        shared["ident"] = np.eye(128, dtype=np.float32)
    maps = []
    for bi in range(B):
        m = dict(shared)
        xt = np.concatenate([ctx[bi], x[bi]], axis=0).T
        m["xT"] = np.ascontiguousarray(xt, np.float32)
        cc = np.stack([c[bi], c_ctx], axis=-1)
        m["cT"] = np.ascontiguousarray(cc.reshape(DC, 128, 2).transpose(1, 0, 2), np.float32)
        for k in p.inputs:
            assert tuple(m[k].shape) == tuple(p.inputs[k]), (k, m[k].shape, p.inputs[k])
        maps.append({k: m[k] for k in p.inputs})
    return maps


def kernel(**inputs):
    inputs = {k: np.asarray(v) for k, v in inputs.items()}
    p = get_prog()
    maps = make_inputs(p, **inputs)
    B = len(maps)
    res = run_bass_kernel_spmd(p.nc, maps, core_ids=list(range(B)))
    outs = [np.asarray(r["outT"]).T for r in res.results]
    return np.ascontiguousarray(np.stack(outs, axis=0).astype(np.float32))
```

```python
import contextlib
import numpy as np
import concourse.bass as bass
import concourse.mybir as mybir
from concourse.bass_utils import run_bass_kernel_spmd

F32 = mybir.dt.float32
F32R = mybir.dt.float32r
import os as _os0
USE_R = _os0.environ.get("K_F32R", "1") == "1"


def rnd(ap):
    return ap.bitcast(F32R) if USE_R else ap
I32 = mybir.dt.int32
AF = mybir.ActivationFunctionType
ALU = mybir.AluOpType
AX = mybir.AxisListType

D = 2048
DC = D // 128
TC = 256
TL = 4096
T = TC + TL
DEPTH = 4
DFF = 5632
EPS = 1e-6
S5W = 1024
MW = 1024
DIN = S5W + 4 * MW + 16
PI = float(np.pi)


class Res:
    __slots__ = ("w", "r")

    def __init__(self):
        self.w = None
        self.r = {}


class TileW(Res):
    __slots__ = ("t", "sub")

    def __init__(self, t):
        super().__init__()
        self.t = t
        self.sub = None

    def __getitem__(self, k):
        return self.t[k]


class Builder:
    SEM_LIMIT = 20000

    def __init__(self):
        self.nc = bass.Bass("TRN2", target_bir_lowering=False)
        nc = self.nc
        self.es = contextlib.ExitStack()
        self.eng = {"pe": nc.tensor, "act": nc.scalar, "dve": nc.vector, "pool": nc.gpsimd, "sp": nc.sync}
        self.sem = {}
        self.cnt = {}
        self.nsem = 0
        for e in self.eng:
            self._new_sem(e)
        self.seen = {e: {} for e in self.eng}
        self.dma_sems = []
        for i in range(12):
            s = self.es.enter_context(nc.semaphore("dq%d" % i))
            self.dma_sems.append([s, 0])
        self.dma_rr = 0
        self.all_res = []
        self.ninst = 0

    def _new_sem(self, e):
        self.nsem += 1
        self.sem[e] = self.es.enter_context(self.nc.semaphore("s_%s_%d" % (e, self.nsem)))
        self.cnt[e] = 0

    def sb(self, stack, name, shape, dt=F32):
        self.nalloc = getattr(self, "nalloc", 0) + 1
        name = "sb%d_%s" % (self.nalloc, name)
        t = stack.enter_context(self.nc.sbuf_tensor(name, list(shape), dt))
        return TileW(t)

    def ps(self, stack, name, shape, dt=F32):
        self.nalloc = getattr(self, "nalloc", 0) + 1
        name = "ps%d_%s" % (self.nalloc, name)
        t = stack.enter_context(self.nc.psum_tensor(name, list(shape), dt))
        return TileW(t)

    def _wait(self, e, tok):
        if tok is None:
            return
        sem, val = tok
        k = id(sem)
        cur = self.seen[e].get(k)
        if cur is not None and cur[1] >= val:
            return
        self.eng[e].wait_ge(sem, val)
        self.seen[e][k] = (sem, val)

    def _deps(self, e, reads, writes, pe_acc=False):
        for r in reads:
            self._wait(e, r.w)
        for w in writes:
            if not (pe_acc and e == "pe"):
                self._wait(e, w.w)
            for oe, tok in w.r.items():
                self._wait(e, tok)

    def _mark(self, tok, e, reads, writes):
        for r in reads:
            r.r[(e, id(tok[0]))] = tok
        for w in writes:
            w.w = tok
            w.r = {}

    def op(self, e, fn, reads=(), writes=(), pe_acc=False):
        self._deps(e, reads, writes, pe_acc)
        if self.cnt[e] >= self.SEM_LIMIT:
            self._new_sem(e)
        ins = fn(self.eng[e])
        self.cnt[e] += 1
        ins.then_inc(self.sem[e], 1)
        tok = (self.sem[e], self.cnt[e])
        self._mark(tok, e, reads, writes)
        self.ninst += 1
        return tok

    def dma(self, out, in_, reads=(), writes=(), q="sp"):
        self._deps(q, reads, writes)
        ent = self.dma_sems[self.dma_rr]
        self.dma_rr = (self.dma_rr + 1) % len(self.dma_sems)
        if ent[1] >= self.SEM_LIMIT:
            self._wait(q, (ent[0], ent[1]))
            ent[0] = self.es.enter_context(self.nc.semaphore("dq_n%d" % self.ninst))
            ent[1] = 0
        self._wait(q, (ent[0], ent[1]))
        self.eng[q].dma_start(out=out, in_=in_).then_inc(ent[0], 16)
        ent[1] += 16
        tok = (ent[0], ent[1])
        self._mark(tok, "dma", reads, writes)
        self.ninst += 1
        return tok

    def barrier(self):
        toks = [(self.sem[e], self.cnt[e]) for e in self.eng if self.cnt[e] > 0]
        toks += [(s, v) for s, v in self.dma_sems if v > 0]
        for e in self.eng:
            for tok in toks:
                self._wait(e, tok)

    def finish(self):
        self.barrier()


def tiles_tokens():
    out = [(0, TC, 1)]
    for i in range(TL // 512):
        out.append((TC + i * 512, 512, 0))
    return out


class Prog:
    def __init__(self, layers=DEPTH, mixers=True, debug=False):
        self.debug = debug
        self.b = Builder()
        self.nc = self.b.nc
        self.layers = layers
        self.mixers = mixers
        self.inputs = {}
        self.build()

    def din(self, name, shape):
        self.inputs[name] = tuple(shape)
        return self.nc.dram_tensor(name, list(shape), F32, kind="ExternalInput").ap()

    def dscr(self, name, shape):
        return self.nc.dram_tensor(name, list(shape), F32, kind="Internal").ap()

    def build(self):
        b, nc = self.b, self.nc
        self.xT = self.din("xT", [D, T])
        self.cT = self.din("cT", [128, DC, 2])
        self.ada_w = self.din("ada_w", [self.layers, D, 6 * D])
        self.ada_bT = self.din("ada_bT", [128, DEPTH, 48 * 2])
        self.nmixT = self.din("nmixT", [128, DEPTH, DC])
        self.nffnT = self.din("nffnT", [128, DEPTH, DC])
        self.nfinT = self.din("nfinT", [128, DC])
        self.w_up = self.din("w_up", [self.layers, 88, 128, 16, 128])
        self.convwT = self.din("convwT", [128, DEPTH, 3, 88])
        self.convbT = self.din("convbT", [128, DEPTH, 88])
        self.w_down = self.din("w_down", [self.layers, 16, 128, 44, 128])
        self.ones_in = self.din("ones", [128, 128])
        self.out = self.nc.dram_tensor("outT", [D, TL], F32, kind="ExternalOutput").ap()
        NE = (self.layers + 1) // 2
        self.NE = NE
        if self.mixers and NE > 0:
            self.w_in = self.din("w_in", [NE, D, DIN])
            self.binT = self.din("binT", [128, 2, 40])
            self.bin_row = self.din("bin_row", [1, 2, DIN])
            self.w_out = self.din("w_out", [NE, 16, 128, 16, 128])
            self.w_in_r = self.din("w_in_r", [NE, 24, 128, 16, 128])
            self.mcwT = self.din("mcwT", [128, 2, 3, 16])
            self.mcbT = self.din("mcbT", [128, 2, 16])
            self.mlnorm = self.din("mlnorm", [128, 2, 1024])
            self.w_glu = self.din("w_glu", [NE, 8, 128, 8, 128])
            self.bgluT = self.din("bgluT", [128, 2, 8])
            self.s5lam = self.din("s5lam", [128, 2, 2, 2, 64])
            self.s5ls = self.din("s5ls", [128, 2, 2, 64])
            self.s5BX = self.din("s5BX", [128, 2, 2, 64, 16])
            self.s5BY = self.din("s5BY", [128, 2, 2, 64, 16])
            self.s5CX = self.din("s5CX", [128, 2, 2, 64, 16])
            self.s5sgn = self.din("s5sgn", [128, 2])
            self.s5gmask = self.din("s5gmask", [128, 8])
            self.s5psw = self.din("s5psw", [128, 128])
            self.s5dT = self.din("s5dT", [128, 2, 8])
            self.masks_in = self.din("masks", [128, 4, 128])
            self.ident_in = self.din("ident", [128, 128])
            self.uT = self.dscr("uT", [1024, T])
            self.qkT = self.dscr("qkT", [2048, T])
            self.qkcT = self.dscr("qkcT", [2048, T])
            self.vtok = self.dscr("vtok", [T, 1024])
            self.otok = self.dscr("otok", [T, 1024])
            self.hd = [self.dscr("hd%d" % i, [T, 1024]) for i in range(2)]
            self.mixT = self.dscr("mixT", [D, T])
        NO = self.layers // 2
        self.NO = NO
        if self.mixers and NO > 0:
            self.muT = self.din("muT", [128, 2, 6, 16])
            self.w_r = self.din("w_r", [NO, 16, 128, 16, 128])
            self.w_k = self.din("w_k", [NO, 16, 128, 16, 128])
            self.w_v = self.din("w_v", [NO, 16, 128, 16, 128])
            self.w_o = self.din("w_o", [NO, 16, 128, 16, 128])
            self.w0T = self.din("w0T", [128, 2, 2, 16])
            self.w1 = self.din("rw_w1", [2, 2, D, 96])
            self.w2 = self.din("rw_w2", [2, 2, 96, D])
            self.a0T = self.din("a0T", [128, 2, 2, 16])
            self.a1 = self.din("rw_a1", [2, 2, D, 96])
            self.a2 = self.din("rw_a2", [2, 2, 96, D])
            self.v0T = self.din("v0T", [128, 1, 16])
            self.v1 = self.din("rw_v1", [1, D, 64])
            self.v2 = self.din("rw_v2", [1, 64, D])
            self.g1 = self.din("rw_g1", [2, D, 256])
            self.g2 = self.din("rw_g2", [2, 256, D])
            self.kkwT = self.din("kkwT", [128, 2, 16])
            self.kawT = self.din("kawT", [128, 2, 16])
            self.rkT = self.din("rkT", [128, 2, 16])
            self.lnwT = self.din("lnwT", [128, 2, 16])
            self.lnbT = self.din("lnbT", [128, 2, 16])
            self.blk64 = self.din("blk64", [128, 128])
            self.hT = self.dscr("hT", [D, T])
            self.rT = self.dscr("rT", [D, T])
            self.kkT = self.dscr("kkT", [D, T])
            self.vT = self.dscr("vT", [D, T])
            self.vfT = self.dscr("vfT", [D, T])
            self.gT = self.dscr("gT", [D, T])
            self.lwT = [self.dscr("lwT%d" % i, [D, T]) for i in range(2)]
            self.kdT = [self.dscr("kdT%d" % i, [D, T]) for i in range(2)]
            self.bdT = [self.dscr("bdT%d" % i, [D, T]) for i in range(2)]
            self.yT = [self.dscr("yT%d" % i, [D, T]) for i in range(2)]
        self.xs = self.dscr("xs", [D, T])
        self.upT = self.dscr("upT", [2 * DFF, T])
        self.actT = self.dscr("actT", [DFF, T])

        with contextlib.ExitStack() as glob:
            self.g = glob
            self.ones = b.sb(glob, "ones", [128, 128])
            b.dma(self.ones[:], self.ones_in[:, :], writes=[self.ones])
            self.mod = b.sb(glob, "mod", [128, DEPTH, 96, 2])
            self.scl = b.sb(glob, "scl", [128, DEPTH, 2, DC, 2])
            if self.mixers:
                self.masks = b.sb(glob, "masks", [128, 4, 128])
                self.ident = b.sb(glob, "ident", [128, 128])
                b.dma(self.masks[:], self.masks_in[:, :, :], writes=[self.masks])
                b.dma(self.ident[:], self.ident_in[:, :], writes=[self.ident])
            self.stage_mod()
            for l in range(self.layers):
                src = self.xT if l == 0 else self.xs
                if self.mixers:
                    if l % 2 == 0:
                        self.stage_even(l, src)
                    else:
                        self.stage_odd(l, src)
                    src = self.xs
                self.stage_ffn(l, src)
            self.stage_final(self.xT if self.layers == 0 else self.xs)
            if getattr(self, "debug", False):
                self.debug_dump()
            b.finish()

    def debug_dump(self):
        b = self.b
        def dout(name, shape):
            return self.nc.dram_tensor(name, list(shape), F32, kind="ExternalOutput").ap()
        d1 = dout("dbg_mod", [128, DEPTH * 96 * 2])
        b.dma(d1[:, :], self.mod[:].rearrange("p l c t -> p (l c t)"), reads=[self.mod])
        d2 = dout("dbg_up", [256, T])
        b.dma(d2[0:128, :], self.upT[0:128, :])
        b.dma(d2[128:256, :], self.upT[DFF:DFF + 128, :])
        d3 = dout("dbg_act", [128, T])
        b.dma(d3[:, :], self.actT[0:128, :])
        d4 = dout("dbg_xs", [128, T])
        b.dma(d4[:, :], self.xs[0:128, :])

    def stage_mod(self):
        b = self.b
        with contextlib.ExitStack() as st:
            ct = b.sb(st, "ct", [128, DC, 2])
            sg = b.sb(st, "sg", [128, DC, 2])
            nm = b.sb(st, "nm", [128, DEPTH, DC])
            nf = b.sb(st, "nf", [128, DEPTH, DC])
            b.dma(ct[:], self.cT[:, :, :], writes=[ct])
            b.dma(nm[:], self.nmixT[:, :, :], writes=[nm])
            b.dma(nf[:], self.nffnT[:, :, :], writes=[nf])
            b.op("act", lambda e: e.activation(out=sg[:], in_=ct[:], func=AF.Sigmoid), reads=[ct], writes=[sg])
            b.op("dve", lambda e: e.tensor_tensor(out=sg[:], in0=sg[:], in1=ct[:], op=ALU.mult), reads=[ct, sg], writes=[sg])
            wts = [b.sb(st, "mw%d" % i, [128, DC, 512]) for i in range(3)]
            pss = [b.ps(st, "mp%d" % i, [128, 4, 2]) for i in range(2)]
            abt = b.sb(st, "abt", [128, DEPTH, 96])
            b.dma(abt[:], self.ada_bT[:, :, 0:96], writes=[abt])
            it = 0
            for l in range(self.layers):
                for nb in range(6 * D // 512):
                    wt = wts[it % 3]
                    ps = pss[it % 2]
                    it += 1
                    b.dma(wt[:], self.ada_w[l, :, nb * 512:(nb + 1) * 512].rearrange("(kc p) n -> p kc n", p=128),
                          writes=[wt])
                    for j in range(4):
                        for kc in range(DC):
                            b.op("pe", lambda e, j=j, kc=kc: e.matmul(ps[:, j, :], wt[:, kc, j * 128:(j + 1) * 128],
                                                                      sg[:, kc, :], start=(kc == 0), stop=(kc == DC - 1)),
                                 reads=[wt, sg], writes=[ps], pe_acc=True)
                    for col in range(2):
                        b.op("dve", lambda e, col=col: e.tensor_tensor(
                            out=self.mod[:, l, nb * 4:(nb + 1) * 4, col], in0=ps[:, :, col],
                            in1=abt[:, l, nb * 4:(nb + 1) * 4], op=ALU.add), reads=[ps, abt], writes=[self.mod])
                for which, (nw, c0) in enumerate(((nm, 16), (nf, 64))):
                    for col in range(2):
                        b.op("dve", lambda e, which=which, nw=nw, c0=c0, col=col: e.scalar_tensor_tensor(
                            out=self.scl[:, l, which, :, col], in0=self.mod[:, l, c0:c0 + DC, col], scalar=1.0,
                            in1=nw[:, l, :], op0=ALU.add, op1=ALU.mult), reads=[self.mod, nw], writes=[self.scl])
            b.barrier()

    def normmod(self, st, xt, ht, tmp, pss, rstd, n, scale_ap, shift_ap, res_extra=()):
        b = self.b
        for c in range(DC):
            b.op("act", lambda e, c=c: e.activation(out=tmp[:, c % 2, 0:n], in_=xt[:, c, 0:n], func=AF.Square),
                 reads=[xt], writes=[tmp.sub[c % 2]])
            b.op("pe", lambda e, c=c: e.matmul(pss[:, 0:n], self.ones[:, :], tmp[:, c % 2, 0:n], start=(c == 0),
                                               stop=(c == DC - 1)), reads=[tmp.sub[c % 2], self.ones], writes=[pss],
                 pe_acc=True)
        b.op("act", lambda e: e.activation(out=rstd[:, 0:n], in_=pss[:, 0:n], func=AF.Sqrt, scale=1.0 / D, bias=self.epsb[:, 0:1]),
             reads=[pss, self.epsb], writes=[rstd])
        b.op("dve", lambda e: e.reciprocal(out=rstd[:, 0:n], in_=rstd[:, 0:n]), reads=[rstd], writes=[rstd])
        for c in range(DC):
            b.op("dve", lambda e, c=c: e.tensor_tensor(out=xt[:, c, 0:n], in0=xt[:, c, 0:n], in1=rstd[:, 0:n], op=ALU.mult),
                 reads=[xt, rstd], writes=[xt])
            b.op("act", lambda e, c=c: e.activation(out=rnd(ht[:, c, 0:n]), in_=xt[:, c, 0:n], func=AF.Identity,
                                                    scale=scale_ap(c), bias=shift_ap(c)),
                 reads=[xt] + list(res_extra), writes=[ht.sub[c]])

    def subres(self, tl, n):
        tl.sub = [Res() for _ in range(n)]
        return tl

    def gemm(self, st, W, K_chunks, n_chunks, rhs_fn, rhs_res, n, evac, wts, pss, n_col0=0, ksub=16, use_r=True):
        b = self.b
        cast = (lambda ap: rnd(ap)) if use_r else (lambda ap: ap)
        for j in range(n_chunks):
            ps = pss[self.psi % len(pss)]
            self.psi += 1
            nk = (K_chunks + ksub - 1) // ksub
            for kb in range(nk):
                k0 = kb * ksub
                kn = min(ksub, K_chunks - k0)
                wt = wts[self.wi % len(wts)]
                self.wi += 1
                b.dma(cast(wt[:, 0:kn, :]), W[n_col0 // 128 + j, :, k0:k0 + kn, :], writes=[wt],
                      q=("pool" if (use_r and USE_R) else "sp"))
                for kk in range(kn):
                    kc = k0 + kk
                    b.op("pe", lambda e, kk=kk, kc=kc, wt=wt, ps=ps: e.matmul(
                        ps[:, 0:n], cast(wt[:, kk, :]), cast(rhs_fn(kc)), start=(kc == 0), stop=(kc == K_chunks - 1)),
                        reads=[wt] + list(rhs_res(kc)), writes=[ps], pe_acc=True)
            evac(j, ps)

    def stage_ffn(self, l, src):
        b = self.b
        self.psi = 0
        self.wi = 0
        with contextlib.ExitStack() as st:
            self.epsb = b.sb(st, "epsb", [128, 1])
            b.op("dve", lambda e: e.memset(self.epsb[:], EPS), writes=[self.epsb])
            xt = b.sb(st, "xt", [128, DC, 512])
            hts = [self.subres(b.sb(st, "ht%d" % i, [128, DC, 512]), DC) for i in range(2)]
            tmp = self.subres(b.sb(st, "tmp", [128, 2, 512]), 2)
            rstd = b.sb(st, "rstd", [128, 512])
            psn = b.ps(st, "psn", [128, 512])
            wts = [b.sb(st, "w%d" % i, [128, 16, 128]) for i in range(3)]
            pss = [b.ps(st, "pg%d" % i, [128, 512]) for i in range(4)]
            obs = [b.sb(st, "ob%d" % i, [128, 512]) for i in range(4)]
            oi = [0]
            tl_ = tiles_tokens()
            groups = [[tl_[0]]] + [[tl_[i], tl_[i + 1]] for i in range(1, len(tl_), 2)]
            for grp in groups:
                for si, (t0, n, isctx) in enumerate(grp):
                    b.dma(xt[:, :, 0:n], src[:, t0:t0 + n].rearrange("(c p) t -> p c t", p=128), writes=[xt])
                    self.normmod(st, xt, hts[si], tmp, psn, rstd, n,
                                 lambda c, isctx=isctx: self.scl[:, l, 1, c, isctx:isctx + 1],
                                 lambda c, isctx=isctx: self.mod[:, l, 48 + c, isctx:isctx + 1], res_extra=[self.scl, self.mod])
                for jn in range(88):
                    wt = wts[self.wi % len(wts)]
                    self.wi += 1
                    b.dma(rnd(wt[:, :, :]), self.w_up[l][jn, :, :, :], writes=[wt], q=("pool" if USE_R else "sp"))
                    for si, (t0, n, isctx) in enumerate(grp):
                        ht = hts[si]
                        ps = pss[self.psi % len(pss)]
                        self.psi += 1
                        for kc in range(DC):
                            b.op("pe", lambda e, kc=kc, ps=ps, wt=wt, ht=ht, n=n: e.matmul(
                                ps[:, 0:n], rnd(wt[:, kc, :]), rnd(ht[:, kc, 0:n]), start=(kc == 0), stop=(kc == DC - 1)),
                                reads=[wt, ht.sub[kc]], writes=[ps], pe_acc=True)
                        ob = obs[oi[0] % 4]
                        oi[0] += 1
                        b.op("act", lambda e, ob=ob, ps=ps, n=n: e.copy(out=ob[:, 0:n], in_=ps[:, 0:n]), reads=[ps], writes=[ob])
                        b.dma(self.upT[jn * 128:(jn + 1) * 128, t0:t0 + n], ob[:, 0:n], reads=[ob], q="pool")
            b.barrier()
        with contextlib.ExitStack() as st:
            cw = b.sb(st, "cw", [128, 3, 88])
            cb = b.sb(st, "cb", [128, 88])
            b.dma(cw[:], self.convwT[:, l, :, :], writes=[cw])
            b.dma(cb[:], self.convbT[:, l, :], writes=[cb])
            NT = 2048
            ua = [b.sb(st, "ua%d" % i, [128, NT + 2]) for i in range(2)]
            ug = [b.sb(st, "ug%d" % i, [128, NT + 2]) for i in range(2)]
            ca = [b.sb(st, "ca%d" % i, [128, NT]) for i in range(2)]
            cg = [b.sb(st, "cg%d" % i, [128, NT]) for i in range(2)]
            segs = [(0, TC, 0, TC), (TC, NT, TC, T), (TC + NT, NT, TC, T)]
            it = 0
            for j in range(44):
                for (t0, n, lo, hi) in segs:
                    A, G, CA, CG = ua[it % 2], ug[it % 2], ca[it % 2], cg[it % 2]
                    it += 1
                    for (U, row) in ((A, j), (G, 44 + j)):
                        a0 = max(t0 - 1, lo)
                        a1 = min(t0 + n + 1, hi)
                        if t0 - 1 < lo:
                            b.op("dve", lambda e, U=U: e.memset(U[:, 0:1], 0.0), writes=[U])
                        if t0 + n + 1 > hi:
                            b.op("dve", lambda e, U=U, n=n: e.memset(U[:, n + 1:n + 2], 0.0), writes=[U])
                        b.dma(U[:, a0 - (t0 - 1):a1 - (t0 - 1)], self.upT[row * 128:(row + 1) * 128, a0:a1], writes=[U])
                    for (U, C, col) in ((A, CA, j), (G, CG, 44 + j)):
                        b.op("dve", lambda e, U=U, C=C, col=col, n=n: e.tensor_scalar(
                            out=C[:, 0:n], in0=U[:, 1:n + 1], scalar1=cw[:, 1, col:col + 1], scalar2=cb[:, col:col + 1],
                            op0=ALU.mult, op1=ALU.add), reads=[U, cw, cb], writes=[C])
                        b.op("dve", lambda e, U=U, C=C, col=col, n=n: e.scalar_tensor_tensor(
                            out=C[:, 0:n], in0=U[:, 0:n], scalar=cw[:, 0, col:col + 1], in1=C[:, 0:n],
                            op0=ALU.mult, op1=ALU.add), reads=[U, cw, C], writes=[C])
                        b.op("dve", lambda e, U=U, C=C, col=col, n=n: e.scalar_tensor_tensor(
                            out=C[:, 0:n], in0=U[:, 2:n + 2], scalar=cw[:, 2, col:col + 1], in1=C[:, 0:n],
                            op0=ALU.mult, op1=ALU.add), reads=[U, cw, C], writes=[C])
                    b.op("act", lambda e, G=G, CG=CG, n=n: e.activation(out=G[:, 0:n], in_=CG[:, 0:n], func=AF.Sigmoid),
                         reads=[CG], writes=[G])
                    b.op("pool", lambda e, G=G, CG=CG, n=n: e.tensor_tensor(out=CG[:, 0:n], in0=CG[:, 0:n], in1=G[:, 0:n], op=ALU.mult),
                         reads=[CG, G], writes=[CG])
                    b.op("pool", lambda e, CA=CA, CG=CG, n=n: e.tensor_tensor(out=CA[:, 0:n], in0=CA[:, 0:n], in1=CG[:, 0:n], op=ALU.mult),
                         reads=[CA, CG], writes=[CA])
                    b.dma(self.actT[j * 128:(j + 1) * 128, t0:t0 + n], CA[:, 0:n], reads=[CA], q="pool")
            b.barrier()
        with contextlib.ExitStack() as st:
            at = self.subres(b.sb(st, "at", [128, 44, 512]), 44)
            xt = b.sb(st, "xt", [128, DC, 512])
            wts = [b.sb(st, "w%d" % i, [128, 11, 128]) for i in range(4)]
            pss = [b.ps(st, "pg%d" % i, [128, 512]) for i in range(4)]
            for (t0, n, isctx) in tiles_tokens():
                b.dma(xt[:, :, 0:n], src[:, t0:t0 + n].rearrange("(c p) t -> p c t", p=128), writes=[xt])
                for q4 in range(4):
                    b.dma(at[:, q4 * 11:(q4 + 1) * 11, 0:n],
                          self.actT[q4 * 11 * 128:(q4 + 1) * 11 * 128, t0:t0 + n].rearrange("(c p) t -> p c t", p=128),
                          writes=[at.sub[c] for c in range(q4 * 11, (q4 + 1) * 11)])

                def evac(j, ps, t0=t0, n=n, isctx=isctx):
                    b.op("dve", lambda e: e.scalar_tensor_tensor(
                        out=xt[:, j, 0:n], in0=ps[:, 0:n], scalar=self.mod[:, l, 80 + j, isctx:isctx + 1], in1=xt[:, j, 0:n],
                        op0=ALU.mult, op1=ALU.add), reads=[ps, self.mod, xt], writes=[xt])

                self.gemm(st, self.w_down[l], 44, DC, lambda kc: at[:, kc, 0:n], lambda kc: [at.sub[kc]], n, evac, wts, pss,
                          ksub=11)
                b.dma(self.xs[:, t0:t0 + n].rearrange("(c p) t -> p c t", p=128), xt[:, :, 0:n], reads=[xt], q="pool")
            b.barrier()

    def stage_final(self, src):
        b = self.b
        with contextlib.ExitStack() as st:
            self.epsb = b.sb(st, "epsb", [128, 1])
            b.op("dve", lambda e: e.memset(self.epsb[:], EPS), writes=[self.epsb])
            nf = b.sb(st, "nfin", [128, DC])
            zero = b.sb(st, "zero", [128, 1])
            b.op("dve", lambda e: e.memset(zero[:], 0.0), writes=[zero])
            b.dma(nf[:], self.nfinT[:, :], writes=[nf])
            xt = b.sb(st, "xt", [128, DC, 512])
            ht = self.subres(b.sb(st, "ht", [128, DC, 512]), DC)
            tmp = self.subres(b.sb(st, "tmp", [128, 2, 512]), 2)
            rstd = b.sb(st, "rstd", [128, 512])
            psn = b.ps(st, "psn", [128, 512])
            for (t0, n, isctx) in tiles_tokens():
                if isctx:
                    continue
                b.dma(xt[:, :, 0:n], src[:, t0:t0 + n].rearrange("(c p) t -> p c t", p=128), writes=[xt])
                self.normmod(st, xt, ht, tmp, psn, rstd, n, lambda c: nf[:, c:c + 1], lambda c: zero[:, 0:1],
                             res_extra=[nf, zero])
                self.out_tok = b.dma(self.out[:, t0 - TC:t0 - TC + n].rearrange("(c p) t -> p c t", p=128), ht[:, :, 0:n],
                                     reads=ht.sub, q="pool")
            b.barrier()


    def conv_pass(self, srcT, dstT, nch, cw, cb, silu, NT=2048):
        b = self.b
        with contextlib.ExitStack() as st:
            us = [b.sb(st, "cu%d" % i, [128, NT + 2]) for i in range(2)]
            cs = [b.sb(st, "cc%d" % i, [128, NT]) for i in range(2)]
            sg = [b.sb(st, "cs%d" % i, [128, NT]) for i in range(2)]
            segs = [(0, TC, 0, TC)] + [(TC + i * NT, NT, TC, T) for i in range(TL // NT)]
            it = 0
            for j in range(nch):
                for (t0, n, lo, hi) in segs:
                    U, C, S = us[it % 2], cs[it % 2], sg[it % 2]
                    it += 1
                    a0 = max(t0 - 1, lo)
                    a1 = min(t0 + n + 1, hi)
                    if t0 - 1 < lo:
                        b.op("dve", lambda e, U=U: e.memset(U[:, 0:1], 0.0), writes=[U])
                    if t0 + n + 1 > hi:
                        b.op("dve", lambda e, U=U, n=n: e.memset(U[:, n + 1:n + 2], 0.0), writes=[U])
                    b.dma(U[:, a0 - (t0 - 1):a1 - (t0 - 1)], srcT[j * 128:(j + 1) * 128, a0:a1], writes=[U])
                    b.op("dve", lambda e, U=U, C=C, j=j, n=n: e.tensor_scalar(
                        out=C[:, 0:n], in0=U[:, 1:n + 1], scalar1=cw[:, 1, j:j + 1], scalar2=cb[:, j:j + 1],
                        op0=ALU.mult, op1=ALU.add), reads=[U, cw, cb], writes=[C])
                    b.op("dve", lambda e, U=U, C=C, j=j, n=n: e.scalar_tensor_tensor(
                        out=C[:, 0:n], in0=U[:, 0:n], scalar=cw[:, 0, j:j + 1], in1=C[:, 0:n],
                        op0=ALU.mult, op1=ALU.add), reads=[U, cw, C], writes=[C])
                    b.op("dve", lambda e, U=U, C=C, j=j, n=n: e.scalar_tensor_tensor(
                        out=C[:, 0:n], in0=U[:, 2:n + 2], scalar=cw[:, 2, j:j + 1], in1=C[:, 0:n],
                        op0=ALU.mult, op1=ALU.add), reads=[U, cw, C], writes=[C])
                    if silu:
                        b.op("act", lambda e, S=S, C=C, n=n: e.activation(out=S[:, 0:n], in_=C[:, 0:n], func=AF.Sigmoid),
                             reads=[C], writes=[S])
                        b.op("pool", lambda e, S=S, C=C, n=n: e.tensor_tensor(out=C[:, 0:n], in0=C[:, 0:n], in1=S[:, 0:n], op=ALU.mult),
                             reads=[C, S], writes=[C])
                    b.dma(dstT[j * 128:(j + 1) * 128, t0:t0 + n], C[:, 0:n], reads=[C], q="pool")
            b.barrier()

    def even_inproj(self, l, j, src):
        b = self.b
        self.psi = 0
        self.wi = 0
        with contextlib.ExitStack() as st:
            self.epsb = b.sb(st, "epsb", [128, 1])
            b.op("dve", lambda e: e.memset(self.epsb[:], EPS), writes=[self.epsb])
            xt = b.sb(st, "xt", [128, DC, 512])
            ht = self.subres(b.sb(st, "ht", [128, DC, 512]), DC)
            tmp = self.subres(b.sb(st, "tmp", [128, 2, 512]), 2)
            rstd = b.sb(st, "rstd", [128, 512])
            psn = b.ps(st, "psn", [128, 512])
            wts = [b.sb(st, "w%d" % i, [128, 16, 128]) for i in range(3)]
            pss = [b.ps(st, "pg%d" % i, [128, 512]) for i in range(3)]
            obs = [b.sb(st, "ob%d" % i, [128, 512]) for i in range(2)]
            wtok = [b.sb(st, "wk%d" % i, [128, 16, 256]) for i in range(2)]
            brow = b.sb(st, "brow", [1, 2064])
            bint = b.sb(st, "bint", [128, 40])
            pst = [b.ps(st, "pt%d" % i, [128, 256]) for i in range(2)]
            otb = [b.sb(st, "otb%d" % i, [128, 256]) for i in range(2)]
            b.dma(brow[:], self.bin_row[0:1, j, 3072:5136], writes=[brow])
            b.dma(bint[:], self.binT[:, j, :], writes=[bint])
            oi = [0]
            ti = 0
            for (t0, n, isctx) in tiles_tokens():
                b.dma(xt[:, :, 0:n], src[:, t0:t0 + n].rearrange("(c p) t -> p c t", p=128), writes=[xt])
                self.normmod(st, xt, ht, tmp, psn, rstd, n,
                             lambda c: self.scl[:, l, 0, c, isctx:isctx + 1],
                             lambda c: self.mod[:, l, c, isctx:isctx + 1], res_extra=[self.scl, self.mod])

                def evac(jc, ps, t0=t0, n=n):
                    ob = obs[oi[0] % 2]
                    oi[0] += 1
                    b.op("act", lambda e: e.activation(out=ob[:, 0:n], in_=ps[:, 0:n], func=AF.Identity,
                                                       bias=bint[:, jc:jc + 1]), reads=[ps, bint], writes=[ob])
                    dst = self.uT[jc * 128:(jc + 1) * 128, t0:t0 + n] if jc < 8 else \
                        self.qkT[(jc - 8) * 128:(jc - 7) * 128, t0:t0 + n]
                    b.dma(dst, ob[:, 0:n], reads=[ob], q="pool")

                self.gemm(st, self.w_in_r[j], DC, 24, lambda kc: ht[:, kc, 0:n], lambda kc: [ht.sub[kc]], n, evac, wts, pss)
                for blk in range(9):
                    c0 = 3072 + blk * 256
                    ncol = 256 if blk < 8 else 16
                    wt = wtok[ti % 2]
                    b.dma(wt[:, :, 0:ncol], self.w_in[j][:, c0:c0 + ncol].rearrange("(kc p) n -> p kc n", p=128), writes=[wt])
                    for ts in range(n // 128):
                        ps = pst[ti % 2]
                        ob = otb[ti % 2]
                        ti += 1
                        for kc in range(DC):
                            b.op("pe", lambda e, kc=kc, ts=ts, ps=ps, wt=wt, ncol=ncol: e.matmul(
                                ps[:, 0:ncol], ht[:, kc, ts * 128:(ts + 1) * 128], wt[:, kc, 0:ncol], start=(kc == 0), stop=False),
                                reads=[wt, ht.sub[kc]], writes=[ps], pe_acc=True)
                        b.op("pe", lambda e, ps=ps, ncol=ncol, c0=c0: e.matmul(
                            ps[:, 0:ncol], self.ones[0:1, 0:128], brow[0:1, c0 - 3072:c0 - 3072 + ncol], start=False, stop=True),
                            reads=[self.ones, brow], writes=[ps], pe_acc=True)
                        tt0 = t0 + ts * 128
                        if blk < 8:
                            b.op("act", lambda e, ob=ob, ps=ps: e.copy(out=ob[:, 0:256], in_=ps[:, 0:256]), reads=[ps], writes=[ob])
                            dst = (self.vtok if blk < 4 else self.otok)[tt0:tt0 + 128, (blk % 4) * 256:(blk % 4 + 1) * 256]
                            b.dma(dst, ob[:, 0:256], reads=[ob], q="pool")
                        else:
                            b.op("act", lambda e, ps=ps, tt0=tt0: e.copy(out=self.Gt[:, tt0 // 128, :], in_=ps[:, 0:16]),
                                 reads=[ps], writes=[self.Gt])
            b.barrier()

    def mlstm(self, j):
        b = self.b
        NCH = T // 128
        with contextlib.ExitStack() as st:
            LF = b.sb(st, "LF", [128, 2, NCH, 4])
            IG = b.sb(st, "IG", [128, 2, NCH, 4])
            Bc = b.sb(st, "Bc", [128, 2, NCH, 4])
            AL = b.sb(st, "AL", [128, 2, NCH, 4])
            BE = b.sb(st, "BE", [128, 2, NCH, 4])
            ALL = b.sb(st, "ALL", [128, 2, NCH, 4])
            gst = contextlib.ExitStack()
            psg = b.ps(gst, "psg", [128, 2, 256])
            psl = b.ps(gst, "psl", [128, 2, 256])
            for d in range(2):
                b.op("act", lambda e, d=d: e.activation(out=LF[:, d], in_=self.Gt[:, :, d * 8 + 4:d * 8 + 8], func=AF.Exp, scale=-1.0),
                     reads=[self.Gt], writes=[LF])
                b.op("dve", lambda e, d=d: e.tensor_copy(out=IG[:, d], in_=self.Gt[:, :, d * 8:d * 8 + 4]), reads=[self.Gt], writes=[IG])
            b.op("dve", lambda e: e.tensor_scalar_add(out=LF[:], in0=LF[:], scalar1=1.0), reads=[LF], writes=[LF])
            b.op("act", lambda e: e.activation(out=LF[:], in_=LF[:], func=AF.Ln), reads=[LF], writes=[LF])
            b.op("dve", lambda e: e.tensor_scalar_mul(out=LF[:], in0=LF[:], scalar1=-1.0), reads=[LF], writes=[LF])
            for d in range(2):
                b.op("pe", lambda e, d=d: e.matmul(psg[:, d, 0:NCH * 4], self.masks[:, d, :], LF[:, d].rearrange("p c h -> p (c h)"),
                                                   start=True, stop=True), reads=[self.masks, LF], writes=[psg], pe_acc=True)
                b.op("pe", lambda e, d=d: e.matmul(psl[:, d, 0:NCH * 4], self.ones[:, :], LF[:, d].rearrange("p c h -> p (c h)"),
                                                   start=True, stop=True), reads=[self.ones, LF], writes=[psl], pe_acc=True)
            b.op("dve", lambda e: e.tensor_copy(out=Bc[:].rearrange("p d c h -> p d (c h)"), in_=psg[:, :, 0:NCH * 4]), reads=[psg], writes=[Bc])
            b.op("act", lambda e: e.activation(out=AL[:], in_=Bc[:], func=AF.Exp), reads=[Bc], writes=[AL])
            b.op("act", lambda e: e.activation(out=ALL[:].rearrange("p d c h -> p d (c h)"), in_=psl[:, :, 0:NCH * 4], func=AF.Exp), reads=[psl], writes=[ALL])
            b.op("dve", lambda e: e.tensor_tensor(out=BE[:], in0=IG[:], in1=Bc[:], op=ALU.subtract), reads=[IG, Bc], writes=[BE])
            b.op("act", lambda e: e.activation(out=BE[:], in_=BE[:], func=AF.Exp), reads=[BE], writes=[BE])
            b.op("dve", lambda e: e.tensor_scalar_mul(out=BE[:], in0=BE[:], scalar1=1.0 / 16.0), reads=[BE], writes=[BE])
            b.barrier()
            gst.close()
            Cst = [[b.sb(st, "C%d%d" % (d, h), [128, 2, 257]) for h in range(4)] for d in range(2)]
            for d in range(2):
                for h in range(4):
                    b.op("pool", lambda e, d=d, h=h: e.memset(Cst[d][h][:], 0.0), writes=[Cst[d][h]])
            qt = [b.sb(st, "q%d" % i, [128, 8, 128]) for i in range(2)]
            kt = [b.sb(st, "k%d" % i, [128, 8, 128]) for i in range(2)]
            ktok = [b.sb(st, "kk%d" % i, [128, 1024]) for i in range(2)]
            Vp = [b.sb(st, "V%d" % i, [128, 4, 257]) for i in range(2)]
            for i in range(2):
                b.op("pool", lambda e, i=i: e.memset(Vp[i][:, :, 256:257], 1.0), writes=[Vp[i]])
            pkt = b.ps(st, "pkt", [128, 2, 512])
            pst_ = [b.ps(st, "pst%d" % i, [128, 512]) for i in range(2)]
            ppp = [b.ps(st, "ppp%d" % i, [128, 512]) for i in range(2)]
            pcc = [b.ps(st, "pcc%d" % i, [128, 512]) for i in range(2)]
            ST = [b.sb(st, "ST%d" % i, [128, 128]) for i in range(2)]
            V2 = [b.sb(st, "V2%d" % i, [128, 257]) for i in range(2)]
            sm = [b.sb(st, "sm%d" % i, [128, 4]) for i in range(2)]
            Hb = [b.sb(st, "Hb%d" % i, [128, 1024]) for i in range(2)]
            tC = b.sb(st, "tC", [128, 2, 257])
            order = [list(range(NCH)), [1, 0] + list(range(NCH - 1, 1, -1))]
            it = 0
            for s_ in range(NCH):
                for d in range(2):
                    c = order[d][s_]
                    t0 = c * 128
                    Q, K_, KT, V, H = qt[it % 2], kt[it % 2], ktok[it % 2], Vp[it % 2], Hb[it % 2]
                    it += 1
                    b.dma(Q[:], self.qkcT[0:1024, t0:t0 + 128].rearrange("(c p) t -> p c t", p=128), writes=[Q])
                    b.dma(K_[:], self.qkcT[1024:2048, t0:t0 + 128].rearrange("(c p) t -> p c t", p=128), writes=[K_])
                    b.dma(V[:, :, 0:256], self.vtok[t0:t0 + 128, :].rearrange("t (h e) -> t h e", h=4), writes=[V])
                    for fc in range(8):
                        b.op("pe", lambda e, fc=fc, K_=K_: e.transpose(pkt[:, fc // 4, (fc % 4) * 128:(fc % 4 + 1) * 128], K_[:, fc, :],
                                                                      self.ident[:, :]), reads=[K_, self.ident], writes=[pkt], pe_acc=True)
                    b.op("act", lambda e, KT=KT: e.copy(out=KT[:].rearrange("p (a x) -> p a x", a=2), in_=pkt[:]), reads=[pkt], writes=[KT])
                    for h in range(4):
                        i2 = (it * 4 + h) % 2
                        pS, pP, S_, V2_, sm_ = pst_[i2], ppp[i2], ST[i2], V2[i2], sm[i2]
                        for dc in range(2):
                            b.op("pe", lambda e, dc=dc, h=h, pS=pS, K_=K_, Q=Q: e.matmul(pS[:, 0:128], K_[:, 2 * h + dc, :], Q[:, 2 * h + dc, :],
                                                                                   start=(dc == 0), stop=(dc == 1)),
                                 reads=[K_, Q], writes=[pS], pe_acc=True)
                        b.op("dve", lambda e, pS=pS, S_=S_, d=d, c=c, h=h: e.scalar_tensor_tensor(
                            out=S_[:], in0=pS[:, 0:128], scalar=BE[:, d, c, h:h + 1], in1=self.masks[:, d, :], op0=ALU.mult, op1=ALU.mult),
                            reads=[pS, BE, self.masks], writes=[S_])
                        b.op("pe", lambda e, pP=pP, S_=S_, V=V, h=h: e.matmul(pP[:, 0:257], S_[:, :], V[:, h, :], start=True, stop=False),
                             reads=[S_, V], writes=[pP], pe_acc=True)
                        for dc in range(2):
                            b.op("pe", lambda e, pP=pP, Q=Q, dc=dc, h=h, d=d: e.matmul(pP[:, 0:257], Q[:, 2 * h + dc, :], Cst[d][h][:, dc, :],
                                                                                 start=False, stop=(dc == 1)),
                                 reads=[Q, Cst[d][h]], writes=[pP], pe_acc=True)
                        b.op("dve", lambda e, pP=pP, sm_=sm_, d=d, c=c, h=h: e.tensor_scalar(
                            out=sm_[:, 3:4], in0=pP[:, 256:257], scalar1=AL[:, d, c, h:h + 1], scalar2=None, op0=ALU.mult),
                            reads=[pP, AL], writes=[sm_])
                        b.op("act", lambda e, sm_=sm_: e.activation(out=sm_[:, 0:1], in_=sm_[:, 3:4], func=AF.Abs),
                             reads=[sm_], writes=[sm_])
                        b.op("dve", lambda e, sm_=sm_: e.tensor_scalar_max(out=sm_[:, 0:1], in0=sm_[:, 0:1], scalar1=1.0),
                             reads=[sm_], writes=[sm_])
                        b.op("dve", lambda e, sm_=sm_: e.reciprocal(out=sm_[:, 1:2], in_=sm_[:, 0:1]), reads=[sm_], writes=[sm_])
                        b.op("dve", lambda e, sm_=sm_, d=d, c=c, h=h: e.tensor_tensor(out=sm_[:, 2:3], in0=sm_[:, 1:2], in1=AL[:, d, c, h:h + 1], op=ALU.mult),
                             reads=[sm_, AL], writes=[sm_])
                        b.op("dve", lambda e, pP=pP, sm_=sm_, H=H, h=h: e.tensor_scalar(
                            out=H[:, h * 256:(h + 1) * 256], in0=pP[:, 0:256], scalar1=sm_[:, 2:3], scalar2=None, op0=ALU.mult),
                             reads=[pP, sm_], writes=[H])
                        b.op("pool", lambda e, V2_=V2_, V=V, h=h, d=d, c=c: e.tensor_scalar(
                            out=V2_[:], in0=V[:, h, :], scalar1=BE[:, d, c, h:h + 1], scalar2=None, op0=ALU.mult),
                            reads=[V, BE], writes=[V2_])
                        for dc in range(2):
                            b.op("pe", lambda e, dc=dc, h=h, KT=KT, V2_=V2_: e.matmul(pcc[dc][:, 0:257], KT[:, (2 * h + dc) * 128:(2 * h + dc + 1) * 128],
                                                                                  V2_[:], start=True, stop=True),
                                 reads=[KT, V2_], writes=[pcc[dc]], pe_acc=True)
                            b.op("dve", lambda e, d=d, h=h, dc=dc: e.tensor_tensor(out=tC[:, dc, :], in0=pcc[dc][:, 0:257], in1=Cst[d][h][:, dc, :], op=ALU.add),
                                 reads=[pcc[dc], Cst[d][h]], writes=[tC])
                        b.op("pool", lambda e, d=d, h=h, c=c: e.tensor_scalar(
                            out=Cst[d][h][:], in0=tC[:], scalar1=ALL[:, d, c, h:h + 1], scalar2=None, op0=ALU.mult),
                            reads=[tC, ALL], writes=[Cst[d][h]])
                    b.dma(self.hd[d][t0:t0 + 128, :], H[:], reads=[H], q="pool")
            b.barrier()
        with contextlib.ExitStack() as st:
            self.epsb = b.sb(st, "epsb", [128, 1])
            b.op("dve", lambda e: e.memset(self.epsb[:], EPS), writes=[self.epsb])
            nw = b.sb(st, "nw", [128, 1024])
            b.dma(nw[:], self.mlnorm[:, j, :], writes=[nw])
            h0 = [b.sb(st, "h0%d" % i, [128, 1024]) for i in range(2)]
            h1 = [b.sb(st, "h1%d" % i, [128, 1024]) for i in range(2)]
            og = [b.sb(st, "og%d" % i, [128, 1024]) for i in range(2)]
            junk = b.sb(st, "junk", [128, 256])
            ss = [b.sb(st, "ss%d" % i, [128, 4]) for i in range(2)]
            ptr = b.ps(st, "ptr", [128, 2, 512])
            ob = [b.sb(st, "fo%d" % i, [128, 1024]) for i in range(2)]
            for c in range(NCH):
                t0 = c * 128
                A, B_, O, SS, OB = h0[c % 2], h1[c % 2], og[c % 2], ss[c % 2], ob[c % 2]
                b.dma(A[:], self.hd[0][t0:t0 + 128, :], writes=[A])
                b.dma(B_[:], self.hd[1][t0:t0 + 128, :], writes=[B_])
                b.dma(O[:], self.otok[t0:t0 + 128, :], writes=[O])
                b.op("dve", lambda e, A=A, B_=B_: e.tensor_tensor(out=A[:], in0=A[:], in1=B_[:], op=ALU.add), reads=[A, B_], writes=[A])
                for h in range(4):
                    b.op("act", lambda e, A=A, SS=SS, h=h: e.activation(out=junk[:], in_=A[:, h * 256:(h + 1) * 256], func=AF.Square,
                                                                      accum_out=SS[:, h:h + 1]), reads=[A], writes=[junk, SS])
                b.op("act", lambda e, SS=SS: e.activation(out=SS[:], in_=SS[:], func=AF.Sqrt, scale=1.0 / 256, bias=self.epsb[:, 0:1]),
                     reads=[SS, self.epsb], writes=[SS])
                b.op("dve", lambda e, SS=SS: e.reciprocal(out=SS[:], in_=SS[:]), reads=[SS], writes=[SS])
                b.op("act", lambda e, O=O: e.activation(out=O[:], in_=O[:], func=AF.Sigmoid), reads=[O], writes=[O])
                b.op("pool", lambda e, O=O: e.tensor_tensor(out=O[:], in0=O[:], in1=nw[:], op=ALU.mult), reads=[O, nw], writes=[O])
                for h in range(4):
                    b.op("dve", lambda e, A=A, O=O, SS=SS, h=h: e.scalar_tensor_tensor(
                        out=A[:, h * 256:(h + 1) * 256], in0=A[:, h * 256:(h + 1) * 256], scalar=SS[:, h:h + 1],
                        in1=O[:, h * 256:(h + 1) * 256], op0=ALU.mult, op1=ALU.mult), reads=[A, O, SS], writes=[A])
                for fc in range(8):
                    b.op("pe", lambda e, A=A, fc=fc: e.transpose(ptr[:, fc // 4, (fc % 4) * 128:(fc % 4 + 1) * 128], A[:, fc * 128:(fc + 1) * 128],
                                                                  self.ident[:, :]), reads=[A, self.ident], writes=[ptr], pe_acc=True)
                b.op("act", lambda e, OB=OB: e.copy(out=OB[:].rearrange("p (a x) -> p a x", a=2), in_=ptr[:]), reads=[ptr], writes=[OB])
                b.dma(self.mixT[1024:2048, t0:t0 + 128].rearrange("(c p) t -> p c t", p=128), OB[:].rearrange("p (c t) -> p c t", c=8),
                      reads=[OB], q="pool")
            b.barrier()

    def even_out(self, l, j, src):
        b = self.b
        self.psi = 0
        self.wi = 0
        with contextlib.ExitStack() as st:
            bg = b.sb(st, "bg", [128, 8])
            b.dma(bg[:], self.bgluT[:, j, :], writes=[bg])
            mt = self.subres(b.sb(st, "mt", [128, DC, 512]), DC)
            mg = self.subres(b.sb(st, "mg", [128, 8, 512]), 8)
            xt = b.sb(st, "xt", [128, DC, 512])
            gt = [b.sb(st, "gt%d" % i, [128, 512]) for i in range(2)]
            wts = [b.sb(st, "w%d" % i, [128, 16, 128]) for i in range(4)]
            pss = [b.ps(st, "pg%d" % i, [128, 512]) for i in range(4)]
            gi = [0]
            for (t0, n, isctx) in tiles_tokens():
                b.dma(xt[:, :, 0:n], src[:, t0:t0 + n].rearrange("(c p) t -> p c t", p=128), writes=[xt])
                b.dma(mt[:, :, 0:n], self.mixT[:, t0:t0 + n].rearrange("(c p) t -> p c t", p=128), writes=mt.sub)

                def evac_glu(jc, ps, n=n):
                    g = gt[gi[0] % 2]
                    gi[0] += 1
                    b.op("act", lambda e: e.activation(out=g[:, 0:n], in_=ps[:, 0:n], func=AF.Sigmoid, bias=bg[:, jc:jc + 1]),
                         reads=[ps, bg], writes=[g])
                    b.op("dve", lambda e: e.tensor_tensor(out=rnd(mg[:, jc, 0:n]), in0=mt[:, jc, 0:n], in1=g[:, 0:n], op=ALU.mult),
                         reads=[g, mt.sub[jc]], writes=[mg.sub[jc]])

                self.gemm(st, self.w_glu[j], 8, 8, lambda kc: mt[:, kc, 0:n], lambda kc: [mt.sub[kc]], n, evac_glu, wts, pss, ksub=8)

                def evac(jc, ps, n=n, isctx=isctx):
                    b.op("dve", lambda e: e.scalar_tensor_tensor(
                        out=xt[:, jc, 0:n], in0=ps[:, 0:n], scalar=self.mod[:, l, 32 + jc, isctx:isctx + 1], in1=xt[:, jc, 0:n],
                        op0=ALU.mult, op1=ALU.add), reads=[ps, self.mod, xt], writes=[xt])

                self.gemm(st, self.w_out[j], DC, DC, lambda kc: (mg[:, kc, 0:n] if kc < 8 else mt[:, kc, 0:n]),
                          lambda kc: [mg.sub[kc] if kc < 8 else mt.sub[kc]], n, evac, wts, pss)
                b.dma(self.xs[:, t0:t0 + n].rearrange("(c p) t -> p c t", p=128), xt[:, :, 0:n], reads=[xt], q="pool")
            b.barrier()

    def s5(self, j):
        b = self.b
        TWO_PI = 2.0 * PI
        CH = 256
        NCHK = T // CH
        with contextlib.ExitStack() as st:
            def t4(name):
                return b.sb(st, name, [128, 2, 64, 1])
            LR, LI, STP, RHO, TH, THR, M_, SINT, COST, ABR, ABI, DEN, T2, AM1, CR, CI, CIS, CRS = [
                t4(n) for n in ("LR", "LI", "STP", "RHO", "TH", "THR", "M_", "SINT", "COST", "ABR", "ABI", "DEN", "T2", "AM1",
                                "CR", "CI", "CIS", "CRS")]
            halfpi = b.sb(st, "halfpi", [128, 1])
            sgn = b.sb(st, "sgn", [128, 2])
            gmask = b.sb(st, "gmask", [128, 8])
            psw = b.sb(st, "psw", [128, 128])
            s5d = b.sb(st, "s5d", [128, 8])
            b.op("dve", lambda e: e.memset(halfpi[:], PI / 2), writes=[halfpi])
            b.dma(sgn[:], self.s5sgn[:, :], writes=[sgn])
            b.dma(gmask[:], self.s5gmask[:, :], writes=[gmask])
            b.dma(psw[:], self.s5psw[:, :], writes=[psw])
            b.dma(s5d[:], self.s5dT[:, j, :], writes=[s5d])
            b.dma(LR[:], self.s5lam[:, 0, j, :, :].unsqueeze(3), writes=[LR])
            b.dma(LI[:], self.s5lam[:, 1, j, :, :].unsqueeze(3), writes=[LI])
            b.dma(STP[:], self.s5ls[:, j, :, :].unsqueeze(3), writes=[STP])

            def tt(out, a, c, op, eng="dve"):
                b.op(eng, lambda e: e.tensor_tensor(out=out[:], in0=a[:], in1=c[:], op=op), reads=[a, c], writes=[out])

            def act(out, a, func, **kw):
                extra = [kw["bias"].tile] if hasattr(kw.get("bias", None), "tile") else []
                b.op("act", lambda e: e.activation(out=out[:], in_=a[:], func=func, **kw), reads=[a], writes=[out])

            act(STP, STP, AF.Exp)
            tt(T2, LR, STP, ALU.mult)
            act(RHO, T2, AF.Exp)
            tt(TH, LI, STP, ALU.mult)
            b.op("dve", lambda e: e.tensor_copy(out=THR[:], in_=TH[:]), reads=[TH], writes=[THR])
            for k in range(4):
                thr = (2 * k + 1) * PI
                b.op("dve", lambda e, thr=thr: e.tensor_scalar(out=M_[:], in0=TH[:], scalar1=thr, scalar2=None, op0=ALU.is_ge),
                     reads=[TH], writes=[M_])
                b.op("dve", lambda e: e.scalar_tensor_tensor(out=THR[:], in0=M_[:], scalar=-TWO_PI, in1=THR[:], op0=ALU.mult, op1=ALU.add),
                     reads=[M_, THR], writes=[THR])
            act(SINT, THR, AF.Sin)
            act(T2, THR, AF.Abs)
            b.op("act", lambda e: e.activation(out=COST[:], in_=T2[:], func=AF.Sin, scale=-1.0, bias=halfpi[:, 0:1]),
                 reads=[T2, halfpi], writes=[COST])
            tt(ABR, RHO, COST, ALU.mult)
            tt(ABI, RHO, SINT, ALU.mult)
            tt(DEN, LR, LR, ALU.mult)
            tt(T2, LI, LI, ALU.mult)
            tt(DEN, DEN, T2, ALU.add)
            b.op("dve", lambda e: e.reciprocal(out=DEN[:], in_=DEN[:]), reads=[DEN], writes=[DEN])
            b.op("dve", lambda e: e.tensor_scalar_add(out=AM1[:], in0=ABR[:], scalar1=-1.0), reads=[ABR], writes=[AM1])
            tt(CR, AM1, LR, ALU.mult)
            tt(T2, ABI, LI, ALU.mult)
            tt(CR, CR, T2, ALU.add)
            tt(CR, CR, DEN, ALU.mult)
            tt(CI, ABI, LR, ALU.mult)
            tt(T2, AM1, LI, ALU.mult)
            tt(CI, CI, T2, ALU.subtract)
            tt(CI, CI, DEN, ALU.mult)
            b.op("dve", lambda e: e.tensor_scalar(out=CIS[:], in0=CI[:], scalar1=sgn[:, 0:1], scalar2=None, op0=ALU.mult), reads=[CI, sgn], writes=[CIS])
            b.op("dve", lambda e: e.tensor_scalar(out=CRS[:], in0=CR[:], scalar1=sgn[:, 1:2], scalar2=None, op0=ALU.mult), reads=[CR, sgn], writes=[CRS])
            BTA = b.sb(st, "BTA", [128, 2, 8, 128])
            BTB = b.sb(st, "BTB", [128, 2, 8, 128])
            CP = b.sb(st, "CP", [128, 2, 64, 16])
            CPADS = [b.sb(st, "CPAD%d" % d, [128, 8, 128]) for d in range(2)]
            for d in range(2):
                b.op("pool", lambda e, d=d: e.memset(CPADS[d][:], 0.0), writes=[CPADS[d]])
            b.dma(CP[:], self.s5CX[:, j, :, :, :], writes=[CP])
            b.op("dve", lambda e: e.tensor_scalar(out=CP[:], in0=CP[:], scalar1=sgn[:, 1:2], scalar2=None, op0=ALU.mult), reads=[CP, sgn], writes=[CP])
            with contextlib.ExitStack() as s2:
                BX = b.sb(s2, "BX", [128, 2, 64, 16])
                BY = b.sb(s2, "BY", [128, 2, 64, 16])
                SA = b.sb(s2, "SA", [128, 2, 64, 16])
                SB = b.sb(s2, "SB", [128, 2, 64, 16])
                TM = b.sb(s2, "TM", [128, 2, 64, 16])
                ptr = b.ps(s2, "ptr5", [128, 512])
                b.dma(BX[:], self.s5BX[:, j, :, :, :], writes=[BX])
                b.dma(BY[:], self.s5BY[:, j, :, :, :], writes=[BY])
                shp = [128, 2, 64, 16]

                def ttb(out, a, col, op):
                    b.op("dve", lambda e: e.tensor_tensor(out=out[:], in0=a[:], in1=col[:].to_broadcast(shp), op=op), reads=[a, col], writes=[out])
                ttb(SA, BX, CR, ALU.mult)
                ttb(TM, BY, CIS, ALU.mult)
                tt(SA, SA, TM, ALU.add)
                ttb(SB, BY, CRS, ALU.mult)
                ttb(TM, BX, CI, ALU.mult)
                tt(SB, SB, TM, ALU.add)
                for (S_, BT_) in ((SA, BTA), (SB, BTB)):
                    for d in range(2):
                        for half in range(2):
                            for k in range(4):
                                fc = half * 4 + k
                                b.op("pe", lambda e, S_=S_, d=d, fc=fc, k=k: e.transpose(
                                    ptr[:, k * 128:(k + 1) * 128], S_[:, d, fc * 8:(fc + 1) * 8, :].rearrange("p g c -> p (g c)"),
                                    self.ident[:, :]), reads=[S_, self.ident], writes=[ptr], pe_acc=True)
                            b.op("act", lambda e, BT_=BT_, d=d, half=half: e.copy(
                                out=BT_[:, d, half * 4:(half + 1) * 4, :].rearrange("p a x -> p (a x)"), in_=ptr[:, :]),
                                reads=[ptr], writes=[BT_])
                b.barrier()
            UT = b.sb(st, "UT", [128, T])
            Y = b.sb(st, "Y", [128, T])
            ER = b.sb(st, "ER", [128, 8, CH])
            EI = b.sb(st, "EI", [128, 8, CH])
            T1 = b.sb(st, "T1", [128, 8, CH // 2])
            T2b = b.sb(st, "T2b", [128, 8, CH // 2])
            RHOT = b.sb(st, "RHOT", [128, 8, CH])
            BPAD = b.sb(st, "BPAD", [128, 2, 8, 128])
            XCAR = b.sb(st, "XCAR", [128, 8])
            BTl = [b.sb(st, "BTl%d" % i, [128, CH]) for i in range(4)]
            TTl = [b.sb(st, "TTl%d" % i, [128, CH]) for i in range(4)]
            Gl = [b.sb(st, "Gl%d" % i, [128, CH]) for i in range(4)]
            Xl = [b.sb(st, "Xl%d" % i, [128, CH]) for i in range(4)]
            pbu = [b.ps(st, "pbu%d" % i, [128, 2, CH]) for i in range(4)]
            psw_ps = [b.ps(st, "psw%d" % i, [128, 512]) for i in range(2)]
            pyy = [b.ps(st, "pyy%d" % i, [128, 512]) for i in range(2)]
            GE = b.sb(st, "GE", [128, T])
            order = [list(range(NCHK)), [0] + list(range(NCHK - 1, 0, -1))]
            it = 0
            yi = 0
            for fc in range(8):
                b.dma(UT[:], self.uT[fc * 128:(fc + 1) * 128, :], writes=[UT])
                for d in range(2):
                    g0 = fc * 8
                    b.op("dve", lambda e, d=d, g0=g0: e.tensor_copy(out=ER[:, :, 0:1], in_=COST[:, d, g0:g0 + 8, :]), reads=[COST], writes=[ER])
                    b.op("dve", lambda e, d=d, g0=g0: e.tensor_copy(out=EI[:, :, 0:1], in_=SINT[:, d, g0:g0 + 8, :]), reads=[SINT], writes=[EI])
                    n = 1
                    while n < CH:
                        bs = [128, 8, n]
                        b.op("dve", lambda e, n=n, bs=bs: e.tensor_tensor(out=T1[:, :, 0:n], in0=ER[:, :, 0:n], in1=ER[:, :, n - 1:n].to_broadcast(bs), op=ALU.mult),
                             reads=[ER], writes=[T1])
                        b.op("pool", lambda e, n=n, bs=bs: e.tensor_tensor(out=T2b[:, :, 0:n], in0=EI[:, :, 0:n], in1=EI[:, :, n - 1:n].to_broadcast(bs), op=ALU.mult),
                             reads=[EI], writes=[T2b])
                        b.op("dve", lambda e, n=n: e.tensor_tensor(out=ER[:, :, n:2 * n], in0=T1[:, :, 0:n], in1=T2b[:, :, 0:n], op=ALU.subtract),
                             reads=[T1, T2b, EI], writes=[ER])
                        b.op("dve", lambda e, n=n, bs=bs: e.tensor_tensor(out=T1[:, :, 0:n], in0=ER[:, :, 0:n], in1=EI[:, :, n - 1:n].to_broadcast(bs), op=ALU.mult),
                             reads=[ER, EI], writes=[T1])
                        b.op("pool", lambda e, n=n, bs=bs: e.tensor_tensor(out=T2b[:, :, 0:n], in0=EI[:, :, 0:n], in1=ER[:, :, n - 1:n].to_broadcast(bs), op=ALU.mult),
                             reads=[EI, ER], writes=[T2b])
                        b.op("dve", lambda e, n=n: e.tensor_tensor(out=EI[:, :, n:2 * n], in0=T1[:, :, 0:n], in1=T2b[:, :, 0:n], op=ALU.add),
                             reads=[T1, T2b, ER], writes=[EI])
                        n *= 2
                    b.op("dve", lambda e, d=d, g0=g0: e.tensor_copy(out=RHOT[:], in_=RHO[:, d, g0:g0 + 8, :].to_broadcast([128, 8, CH])),
                         reads=[RHO], writes=[RHOT])
                    for gp in range(8):
                        b.op("pool", lambda e, gp=gp, d=d, fc=fc: e.tensor_scalar(out=BPAD[:, 0, gp, :], in0=BTA[:, d, fc, :], scalar1=gmask[:, gp:gp + 1],
                                                                                  scalar2=None, op0=ALU.mult), reads=[BTA, gmask], writes=[BPAD])
                        b.op("pool", lambda e, gp=gp, d=d, fc=fc: e.tensor_scalar(out=BPAD[:, 1, gp, :], in0=BTB[:, d, fc, :], scalar1=gmask[:, gp:gp + 1],
                                                                                  scalar2=None, op0=ALU.mult), reads=[BTB, gmask], writes=[BPAD])
                        b.op("dve", lambda e, gp=gp, d=d, g0=g0: e.tensor_copy(out=CPADS[d][:, gp, gp * 16:(gp + 1) * 16], in_=CP[:, d, g0 + gp, :]),
                             reads=[CP], writes=[CPADS[d]])
                    b.op("dve", lambda e: e.memset(XCAR[:], 0.0), writes=[XCAR])
                    for ck in order[d]:
                        c0 = ck * CH
                        py = pyy[yi % 2]
                        yi += 1
                        if d == 0:
                            rv = lambda ap: ap
                        else:
                            rv = lambda ap: ap[:, ::-1]
                        last = CH - 1 if d == 0 else 0

                        def unit(gp, c0=c0, py=py, rv=rv, last=last, d=d):
                            k4 = gp % 4
                            BT_, TT_, G_, X_, pb = BTl[k4], TTl[k4], Gl[k4], Xl[k4], pbu[k4]
                            pwt = psw_ps[k4 // 2]
                            pw = pwt[:, (k4 % 2) * CH:(k4 % 2 + 1) * CH]
                            if d == 0:
                                cosv, sinv = ER[:, gp, :], EI[:, gp, :]
                            else:
                                cosv, sinv = ER[:, gp, ::-1], EI[:, gp, ::-1]
                            for ab in range(2):
                                b.op("pe", lambda e, ab=ab: e.matmul(pb[:, ab, :], BPAD[:, ab, gp, :], UT[:, c0:c0 + CH], start=True, stop=True),
                                     reads=[BPAD, UT], writes=[pb], pe_acc=True)
                            yield
                            b.op("dve", lambda e: e.tensor_tensor(out=BT_[:], in0=pb[:, 0, :], in1=cosv, op=ALU.mult), reads=[pb, ER], writes=[BT_])
                            b.op("dve", lambda e: e.tensor_tensor(out=TT_[:], in0=pb[:, 1, :], in1=sinv, op=ALU.mult), reads=[pb, EI], writes=[TT_])
                            yield
                            b.op("pool", lambda e: e.tensor_tensor(out=BT_[:], in0=BT_[:], in1=TT_[:], op=ALU.add), reads=[BT_, TT_], writes=[BT_])
                            yield
                            b.op("dve", lambda e: e.tensor_tensor_scan(rv(G_[:]), RHOT[:, gp, :], rv(BT_[:]), XCAR[:, gp:gp + 1], ALU.mult, ALU.add),
                                 reads=[RHOT, BT_, XCAR], writes=[G_])
                            yield
                            b.op("pe", lambda e: e.matmul(pw, psw[:, :], G_[:], start=True, stop=True), reads=[psw, G_], writes=[pwt], pe_acc=True)
                            b.op("pool", lambda e: e.tensor_tensor(out=X_[:], in0=G_[:], in1=cosv, op=ALU.mult), reads=[G_, ER], writes=[X_])
                            yield
                            b.op("dve", lambda e: e.tensor_tensor(out=TT_[:], in0=pw, in1=sinv, op=ALU.mult), reads=[pwt, EI], writes=[TT_])
                            yield
                            b.op("pool", lambda e: e.tensor_tensor(out=X_[:], in0=X_[:], in1=TT_[:], op=ALU.subtract), reads=[X_, TT_], writes=[X_])
                            yield
                            b.op("act", lambda e: e.copy(out=XCAR[:, gp:gp + 1], in_=X_[:, last:last + 1]), reads=[X_], writes=[XCAR])
                            b.op("pe", lambda e: e.matmul(py[:, 0:CH], CPADS[d][:, gp, :], X_[:], start=(gp == 0), stop=(gp == 7)),
                                 reads=[CPADS[d], X_], writes=[py], pe_acc=True)
                            yield

                        for half in range(2):
                            gens = [unit(half * 4 + q) for q in range(4)]
                            while gens:
                                for g_ in list(gens):
                                    try:
                                        next(g_)
                                    except StopIteration:
                                        gens.remove(g_)
                        if d == 0:
                            b.op("act", lambda e, py=py, c0=c0: e.copy(out=Y[:, c0:c0 + CH], in_=py[:, 0:CH]), reads=[py], writes=[Y])
                        else:
                            b.op("dve", lambda e, py=py, c0=c0: e.tensor_tensor(out=Y[:, c0:c0 + CH], in0=py[:, 0:CH], in1=Y[:, c0:c0 + CH], op=ALU.add),
                                 reads=[py, Y], writes=[Y])
                b.op("dve", lambda e, fc=fc: e.scalar_tensor_tensor(out=Y[:], in0=UT[:], scalar=s5d[:, fc:fc + 1], in1=Y[:], op0=ALU.mult, op1=ALU.add),
                     reads=[UT, s5d, Y], writes=[Y])
                b.op("pool", lambda e: e.tensor_tensor(out=GE[:], in0=Y[:], in1=Y[:], op=ALU.mult), reads=[Y], writes=[GE])
                b.op("dve", lambda e: e.tensor_scalar(out=GE[:], in0=GE[:], scalar1=0.044715, scalar2=1.0, op0=ALU.mult, op1=ALU.add),
                     reads=[GE], writes=[GE])
                b.op("pool", lambda e: e.tensor_tensor(out=GE[:], in0=GE[:], in1=Y[:], op=ALU.mult), reads=[GE, Y], writes=[GE])
                b.op("act", lambda e: e.activation(out=GE[:], in_=GE[:], func=AF.Tanh, scale=0.7978845608028654), reads=[GE], writes=[GE])
                b.op("dve", lambda e: e.scalar_tensor_tensor(out=GE[:], in0=GE[:], scalar=1.0, in1=Y[:], op0=ALU.add, op1=ALU.mult),
                     reads=[GE, Y], writes=[GE])
                b.op("pool", lambda e: e.tensor_scalar(out=GE[:], in0=GE[:], scalar1=0.5, scalar2=None, op0=ALU.mult), reads=[GE], writes=[GE])
                b.dma(self.mixT[fc * 128:(fc + 1) * 128, :], GE[:], reads=[GE], q="pool")
            b.barrier()

    def odd_norm(self, l, src):
        b = self.b
        with contextlib.ExitStack() as st:
            self.epsb = b.sb(st, "epsb", [128, 1])
            b.op("dve", lambda e: e.memset(self.epsb[:], EPS), writes=[self.epsb])
            xt = b.sb(st, "xt", [128, DC, 512])
            ht = self.subres(b.sb(st, "ht", [128, DC, 512]), DC)
            tmp = self.subres(b.sb(st, "tmp", [128, 2, 512]), 2)
            rstd = b.sb(st, "rstd", [128, 512])
            psn = b.ps(st, "psn", [128, 512])
            for (t0, n, isctx) in tiles_tokens():
                b.dma(xt[:, :, 0:n], src[:, t0:t0 + n].rearrange("(c p) t -> p c t", p=128), writes=[xt])
                self.normmod(st, xt, ht, tmp, psn, rstd, n,
                             lambda c: self.scl[:, l, 0, c, isctx:isctx + 1],
                             lambda c: self.mod[:, l, c, isctx:isctx + 1], res_extra=[self.scl, self.mod])
                b.dma(self.hT[:, t0:t0 + n].rearrange("(c p) t -> p c t", p=128), ht[:, :, 0:n], reads=ht.sub, q="pool")
            b.barrier()

    def odd_proj(self, l, j):
        b = self.b
        self.psi = 0
        self.wi = 0
        N = 256
        with contextlib.ExitStack() as st:
            mu = b.sb(st, "mu", [128, 6, 16])
            kkw = b.sb(st, "kkw", [128, 16])
            kaw = b.sb(st, "kaw", [128, 16])
            oma = b.sb(st, "oma", [128, 16])
            nw0 = b.sb(st, "nw0", [128, 2, 16])
            a0 = b.sb(st, "a0", [128, 2, 16])
            v0 = b.sb(st, "v0", [128, 16])
            blk = b.sb(st, "blk", [128, 128])
            mhalf = b.sb(st, "mhalf", [128, 1])
            tiny = b.sb(st, "tiny", [128, 1])
            b.dma(mu[:], self.muT[:, j, :, :], writes=[mu])
            b.dma(kkw[:], self.kkwT[:, j, :], writes=[kkw])
            b.dma(kaw[:], self.kawT[:, j, :], writes=[kaw])
            b.dma(nw0[:], self.w0T[:, j, :, :], writes=[nw0])
            b.dma(a0[:], self.a0T[:, j, :, :], writes=[a0])
            b.dma(blk[:], self.blk64[:, :], writes=[blk])
            if j > 0:
                b.dma(v0[:], self.v0T[:, j - 1, :], writes=[v0])
            b.op("dve", lambda e: e.tensor_scalar_mul(out=nw0[:], in0=nw0[:], scalar1=-1.0), reads=[nw0], writes=[nw0])
            b.op("dve", lambda e: e.tensor_scalar(out=oma[:], in0=kaw[:], scalar1=-1.0, scalar2=1.0, op0=ALU.mult, op1=ALU.add), reads=[kaw], writes=[oma])
            b.op("dve", lambda e: e.memset(mhalf[:], -0.5), writes=[mhalf])
            b.op("dve", lambda e: e.memset(tiny[:], 0.0), writes=[tiny])
            H = b.sb(st, "H", [128, DC, N])
            DX = b.sb(st, "DX", [128, DC, N])
            X = self.subres(b.sb(st, "X", [128, DC, N]), DC)
            Kt = self.subres(b.sb(st, "Kt", [128, DC, N]), DC)
            KK = self.subres(b.sb(st, "KK", [128, DC, N]), DC)
            wts = [b.sb(st, "w%d" % i, [128, 16, 128]) for i in range(3)]
            pss = [b.ps(st, "pg%d" % i, [128, 512]) for i in range(3)]
            obs = [b.sb(st, "ob%d" % i, [128, N]) for i in range(4)]
            sq = [b.sb(st, "sq%d" % i, [128, N]) for i in range(2)]
            pl = [b.ps(st, "pl%d" % i, [128, 512]) for i in range(2)]
            pq = [b.ps(st, "pq%d" % i, [128, 512]) for i in range(2)]
            L1 = [b.sb(st, "L1%d" % i, [128, 2, N]) for i in range(2)]
            w1t = [b.sb(st, "w1t%d" % i, [128, 16, 256]) for i in range(1)]
            w2t = [b.sb(st, "w2t%d" % i, [128, 2, 2048]) for i in range(2)]
            vft = [b.sb(st, "vft%d" % i, [128, N]) for i in range(2)]
            cnt = {"o": 0, "s": 0, "l": 0, "w": 0, "q": 0, "v": 0}

            def nxt(lst, key):
                t_ = lst[cnt[key] % len(lst)]
                cnt[key] += 1
                return t_

            def store(dst, rows0, t0, ob, n):
                b.dma(dst[rows0:rows0 + 128, t0:t0 + n], ob[:, 0:n], reads=[ob], q="pool")

            tiles = [(0, TC, 1)] + [(TC + i * N, N, 0) for i in range(TL // N)]
            import os as _os
            SUB = int(_os.environ.get("ODD_SUB", "99"))
            tiles = tiles[:int(_os.environ.get("ODD_TILES", "99"))]
            for (t0, n, isctx) in tiles:
                b.dma(H[:], self.hT[:, t0:t0 + n].rearrange("(c p) t -> p c t", p=128), writes=[H])
                hv = lambda c0, c1, a, e_: self.hT[c0 * 128:c1 * 128, a:e_].rearrange("(c p) t -> p c t", p=128)
                if isctx:
                    b.op("pool", lambda e: e.memset(DX[:, 0:8, 0:1], 0.0), writes=[DX])
                    b.op("pool", lambda e: e.memset(DX[:, 8:16, n - 1:n], 0.0), writes=[DX])
                    b.dma(DX[:, 0:8, 1:n], hv(0, 8, 0, n - 1), writes=[DX])
                    b.dma(DX[:, 8:16, 0:n - 1], hv(8, 16, 1, n), writes=[DX])
                else:
                    first = (t0 == TC)
                    lastt = (t0 + n == T)
                    b.dma(DX[:, 0:4, :], hv(0, 4, t0 - 1, t0 + n - 1), writes=[DX])
                    if lastt:
                        b.dma(DX[:, 4:8, 0:n - 1], hv(4, 8, t0 + 1, t0 + n), writes=[DX])
                    else:
                        b.dma(DX[:, 4:8, :], hv(4, 8, t0 + 1, t0 + n + 1), writes=[DX])
                    b.dma(DX[:, 8:12, :], hv(8, 12, t0 - 64, t0 + n - 64), writes=[DX])
                    if lastt:
                        b.dma(DX[:, 12:16, 0:n - 64], hv(12, 16, t0 + 64, t0 + n), writes=[DX])
                        b.op("pool", lambda e: e.memset(DX[:, 12:16, n - 64:n], 0.0), writes=[DX])
                    else:
                        b.dma(DX[:, 12:16, :], hv(12, 16, t0 + 64, t0 + n + 64), writes=[DX])
                    if first:
                        b.op("pool", lambda e: e.memset(DX[:, 8:12, 0:64], 0.0), writes=[DX])
                    for c in range(4):
                        b.op("pool", lambda e, c=c: e.memset(DX[:, c, :].rearrange("p (r w) -> p r w", w=64)[:, :, 0:1], 0.0), writes=[DX])
                        b.op("pool", lambda e, c=c: e.memset(DX[:, 4 + c, :].rearrange("p (r w) -> p r w", w=64)[:, :, 63:64], 0.0), writes=[DX])
                b.op("dve", lambda e: e.tensor_tensor(out=DX[:], in0=DX[:], in1=H[:], op=ALU.subtract), reads=[DX, H], writes=[DX])

                def mix(i):
                    for c in range(DC):
                        b.op("dve", lambda e, c=c: e.scalar_tensor_tensor(
                            out=rnd(X[:, c, 0:n]), in0=DX[:, c, 0:n], scalar=mu[:, i, c:c + 1], in1=H[:, c, 0:n], op0=ALU.mult, op1=ALU.add),
                            reads=[DX, mu, H], writes=[X.sub[c]])

                xr = lambda kc: X[:, kc, 0:n]
                xres = lambda kc: [X.sub[kc]]

                def lora1(W1, r_, func, li_):
                    wt = nxt(w1t, "w")
                    b.dma(wt[:, :, 0:r_], W1.rearrange("(kc p) r -> p kc r", p=128), writes=[wt])
                    Lt = L1[li_]
                    for m0 in range(0, r_, 128):
                        mm = min(128, r_ - m0)
                        ps = nxt(pl, "l")
                        for kc in range(DC):
                            b.op("pe", lambda e, kc=kc, ps=ps, wt=wt, m0=m0, mm=mm: e.matmul(ps[0:mm, 0:n], wt[:, kc, m0:m0 + mm], X[:, kc, 0:n],
                                                                                       start=(kc == 0), stop=(kc == DC - 1)),
                                 reads=[wt, X.sub[kc]], writes=[ps], pe_acc=True)
                        b.op("act", lambda e, ps=ps, Lt=Lt, m0=m0, mm=mm: e.activation(out=Lt[0:mm, m0 // 128, 0:n], in_=ps[0:mm, 0:n], func=func),
                             reads=[ps], writes=[Lt])
                    return Lt

                def lora2_load(W2, r_):
                    wt = nxt(w2t, "q")
                    for m0 in range(0, r_, 128):
                        mm = min(128, r_ - m0)
                        b.dma(wt[0:mm, m0 // 128, :], W2[m0:m0 + mm, :], writes=[wt])
                    return wt

                def lora2(ps, wt, Lt, r_, jc):
                    nk = (r_ + 127) // 128
                    for ki in range(nk):
                        mm = min(128, r_ - ki * 128)
                        b.op("pe", lambda e, ki=ki, mm=mm: e.matmul(ps[:, 0:n], wt[0:mm, ki, jc * 128:(jc + 1) * 128], Lt[0:mm, ki, 0:n],
                                                                      start=(ki == 0), stop=(ki == nk - 1)),
                             reads=[wt, Lt], writes=[ps], pe_acc=True)

                mix(0)

                def ev_r(jc, ps):
                    ob = nxt(obs, "o")
                    b.op("act", lambda e: e.copy(out=ob[:, 0:n], in_=ps[:, 0:n]), reads=[ps], writes=[ob])
                    store(self.rT, jc * 128, t0, ob, n)
                self.gemm(st, self.w_r[j], DC, DC, xr, xres, n, ev_r, wts, pss)
                if SUB < 2:
                    continue
                mix(2)

                def ev_k(jc, ps):
                    b.op("act", lambda e: e.copy(out=Kt[:, jc, 0:n], in_=ps[:, 0:n]), reads=[ps], writes=[Kt.sub[jc]])

                def kk_post():
                    for jc in range(DC):
                        b.op("dve", lambda e, jc=jc: e.tensor_scalar(out=KK[:, jc, 0:n], in0=Kt[:, jc, 0:n], scalar1=kkw[:, jc:jc + 1], scalar2=None, op0=ALU.mult),
                             reads=[Kt.sub[jc], kkw], writes=[KK.sub[jc]])
                        sq_ = nxt(sq, "s")
                        b.op("act", lambda e, jc=jc, sq_=sq_: e.activation(out=sq_[:, 0:n], in_=KK[:, jc, 0:n], func=AF.Square), reads=[KK.sub[jc]], writes=[sq_])
                        pq_ = nxt(pq, "v")
                        b.op("pe", lambda e, sq_=sq_, pq_=pq_: e.matmul(pq_[:, 0:n], blk[:, :], sq_[:, 0:n], start=True, stop=True), reads=[blk, sq_], writes=[pq_], pe_acc=True)
                        b.op("dve", lambda e, sq_=sq_, pq_=pq_: e.tensor_scalar_max(out=sq_[:, 0:n], in0=pq_[:, 0:n], scalar1=1e-24), reads=[pq_], writes=[sq_])
                        b.op("act", lambda e, sq_=sq_: e.activation(out=sq_[:, 0:n], in_=sq_[:, 0:n], func=AF.Sqrt), reads=[sq_], writes=[sq_])
                        b.op("dve", lambda e, sq_=sq_: e.reciprocal(out=sq_[:, 0:n], in_=sq_[:, 0:n]), reads=[sq_], writes=[sq_])
                        b.op("dve", lambda e, jc=jc, sq_=sq_: e.tensor_tensor(out=KK[:, jc, 0:n], in0=KK[:, jc, 0:n], in1=sq_[:, 0:n], op=ALU.mult),
                             reads=[KK.sub[jc], sq_], writes=[KK.sub[jc]])
                        b.dma(self.kkT[jc * 128:(jc + 1) * 128, t0:t0 + n], KK[:, jc, 0:n], reads=[KK.sub[jc]], q="pool")
                self.gemm(st, self.w_k[j], DC, DC, xr, xres, n, ev_k, wts, pss)
                kk_post()
                if SUB < 3:
                    continue
                mix(3)
                if j > 0:
                    Lv = lora1(self.v1[j - 1], 64, AF.Copy, 0)
                    wv2 = lora2_load(self.v2[j - 1], 64)

                def ev_v(jc, ps):
                    ob = nxt(obs, "o")
                    if j == 0:
                        b.op("act", lambda e: e.copy(out=ob[:, 0:n], in_=ps[:, 0:n]), reads=[ps], writes=[ob])
                        store(self.vfT, jc * 128, t0, ob, n)
                    else:
                        pq_ = nxt(pq, "v")
                        lora2(pq_, wv2, Lv, 64, jc)
                        sg_ = nxt(sq, "s")
                        vf_ = nxt(vft, "v")
                        b.dma(vf_[:, 0:n], self.vfT[jc * 128:(jc + 1) * 128, t0:t0 + n], writes=[vf_])
                        b.op("act", lambda e: e.activation(out=sg_[:, 0:n], in_=pq_[:, 0:n], func=AF.Sigmoid, bias=v0[:, jc:jc + 1]),
                             reads=[pq_, v0], writes=[sg_])
                        b.op("dve", lambda e: e.tensor_tensor(out=vf_[:, 0:n], in0=vf_[:, 0:n], in1=ps[:, 0:n], op=ALU.subtract), reads=[vf_, ps], writes=[vf_])
                        b.op("pool", lambda e: e.tensor_tensor(out=vf_[:, 0:n], in0=vf_[:, 0:n], in1=sg_[:, 0:n], op=ALU.mult), reads=[vf_, sg_], writes=[vf_])
                        b.op("dve", lambda e: e.tensor_tensor(out=ob[:, 0:n], in0=vf_[:, 0:n], in1=ps[:, 0:n], op=ALU.add), reads=[vf_, ps], writes=[ob])
                        store(self.vT, jc * 128, t0, ob, n)
                self.gemm(st, self.w_v[j], DC, DC, xr, xres, n, ev_v, wts, pss)
                if SUB < 4:
                    continue
                mix(1)
                for d in range(2):
                    Lw = lora1(self.w1[j, d], 96, AF.Tanh, d)
                    ww2 = lora2_load(self.w2[j, d], 96)
                    for jc in range(DC):
                        pq_ = nxt(pq, "v")
                        lora2(pq_, ww2, Lw, 96, jc)
                        ob = nxt(obs, "o")
                        b.op("act", lambda e: e.activation(out=ob[:, 0:n], in_=pq_[:, 0:n], func=AF.Exp, scale=-1.0, bias=nw0[:, d, jc:jc + 1]),
                             reads=[pq_, nw0], writes=[ob])
                        b.op("dve", lambda e: e.tensor_scalar_add(out=ob[:, 0:n], in0=ob[:, 0:n], scalar1=1.0), reads=[ob], writes=[ob])
                        b.op("act", lambda e: e.activation(out=ob[:, 0:n], in_=ob[:, 0:n], func=AF.Ln), reads=[ob], writes=[ob])
                        b.op("act", lambda e: e.activation(out=ob[:, 0:n], in_=ob[:, 0:n], func=AF.Exp, scale=-1.0, bias=mhalf[:, 0:1]),
                             reads=[ob, mhalf], writes=[ob])
                        b.op("dve", lambda e: e.tensor_scalar_mul(out=ob[:, 0:n], in0=ob[:, 0:n], scalar1=-1.0), reads=[ob], writes=[ob])
                        store(self.lwT[d], jc * 128, t0, ob, n)
                if SUB < 5:
                    continue
                mix(4)
                for d in range(2):
                    La = lora1(self.a1[j, d], 96, AF.Copy, d)
                    wa2 = lora2_load(self.a2[j, d], 96)
                    for jc in range(DC):
                        pq_ = nxt(pq, "v")
                        lora2(pq_, wa2, La, 96, jc)
                        sa_ = nxt(sq, "s")
                        b.op("act", lambda e: e.activation(out=sa_[:, 0:n], in_=pq_[:, 0:n], func=AF.Sigmoid, bias=a0[:, d, jc:jc + 1]),
                             reads=[pq_, a0], writes=[sa_])
                        ob = nxt(obs, "o")
                        b.op("dve", lambda e: e.tensor_tensor(out=ob[:, 0:n], in0=KK[:, jc, 0:n], in1=sa_[:, 0:n], op=ALU.mult),
                             reads=[KK.sub[jc], sa_], writes=[ob])
                        store(self.bdT[d], jc * 128, t0, ob, n)
                        ob2 = nxt(obs, "o")
                        b.op("dve", lambda e: e.tensor_scalar(out=ob2[:, 0:n], in0=sa_[:, 0:n], scalar1=kaw[:, jc:jc + 1], scalar2=oma[:, jc:jc + 1],
                                                              op0=ALU.mult, op1=ALU.add), reads=[sa_, kaw, oma], writes=[ob2])
                        b.op("pool", lambda e: e.tensor_tensor(out=ob2[:, 0:n], in0=ob2[:, 0:n], in1=Kt[:, jc, 0:n], op=ALU.mult),
                             reads=[ob2, Kt.sub[jc]], writes=[ob2])
                        store(self.kdT[d], jc * 128, t0, ob2, n)
                if SUB < 6:
                    continue
                mix(5)
                Lg = lora1(self.g1[j], 256, AF.Sigmoid, 0)
                wg2 = lora2_load(self.g2[j], 256)
                for jc in range(DC):
                    pq_ = nxt(pq, "v")
                    lora2(pq_, wg2, Lg, 256, jc)
                    ob = nxt(obs, "o")
                    b.op("act", lambda e: e.copy(out=ob[:, 0:n], in_=pq_[:, 0:n]), reads=[pq_], writes=[ob])
                    store(self.gT, jc * 128, t0, ob, n)
            b.barrier()

    def rwkv_scan(self, j):
        b = self.b
        L = 128
        NCH = T // L
        vsrc = self.vfT if j == 0 else self.vT
        with contextlib.ExitStack() as st:
            onesL = b.sb(st, "onesL", [128, L])
            b.op("dve", lambda e: e.memset(onesL[:], 1.0), writes=[onesL])
            MK = [b.sb(st, "MK%d" % d, [128, 2, 2 * L]) for d in range(2)]
            MN = [b.sb(st, "MN%d" % d, [128, 2, L]) for d in range(2)]
            for hh in range(2):
                b.op("dve", lambda e, hh=hh: e.tensor_copy(out=MK[0][:, hh, 0:L], in_=self.masks[:, 2, :]), reads=[self.masks], writes=[MK[0]])
                b.op("dve", lambda e, hh=hh: e.tensor_copy(out=MK[0][:, hh, L:2 * L], in_=self.masks[:, 0, :]), reads=[self.masks], writes=[MK[0]])
                b.op("dve", lambda e, hh=hh: e.tensor_copy(out=MK[1][:, hh, 0:L], in_=self.masks[:, 3, :]), reads=[self.masks], writes=[MK[1]])
                b.op("dve", lambda e, hh=hh: e.tensor_copy(out=MK[1][:, hh, L:2 * L], in_=self.masks[:, 1, :]), reads=[self.masks], writes=[MK[1]])
                b.op("dve", lambda e, hh=hh: e.tensor_copy(out=MN[0][:, hh, :], in_=self.masks[:, 3, :]), reads=[self.masks], writes=[MN[0]])
                b.op("dve", lambda e, hh=hh: e.tensor_copy(out=MN[1][:, hh, :], in_=self.masks[:, 2, :]), reads=[self.masks], writes=[MN[1]])
            ST_ = [b.sb(st, "ST%d" % d, [128, 64]) for d in range(2)]

            def per_d(name, shape):
                return [b.sb(st, "%s%d" % (name, i), shape) for i in range(2)]
            Rt, LWt, KDt, BDt, KKt, Vt = [per_d(nm, [128, L]) for nm in ("Rt", "LWt", "KDt", "BDt", "KKt", "Vt")]
            CS, EP, EM, EA = [per_d(nm, [128, L]) for nm in ("CS", "EP", "EM", "EA")]
            ART = per_d("ART", [128, 2 * L])
            KH, BH = per_d("KH", [128, L]), per_d("BH", [128, L])
            VT, KHT, BHT = per_d("VT", [128, L]), per_d("KHT", [128, L]), per_d("BHT", [128, L])
            AKR, NRB = per_d("AKR", [128, 2, 2 * L]), per_d("NRB", [128, 2, 2 * L])
            PPa, PPb = per_d("PPa", [128, 2, 2 * L]), per_d("PPb", [128, 2, 2 * L])
            XXa, XXb = per_d("XXa", [128, 2, 64]), per_d("XXb", [128, 2, 64])
            YO = per_d("YO", [128, L])
            BA = [b.ps(st, "bA%d" % d, [128, 512]) for d in range(2)]
            BB = [b.ps(st, "bB%d" % d, [128, 2, 2 * L]) for d in range(2)]
            BC = [b.ps(st, "bC%d" % d, [128, 512]) for d in range(2)]
            BD = [b.ps(st, "bD%d" % d, [128, 512]) for d in range(2)]
            order = [list(range(NCH)), [1, 0] + list(range(NCH - 1, 1, -1))]

            def stream(fc, d):
                rows = slice(fc * 128, (fc + 1) * 128)
                S_ = ST_[d]
                bA, bB, bC, bD = BA[d], BB[d], BC[d], BD[d]
                b.op("pool", lambda e: e.memset(S_[:], 0.0), writes=[S_])
                rv = (lambda ap: ap) if d == 0 else (lambda ap: ap[:, ::-1])
                lastc = L - 1 if d == 0 else 0
                R_, LW_, KD_, BD_, KK_, V_ = Rt[d], LWt[d], KDt[d], BDt[d], KKt[d], Vt[d]
                cs, ep, em, ea, art, kh, bh = CS[d], EP[d], EM[d], EA[d], ART[d], KH[d], BH[d]
                vt, kht, bht = VT[d], KHT[d], BHT[d]
                akr, nrb = AKR[d], NRB[d]
                pxs = [(bA, 384), (bD, 256)]
                for c in order[d]:
                    t0 = c * L
                    for (tl_, src_) in ((R_, self.rT), (LW_, self.lwT[d]), (KD_, self.kdT[d]), (BD_, self.bdT[d]), (KK_, self.kkT), (V_, vsrc)):
                        b.dma(tl_[:], src_[rows, t0:t0 + L], writes=[tl_])
                    yield
                    b.op("dve", lambda e: e.tensor_tensor_scan(rv(cs[:]), onesL[:], rv(LW_[:]), 0.0, ALU.mult, ALU.add),
                         reads=[onesL, LW_], writes=[cs])
                    b.op("act", lambda e: e.activation(out=ep[:], in_=cs[:], func=AF.Exp), reads=[cs], writes=[ep])
                    b.op("act", lambda e: e.activation(out=em[:], in_=cs[:], func=AF.Exp, scale=-1.0), reads=[cs], writes=[em])
                    b.op("pool", lambda e: e.tensor_tensor(out=ea[:], in0=cs[:], in1=LW_[:], op=ALU.subtract), reads=[cs, LW_], writes=[ea])
                    b.op("act", lambda e: e.activation(out=ea[:], in_=ea[:], func=AF.Exp), reads=[ea], writes=[ea])
                    yield
                    b.op("dve", lambda e: e.scalar_tensor_tensor(out=art[:, 0:L], in0=KK_[:], scalar=-1.0, in1=ea[:], op0=ALU.mult, op1=ALU.mult),
                         reads=[KK_, ea], writes=[art])
                    b.op("pool", lambda e: e.tensor_tensor(out=art[:, L:2 * L], in0=R_[:], in1=ep[:], op=ALU.mult), reads=[R_, ep], writes=[art])
                    b.op("dve", lambda e: e.tensor_tensor(out=kh[:], in0=KD_[:], in1=em[:], op=ALU.mult), reads=[KD_, em], writes=[kh])
                    b.op("pool", lambda e: e.tensor_tensor(out=bh[:], in0=BD_[:], in1=em[:], op=ALU.mult), reads=[BD_, em], writes=[bh])
                    yield
                    for qi, srcq in enumerate((V_, kh, bh)):
                        b.op("pe", lambda e, qi=qi, srcq=srcq: e.transpose(bA[:, qi * L:(qi + 1) * L], srcq[:], self.ident[:, :]),
                             reads=[srcq, self.ident], writes=[bA], pe_acc=True)
                    b.op("act", lambda e: e.copy(out=vt[:], in_=bA[:, 0:L]), reads=[bA], writes=[vt])
                    b.op("act", lambda e: e.copy(out=kht[:], in_=bA[:, L:2 * L]), reads=[bA], writes=[kht])
                    b.op("act", lambda e: e.copy(out=bht[:], in_=bA[:, 2 * L:3 * L]), reads=[bA], writes=[bht])
                    for hh in range(2):
                        hs = slice(hh * 64, (hh + 1) * 64)
                        b.op("pe", lambda e, hh=hh, hs=hs: e.matmul(bB[:, hh, :], kh[hs, :], art[hs, :], start=True, stop=True),
                             reads=[kh, art], writes=[bB], pe_acc=True)
                        b.op("pe", lambda e, hh=hh, hs=hs: e.matmul(bC[:, hh * 256:(hh + 1) * 256], bh[hs, :], art[hs, :], start=True, stop=True),
                             reads=[bh, art], writes=[bC], pe_acc=True)
                        b.op("pe", lambda e, hh=hh, hs=hs: e.matmul(bD[:, hh * L:(hh + 1) * L], art[hs, 0:L], bh[hs, :], start=True, stop=True),
                             reads=[bh, art], writes=[bD], pe_acc=True)
                    yield
                    pp = PPa[d]
                    b.op("dve", lambda e: e.tensor_tensor(out=akr[:], in0=bB[:], in1=MK[d][:], op=ALU.mult), reads=[bB, MK[d]], writes=[akr])
                    b.op("dve", lambda e: e.tensor_tensor(out=nrb[:], in0=bC[:].rearrange("p (h x) -> p h x", h=2), in1=MK[d][:], op=ALU.mult),
                         reads=[bC, MK[d]], writes=[nrb])
                    b.op("dve", lambda e: e.tensor_tensor(out=pp[:, :, 0:L], in0=bD[:, 0:2 * L].rearrange("p (h x) -> p h x", h=2), in1=MN[d][:], op=ALU.mult),
                         reads=[bD, MN[d]], writes=[pp])
                    b.op("pool", lambda e: e.tensor_copy(out=pp[:, :, L:2 * L], in_=nrb[:, :, 0:L]), reads=[nrb], writes=[pp])
                    yield
                    pt_, po_ = pxs[0]
                    for hh in range(2):
                        hs = slice(hh * 64, (hh + 1) * 64)
                        b.op("pe", lambda e, hh=hh, hs=hs: e.matmul(pt_[:, po_ + hh * 64:po_ + (hh + 1) * 64], art[hs, 0:L], S_[hs, :], start=True, stop=False),
                             reads=[art, S_], writes=[pt_], pe_acc=True)
                        b.op("pe", lambda e, hh=hh, hs=hs: e.matmul(pt_[:, po_ + hh * 64:po_ + (hh + 1) * 64], akr[:, hh, 0:L], vt[:, hs], start=False, stop=True),
                             reads=[akr, vt], writes=[pt_], pe_acc=True)
                    xc = XXa[d]
                    b.op("act", lambda e: e.copy(out=xc[:].rearrange("p h v -> p (h v)"), in_=pt_[:, po_:po_ + 128]), reads=[pt_], writes=[xc])
                    yield
                    cur = pp
                    for lev in range(7):
                        pt_, po_ = pxs[(lev + 1) % 2]
                        for hh in range(2):
                            b.op("pe", lambda e, hh=hh, cur=cur, xc=xc, pt_=pt_, po_=po_: e.matmul(pt_[:, po_ + hh * 64:po_ + (hh + 1) * 64], cur[:, hh, L:2 * L], xc[:, hh, :],
                                                                                     start=True, stop=True), reads=[cur, xc], writes=[pt_], pe_acc=True)
                        if lev < 6:
                            for hh in range(2):
                                b.op("pe", lambda e, hh=hh, cur=cur: e.matmul(bB[:, hh, 0:L], cur[:, hh, L:2 * L], cur[:, hh, 0:L], start=True, stop=True),
                                     reads=[cur], writes=[bB], pe_acc=True)
                                b.op("pe", lambda e, hh=hh, cur=cur: e.matmul(bB[:, hh, L:2 * L], cur[:, hh, 0:L], cur[:, hh, L:2 * L], start=True, stop=True),
                                     reads=[cur], writes=[bB], pe_acc=True)
                        yield
                        xn = XXb[d] if xc is XXa[d] else XXa[d]
                        b.op("dve", lambda e, xn=xn, xc=xc, pt_=pt_, po_=po_: e.tensor_tensor(out=xn[:].rearrange("p h v -> p (h v)"), in0=pt_[:, po_:po_ + 128],
                                                                                   in1=xc[:].rearrange("p h v -> p (h v)"), op=ALU.add),
                             reads=[pt_, xc], writes=[xn])
                        if lev < 6:
                            nxt_ = PPb[d] if cur is PPa[d] else PPa[d]
                            b.op("act", lambda e, nxt_=nxt_: e.copy(out=nxt_[:], in_=bB[:]), reads=[bB], writes=[nxt_])
                            cur = nxt_
                        xc = xn
                        yield
                    U = xc
                    for hh in range(2):
                        hs = slice(hh * 64, (hh + 1) * 64)
                        b.op("pe", lambda e, hs=hs: e.matmul(bC[hs, 0:L], S_[hs, :], art[hs, L:2 * L], start=True, stop=False),
                             reads=[S_, art], writes=[bC], pe_acc=True)
                        b.op("pe", lambda e, hs=hs, hh=hh: e.matmul(bC[hs, 0:L], vt[:, hs], akr[:, hh, L:2 * L], start=False, stop=False),
                             reads=[vt, akr], writes=[bC], pe_acc=True)
                        b.op("pe", lambda e, hs=hs, hh=hh, U=U: e.matmul(bC[hs, 0:L], U[:, hh, :], nrb[:, hh, L:2 * L], start=False, stop=True),
                             reads=[U, nrb], writes=[bC], pe_acc=True)
                    yield
                    yo = YO[d]
                    b.op("act", lambda e: e.copy(out=yo[:], in_=bC[:, 0:L]), reads=[bC], writes=[yo])
                    b.dma(self.yT[d][rows, t0:t0 + L], yo[:], reads=[yo], q="pool")
                    for hh in range(2):
                        hs = slice(hh * 64, (hh + 1) * 64)
                        b.op("pe", lambda e, hs=hs: e.matmul(bC[hs, 256:320], self.ident[hs, hs], S_[hs, :], start=True, stop=False),
                             reads=[self.ident, S_], writes=[bC], pe_acc=True)
                        b.op("pe", lambda e, hs=hs: e.matmul(bC[hs, 256:320], kht[:, hs], vt[:, hs], start=False, stop=False),
                             reads=[kht, vt], writes=[bC], pe_acc=True)
                        b.op("pe", lambda e, hs=hs, hh=hh, U=U: e.matmul(bC[hs, 256:320], bht[:, hs], U[:, hh, :], start=False, stop=True),
                             reads=[bht, U], writes=[bC], pe_acc=True)
                    yield
                    b.op("dve", lambda e: e.tensor_scalar(out=S_[:], in0=bC[:, 256:320], scalar1=ep[:, lastc:lastc + 1], scalar2=None, op0=ALU.mult),
                         reads=[bC, ep], writes=[S_])
                    yield

            for fc in range(DC):
                gens = [stream(fc, 0), stream(fc, 1)]
                while gens:
                    for g in list(gens):
                        try:
                            next(g)
                        except StopIteration:
                            gens.remove(g)
            b.barrier()

    def odd_out(self, l, j, src):
        b = self.b
        self.psi = 0
        self.wi = 0
        N = 256
        with contextlib.ExitStack() as st:
            lnw = b.sb(st, "lnw", [128, 16])
            lnb = b.sb(st, "lnb", [128, 16])
            rk = b.sb(st, "rk", [128, 16])
            blk = b.sb(st, "blk", [128, 128])
            epsl = b.sb(st, "epsl", [128, 1])
            b.dma(lnw[:], self.lnwT[:, j, :], writes=[lnw])
            b.dma(lnb[:], self.lnbT[:, j, :], writes=[lnb])
            b.dma(rk[:], self.rkT[:, j, :], writes=[rk])
            b.dma(blk[:], self.blk64[:, :], writes=[blk])
            b.op("dve", lambda e: e.memset(epsl[:], 64e-5), writes=[epsl])
            vsrc = self.vfT if j == 0 else self.vT
            MT = self.subres(b.sb(st, "MT", [128, DC, N]), DC)
            xt = b.sb(st, "xt", [128, DC, N])
            def dbl(name):
                return [b.sb(st, "%s%d" % (name, i), [128, N]) for i in range(2)]
            Y0, Y1, RR, K0, K1, VV, GG, TA, TB = [dbl(nm) for nm in ("Y0", "Y1", "RR", "K0", "K1", "VV", "GG", "TA", "TB")]
            pq = [b.ps(st, "pq%d" % i, [128, 512]) for i in range(3)]
            wts = [b.sb(st, "w%d" % i, [128, 16, 128]) for i in range(3)]
            pss = [b.ps(st, "pg%d" % i, [128, 512]) for i in range(3)]
            tiles = [(0, TC, 1)] + [(TC + i * N, N, 0) for i in range(TL // N)]
            it = 0
            qi = 0
            for (t0, n, isctx) in tiles:
                b.dma(xt[:], src[:, t0:t0 + n].rearrange("(c p) t -> p c t", p=128), writes=[xt])
                for c in range(DC):
                    i2 = it % 2
                    it += 1
                    rows = slice(c * 128, (c + 1) * 128)
                    y0, y1, rr, k0, k1, vv, gg, ta, tb = Y0[i2], Y1[i2], RR[i2], K0[i2], K1[i2], VV[i2], GG[i2], TA[i2], TB[i2]
                    for (tl_, src_) in ((y0, self.yT[0]), (y1, self.yT[1]), (rr, self.rT), (k0, self.kdT[0]), (k1, self.kdT[1]), (vv, vsrc), (gg, self.gT)):
                        b.dma(tl_[:], src_[rows, t0:t0 + n], writes=[tl_])
                    b.op("dve", lambda e, y0=y0, y1=y1: e.tensor_tensor(out=y0[:], in0=y0[:], in1=y1[:], op=ALU.add), reads=[y0, y1], writes=[y0])
                    p1 = pq[qi % 3]; qi += 1
                    b.op("pe", lambda e, p1=p1, y0=y0: e.matmul(p1[:, 0:n], blk[:, :], y0[:], start=True, stop=True), reads=[blk, y0], writes=[p1], pe_acc=True)
                    b.op("dve", lambda e, p1=p1, y0=y0: e.scalar_tensor_tensor(out=y0[:], in0=p1[:, 0:n], scalar=-1.0 / 64, in1=y0[:], op0=ALU.mult, op1=ALU.add),
                         reads=[p1, y0], writes=[y0])
                    b.op("act", lambda e, ta=ta, y0=y0: e.activation(out=ta[:], in_=y0[:], func=AF.Square), reads=[y0], writes=[ta])
                    p2 = pq[qi % 3]; qi += 1
                    b.op("pe", lambda e, p2=p2, ta=ta: e.matmul(p2[:, 0:n], blk[:, :], ta[:], start=True, stop=True), reads=[blk, ta], writes=[p2], pe_acc=True)
                    b.op("act", lambda e, ta=ta, p2=p2: e.activation(out=ta[:], in_=p2[:, 0:n], func=AF.Sqrt, scale=1.0 / 64, bias=epsl[:, 0:1]),
                         reads=[p2, epsl], writes=[ta])
                    b.op("dve", lambda e, ta=ta: e.reciprocal(out=ta[:], in_=ta[:]), reads=[ta], writes=[ta])
                    b.op("dve", lambda e, ta=ta, y0=y0: e.tensor_tensor(out=y0[:], in0=y0[:], in1=ta[:], op=ALU.mult), reads=[y0, ta], writes=[y0])
                    b.op("act", lambda e, y0=y0, c=c: e.activation(out=y0[:], in_=y0[:], func=AF.Identity, scale=lnw[:, c:c + 1], bias=lnb[:, c:c + 1]),
                         reads=[y0, lnw, lnb], writes=[y0])
                    b.op("pool", lambda e, k0=k0, k1=k1: e.tensor_tensor(out=k0[:], in0=k0[:], in1=k1[:], op=ALU.add), reads=[k0, k1], writes=[k0])
                    b.op("pool", lambda e, k0=k0, rr=rr: e.tensor_tensor(out=k0[:], in0=k0[:], in1=rr[:], op=ALU.mult), reads=[k0, rr], writes=[k0])
                    b.op("pool", lambda e, k0=k0, c=c: e.tensor_scalar(out=k0[:], in0=k0[:], scalar1=rk[:, c:c + 1], scalar2=None, op0=ALU.mult), reads=[k0, rk], writes=[k0])
                    p3 = pq[qi % 3]; qi += 1
                    b.op("pe", lambda e, p3=p3, k0=k0: e.matmul(p3[:, 0:n], blk[:, :], k0[:], start=True, stop=True), reads=[blk, k0], writes=[p3], pe_acc=True)
                    b.op("dve", lambda e, tb=tb, p3=p3, vv=vv: e.tensor_tensor(out=tb[:], in0=p3[:, 0:n], in1=vv[:], op=ALU.mult), reads=[p3, vv], writes=[tb])
                    b.op("dve", lambda e, tb=tb, y0=y0: e.tensor_tensor(out=tb[:], in0=tb[:], in1=y0[:], op=ALU.add), reads=[tb, y0], writes=[tb])
                    b.op("pool", lambda e, tb=tb, gg=gg, c=c: e.tensor_tensor(out=rnd(MT[:, c, :]), in0=tb[:], in1=gg[:], op=ALU.mult), reads=[tb, gg], writes=[MT.sub[c]])

                def evac(jc, ps, isctx=isctx):
                    b.op("dve", lambda e: e.scalar_tensor_tensor(
                        out=xt[:, jc, :], in0=ps[:, 0:n], scalar=self.mod[:, l, 32 + jc, isctx:isctx + 1], in1=xt[:, jc, :],
                        op0=ALU.mult, op1=ALU.add), reads=[ps, self.mod, xt], writes=[xt])
                self.gemm(st, self.w_o[j], DC, DC, lambda kc: MT[:, kc, :], lambda kc: [MT.sub[kc]], n, evac, wts, pss)
                b.dma(self.xs[:, t0:t0 + n].rearrange("(c p) t -> p c t", p=128), xt[:], reads=[xt], q="pool")
            b.barrier()

    def stage_odd(self, l, src):
        j = l // 2
        self.odd_norm(l, src)
        self.odd_proj(l, j)
        self.rwkv_scan(j)
        self.odd_out(l, j, src)

    def stage_even(self, l, src):
        b = self.b
        j = l // 2
        with contextlib.ExitStack() as lst:
            self.Gt = b.sb(lst, "Gt", [128, T // 128, 16])
            self.even_inproj(l, j, src)
            with contextlib.ExitStack() as st:
                cw = b.sb(st, "mcw", [128, 3, 16])
                cb = b.sb(st, "mcb", [128, 16])
                b.dma(cw[:], self.mcwT[:, j, :, :], writes=[cw])
                b.dma(cb[:], self.mcbT[:, j, :], writes=[cb])
                self.conv_pass(self.qkT, self.qkcT, 16, cw, cb, True)
            self.s5(j)
            self.mlstm(j)
        self.even_out(l, j, src)

def relay(w):
    w = np.asarray(w, np.float32)
    lead = w.shape[:-2]
    K_, N_ = w.shape[-2:]
    w = w.reshape(lead + (K_ // 128, 128, N_ // 128, 128))
    nd = len(lead)
    w = np.transpose(w, tuple(range(nd)) + (nd + 2, nd + 1, nd + 0, nd + 3))
    return np.ascontiguousarray(w)


def fm(v):
    v = np.asarray(v, np.float32)
    lead = v.shape[:-1]
    c = v.shape[-1] // 128
    return np.ascontiguousarray(np.moveaxis(v.reshape(lead + (c, 128)), -1, 0))


_PROG = {}


def get_prog(layers=DEPTH, mixers=True):
    key = (layers, mixers)
    if key not in _PROG:
        _PROG[key] = Prog(layers, mixers)
    return _PROG[key]


def make_inputs(p, x, c, ctx, c_ctx, ada_w, ada_b, norm_mix, norm_ffn, ffn_w_up, ffn_conv_w, ffn_conv_b, ffn_w_down,
                norm_final, **kw):
    B = x.shape[0]
    shared = {}
    shared["ada_w"] = np.ascontiguousarray(ada_w[:p.layers], np.float32)
    ab = fm(ada_b)
    shared["ada_bT"] = ab
    shared["nmixT"] = fm(norm_mix)
    shared["nffnT"] = fm(norm_ffn)
    shared["nfinT"] = fm(norm_final)
    shared["w_up"] = relay(ffn_w_up[:p.layers])
    shared["convwT"] = fm(ffn_conv_w)
    shared["convbT"] = fm(ffn_conv_b)
    shared["w_down"] = relay(ffn_w_down[:p.layers])
    shared["ones"] = np.ones((128, 128), np.float32)
    if p.mixers:
        NE = p.NE
        shared["w_in"] = np.ascontiguousarray(kw["ev_w_in"][:NE], np.float32)
        shared["binT"] = fm(kw["ev_b_in"][:, :5120])
        shared["bin_row"] = np.ascontiguousarray(kw["ev_b_in"][None], np.float32)
        shared["w_out"] = relay(kw["ev_w_out"][:NE])
        shared["w_in_r"] = relay(kw["ev_w_in"][:NE, :, :3072])
        shared["mcwT"] = fm(kw["ml_conv_w"])
        shared["mcbT"] = fm(kw["ml_conv_b"])
        shared["mlnorm"] = np.ascontiguousarray(np.broadcast_to(kw["ml_norm"][None], (128, 2, 1024)), np.float32)
        shared["w_glu"] = relay(kw["s5_w_glu"][:NE])
        shared["bgluT"] = fm(kw["s5_b_glu"])
        if p.NO > 0:
            NO = p.NO
            f32 = lambda a: np.ascontiguousarray(a, np.float32)
            shared["muT"] = fm(kw["rw_mu"])
            shared["w_r"] = relay(kw["rw_w_r"][:NO]); shared["w_k"] = relay(kw["rw_w_k"][:NO])
            shared["w_v"] = relay(kw["rw_w_v"][:NO]); shared["w_o"] = relay(kw["rw_w_o"][:NO])
            shared["w0T"] = fm(kw["rw_w0"]); shared["a0T"] = fm(kw["rw_a0"]); shared["v0T"] = fm(kw["rw_v0"])
            for nm in ("rw_w1", "rw_w2", "rw_a1", "rw_a2", "rw_v1", "rw_v2", "rw_g1", "rw_g2"):
                shared[nm] = f32(kw[nm])
            shared["kkwT"] = fm(kw["rw_k_k"]); shared["kawT"] = fm(kw["rw_k_a"])
            shared["rkT"] = fm(kw["rw_r_k"].reshape(2, 2048))
            shared["lnwT"] = fm(kw["rw_ln_w"]); shared["lnbT"] = fm(kw["rw_ln_b"])
            bb_ = np.zeros((128, 128), np.float32)
            bb_[:64, :64] = 1.0
            bb_[64:, 64:] = 1.0
            shared["blk64"] = bb_
        dup = lambda a: np.concatenate([a, a], axis=0)
        lr = np.transpose(kw["s5_lam_re"], (3, 0, 1, 2))
        li = np.transpose(kw["s5_lam_im"], (3, 0, 1, 2))
        shared["s5lam"] = np.ascontiguousarray(np.stack([dup(lr), dup(li)], axis=1), np.float32)
        shared["s5ls"] = np.ascontiguousarray(np.broadcast_to(kw["s5_log_step"][None], (128, 2, 2, 64)), np.float32)
        bre = np.transpose(kw["s5_b_re"], (3, 0, 1, 2, 4))
        bim = np.transpose(kw["s5_b_im"], (3, 0, 1, 2, 4))
        shared["s5BX"] = np.ascontiguousarray(np.concatenate([bre, bim], axis=0), np.float32)
        shared["s5BY"] = np.ascontiguousarray(np.concatenate([bim, bre], axis=0), np.float32)
        cre = np.transpose(kw["s5_c_re"], (4, 0, 1, 2, 3))
        cim = np.transpose(kw["s5_c_im"], (4, 0, 1, 2, 3))
        shared["s5CX"] = np.ascontiguousarray(np.concatenate([cre, cim], axis=0), np.float32)
        sg = np.ones((128, 2), np.float32)
        sg[:64, 0] = -1.0
        sg[64:, 1] = -1.0
        shared["s5sgn"] = sg
        shared["s5gmask"] = (np.arange(128)[:, None] // 16 == np.arange(8)[None, :]).astype(np.float32)
        psw = np.zeros((128, 128), np.float32)
        for m in range(64):
            psw[m + 64, m] = 1.0
            psw[m, m + 64] = -1.0
        shared["s5psw"] = psw
        shared["s5dT"] = fm(kw["s5_d"])
        ii = np.arange(128)
        up = (ii[:, None] <= ii[None, :]).astype(np.float32)
        shared["masks"] = np.ascontiguousarray(np.stack([up, up.T, (ii[:, None] < ii[None, :]).astype(np.float32),
                                                         (ii[:, None] > ii[None, :]).astype(np.float32)], axis=1))
        shared["ident"] = np.eye(128, dtype=np.float32)
    maps = []
    for bi in range(B):
        m = dict(shared)
        xt = np.concatenate([ctx[bi], x[bi]], axis=0).T
        m["xT"] = np.ascontiguousarray(xt, np.float32)
        cc = np.stack([c[bi], c_ctx], axis=-1)
        m["cT"] = np.ascontiguousarray(cc.reshape(DC, 128, 2).transpose(1, 0, 2), np.float32)
        for k in p.inputs:
            assert tuple(m[k].shape) == tuple(p.inputs[k]), (k, m[k].shape, p.inputs[k])
        maps.append({k: m[k] for k in p.inputs})
    return maps


def kernel(**inputs):
    inputs = {k: np.asarray(v) for k, v in inputs.items()}
    p = get_prog()
    maps = make_inputs(p, **inputs)
    B = len(maps)
    res = run_bass_kernel_spmd(p.nc, maps, core_ids=list(range(B)))
    outs = [np.asarray(r["outT"]).T for r in res.results]
    return np.ascontiguousarray(np.stack(outs, axis=0).astype(np.float32))
```
